# Optimizing a Trainium2 kernel written in Bass

```python
import math
import jax
import jax.numpy as jnp
from jax import lax
import numpy as np

D_MODEL = 1024
BATCH = 8
SEQ = 2048
DEPTH = 2

CTX_LEN = 256
GRID_W = 64
N_MIXERS = 4
GROUP = D_MODEL // N_MIXERS
HEAD_DIM = 64
GROUP_HEADS = GROUP // HEAD_DIM
D_FF = ((8 * D_MODEL // 3 + 255) // 256) * 256
N_MOD = 9
NORM_EPS = 1e-6

HY_ORDER = 2
HY_SHORT_W = 3
HY_BANDS = 16
HY_EMB = 1 + 2 * HY_BANDS
HY_FILTER_HIDDEN = 64
HY_TARGET = 1e-2
HY_SHORT_DECAY_PCT = 0.3
HY_LONG_DECAY_PCT = 1.5

NA_WIN_ROWS = 8
NA_WIN_COLS = 16

DN_SHORT_W = 3
DN_CHUNK = 64

RW_DECAY_RANK = 32
RW_AAA_RANK = 32
RW_GATE_RANK = 64
RW_LN_EPS = 64e-5

HY_COLS = 3 * GROUP
NA_COLS = 3 * GROUP
DN_COLS = 4 * GROUP + 4 * GROUP_HEADS
RW_COLS = 3 * GROUP + RW_DECAY_RANK + RW_AAA_RANK + RW_GATE_RANK
P_TOTAL = HY_COLS + NA_COLS + DN_COLS + RW_COLS

kernel_name = 'hybrid_hyena_natten_deltanet_rwkv7_dit_block'


def rms_norm(x, w, eps=NORM_EPS):
    xf = x.astype(jnp.float32)
    y = xf * lax.rsqrt(jnp.mean(xf * xf, axis=-1, keepdims=True) + eps)
    return (y * w.astype(jnp.float32)).astype(x.dtype)


def l2_normalize(x, eps=1e-6):
    xf = x.astype(jnp.float32)
    return xf * lax.rsqrt(jnp.sum(xf * xf, axis=-1, keepdims=True) + eps)


def _rev(t):
    return jnp.flip(t, axis=1)


def _same(t):
    return t


def centred_dwconv(u, w):
    k_w = w.shape[0]
    pad = k_w // 2
    n = u.shape[1]
    up = jnp.pad(u, ((0, 0), (pad, pad), (0, 0)))
    out = up[:, 0:n] * w[0]
    for i in range(1, k_w):
        out = out + up[:, i:i + n] * w[i]
    return out


def token_shift_centred(u, mu):
    prev = jnp.pad(u, ((0, 0), (1, 0), (0, 0)))[:, :-1]
    nxt = jnp.pad(u, ((0, 0), (0, 1), (0, 0)))[:, 1:]
    return u + mu[0] * (prev - u) + mu[1] * (nxt - u)


def swiglu(h, w_gu, w_down):
    gate, up = jnp.split(h @ w_gu, 2, axis=-1)
    return (jax.nn.silu(gate) * up) @ w_down


def modulation(cond, w_mod, b_mod):
    m = jax.nn.silu(cond) @ w_mod + b_mod
    return m.reshape(cond.shape[:-1] + (N_MOD, D_MODEL))


def adaln(x, norm_w, mod, i):
    return rms_norm(x, norm_w) * (1.0 + mod[:, 3 * i + 1, None]) + mod[:, 3 * i, None]


def gate_of(mod, i):
    return mod[:, 3 * i + 2, None]


def hyena_filters(n, f_w1, f_b1, f_w2, f_b2, f_w3, f_freq):
    f32 = jnp.float32
    t = jnp.linspace(0.0, 1.0, n, dtype=f32)[:, None]
    ang = 2.0 * math.pi * jnp.arange(n, dtype=f32)[:, None] / n
    bands = jnp.linspace(1e-4, HY_BANDS - 1, HY_BANDS, dtype=f32)[None]
    z = jnp.concatenate([t, jnp.cos(bands * ang), -jnp.sin(bands * ang)], axis=-1)
    freq = f_freq.astype(f32)
    hid = jnp.sin(freq * (z @ f_w1.astype(f32) + f_b1.astype(f32)))
    hid = jnp.sin(freq * (hid @ f_w2.astype(f32) + f_b2.astype(f32)))
    h = (hid @ f_w3.astype(f32)).reshape(n, 2, HY_ORDER, GROUP)
    max_decay = math.log(HY_TARGET) / HY_SHORT_DECAY_PCT
    min_decay = math.log(HY_TARGET) / HY_LONG_DECAY_PCT
    deltas = jnp.linspace(min_decay, max_decay, HY_ORDER * GROUP, dtype=f32).reshape(HY_ORDER, GROUP)
    h = h * jnp.exp(-t[:, :, None, None] * jnp.abs(deltas))
    k = jnp.concatenate([h[:, 0], jnp.zeros((1, HY_ORDER, GROUP), f32), h[:0:-1, 1]], axis=0)
    return k / jnp.sum(jnp.abs(k), axis=0, keepdims=True)


def fft_longconv(u, k, bias):
    n = u.shape[1]
    uf = jnp.fft.rfft(u.astype(jnp.float32), n=2 * n, axis=1)
    kf = jnp.fft.rfft(k, n=2 * n, axis=0)
    y = jnp.fft.irfft(uf * kf[None], n=2 * n, axis=1)[:, :n]
    return (y + u.astype(jnp.float32) * bias.astype(jnp.float32)).astype(u.dtype)


def hyena_mixer(slab, conv_w, f_w1, f_b1, f_w2, f_b2, f_w3, f_freq, bias):
    u = centred_dwconv(slab, conv_w)
    v, x1, x2 = jnp.split(u, 3, axis=-1)
    k = hyena_filters(slab.shape[1], f_w1, f_b1, f_w2, f_b2, f_w3, f_freq)
    z = x1 * fft_longconv(v, k[:, 0], bias[0])
    return x2 * fft_longconv(z, k[:, 1], bias[1])


def na_qkv(slab, q_norm, k_norm):
    b, n, _ = slab.shape
    q, k, v = (t.reshape(b, n, GROUP_HEADS, HEAD_DIM) for t in jnp.split(slab, 3, axis=-1))
    return rms_norm(q, q_norm), rms_norm(k, k_norm), v


def na_latent(q, k, v, kc, vc, rpb):
    b, n, h, d = q.shape
    rows = n // GRID_W
    wr = min(NA_WIN_ROWS, rows)
    scale = d ** -0.5
    qg = q.reshape(b, rows, GRID_W, h, d)
    kg = k.reshape(b, rows, GRID_W, h, d)
    vg = v.reshape(b, rows, GRID_W, h, d)
    r = jnp.arange(rows)
    row_idx = jnp.clip(r - wr // 2, 0, rows - wr)[:, None] + jnp.arange(wr)[None]
    k_blk = kg[:, row_idx]
    v_blk = vg[:, row_idx]
    col = jnp.arange(GRID_W)
    col_start = jnp.clip(col - NA_WIN_COLS // 2, 0, GRID_W - NA_WIN_COLS)
    in_win = (col[None, :] >= col_start[:, None]) & (col[None, :] < col_start[:, None] + NA_WIN_COLS)
    d_row = row_idx - r[:, None]
    d_col = jnp.clip(col[None, :] - col[:, None], 1 - NA_WIN_COLS, NA_WIN_COLS - 1)
    bias = rpb[:, (d_row + NA_WIN_ROWS - 1)[:, None, :, None], (d_col + NA_WIN_COLS - 1)[None, :, None, :]]
    s_loc = jnp.einsum('brchd,brwjhd->bhrcwj', qg, k_blk).astype(jnp.float32) * scale
    s_loc = jnp.where(in_win[:, None, :], s_loc + bias[None].astype(jnp.float32), -jnp.inf)
    s_ctx = jnp.einsum('brchd,bmhd->bhrcm', qg, kc).astype(jnp.float32) * scale
    n_loc = wr * GRID_W
    p = jax.nn.softmax(jnp.concatenate([s_loc.reshape(b, h, rows, GRID_W, n_loc), s_ctx], axis=-1), axis=-1)
    p = p.astype(v.dtype)
    p_loc = p[..., :n_loc].reshape(b, h, rows, GRID_W, wr, GRID_W)
    o = jnp.einsum('bhrcwj,brwjhd->brchd', p_loc, v_blk) + jnp.einsum('bhrcm,bmhd->brchd', p[..., n_loc:], vc)
    return o.reshape(b, n, h * d)


def na_context(qc, kc, vc):
    b, m, h, d = qc.shape
    s = jnp.einsum('bqhd,bkhd->bhqk', qc, kc).astype(jnp.float32) * d ** -0.5
    p = jax.nn.softmax(s, axis=-1).astype(vc.dtype)
    return jnp.einsum('bhqk,bkhd->bqhd', p, vc).reshape(b, m, h * d)


def gated_delta_chunked(q, k, v, beta, g, s0):
    f32 = jnp.float32
    q, k, v, beta, g = (t.astype(f32) for t in (q, k, v, beta, g))
    b, n, h, dk = q.shape
    dv = v.shape[-1]
    n_chunks = n // DN_CHUNK

    def blocks(t):
        t = t.reshape((b, n_chunks, DN_CHUNK) + t.shape[2:])
        return jnp.moveaxis(t, 3, 1)

    q = blocks(q) * dk ** -0.5
    k, v, beta, g = blocks(k), blocks(v), blocks(beta), blocks(g)
    gcum = jnp.cumsum(g, axis=-1)
    lower = jnp.tril(jnp.ones((DN_CHUNK, DN_CHUNK), bool))
    strict = jnp.tril(jnp.ones((DN_CHUNK, DN_CHUNK), bool), -1)
    decay = jnp.exp(jnp.where(lower, gcum[..., :, None] - gcum[..., None, :], -jnp.inf))
    kb = k * beta[..., None]
    m = jnp.where(strict, jnp.einsum('bhnik,bhnjk->bhnij', kb, k) * decay, 0.0)
    rhs = jnp.concatenate([v * beta[..., None], kb * jnp.exp(gcum)[..., None]], axis=-1)
    sol = lax.linalg.triangular_solve(m, rhs, left_side=True, lower=True, unit_diagonal=True)
    u, w = sol[..., :dv], sol[..., dv:]
    attn = jnp.einsum('bhnik,bhnjk->bhnij', q, k) * decay
    q_dec = q * jnp.exp(gcum)[..., None]
    k_dec = k * jnp.exp(gcum[..., -1:] - gcum)[..., None]
    g_last = jnp.exp(gcum[..., -1])

    def step(s, xs):
        u_i, w_i, attn_i, q_i, k_i, gl_i = xs
        v_new = u_i - jnp.einsum('bhck,bhkv->bhcv', w_i, s)
        o_i = jnp.einsum('bhck,bhkv->bhcv', q_i, s) + jnp.einsum('bhij,bhjv->bhiv', attn_i, v_new)
        s = s * gl_i[..., None, None] + jnp.einsum('bhck,bhcv->bhkv', k_i, v_new)
        return s, o_i

    xs = tuple(jnp.moveaxis(t, 2, 0) for t in (u, w, attn, q_dec, k_dec, g_last))
    s_final, o = lax.scan(step, s0.astype(f32), xs)
    o = jnp.moveaxis(o, 0, 2)
    return jnp.moveaxis(o, 1, 3).reshape(b, n, h, dv), s_final


def deltanet_inputs(slab, conv_w, a_log, dt_bias):
    b, n, _ = slab.shape
    qkv = jax.nn.silu(centred_dwconv(slab[..., :3 * GROUP], conv_w))
    q, k, v = (t.reshape(b, n, GROUP_HEADS, HEAD_DIM) for t in jnp.split(qkv, 3, axis=-1))
    z = slab[..., 3 * GROUP:4 * GROUP]
    ba = slab[..., 4 * GROUP:].astype(jnp.float32).reshape(b, n, 2, 2, GROUP_HEADS)
    beta = jax.nn.sigmoid(ba[:, :, :, 0])
    g = -jnp.exp(a_log.astype(jnp.float32)) * jax.nn.softplus(ba[:, :, :, 1] + dt_bias.astype(jnp.float32))
    return l2_normalize(q), l2_normalize(k), v, z, beta, g


def deltanet_gated_out(o, z, norm_w):
    b, n = z.shape[:2]
    zh = z.reshape(b, n, GROUP_HEADS, HEAD_DIM).astype(jnp.float32)
    return (rms_norm(o, norm_w) * jax.nn.silu(zh)).reshape(b, n, GROUP)


def deltanet_mixer(slab, slab_c, conv_w, a_log, dt_bias, norm_w, need_ctx):
    q, k, v, z, beta, g = deltanet_inputs(slab, conv_w, a_log, dt_bias)
    qc, kc, vc, zc, betac, gc = deltanet_inputs(slab_c, conv_w, a_log, dt_bias)
    b = slab.shape[0]
    o_lat, o_ctx = [], []
    for d in range(2):
        orient = _rev if d == 1 else _same
        s0 = jnp.zeros((b, GROUP_HEADS, HEAD_DIM, HEAD_DIM), jnp.float32)
        oc_d, s_ctx = gated_delta_chunked(*(orient(t) for t in (qc, kc, vc, betac[:, :, d], gc[:, :, d])), s0)
        ol_d, _ = gated_delta_chunked(*(orient(t) for t in (q, k, v, beta[:, :, d], g[:, :, d])), s_ctx)
        o_lat.append(orient(ol_d))
        o_ctx.append(orient(oc_d))
    y = deltanet_gated_out(o_lat[0] + o_lat[1], z, norm_w)
    yc = deltanet_gated_out(o_ctx[0] + o_ctx[1], zc, norm_w) if need_ctx else None
    return y, yc


def rwkv7_scan(r, w, k, v, a, b, s0):
    def step(s, xs):
        r_t, w_t, k_t, v_t, a_t, b_t = xs
        s = (s * w_t[:, :, None, :]
             + jnp.einsum('bhvk,bhk->bhv', s, a_t)[..., None] * b_t[:, :, None, :]
             + v_t[..., None] * k_t[:, :, None, :])
        return s, jnp.einsum('bhvk,bhk->bhv', s, r_t)
    xs = tuple(jnp.moveaxis(t.astype(jnp.float32), 1, 0) for t in (r, w, k, v, a, b))
    s_final, y = lax.scan(step, s0, xs)
    return jnp.moveaxis(y, 0, 1), s_final


def rwkv_inputs(slab, mu, w0, w_up, a0, a_up, g_up, k_k, k_a):
    b, n, _ = slab.shape
    s = token_shift_centred(slab, mu).astype(jnp.float32)
    o1 = 3 * GROUP
    r, k, v, wd, ad, gd = jnp.split(
        s, [GROUP, 2 * GROUP, o1, o1 + RW_DECAY_RANK, o1 + RW_DECAY_RANK + RW_AAA_RANK], axis=-1)

    def heads(t):
        return t.reshape(b, n, GROUP_HEADS, HEAD_DIM)

    kk = l2_normalize(heads(k * k_k))
    dirs = []
    for d in range(2):
        w_log = -jax.nn.softplus(-(w0[d] + jnp.tanh(wd) @ w_up[d])) - 0.5
        a = jax.nn.sigmoid(a0[d] + ad @ a_up[d])
        dirs.append((heads(jnp.exp(-jnp.exp(w_log))), heads(k * (1.0 + (a - 1.0) * k_a)), heads(a)))
    g = jax.nn.sigmoid(gd) @ g_up
    return heads(r), heads(v), kk, g, dirs


def rwkv_output(y, r, v, k_dirs, g, r_k, ln_w, ln_b):
    b, n = y.shape[:2]
    mean = jnp.mean(y, axis=-1, keepdims=True)
    var = jnp.mean(jnp.square(y - mean), axis=-1, keepdims=True)
    yn = ((y - mean) * lax.rsqrt(var + RW_LN_EPS)).reshape(b, n, GROUP) * ln_w + ln_b
    bonus = (jnp.sum(r * (k_dirs[0] + k_dirs[1]) * r_k, axis=-1, keepdims=True) * v).reshape(b, n, GROUP)
    return (yn + bonus) * g


def rwkv_mixer(slab, slab_c, rw, need_ctx):
    mu, w0, w_up, a0, a_up, g_up, k_k, k_a, r_k, ln_w, ln_b = rw
    r, v, kk, g, dirs = rwkv_inputs(slab, mu, w0, w_up, a0, a_up, g_up, k_k, k_a)
    rc, vc, kkc, gc, dirs_c = rwkv_inputs(slab_c, mu, w0, w_up, a0, a_up, g_up, k_k, k_a)
    b = slab.shape[0]
    y_lat, y_ctx = [], []
    for d in range(2):
        orient = _rev if d == 1 else _same
        (w_l, k_l, a_l), (w_c, k_c, a_c) = dirs[d], dirs_c[d]
        s0 = jnp.zeros((b, GROUP_HEADS, HEAD_DIM, HEAD_DIM), jnp.float32)
        yc_d, s_ctx = rwkv7_scan(*(orient(t) for t in (rc, w_c, k_c, vc, -kkc, kkc * a_c)), s0)
        yl_d, _ = rwkv7_scan(*(orient(t) for t in (r, w_l, k_l, v, -kk, kk * a_l)), s_ctx)
        y_lat.append(orient(yl_d))
        y_ctx.append(orient(yc_d))
    y = rwkv_output(y_lat[0] + y_lat[1], r, v, [dirs[0][1], dirs[1][1]], g, r_k, ln_w, ln_b)
    if not need_ctx:
        return y, None
    yc = rwkv_output(y_ctx[0] + y_ctx[1], rc, vc, [dirs_c[0][1], dirs_c[1][1]], gc, r_k, ln_w, ln_b)
    return y, yc


def hybrid_mixer(h, hc, w_in, w_out, hy, na, dn, rw, need_ctx):
    cuts = [HY_COLS, HY_COLS + NA_COLS, HY_COLS + NA_COLS + DN_COLS]
    hy_s, na_s, dn_s, rw_s = jnp.split(h @ w_in, cuts, axis=-1)
    hy_c, na_c, dn_c, rw_c = jnp.split(hc @ w_in, cuts, axis=-1)
    q, k, v = na_qkv(na_s, na[0], na[1])
    qc, kc, vc = na_qkv(na_c, na[0], na[1])
    y_dn, yc_dn = deltanet_mixer(dn_s, dn_c, dn[0], dn[1], dn[2], dn[3], need_ctx)
    y_rw, yc_rw = rwkv_mixer(rw_s, rw_c, rw, need_ctx)
    groups = [hyena_mixer(hy_s, *hy), na_latent(q, k, v, kc, vc, na[2]), y_dn, y_rw]
    y = jnp.concatenate([t.astype(h.dtype) for t in groups], axis=-1) @ w_out
    if not need_ctx:
        return y, None
    groups_c = [hyena_mixer(hy_c, *hy), na_context(qc, kc, vc), yc_dn, yc_rw]
    yc = jnp.concatenate([t.astype(hc.dtype) for t in groups_c], axis=-1) @ w_out
    return y, yc


def setup_inputs(seed: int = 0) -> dict:
    key = jax.random.key(seed)
    keys = iter(jax.random.split(key, 48))
    f32 = jnp.float32

    def nrm(shape, scale):
        return scale * jax.random.normal(next(keys), shape, f32)

    def unif(shape, lo, hi):
        return jax.random.uniform(next(keys), shape, f32, lo, hi)

    nl, d, g, h, hd = DEPTH, D_MODEL, GROUP, GROUP_HEADS, HEAD_DIM
    dt = jnp.exp(unif((nl, 2, h), math.log(1e-3), math.log(1e-1)))
    return {
        'x': nrm((BATCH, SEQ, d), 1.0),
        'c': nrm((BATCH, d), 1.0),
        'ctx': nrm((BATCH, CTX_LEN, d), 1.0),
        'c_ctx': nrm((d,), 1.0),
        'w_mod': nrm((nl, d, N_MOD * d), 0.5 * d ** -0.5),
        'b_mod': nrm((nl, N_MOD * d), 0.02),
        'norm_w': 1.0 + nrm((nl, 3, d), 0.02),
        'ffn_w_gu': nrm((nl, 2, d, 2 * D_FF), d ** -0.5),
        'ffn_w_down': nrm((nl, 2, D_FF, d), D_FF ** -0.5),
        'w_in': nrm((nl, d, P_TOTAL), d ** -0.5),
        'w_out': nrm((nl, d, d), d ** -0.5),
        'hy_conv': nrm((nl, HY_SHORT_W, 3 * g), HY_SHORT_W ** -0.5),
        'hy_f_w1': nrm((nl, HY_EMB, HY_FILTER_HIDDEN), HY_EMB ** -0.5),
        'hy_f_b1': nrm((nl, HY_FILTER_HIDDEN), 0.02),
        'hy_f_w2': nrm((nl, HY_FILTER_HIDDEN, HY_FILTER_HIDDEN), HY_FILTER_HIDDEN ** -0.5),
        'hy_f_b2': nrm((nl, HY_FILTER_HIDDEN), 0.02),
        'hy_f_w3': nrm((nl, HY_FILTER_HIDDEN, 2 * HY_ORDER * g), HY_FILTER_HIDDEN ** -0.5),
        'hy_f_freq': 1.0 + nrm((nl, HY_FILTER_HIDDEN), 0.02),
        'hy_bias': nrm((nl, HY_ORDER, g), 0.1),
        'na_q_norm': 1.0 + nrm((nl, hd), 0.02),
        'na_k_norm': 1.0 + nrm((nl, hd), 0.02),
        'na_rpb': nrm((nl, h, 2 * NA_WIN_ROWS - 1, 2 * NA_WIN_COLS - 1), 0.1),
        'dn_conv': nrm((nl, DN_SHORT_W, 3 * g), DN_SHORT_W ** -0.5),
        'dn_a_log': jnp.log(unif((nl, 2, h), 1.0, 16.0)),
        'dn_dt_bias': dt + jnp.log(-jnp.expm1(-dt)),
        'dn_norm': 1.0 + nrm((nl, hd), 0.02),
        'rw_mu': unif((nl, 2, RW_COLS), 0.0, 0.5),
        'rw_w0': unif((nl, 2, g), -6.5, -1.5),
        'rw_w_up': nrm((nl, 2, RW_DECAY_RANK, g), 0.5 * RW_DECAY_RANK ** -0.5),
        'rw_a0': nrm((nl, 2, g), 0.1),
        'rw_a_up': nrm((nl, 2, RW_AAA_RANK, g), 0.5 * RW_AAA_RANK ** -0.5),
        'rw_g_up': nrm((nl, RW_GATE_RANK, g), RW_GATE_RANK ** -0.5),
        'rw_k_k': 0.85 + nrm((nl, g), 0.02),
        'rw_k_a': 1.0 + nrm((nl, g), 0.02),
        'rw_r_k': nrm((nl, h, hd), 0.1),
        'rw_ln_w': 1.0 + nrm((nl, g), 0.02),
        'rw_ln_b': nrm((nl, g), 0.02),
    }


def reference(x, c, ctx, c_ctx, w_mod, b_mod, norm_w, ffn_w_gu, ffn_w_down, w_in, w_out,
              hy_conv, hy_f_w1, hy_f_b1, hy_f_w2, hy_f_b2, hy_f_w3, hy_f_freq, hy_bias,
              na_q_norm, na_k_norm, na_rpb, dn_conv, dn_a_log, dn_dt_bias, dn_norm,
              rw_mu, rw_w0, rw_w_up, rw_a0, rw_a_up, rw_g_up, rw_k_k, rw_k_a, rw_r_k, rw_ln_w, rw_ln_b):
    xc = ctx
    for l in range(DEPTH):
        need_ctx = l < DEPTH - 1
        mod = modulation(c, w_mod[l], b_mod[l])
        mod_c = modulation(c_ctx, w_mod[l], b_mod[l])[None]
        x = x + 0.5 * gate_of(mod, 0) * swiglu(adaln(x, norm_w[l, 0], mod, 0), ffn_w_gu[l, 0], ffn_w_down[l, 0])
        xc = xc + 0.5 * gate_of(mod_c, 0) * swiglu(adaln(xc, norm_w[l, 0], mod_c, 0), ffn_w_gu[l, 0], ffn_w_down[l, 0])
        hy = (hy_conv[l], hy_f_w1[l], hy_f_b1[l], hy_f_w2[l], hy_f_b2[l], hy_f_w3[l], hy_f_freq[l], hy_bias[l])
        na = (na_q_norm[l], na_k_norm[l], na_rpb[l])
        dn = (dn_conv[l], dn_a_log[l], dn_dt_bias[l], dn_norm[l])
        rw = (rw_mu[l], rw_w0[l], rw_w_up[l], rw_a0[l], rw_a_up[l], rw_g_up[l], rw_k_k[l], rw_k_a[l],
              rw_r_k[l], rw_ln_w[l], rw_ln_b[l])
        y, yc = hybrid_mixer(adaln(x, norm_w[l, 1], mod, 1), adaln(xc, norm_w[l, 1], mod_c, 1),
                             w_in[l], w_out[l], hy, na, dn, rw, need_ctx)
        x = x + gate_of(mod, 1) * y
        x = x + 0.5 * gate_of(mod, 2) * swiglu(adaln(x, norm_w[l, 2], mod, 2), ffn_w_gu[l, 1], ffn_w_down[l, 1])
        if need_ctx:
            xc = xc + gate_of(mod_c, 1) * yc
            xc = xc + 0.5 * gate_of(mod_c, 2) * swiglu(adaln(xc, norm_w[l, 2], mod_c, 2), ffn_w_gu[l, 1], ffn_w_down[l, 1])
    return x
```

```python
import contextlib
import numpy as np
import concourse.bass as bass
import concourse.mybir as mybir
from concourse.bass_utils import run_bass_kernel_spmd

F32 = mybir.dt.float32
BF16 = mybir.dt.bfloat16
I32 = mybir.dt.int32
AF = mybir.ActivationFunctionType
ALU = mybir.AluOpType
AX = mybir.AxisListType

ENGS = ("tensor", "vector", "scalar", "gpsimd", "sync")
N_DSEM = 12
SEM_EPOCH = 1 << 40


class Tok:
    __slots__ = ("name", "w", "r")

    def __init__(self, name):
        self.name = name
        self.w = None
        self.r = []


class T:
    def __init__(self, k, name, t, space):
        self.k = k
        self.name = name
        self.t = t
        self.space = space
        self.tok = Tok(name)
        self.subs = {}

    def __getitem__(self, key):
        return V(self.t[key] if not isinstance(self.t, bass.AP) else self.t[key], [self.tok])

    def v(self):
        return self[:]

    def sub(self, key, idx):
        if key not in self.subs:
            self.subs[key] = Tok(f"{self.name}.{key}")
        return V(self.t[idx], [self.subs[key]])


class V:
    __slots__ = ("ap", "toks")

    def __init__(self, ap, toks):
        self.ap = ap
        self.toks = toks

    def __getitem__(self, key):
        return V(self.ap[key], self.toks)

    def re(self, s, **kw):
        return V(self.ap.rearrange(s, **kw), self.toks)

    def bc(self, shape):
        return V(self.ap.to_broadcast(shape), self.toks)

    def bitcast(self, dt):
        return V(self.ap.bitcast(dt), self.toks)

    def with_toks(self, toks):
        return V(self.ap, toks)


def _ap(x):
    return x.ap if isinstance(x, V) else x


class K:
    def __init__(self):
        self.nc = bass.Bass("TRN2", target_bir_lowering=False)
        self.stack = contextlib.ExitStack()
        self.q = {e: [] for e in ENGS}
        self.cnt = {e: 0 for e in ENGS}
        self.epoch = {e: 0 for e in ENGS}
        self.sems = {}
        self.seen = {e: {} for e in ENGS}
        self.dsem = {}
        self.dcnt = {}
        self.dq_i = {e: 0 for e in ENGS}
        self.uid = 0
        self.out_deps = []
        self.stage_stack = None

    def _name(self, n):
        self.uid += 1
        return f"{n}_{self.uid}"

    def sb(self, name, shape, dtype, stack=None):
        t = (stack or self.stage_stack or self.stack).enter_context(self.nc.sbuf_tensor(self._name(name), list(shape), dtype))
        return T(self, name, t, "sb")

    def ps(self, name, shape, dtype=F32, stack=None):
        t = (stack or self.stage_stack or self.stack).enter_context(self.nc.psum_tensor(self._name(name), list(shape), dtype))
        return T(self, name, t, "ps")

    def dram(self, name, shape, dtype, kind="Internal"):
        t = self.nc.dram_tensor(name, list(shape), dtype, kind=kind).ap()
        return T(self, name, t, "dram")

    def _sem(self, key):
        if key not in self.sems:
            self.sems[key] = self.stack.enter_context(self.nc.semaphore(self._name("s")))
        return self.sems[key]

    def _deps(self, reads, writes, pe_accum=False):
        deps = []
        for v in reads:
            for t in v.toks:
                if t.w is not None:
                    deps.append(t.w)
        for v in writes:
            for t in v.toks:
                if t.w is not None and not pe_accum:
                    deps.append(t.w)
                if not pe_accum:
                    deps.extend(t.r)
        return deps

    def _commit(self, me, reads, writes, pe_accum=False):
        for v in reads:
            for t in v.toks:
                t.r.append(me)
        for v in writes:
            for t in v.toks:
                t.w = me
                if not pe_accum:
                    t.r = []

    def _waits(self, eng, deps):
        need = {}
        for (sk, val) in deps:
            if self.seen[eng].get(sk, 0) >= val:
                continue
            if need.get(sk, 0) < val:
                need[sk] = val
        for sk, val in need.items():
            self.seen[eng][sk] = val
        return list(need.items())

    @contextlib.contextmanager
    def defer(self):
        prev = getattr(self, "_defer", None)
        lst = []
        self._defer = lst
        try:
            yield lst
        finally:
            self._defer = prev

    def replay(self, lists):
        lists = [l for l in lists if l]
        pos = [0] * len(lists)
        total = sum(len(l) for l in lists)
        for _ in range(total):
            j = min(range(len(lists)), key=lambda i: (pos[i] / len(lists[i])) if pos[i] < len(lists[i]) else 2.0)
            kind, a, kw = lists[j][pos[j]]
            pos[j] += 1
            getattr(self, kind)(*a, **kw)

    def op(self, eng, fn, reads=(), writes=(), pe_accum=False, same_eng_sync=True):
        if getattr(self, "_defer", None) is not None:
            self._defer.append(("op", (eng, fn), dict(reads=reads, writes=writes, pe_accum=pe_accum, same_eng_sync=same_eng_sync)))
            return None
        reads = [r for r in reads if isinstance(r, V)]
        writes = [w for w in writes if isinstance(w, V)]
        deps = self._deps(reads, writes, pe_accum)
        if eng == "tensor" or not same_eng_sync:
            deps = [d for d in deps if d[0][0] != eng or d[0][0] == "dma"]
        if self.cnt[eng] >= SEM_EPOCH:
            self.epoch[eng] += 1
            self.cnt[eng] = 0
        sk = (eng, self.epoch[eng])
        self._sem(sk)
        self.cnt[eng] += 1
        me = (sk, self.cnt[eng])
        waits = self._waits(eng, deps)
        self.q[eng].append((fn, waits, (sk, 1), self.cnt[eng]))
        self._commit(me, reads, writes, pe_accum)
        return me

    def dma(self, eng, out, in_, **kw):
        if getattr(self, "_defer", None) is not None:
            self._defer.append(("dma", (eng, out, in_), dict(kw)))
            return None
        deps = self._deps([in_], [out])
        i = self.dq_i[eng]
        self.dq_i[eng] += 1
        sk = ("dma", eng, i % N_DSEM)
        self._sem(sk)
        prev = self.dcnt.get(sk, 0)
        if prev:
            deps.append((sk, prev))
        self.dcnt[sk] = prev + 16
        me = (sk, prev + 16)
        waits = self._waits(eng, deps)
        o, a = _ap(out), _ap(in_)
        self.q[eng].append((lambda e: e.dma_start(out=o, in_=a, **kw), waits, (sk, 16), None))
        self._commit(me, [in_], [out])
        return me

    def barrier(self):
        alld = []
        for e in ENGS:
            for ep in range(self.epoch[e] + 1):
                sk = (e, ep)
                if sk in self.sems:
                    alld.append((sk, self.cnt[e] if ep == self.epoch[e] else SEM_EPOCH))
        for sk, v in self.dcnt.items():
            alld.append((sk, v))
        for e in ENGS:
            w = self._waits(e, alld)
            if w:
                self.q[e].append((None, w, None, None))

    def flush(self):
        import bisect
        nc = self.nc
        q = self.q
        sems = self.sems
        if not hasattr(self, "base_idx"):
            self.base_idx, self.base_val = {}, {}
        targets = {}
        for ename in ENGS:
            for (fn, waits, inc, idx) in q[ename]:
                for (sk, val) in waits:
                    if sk[0] != "dma":
                        targets.setdefault(sk, set()).add(val)
        for ename in ENGS:
            sk = (ename, self.epoch[ename])
            if sk in sems:
                targets.setdefault(sk, set()).add(self.cnt[ename])
        tl = {}
        for sk, st in targets.items():
            b = self.base_idx.get(sk, 0)
            tl[sk] = sorted(v for v in st if v > b)

        def val_of(sk, v):
            b = self.base_idx.get(sk, 0)
            bv = self.base_val.get(sk, 0)
            if v <= b:
                return bv
            return bv + bisect.bisect_left(tl[sk], v) + 1

        with nc.Block() as block:
            for ename in ENGS:
                ops = q[ename]
                if not ops:
                    continue

                def body(e, ops=ops):
                    for (fn, waits, inc, idx) in ops:
                        for (sk, val) in waits:
                            if sk[0] == "dma":
                                e.wait_ge(sems[sk], val)
                            else:
                                e.wait_ge(sems[sk], val_of(sk, val))
                        if fn is not None:
                            ins = fn(e)
                            if inc is not None:
                                if inc[0][0] == "dma":
                                    ins.then_inc(sems[inc[0]], inc[1])
                                else:
                                    lst = tl.get(inc[0], ())
                                    j = bisect.bisect_left(lst, idx)
                                    if j < len(lst) and lst[j] == idx:
                                        ins.then_inc(sems[inc[0]], 1)
                getattr(block, ename)(body)
        for sk, lst in tl.items():
            self.base_val[sk] = self.base_val.get(sk, 0) + len(lst)
            if sk[0] != "dma":
                self.base_idx[sk] = self.cnt[sk[0]]
        self.q = {e: [] for e in ENGS}

    @contextlib.contextmanager
    def stage(self, name=None):
        st = contextlib.ExitStack()
        prev = self.stage_stack
        self.stage_stack = st
        self.stage_i = getattr(self, "stage_i", 0) + 1
        with st:
            yield st
            self.barrier()
            if getattr(self, "scopes", False):
                import inspect
                nm = name or inspect.stack()[2].function
                with self.nc.named_scope(f"s{self.stage_i:03d}_{nm}"):
                    self.flush()
            else:
                self.flush()
        self.stage_stack = prev

    def mm(self, out, lhsT, rhs, start=True, stop=True, **kw):
        o, l, r = _ap(out), _ap(lhsT), _ap(rhs)
        return self.op("tensor", lambda e: e.matmul(o, l, r, start=start, stop=stop, **kw),
                       reads=[lhsT, rhs], writes=[out], pe_accum=not start)

    def tr(self, out, in_, ident):
        o, i, d = _ap(out), _ap(in_), _ap(ident)
        return self.op("tensor", lambda e: e.transpose(o, i, d), reads=[in_, ident], writes=[out])

    def act(self, out, in_, func, bias=None, scale=None, accum_out=None, eng="scalar"):
        o, i = _ap(out), _ap(in_)
        kw = {}
        rd = [in_]
        if bias is not None:
            kw["bias"] = _ap(bias)
            rd.append(bias)
        if scale is not None:
            kw["scale"] = _ap(scale)
            rd.append(scale)
        wr = [out]
        if accum_out is not None:
            kw["accum_out"] = _ap(accum_out)
            wr.append(accum_out)
        return self.op("scalar", lambda e: e.activation(o, i, func, **kw), reads=rd, writes=wr)

    def tt(self, out, in0, in1, op, eng="vector"):
        o, a, b = _ap(out), _ap(in0), _ap(in1)
        return self.op(eng, lambda e: e.tensor_tensor(o, a, b, op), reads=[in0, in1], writes=[out])

    def ts(self, out, in0, s1, op0, s2=None, op1=None, eng="vector", accum_out=None):
        o, a = _ap(out), _ap(in0)
        x1, x2 = _ap(s1), _ap(s2)
        kw = {}
        if op1 is not None:
            kw["op1"] = op1
        wr = [out]
        if accum_out is not None:
            kw["accum_out"] = _ap(accum_out)
            wr.append(accum_out)
        return self.op(eng, lambda e: e.tensor_scalar(o, a, x1, x2, op0, **kw), reads=[in0, s1, s2], writes=wr)

    def stt(self, out, in0, scalar, in1, op0, op1):
        o, a, s, b = _ap(out), _ap(in0), _ap(scalar), _ap(in1)
        return self.op("vector", lambda e: e.scalar_tensor_tensor(o, a, s, b, op0, op1), reads=[in0, scalar, in1], writes=[out])

    def copy(self, out, in_, eng="vector"):
        o, i = _ap(out), _ap(in_)
        if eng == "scalar":
            return self.op("scalar", lambda e: e.copy(o, i), reads=[in_], writes=[out])
        return self.op(eng, lambda e: e.tensor_copy(o, i), reads=[in_], writes=[out])

    def memset(self, out, val, eng="vector"):
        o = _ap(out)
        return self.op(eng, lambda e: e.memset(o, val), reads=[], writes=[out])

    def recip(self, out, in_):
        o, i = _ap(out), _ap(in_)
        return self.op("vector", lambda e: e.reciprocal(o, i), reads=[in_], writes=[out])

    def reduce(self, out, in_, op=None, axis=None, **kw):
        o, i = _ap(out), _ap(in_)
        op = op or ALU.add
        axis = axis or AX.X
        return self.op("vector", lambda e: e.tensor_reduce(o, i, axis, op, **kw), reads=[in_], writes=[out])

D = 1024
SEQ = 2048
CTX = 256
NT = SEQ + CTX
DFF = 2816
NL = 2
EPS = 1e-6
TILES = [(0, 256, 1)] + [(256 + 512 * i, 512, 0) for i in range(4)]
QS = [(0, 6), (6, 6), (12, 5), (17, 5)]


class Pool:
    def __init__(self, k, name, shape, dtype, n, space="sb", stack=None):
        mk = k.sb if space == "sb" else k.ps
        self.bufs = [mk(f"{name}{i}", shape, dtype, stack=stack) for i in range(n)]
        self.i = 0

    def get(self):
        b = self.bufs[self.i % len(self.bufs)]
        self.i += 1
        return b


class Ctx:
    pass


def xt_view(C, t0, n):
    toks = [C.XT.subtok(i) for i, (s, sz, w) in enumerate(TILES) if s < t0 + n and t0 < s + sz]
    ap = C.XT.t.rearrange("(c p) t -> p c t", p=128)[:, :, t0:t0 + n]
    return V(ap, toks)


IN_SHAPES = {
    "xT": [D, NT], "cT": [128, 8, 2], "b_modT": [NL, 128, 72], "norm_wT": [NL, 128, 24],
    "w_mod": [NL, D, 9 * D], "ffn_w_gu": [NL, 2, D, 2 * DFF], "ffn_w_down": [NL, 2, DFF, D],
    "w_in": [NL, D, 3472], "w_out": [NL, D, D], "identD": [128, 128],
}


class LazyIn:
    def __init__(self, k, C):
        self.k, self.C, self.d = k, C, {}

    def __call__(self, name):
        if name not in self.d:
            self.d[name] = self.k.dram(name, IN_SHAPES[name], IN_DTYPES.get(name, F32), kind="ExternalInput")
        return self.d[name]


def declare_io(k, C, debug_out=()):
    C.I = LazyIn(k, C)
    C.XT = C.I("xT")
    C.XT.subtok = lambda i: C.XT.subs.setdefault(i, Tok(f"XT.{i}"))
    declare_scratch(k, C)
    if debug_out == "gin":
        IN_SHAPES["g_in"] = [NT, 1024]
        C.S["G"] = C.I("g_in")
    C.OUT = k.dram("outT", [D, SEQ], F32, kind="ExternalOutput")
    C.DBG = k.dram("dbgT", [D, NT], F32, kind="ExternalOutput") if debug_out else None


def setup_consts(k, C):
    C.ones_bf = k.sb("ones_bf", [128, 128], BF16, stack=k.stack)
    k.memset(C.ones_bf[:], 1.0)
    C.eps_t = k.sb("eps_t", [128, 1], F32, stack=k.stack)
    k.memset(C.eps_t[:], EPS)
    C.ident = k.sb("ident", [128, 128], F32, stack=k.stack)
    k.dma("sync", C.ident[:], C.I("identD")[:])
    C.P = [k.sb(f"P{l}", [128, 9, 8, 2], F32, stack=k.stack) for l in range(NL)]


def mod_stage(k, C):
    with k.stage():
        ct = k.sb("ct", [128, 8, 2], F32)
        sc = k.sb("sc", [128, 8, 2], F32)
        k.dma("sync", ct[:], C.I("cT")[:])
        k.act(sc[:], ct[:], AF.Silu)
        wpool = Pool(k, "wm", [128, 8, 512], F32, 3)
        for l in range(NL):
            bm = k.sb(f"bm{l}", [128, 72], F32)
            nw = k.sb(f"nw{l}", [128, 24], F32)
            k.dma("sync", bm[:], C.I("b_modT")[l])
            k.dma("sync", nw[:], C.I("norm_wT")[l])
            pm = k.ps(f"pm{l}", [128, 72, 2], F32)
            wsrc = C.I("w_mod").t[l].rearrange("(kc p) n -> p kc n", p=128)
            for g in range(18):
                wm = wpool.get()
                k.dma("sync", wm[:], V(wsrc[:, :, g * 512:(g + 1) * 512], [C.I("w_mod").tok]))
                for c4 in range(4):
                    ci = g * 4 + c4
                    for kc in range(8):
                        k.mm(pm[:, ci, :], wm[:, kc, c4 * 128:(c4 + 1) * 128], sc[:, kc, :],
                             start=(kc == 0), stop=(kc == 7))
            P = C.P[l]
            Pv = P[:].re("p a c w -> p (a c) w")
            k.tt(Pv, pm[:], bm[:, :, None].bc([128, 72, 2]), ALU.add)
            for s in range(3):
                k.stt(P[:, 3 * s + 1], P[:, 3 * s + 1], 1.0,
                      nw[:, s * 8:(s + 1) * 8, None].bc([128, 8, 2]), ALU.add, ALU.mult)
                if s != 1:
                    k.ts(P[:, 3 * s + 2], P[:, 3 * s + 2], 0.5, ALU.mult)


def hv_(hall, ti, c, t0, n):
    return hall.sub(ti, (slice(None), c, slice(t0, t0 + n)))


def norm_pass(k, C, P, s, hall, xpool, sq, pss, rs, tmpp, skip_ctx=False):
    for ti, (t0, n, w) in enumerate(TILES):
        if w == 1 and skip_ctx:
            continue
        xt = xpool.get()
        k.dma("sync", xt[:, :, :n], xt_view(C, t0, n))
        k.act(sq[:, :, :n], xt[:, :, :n], AF.Square)
        for c in range(8):
            k.mm(pss[:, :n], C.ones_bf[:], sq[:, c, :n], start=(c == 0), stop=(c == 7))
        k.act(rs[:, :n], pss[:, :n], AF.Sqrt, scale=1.0 / D, bias=C.eps_t[:])
        k.recip(rs[:, :n], rs[:, :n])
        for c in range(8):
            tmp = tmpp.get()
            k.stt(tmp[:, :n], xt[:, c, :n], P[:, 3 * s + 1, c, w:w + 1], rs[:, :n], ALU.mult, ALU.mult)
            k.act(hv_(hall, ti, c, t0, n), tmp[:, :n], AF.Identity, bias=P[:, 3 * s, c, w:w + 1])


def ffn_stage(k, C, l, f, s, skip_ctx=False):
    P = C.P[l]
    Wgu = C.I("ffn_w_gu").t[l, f].rearrange("(kc p) n -> p kc n", p=128)
    Wdn = C.I("ffn_w_down").t[l, f].rearrange("(j p) n -> p j n", p=128)
    with k.stage():
        hall = k.sb("hall", [128, 8, NT], BF16)
        xpool = Pool(k, "xt", [128, 8, 512], F32, 2)
        sq = k.sb("sq", [128, 8, 512], BF16)
        tmpp = Pool(k, "tmp", [128, 512], F32, 2)
        rs = k.sb("rs", [128, 512], F32)
        pss = k.ps("pss", [128, 512], F32)
        wgp = Pool(k, "wg", [128, 8, 2, 6 * 128], BF16, 2)
        wdp = Pool(k, "wd", [128, 6, D], BF16, 2)
        actp = Pool(k, "act", [128, 6, 512], BF16, 2)
        sgp = Pool(k, "sg", [128, 512], F32, 2)
        pgp = Pool(k, "pg", [128, 512], F32, 2, space="ps")
        pup = Pool(k, "pu", [128, 512], F32, 2, space="ps")
        pdp = Pool(k, "pd", [128, 512], F32, 2, space="ps")

        def hv(ti, c, t0, n):
            return hall.sub(ti, (slice(None), c, slice(t0, t0 + n)))

        norm_pass(k, C, P, s, hall, xpool, sq, pss, rs, tmpp, skip_ctx)
        for qi, (j0, nj) in enumerate(QS):
            wg = wgp.get()
            wd = wdp.get()
            k.dma("gpsimd", wg[:, :, 0, :nj * 128], V(Wgu[:, :, j0 * 128:(j0 + nj) * 128], [C.I("ffn_w_gu").tok]))
            k.dma("gpsimd", wg[:, :, 1, :nj * 128], V(Wgu[:, :, DFF + j0 * 128:DFF + (j0 + nj) * 128], [C.I("ffn_w_gu").tok]))
            k.dma("gpsimd", wd[:, :nj, :], V(Wdn[:, j0:j0 + nj, :], [C.I("ffn_w_down").tok]))
            for ti, (t0, n, w) in enumerate(TILES):
                if w == 1 and skip_ctx:
                    continue
                act = actp.get()
                for jj in range(nj):
                    pg = pgp.get()
                    pu = pup.get()
                    for kc in range(8):
                        k.mm(pg[:, :n], wg[:, kc, 0, jj * 128:(jj + 1) * 128], hv(ti, kc, t0, n),
                             start=(kc == 0), stop=(kc == 7))
                    for kc in range(8):
                        k.mm(pu[:, :n], wg[:, kc, 1, jj * 128:(jj + 1) * 128], hv(ti, kc, t0, n),
                             start=(kc == 0), stop=(kc == 7))
                    sg = sgp.get()
                    k.act(sg[:, :n], pg[:, :n], AF.Silu)
                    k.tt(act[:, jj, :n], sg[:, :n], pu[:, :n], ALU.mult)
                xr = xpool.get()
                k.dma("sync", xr[:, :, :n], xt_view(C, t0, n))
                for m in range(8):
                    pd = pdp.get()
                    for jj in range(nj):
                        k.mm(pd[:, :n], wd[:, jj, m * 128:(m + 1) * 128], act[:, jj, :n],
                             start=(jj == 0), stop=(jj == nj - 1))
                    k.stt(xr[:, m, :n], pd[:, :n], P[:, 3 * s + 2, m, w:w + 1], xr[:, m, :n], ALU.mult, ALU.add)
                k.dma("sync", xt_view(C, t0, n), xr[:, :, :n])


PT = 3472
TM_GROUPS = [
    (0, 512, "HY", 0), (512, 256, "HY", 512),
    (1280, 256, "NAV", 0),
    (1536, 512, "DN", 0), (2048, 512, "DN", 512), (2560, 16, "DN", 1024),
    (2576, 512, "RW", 0), (3088, 256, "RW", 512),
]
FM_CHUNKS = [(768, 128, "NAQK", 0), (896, 128, "NAQK", 1), (1024, 128, "NAQK", 2), (1152, 128, "NAQK", 3),
             (3344, 32, "RWLR", 0), (3376, 32, "RWLR", 1), (3408, 64, "RWLR", 2)]


def declare_scratch(k, C):
    def mk(name, shape):
        kind = "ExternalOutput" if name in C.dbg_names else "Internal"
        return k.dram("s_" + name.lower(), shape, F32, kind=kind)
    C.S = {
        "HY": mk("HY", [NT, 768]),
        "NAV": mk("NAV", [NT, 256]),
        "DN": mk("DN", [NT, 1040]),
        "RW": mk("RW", [NT, 768]),
        "NAQK": mk("NAQK", [512, NT]),
        "RWLR": mk("RWLR", [3, 64, NT]),
        "RWP": mk("RWP", [NT, 10, 256]),
        "RWY": mk("RWY", [2, NT, 256]),
        "G": mk("G", [NT, 1024]),
        "DNQKV": mk("DNQKV", [NT, 768]),
        "DNGB": mk("DNGB", [NT, 16]),
        "DNO": mk("DNO", [2, NT, 256]),
    }


def proj_stage(k, C, l):
    P = C.P[l]
    Win = C.I("w_in").t[l].rearrange("(kc p) n -> p kc n", p=128)
    with k.stage():
        hall = k.sb("hall", [128, 8, NT], BF16)
        xpool = Pool(k, "xt", [128, 8, 512], F32, 2)
        sq = k.sb("sq", [128, 8, 512], BF16)
        tmpp = Pool(k, "tmp", [128, 512], F32, 2)
        rs = k.sb("rs", [128, 512], F32)
        pss = k.ps("pss", [128, 512], F32)
        win = k.sb("win", [128, 8, PT], BF16)
        for kc in range(8):
            k.dma("gpsimd", win[:, kc, :], V(Win[:, kc, :], [C.I("w_in").tok]))
        norm_pass(k, C, P, 1, hall, xpool, sq, pss, rs, tmpp)
        pp = Pool(k, "pp", [128, 512], F32, 4, space="ps")
        rowp = Pool(k, "row", [128, 2832], F32, 2)
        fmp = Pool(k, "fm", [128, 7, 512], F32, 2)
        ev = 0
        for tc in range(NT // 128):
            t0 = tc * 128
            ti = 0 if t0 < 256 else 1 + (t0 - 256) // 512
            row = rowp.get()
            off = 0
            offs = []
            for (c0, nc_, name, dcol) in TM_GROUPS:
                ps = pp.get()
                for kc in range(8):
                    k.mm(ps[:, :nc_], hv_(hall, ti, kc, t0, 128), win[:, kc, c0:c0 + nc_],
                         start=(kc == 0), stop=(kc == 7))
                k.copy(row[:, off:off + nc_], ps[:, :nc_], eng=("scalar" if ev % 2 else "vector"))
                ev += 1
                offs.append((off, nc_, name, dcol))
                off += nc_
            for name in ("HY", "NAV", "DN", "RW"):
                gs = [g for g in offs if g[2] == name]
                o0 = gs[0][0]
                tot = sum(g[1] for g in gs)
                k.dma("sync", C.S[name][t0:t0 + 128, 0:tot], row[:, o0:o0 + tot])
        for ti, (t0, n, w) in enumerate(TILES):
            fm = fmp.get()
            for i, (c0, ncl, name, di) in enumerate(FM_CHUNKS):
                ps = pp.get()
                for kc in range(8):
                    k.mm(ps[0:ncl, :n], win[:, kc, c0:c0 + ncl], hv_(hall, ti, kc, t0, n),
                         start=(kc == 0), stop=(kc == 7))
                k.copy(fm[0:ncl, i, :n], ps[0:ncl, :n], eng=("scalar" if ev % 2 else "vector"))
                ev += 1
            k.dma("sync", V(C.S["NAQK"].t.rearrange("(c p) t -> p c t", p=128)[:, :, t0:t0 + n], [C.S["NAQK"].tok]),
                  fm[:, 0:4, :n])
            for g_, ncl in ((0, 32), (1, 32), (2, 64)):
                k.dma("sync", C.S["RWLR"][g_, 0:ncl, t0:t0 + n], fm[0:ncl, 4 + g_, :n])


def outproj_stage(k, C, l, need_ctx):
    P = C.P[l]
    Wout = C.I("w_out").t[l].rearrange("(kc p) n -> p kc n", p=128)
    with k.stage():
        wo = k.sb("wo", [128, 8, D], BF16)
        k.dma("gpsimd", wo[:], V(Wout, [C.I("w_out").tok]))
        gpool = Pool(k, "gt", [128, D], F32, 3)
        gT = Pool(k, "gT", [128, 8, 512], BF16, 2)
        xpool = Pool(k, "xt", [128, 8, 512], F32, 2)
        ptp = Pool(k, "ptp", [128, 4, 128], F32, 2, space="ps")
        pyp = Pool(k, "py", [128, 512], F32, 2, space="ps")
        ev = 0
        for ti, (t0, n, w) in enumerate(TILES):
            if w == 1 and not need_ctx:
                continue
            g = gT.get()
            for sc_ in range(n // 128):
                gt = gpool.get()
                k.dma("sync", gt[:], C.S["G"][t0 + sc_ * 128:t0 + (sc_ + 1) * 128, :])
                for half in range(2):
                    pt = ptp.get()
                    for q in range(4):
                        c = half * 4 + q
                        k.tr(pt[:, q, :], gt[:, c * 128:(c + 1) * 128], C.ident[:])
                    k.copy(g[:, half * 4:(half + 1) * 4, sc_ * 128:(sc_ + 1) * 128], pt[:],
                           eng=("scalar" if ev % 2 else "vector"))
                    ev += 1
            xr = xpool.get()
            k.dma("sync", xr[:, :, :n], xt_view(C, t0, n))
            for m in range(8):
                py = pyp.get()
                for kc in range(8):
                    k.mm(py[:, :n], wo[:, kc, m * 128:(m + 1) * 128], g[:, kc, :n], start=(kc == 0), stop=(kc == 7))
                k.stt(xr[:, m, :n], py[:, :n], P[:, 5, m, w:w + 1], xr[:, m, :n], ALU.mult, ALU.add)
            k.dma("sync", xt_view(C, t0, n), xr[:, :, :n])


def out_stage(k, C):
    with k.stage():
        xpool = Pool(k, "xo", [128, 8, 512], F32, 2)
        last = []
        for i in range(4):
            xo = xpool.get()
            k.dma("sync", xo[:], xt_view(C, 256 + 512 * i, 512))
            dst = V(C.OUT.t.rearrange("(c p) t -> p c t", p=128)[:, :, 512 * i:512 * (i + 1)], [C.OUT.tok])
            k.dma("sync", dst, xo[:])
        if C.DBG is not None:
            for ti, (t0, n, w) in enumerate(TILES):
                xo = xpool.get()
                k.dma("sync", xo[:, :, :n], xt_view(C, t0, n))
                dst = V(C.DBG.t.rearrange("(c p) t -> p c t", p=128)[:, :, t0:t0 + n], [C.DBG.tok])
                k.dma("sync", dst, xo[:, :, :n])


def build(plan=None, dbg_names=(), scopes=False):
    k = K()
    k.scopes = scopes
    C = Ctx()
    C.dbg_names = set(dbg_names)
    declare_io(k, C, debug_out=(False if plan in (None, 'full') else ("gin" if plan == "m3" else True)))
    with k.stage():
        setup_consts(k, C)
    mod_stage(k, C)
    plan = plan or "full"
    if plan == "full":
        for l in range(NL):
            need_ctx = l < NL - 1
            ffn_stage(k, C, l, 0, 0)
            proj_stage(k, C, l)
            for nm in ("hy", "na", "dnrw"):
                globals()[nm + "_stage"](k, C, l, need_ctx)
            outproj_stage(k, C, l, need_ctx)
            ffn_stage(k, C, l, 1, 2, skip_ctx=not need_ctx)
    if plan == "ffn1":
        ffn_stage(k, C, 0, 0, 0)
    if plan == "m1":
        proj_stage(k, C, 0)
    if plan == "m3":
        outproj_stage(k, C, 0, True)
    if plan.startswith("mix:"):
        proj_stage(k, C, 0)
        for nm in plan[4:].split(","):
            globals()[nm + "_stage"](k, C, 0, True)
    out_stage(k, C)
    return k, C


def host_prep(inp, b, used):
    f = np.float32
    m = {}
    for name in used:
        if name == "xT":
            v = np.concatenate([inp["ctx"][b], inp["x"][b]], axis=0).T
        elif name == "cT":
            v = np.stack([inp["c"][b].reshape(8, 128).T, inp["c_ctx"].reshape(8, 128).T], axis=-1)
        elif name == "b_modT":
            v = inp["b_mod"].reshape(NL, 72, 128).transpose(0, 2, 1)
        elif name == "norm_wT":
            v = inp["norm_w"].reshape(NL, 24, 128).transpose(0, 2, 1)
        elif name == "identD":
            v = np.eye(128, dtype=f)
        elif name in HOST_LAYOUT:
            v = HOST_LAYOUT[name](inp, b)
        else:
            v = inp[name]
        m[name] = np.ascontiguousarray(np.asarray(v, dtype=(ml_dtypes.bfloat16 if name in IN_DTYPES else f)))
    return m


HOST_LAYOUT = {}


def run(inputs, plan=None, dbg_names=(), extra=None, ncores=8):
    k, C = build(plan, dbg_names)
    extra = extra or {}
    used = [n for n in C.I.d.keys() if n not in extra]
    in_maps = [host_prep(inputs, b, used) for b in range(ncores)]
    for m in in_maps:
        m.update(extra)
    res = run_bass_kernel_spmd(k.nc, in_maps, core_ids=list(range(ncores)))
    return res


def kernel(**inputs):
    inputs = {kk: np.asarray(v) for kk, v in inputs.items()}
    res = run(inputs)
    out = np.stack([np.ascontiguousarray(r["outT"].T) for r in res.results], axis=0)
    return out.astype(np.float32)


IN_SHAPES["na_nT"] = [NL, 128, 2]
IN_SHAPES["na_biasT"] = [NL, 128, 8, 4, 4, 64]


def _na_nT(inp, b):
    return np.stack([np.tile(inp["na_q_norm"], (1, 2)), np.tile(inp["na_k_norm"], (1, 2))], axis=-1)


def _na_biasT(inp, b):
    a = np.arange(2)[:, None, None, None, None]
    kcol = np.arange(64)[None, :, None, None, None]
    rho = np.arange(8)[None, None, :, None, None]
    i = np.arange(4)[None, None, None, :, None]
    qcol = np.arange(64)[None, None, None, None, :]
    drow = 2 * i + a - rho + 0 * kcol + 0 * qcol
    dcol = np.clip(kcol - qcol, -15, 15) + 0 * drow
    cs = np.clip(qcol - 8, 0, 48)
    inwin = ((kcol >= cs) & (kcol < cs + 16)) & (drow > -100)
    rpb = inp["na_rpb"]
    g = rpb[:, :, drow + 7, dcol + 15]
    g = np.where(inwin[None, None], g, np.float32(-30000.0))
    g = g.transpose(0, 2, 3, 4, 1, 5, 6).reshape(NL, 128, 8, 4, 4, 64)
    return g


HOST_LAYOUT["na_nT"] = _na_nT
HOST_LAYOUT["na_biasT"] = _na_biasT


def na_stage(k, C, l, need_ctx):
    with contextlib.ExitStack() as outer:
        na_stage_(k, C, l, need_ctx, outer)


def na_stage_(k, C, l, need_ctx, outer):
    E = k.sb("E", [128, 8, 4, 4, 64], BF16, stack=outer)
    QK = k.sb("QK", [128, 4, NT], BF16, stack=outer)
    QM = k.sb("QM", [128, 2, 2, NT], BF16, stack=outer)
    Ve = k.sb("Ve", [128, 18, 4, 65], BF16, stack=outer)
    Vo = k.sb("Vo", [128, 17, 4, 65], BF16, stack=outer)
    with k.stage():
        bd = k.sb("bd", [128, 128], BF16)
        k.memset(bd[:], 0.0)
        k.memset(bd[0:64, 0:64], 1.0)
        k.memset(bd[64:128, 64:128], 1.0)
        nw = k.sb("nw", [128, 2], F32)
        k.dma("sync", nw[:], C.I("na_nT")[l])
        k.memset(QM[:], 0.0, eng="gpsimd")
        for half in range(2):
            bt = k.sb(f"bt{half}", [128, 4, 4, 4, 64], F32)
            k.dma("sync", bt[:], C.I("na_biasT")[l, :, half * 4:(half + 1) * 4])
            k.act(E[:, half * 4:(half + 1) * 4], bt[:], AF.Exp)
        qkp = Pool(k, "qk", [128, 4, 512], F32, 2)
        sqp = Pool(k, "sq", [128, 4, 512], BF16, 2)
        rnp = Pool(k, "rn", [128, 512], F32, 2)
        pss = Pool(k, "pss", [128, 512], F32, 2, space="ps")
        src = C.S["NAQK"].t.rearrange("(c p) t -> p c t", p=128)
        for ti, (t0, n, w) in enumerate(TILES):
            qk = qkp.get()
            k.dma("sync", qk[:, :, :n], V(src[:, :, t0:t0 + n], [C.S["NAQK"].tok]))
            sq = sqp.get()
            k.act(sq[:, :, :n], qk[:, :, :n], AF.Square)
            for c in range(4):
                ps = pss.get()
                k.mm(ps[:, :n], bd[:], sq[:, c, :n])
                rn = rnp.get()
                k.act(rn[:, :n], ps[:, :n], AF.Sqrt, scale=1.0 / 64, bias=C.eps_t[:])
                k.recip(rn[:, :n], rn[:, :n])
                if c >= 2:
                    k.stt(QK[:, c, t0:t0 + n], qk[:, c, :n], nw[:, 1:2], rn[:, :n], ALU.mult, ALU.mult)
                else:
                    for par in range(2):
                        pp_ = slice(par * 64, par * 64 + 64)
                        k.stt(QM[pp_, c, par, t0:t0 + n], qk[pp_, c, :n], nw[pp_, 0:1], rn[pp_, :n], ALU.mult, ALU.mult)
        k.memset(Ve[:], 1.0)
        k.memset(Vo[:], 1.0, eng="gpsimd")
        nav = C.S["NAV"]
        vst = k.sb("vst", [128, 18, 256], F32)
        vso = k.sb("vso", [128, 17, 256], F32)
        for c0 in range(0, 18, 6):
            c1 = min(18, c0 + 6)
            k.dma("sync", vst[:, c0:c1, :], V(nav.t[c0 * 128:c1 * 128, :].rearrange("(c p) d -> p c d", p=128), [nav.tok]))
            c1o = min(17, c0 + 6)
            k.dma("sync", vso[:, c0:c1o, :], V(nav.t[64 + c0 * 128:64 + c1o * 128, :].rearrange("(c p) d -> p c d", p=128), [nav.tok]))
        k.copy(Ve[:, :, :, 0:64], vst[:].re("p c (h d) -> p c h d", d=64), eng="vector")
        k.copy(Vo[:, :, :, 0:64], vso[:].re("p c (h d) -> p c h d", d=64), eng="gpsimd")
    import os
    if os.environ.get("NA_STOP") == "1":
        return
    with k.stage():
        psp = Pool(k, "ps", [128, 4, 6, 64], F32, 2, space="ps")
        pop = Pool(k, "po", [64, 4, 65], F32, 2, space="ps")
        Pp = Pool(k, "P", [128, 4, 6, 64], BF16, 2)
        rdp = Pool(k, "rd", [64, 4], F32, 2)
        orp = Pool(k, "orow", [64, 256], F32, 3)
        G = C.S["G"]
        for r in range(32):
            start = min(max(r - 4, 0), 24)
            rho = r - start
            qt0 = 256 + 64 * r
            ps = psp.get()
            for h in range(4):
                hp = slice((h % 2) * 64, (h % 2) * 64 + 64)
                for i in range(6):
                    kt0 = (256 + 64 * (start + 2 * i)) if i < 4 else (i - 4) * 128
                    k.mm(ps[:, h, i, :], QK[:, 2 + h // 2, kt0:kt0 + 128], QM[:, h // 2, h % 2, qt0:qt0 + 64])
            Pt = Pp.get()
            k.act(Pt[:], ps[:], AF.Exp, scale=0.125)
            k.tt(Pt[:, :, 0:4, :], Pt[:, :, 0:4, :], E[:, rho], ALU.mult)
            po = pop.get()
            for h in range(4):
                for i in range(6):
                    if i < 4:
                        r0 = start + 2 * i
                        vv = Ve[:, 2 + r0 // 2, h, :] if r0 % 2 == 0 else Vo[:, (r0 + 3) // 2, h, :]
                    else:
                        vv = Ve[:, i - 4, h, :]
                    k.mm(po[:, h, :], Pt[:, h, i, :], vv, start=(i == 0), stop=(i == 5))
            rd = rdp.get()
            k.recip(rd[:], po[:, :, 64])
            orow = orp.get()
            k.tt(orow[:].re("p (h d) -> p h d", d=64), po[:, :, 0:64], rd[:, :, None].bc([64, 4, 64]), ALU.mult)
            k.dma("sync", G[qt0:qt0 + 64, 256:512], orow[:])
    if os.environ.get("NA_STOP") == "2":
        return
    with k.stage():
        G = C.S["G"]
        if need_ctx:
            pcp = Pool(k, "pc", [128, 2, 256], F32, 2, space="ps")
            pocp = Pool(k, "poc", [128, 65], F32, 2, space="ps")
            Pcp = Pool(k, "Pc", [128, 2, 256], BF16, 2)
            oc = k.sb("oc", [128, 2, 256], F32)
            rdc = Pool(k, "rdc", [128, 1], F32, 2)
            for h in range(4):
                hp = slice((h % 2) * 64, (h % 2) * 64 + 64)
                pc = pcp.get()
                for kc in range(2):
                    k.mm(pc[:, kc, :], QK[:, 2 + h // 2, kc * 128:(kc + 1) * 128], QM[:, h // 2, h % 2, 0:256])
                Pc = Pcp.get()
                k.act(Pc[:], pc[:], AF.Exp, scale=0.125)
                for qc in range(2):
                    poc = pocp.get()
                    for kc in range(2):
                        k.mm(poc[:], Pc[:, kc, qc * 128:(qc + 1) * 128], Ve[:, kc, h, :], start=(kc == 0), stop=(kc == 1))
                    rd = rdc.get()
                    k.recip(rd[:], poc[:, 64:65])
                    k.ts(oc[:, qc, h * 64:(h + 1) * 64], poc[:, 0:64], rd[:, 0:1], ALU.mult)
            k.dma("sync", V(G.t[0:256, 256:512].rearrange("(c p) n -> p c n", p=128), [G.tok]), oc[:])


import math
import ml_dtypes

HY_BANDS = 16


def _hy_consts(n):
    f64 = np.float64
    N = 2 * n
    nch = n // 128
    t = np.linspace(0.0, 1.0, n, dtype=np.float32).astype(f64)
    ang = 2.0 * math.pi * np.arange(n, dtype=f64) / n
    bands = np.linspace(1e-4, HY_BANDS - 1, HY_BANDS, dtype=np.float32).astype(f64)[None]
    z = np.concatenate([t[:, None], np.cos(bands * ang[:, None]), -np.sin(bands * ang[:, None])], axis=-1)
    max_decay = math.log(1e-2) / 0.3
    min_decay = math.log(1e-2) / 1.5
    deltas = np.abs(np.linspace(min_decay, max_decay, 512, dtype=np.float32).astype(f64))
    idx = np.arange(n, dtype=f64) + 0.5
    ph = 2.0 * math.pi * np.outer(idx, idx) / N
    C4 = np.cos(ph)
    S4 = np.sin(ph)

    def tile_(M):
        return np.ascontiguousarray(M.reshape(nch, 128, nch, 128).transpose(2, 1, 0, 3).reshape(nch, 128, nch * 128))
    w = 2.0 * math.pi * idx / N
    cs = np.stack([(2.0 / N) * np.cos(w / 2), (2.0 / N) * np.sin(w / 2), -(2.0 / N) * np.cos(w / 2)], axis=-1)
    return {
        "zT": np.ascontiguousarray(z.T.astype(np.float32)),
        "ntcol": np.ascontiguousarray((-t).reshape(nch, 128).T.astype(np.float32)),
        "absdelta": np.ascontiguousarray(np.tile(deltas[None, :], (128, 1)).astype(np.float32)),
        "C4t": tile_(C4).astype(ml_dtypes.bfloat16),
        "S4t": tile_(S4).astype(ml_dtypes.bfloat16),
        "cs": np.ascontiguousarray(cs.reshape(nch, 128, 3).transpose(1, 0, 2).astype(np.float32)),
    }


_HYC = {}


def hy_const(n, name):
    if n not in _HYC:
        _HYC[n] = _hy_consts(n)
    return _HYC[n][name]


IN_DTYPES = {}
for _n in (SEQ, CTX):
    _nch = _n // 128
    IN_SHAPES[f"hy_zT_{_n}"] = [33, _n]
    IN_SHAPES[f"hy_ntcol_{_n}"] = [128, _nch]
    IN_SHAPES[f"hy_cs_{_n}"] = [128, _nch, 3]
    IN_SHAPES[f"hy_C4t_{_n}"] = [_nch, 128, _nch * 128]
    IN_SHAPES[f"hy_S4t_{_n}"] = [_nch, 128, _nch * 128]
    IN_DTYPES[f"hy_C4t_{_n}"] = BF16
    IN_DTYPES[f"hy_S4t_{_n}"] = BF16
    for _nm in ("zT", "ntcol", "cs", "C4t", "S4t"):
        HOST_LAYOUT[f"hy_{_nm}_{_n}"] = (lambda inp, b, _n=_n, _nm=_nm: hy_const(_n, _nm))
IN_SHAPES["hy_absdelta"] = [128, 512]
HOST_LAYOUT["hy_absdelta"] = lambda inp, b: hy_const(CTX, "absdelta")
IN_SHAPES["hy_fcol"] = [NL, 64, 3]
HOST_LAYOUT["hy_fcol"] = lambda inp, b: np.stack([inp["hy_f_b1"], inp["hy_f_b2"], inp["hy_f_freq"]], axis=-1)
for _nm, _sh in (("hy_f_w1", [NL, 33, 64]), ("hy_f_w2", [NL, 64, 64]), ("hy_f_w3", [NL, 64, 1024]),
                 ("hy_conv", [NL, 3, 768]), ("hy_bias", [NL, 2, 256])):
    IN_SHAPES[_nm] = _sh


def bc_load(k, dst, src_ap, tok, eng="sync"):
    P = dst.ap.shape[0]
    k.dma(eng, dst, V(src_ap.partition_broadcast(P), [tok]))


def sin_rr(k, out, in_ps, scale_col, bias_col, pools, npi, nfree):
    a1, a2, ai = pools
    t1 = a1.get()
    t2 = a2.get()
    ti = ai.get()
    P = out.ap.shape[0]
    k.act(t1[:P, :nfree], in_ps, AF.Identity, scale=scale_col, bias=bias_col)
    k.ts(t2[:P, :nfree], t1[:P, :nfree], 1.0 / (2.0 * math.pi), ALU.mult, 64.5, ALU.add)
    k.copy(ti[:P, :nfree], t2[:P, :nfree])
    k.tt(t1[:P, :nfree], t2[:P, :nfree], ti[:P, :nfree], ALU.subtract)
    k.stt(t2[:P, :nfree], t1[:P, :nfree], 0.0, t1[:P, :nfree], ALU.is_lt, ALU.add)
    k.act(out, t2[:P, :nfree], AF.Sin, scale=2.0 * math.pi, bias=npi[:P, 0:1])


def hyf_stage(k, C, l, n, KRI):
    nch = n // 128
    TW = min(512, n)
    with contextlib.ExitStack() as outer:
        hyf_stage_(k, C, l, n, KRI, nch, TW, outer)


def hyf_stage_(k, C, l, n, KRI, nch, TW, outer):
    Pb = k.sb("Pb", [128, nch, 512], BF16, stack=outer)
    Qb = k.sb("Qb", [128, nch, 512], BF16, stack=outer)
    cs = k.sb("cs", [128, nch, 3], F32, stack=outer)
    with k.stage():
        w1 = k.sb("w1", [33, 64], F32)
        w2 = k.sb("w2", [64, 64], F32)
        w3 = k.sb("w3", [64, 1024], F32)
        fcol = k.sb("fcol", [64, 3], F32)
        fb = k.sb("fb", [64, 2], F32)
        zT = k.sb("zT", [33, n], F32)
        npi = k.sb("npi", [128, 1], F32)
        ones_f = k.sb("ones_f", [128, 128], F32)
        k.memset(npi[:], -math.pi)
        k.memset(ones_f[:], 1.0)
        k.dma("sync", w1[:], C.I("hy_f_w1")[l])
        k.dma("sync", w2[:], C.I("hy_f_w2")[l])
        k.dma("sync", w3[:], C.I("hy_f_w3")[l])
        k.dma("sync", fcol[:], C.I("hy_fcol")[l])
        k.dma("sync", zT[:], C.I(f"hy_zT_{n}")[:])
        k.tt(fb[:], fcol[:, 0:2], fcol[:, 2:3].bc([64, 2]), ALU.mult)
        hid1 = k.sb("hid1", [64, n], F32)
        hid2 = k.sb("hid2", [64, n], F32)
        pools = (Pool(k, "a1", [64, 512], F32, 2), Pool(k, "a2", [64, 512], F32, 2), Pool(k, "ai", [64, 512], I32, 2))
        pmp = Pool(k, "pm", [64, 512], F32, 2, space="ps")
        for t0 in range(0, n, TW):
            ps = pmp.get()
            k.mm(ps[:, :TW], w1[:], zT[:, t0:t0 + TW])
            sin_rr(k, hid1[:, t0:t0 + TW], ps[:, :TW], fcol[:, 2:3], fb[:, 0:1], pools, npi, TW)
        for t0 in range(0, n, TW):
            ps = pmp.get()
            k.mm(ps[:, :TW], w2[:], hid1[:, t0:t0 + TW])
            sin_rr(k, hid2[:, t0:t0 + TW], ps[:, :TW], fcol[:, 2:3], fb[:, 1:2], pools, npi, TW)
        absd = k.sb("absd", [128, 512], F32)
        ntc = k.sb("ntc", [128, nch], F32)
        k.dma("sync", absd[:], C.I("hy_absdelta")[:])
        k.dma("sync", ntc[:], C.I(f"hy_ntcol_{n}")[:])
        k.dma("sync", cs[:], C.I(f"hy_cs_{n}")[:])
        HF = k.sb("HF", [128, nch, 512], F32)
        HB = k.sb("HB", [128, nch, 512], F32)
        php = Pool(k, "ph", [128, 512], F32, 3, space="ps")
        pl1 = k.ps("pl1", [128, 512], F32)
        winp = Pool(k, "win", [128, 512], F32, 2)
        absp = Pool(k, "abs", [128, 512], F32, 3)
        for c in range(nch):
            ph0 = php.get()
            ph1 = php.get()
            k.mm(ph0[:], hid2[:, c * 128:(c + 1) * 128], w3[:, 0:512])
            k.mm(ph1[:], hid2[:, c * 128:(c + 1) * 128], w3[:, 512:1024])
            win = winp.get()
            k.act(win[:], absd[:], AF.Exp, scale=ntc[:, c:c + 1])
            k.tt(HF[:, c, :], ph0[:], win[:], ALU.mult)
            k.tt(HB[:, c, :], ph1[:], win[:], ALU.mult)
            if c == 0:
                k.memset(HB[0:1, 0, :], 0.0)
            for j, src in enumerate((HF, HB)):
                ab = absp.get()
                k.act(ab[:], src[:, c, :], AF.Abs)
                k.mm(pl1[:], ones_f[:], ab[:], start=(c == 0 and j == 0), stop=(c == nch - 1 and j == 1))
        rn = k.sb("rn", [128, 512], F32)
        k.recip(rn[:], pl1[:])
        for c in range(nch):
            t = absp.get()
            k.tt(t[:], HF[:, c, :], HB[:, c, :], ALU.add, eng="gpsimd")
            k.tt(Pb[:, c, :], t[:], rn[:], ALU.mult, eng="gpsimd")
            t = absp.get()
            k.tt(t[:], HF[:, c, :], HB[:, c, :], ALU.subtract)
            k.tt(Qb[:, c, :], t[:], rn[:], ALU.mult)
    with k.stage():
        absp = Pool(k, "abs2", [128, 512], F32, 3)
        cbp = Pool(k, "cb", [128, nch, 128], BF16, 2)
        sbp = Pool(k, "sbk", [128, nch, 128], BF16, 2)
        psp = Pool(k, "psq", [128, 512], F32, 4, space="ps")
        kop = Pool(k, "ko", [128, 2, 512], F32, 2)
        C4 = C.I(f"hy_C4t_{n}")
        S4 = C.I(f"hy_S4t_{n}")
        for fc in range(nch):
            cb = cbp.get()
            sb_ = sbp.get()
            k.dma("sync", cb[:].re("p c m -> p (c m)"), C4[fc])
            k.dma("sync", sb_[:].re("p c m -> p (c m)"), S4[fc])
            pPc, pPs, pQc, pQs = psp.get(), psp.get(), psp.get(), psp.get()
            for c in range(nch):
                st, sp = (c == 0), (c == nch - 1)
                k.mm(pPc[:], cb[:, c, :], Pb[:, c, :], start=st, stop=sp)
                k.mm(pPs[:], sb_[:, c, :], Pb[:, c, :], start=st, stop=sp)
                k.mm(pQc[:], cb[:, c, :], Qb[:, c, :], start=st, stop=sp)
                k.mm(pQs[:], sb_[:, c, :], Qb[:, c, :], start=st, stop=sp)
            ko = kop.get()
            t = absp.get()
            k.ts(t[:], pPc[:], cs[:, fc, 0:1], ALU.mult)
            k.stt(ko[:, 0, :], pPs[:], cs[:, fc, 1:2], t[:], ALU.mult, ALU.add)
            t = absp.get()
            k.ts(t[:], pQc[:], cs[:, fc, 1:2], ALU.mult)
            k.stt(ko[:, 1, :], pQs[:], cs[:, fc, 2:3], t[:], ALU.mult, ALU.add)
            k.dma("sync", KRI[fc * 128:(fc + 1) * 128], ko[:])


def hyc_stage(k, C, l, n, base, KRI):
    nch = n // 128
    HYs = C.S["HY"]
    G = C.S["G"]
    with k.stage():
        cw = k.sb("cw", [128, 3, 768], F32)
        hb = k.sb("hb", [128, 2, 256], F32)
        for i in range(3):
            bc_load(k, cw[:, i, :], C.I("hy_conv").t[l, i], C.I("hy_conv").tok)
        for i in range(2):
            bc_load(k, hb[:, i, :], C.I("hy_bias").t[l, i], C.I("hy_bias").tok)
        v = k.sb("v", [128, nch, 256], F32)
        x1 = k.sb("x1", [128, nch, 256], F32)
        x2 = k.sb("x2", [128, nch, 256], F32)
        vb = k.sb("vb", [128, nch, 256], BF16)
        z = k.sb("z", [128, nch, 256], F32)
        zb = k.sb("zb", [128, nch, 256], BF16)
        Yc = k.sb("Yc", [128, nch, 256], BF16)
        Ys = k.sb("Ys", [128, nch, 256], BF16)
        sp_ = Pool(k, "scur", [128, 768], F32, 2)
        pp_ = Pool(k, "sprev", [128, 768], F32, 2)
        np_ = Pool(k, "snext", [128, 768], F32, 2)
        up = Pool(k, "u", [128, 768], F32, 2)
        tp = Pool(k, "tt", [128, 768], F32, 2)
        for c in range(nch):
            r0 = base + c * 128
            sc_, sp, sn = sp_.get(), pp_.get(), np_.get()
            k.dma("sync", sc_[:], HYs[r0:r0 + 128, :])
            if c == 0:
                k.memset(sp[0:1, :], 0.0)
                k.dma("sync", sp[1:128, :], HYs[r0:r0 + 127, :])
            else:
                k.dma("sync", sp[:], HYs[r0 - 1:r0 + 127, :])
            if c == nch - 1:
                k.memset(sn[:], 0.0)
                k.dma("sync", sn[0:127, :], HYs[r0 + 1:r0 + 128, :])
            else:
                k.dma("sync", sn[:], HYs[r0 + 1:r0 + 129, :])
            u = up.get()
            t1 = tp.get()
            t2 = tp.get()
            k.tt(u[:], sc_[:], cw[:, 1, :], ALU.mult)
            k.tt(t1[:], sp[:], cw[:, 0, :], ALU.mult, eng="gpsimd")
            k.tt(t2[:], sn[:], cw[:, 2, :], ALU.mult, eng="gpsimd")
            k.tt(u[:], u[:], t1[:], ALU.add)
            k.tt(v[:, c, :], u[:, 0:256], t2[:, 0:256], ALU.add)
            k.tt(x1[:, c, :], u[:, 256:512], t2[:, 256:512], ALU.add)
            k.tt(x2[:, c, :], u[:, 512:768], t2[:, 512:768], ALU.add)
            k.copy(vb[:, c, :], v[:, c, :], eng="scalar")
        cbp = Pool(k, "cb", [128, nch, 128], BF16, 3)
        sbp = Pool(k, "sbk", [128, nch, 128], BF16, 3)
        pup = Pool(k, "pU", [128, 256], F32, 4, space="ps")
        pyp = Pool(k, "pY", [128, 256], F32, 2, space="ps")
        krp = Pool(k, "kr", [128, 2, 512], F32, 2)
        usp = Pool(k, "us", [128, 2, 256], F32, 2)
        tp2 = Pool(k, "t2", [128, 256], F32, 6)
        outp = Pool(k, "yo", [128, 256], F32, 2)
        C4 = C.I(f"hy_C4t_{n}")
        S4 = C.I(f"hy_S4t_{n}")
        for o in range(2):
            src = vb if o == 0 else zb
            for fc in range(nch):
                cb = cbp.get()
                sb_ = sbp.get()
                k.dma("sync", cb[:].re("p c m -> p (c m)"), C4[fc])
                k.dma("sync", sb_[:].re("p c m -> p (c m)"), S4[fc])
                pUc, pUs = pup.get(), pup.get()
                for c in range(nch):
                    st, sp = (c == 0), (c == nch - 1)
                    k.mm(pUc[:], cb[:, c, :], src[:, c, :], start=st, stop=sp)
                    k.mm(pUs[:], sb_[:, c, :], src[:, c, :], start=st, stop=sp)
                kr = krp.get()
                k.dma("sync", kr[:], KRI[fc * 128:(fc + 1) * 128])
                us = usp.get()
                k.copy(us[:, 0, :], pUc[:], eng="scalar")
                k.copy(us[:, 1, :], pUs[:], eng="scalar")
                KR = kr[:, 0, o * 256:(o + 1) * 256]
                KI = kr[:, 1, o * 256:(o + 1) * 256]
                a, b_, c_, d_ = tp2.get(), tp2.get(), tp2.get(), tp2.get()
                k.tt(a[:], us[:, 0, :], KR, ALU.mult)
                k.tt(b_[:], us[:, 1, :], KI, ALU.mult, eng="gpsimd")
                k.tt(Yc[:, fc, :], a[:], b_[:], ALU.add)
                k.tt(c_[:], us[:, 1, :], KR, ALU.mult, eng="gpsimd")
                k.tt(d_[:], us[:, 0, :], KI, ALU.mult)
                k.tt(Ys[:, fc, :], c_[:], d_[:], ALU.subtract, eng="gpsimd")
            for tc in range(nch):
                cb = cbp.get()
                sb_ = sbp.get()
                k.dma("sync", cb[:].re("p c m -> p (c m)"), C4[tc])
                k.dma("sync", sb_[:].re("p c m -> p (c m)"), S4[tc])
                py = pyp.get()
                for f in range(nch):
                    k.mm(py[:], cb[:, f, :], Yc[:, f, :], start=(f == 0), stop=False)
                    k.mm(py[:], sb_[:, f, :], Ys[:, f, :], start=False, stop=(f == nch - 1))
                a, b_ = tp2.get(), tp2.get()
                if o == 0:
                    k.tt(a[:], v[:, tc, :], hb[:, 0, :], ALU.mult, eng="gpsimd")
                    k.tt(b_[:], py[:], a[:], ALU.add)
                    k.tt(z[:, tc, :], b_[:], x1[:, tc, :], ALU.mult)
                    k.copy(zb[:, tc, :], z[:, tc, :], eng="scalar")
                else:
                    k.tt(a[:], z[:, tc, :], hb[:, 1, :], ALU.mult, eng="gpsimd")
                    k.tt(b_[:], py[:], a[:], ALU.add)
                    yo = outp.get()
                    k.tt(yo[:], b_[:], x2[:, tc, :], ALU.mult)
                    k.dma("sync", G[base + tc * 128:base + (tc + 1) * 128, 0:256], yo[:])


def hy_stage(k, C, l, need_ctx):
    if not hasattr(C, "KRI"):
        C.KRI = {n: k.dram(f"s_kri{n}", [n, 2, 512], F32) for n in (SEQ, CTX)}
    hyf_stage(k, C, l, SEQ, C.KRI[SEQ])
    hyc_stage(k, C, l, SEQ, CTX, C.KRI[SEQ])
    if need_ctx:
        hyf_stage(k, C, l, CTX, C.KRI[CTX])
        hyc_stage(k, C, l, CTX, 0, C.KRI[CTX])


def _chunk_consts():
    i = np.arange(64)
    tri = np.zeros((64, 2, 64), np.float32)
    tri[:, 0, :] = (i[:, None] <= i[None, :])
    tri[:, 1, :] = (i[:, None] >= i[None, :])
    after_eq = np.zeros((64, 8, 64), bool)
    after_st = np.zeros((64, 8, 64), bool)
    before_st = np.zeros((64, 8, 64), bool)
    for e in range(8):
        if e < 4:
            after_eq[:, e, :] = i[None, :] >= i[:, None]
            after_st[:, e, :] = i[None, :] > i[:, None]
            before_st[:, e, :] = i[None, :] < i[:, None]
        else:
            after_eq[:, e, :] = i[None, :] <= i[:, None]
            after_st[:, e, :] = i[None, :] < i[:, None]
            before_st[:, e, :] = i[None, :] > i[:, None]
    NEG = np.float32(-30000.0)
    mneg = np.stack([np.where(after_eq, 0, NEG), np.where(before_st, 0, NEG)], axis=1).astype(np.float32)
    m01 = np.stack([after_eq, after_st, before_st], axis=1).astype(np.float32)
    ident8 = np.tile(np.eye(64, dtype=np.float32)[:, None, :], (1, 8, 1))
    return {"c_tri": tri, "c_mneg": np.ascontiguousarray(mneg), "c_m01": np.ascontiguousarray(m01), "c_ident8": ident8}


_CC = {}


def _cc(name):
    if not _CC:
        _CC.update(_chunk_consts())
    return _CC[name]


for _nm, _sh in (("c_tri", [64, 2, 64]), ("c_mneg", [64, 2, 8, 64]), ("c_m01", [64, 3, 8, 64]), ("c_ident8", [64, 8, 64])):
    IN_SHAPES[_nm] = _sh
    HOST_LAYOUT[_nm] = (lambda inp, b, _nm=_nm: _cc(_nm))

NCH = NT // 64
FORD = list(range(NCH))
BORD = [3, 2, 1, 0] + list(range(NCH - 1, 3, -1))


def tri_inverse(k, M, A, ident8, pool_ps, pool_f, pool_b, levels=5):
    X = pool_f.get()
    k.tt(X[:], ident8[:], A[:], ALU.subtract)
    Mk, Ak = M, A
    for lev in range(1, levels + 1):
        pM = pool_ps.get()
        for e in range(8):
            k.mm(pM[:, e, :], Ak[:, e, :], Mk[:, e, :])
        if lev < levels:
            pA = pool_ps.get()
            for e in range(8):
                k.mm(pA[:, e, :], Mk[:, e, :], Ak[:, e, :])
            An = pool_f.get()
            k.copy(An[:], pA[:], eng="vector")
        Mn = pool_f.get()
        k.copy(Mn[:], pM[:], eng="scalar")
        pX = pool_ps.get()
        for e in range(8):
            k.mm(pX[:, e, :], Mn[:, e, :], X[:, e, :])
        Xn = pool_f.get()
        k.tt(Xn[:], X[:], pX[:], ALU.add)
        X = Xn
        Mk = Mn
        if lev < levels:
            Ak = An
    Xb = None
    if pool_b is not None:
        Xb = pool_b.get()
        k.copy(Xb[:], X[:], eng="gpsimd")
    return X, Xb


IN_SHAPES["dn_conv"] = [NL, 3, 768]
IN_SHAPES["dn_dtb8"] = [NL, 8]
IN_SHAPES["dn_alog8"] = [NL, 8]
IN_SHAPES["dn_normT"] = [NL, 256]
HOST_LAYOUT["dn_dtb8"] = lambda inp, b: inp["dn_dt_bias"].reshape(NL, 8)
HOST_LAYOUT["dn_alog8"] = lambda inp, b: inp["dn_a_log"].reshape(NL, 8)
HOST_LAYOUT["dn_normT"] = lambda inp, b: np.tile(inp["dn_norm"], (1, 4))


class Lane:
    pass


def run_lanes(k, nlanes, mkpools, body, items):
    lanes = [mkpools(i) for i in range(nlanes)]
    items = list(items)
    for i0 in range(0, len(items), nlanes):
        lists = []
        for j, it in enumerate(items[i0:i0 + nlanes]):
            with k.defer() as L:
                body(it, lanes[j])
            lists.append(L)
        k.replay(lists)


def shifted_loads(k, src, r0, ncols, seq_first, seq_last, pools):
    pc, pp, pn = pools
    sc_, sp, sn = pc.get(), pp.get(), pn.get()
    W = sc_.ap_shape[1]
    k.dma("sync", sc_[:], src[r0:r0 + 128, 0:W])
    if seq_first:
        k.memset(sp[0:1, :], 0.0)
        k.dma("sync", sp[1:128, :], src[r0:r0 + 127, 0:ncols])
    else:
        k.dma("sync", sp[:], src[r0 - 1:r0 + 127, 0:ncols])
    if seq_last:
        k.memset(sn[:], 0.0)
        k.dma("sync", sn[0:127, :], src[r0 + 1:r0 + 128, 0:ncols])
    else:
        k.dma("sync", sn[:], src[r0 + 1:r0 + 129, 0:ncols])
    return sc_, sp, sn


def dn_pre_stage(k, C, l):
    S = C.S
    with k.stage():
        cw = k.sb("cw", [128, 3, 768], F32)
        for i in range(3):
            bc_load(k, cw[:, i, :], C.I("dn_conv").t[l, i], C.I("dn_conv").tok)
        dtb = k.sb("dtb", [128, 8], F32)
        nA = k.sb("nA", [128, 8], F32)
        bc_load(k, dtb[:], C.I("dn_dtb8").t[l], C.I("dn_dtb8").tok)
        bc_load(k, nA[:], C.I("dn_alog8").t[l], C.I("dn_alog8").tok)
        k.act(nA[:], nA[:], AF.Exp)
        k.ts(nA[:], nA[:], -1.0, ALU.mult)
        def mkpools(i):
            L = Lane()
            L.pc = Pool(k, f"scur{i}", [128, 1040], F32, 2)
            L.pp = Pool(k, f"sprev{i}", [128, 768], F32, 2)
            L.pn = Pool(k, f"snext{i}", [128, 768], F32, 2)
            for p_ in (L.pc, L.pp, L.pn):
                for b_ in p_.bufs:
                    b_.ap_shape = b_.t.shape
            L.up = Pool(k, f"u{i}", [128, 768], F32, 1)
            L.tp = Pool(k, f"tt{i}", [128, 768], F32, 3)
            L.qp = Pool(k, f"qo{i}", [128, 768], F32, 2)
            L.gbp = Pool(k, f"gb{i}", [128, 16], F32, 2)
            L.smp = Pool(k, f"sm{i}", [128, 8], F32, 4)
            return L

        def body(tc, L):
            r0 = tc * 128
            sc_, sp, sn = shifted_loads(k, S["DN"], r0, 768, tc in (0, 2), tc in (1, NT // 128 - 1), (L.pc, L.pp, L.pn))
            u, t1, t2 = L.up.get(), L.tp.get(), L.tp.get()
            k.tt(u[:], sc_[:, 0:768], cw[:, 1, :], ALU.mult)
            k.tt(t1[:], sp[:], cw[:, 0, :], ALU.mult, eng="gpsimd")
            k.tt(t2[:], sn[:], cw[:, 2, :], ALU.mult, eng="gpsimd")
            k.tt(u[:], u[:], t1[:], ALU.add)
            k.tt(u[:], u[:], t2[:], ALU.add)
            qo = L.qp.get()
            k.act(qo[:], u[:], AF.Silu)
            sq = L.tp.get()
            k.tt(sq[:, 0:512], qo[:, 0:512], qo[:, 0:512], ALU.mult, eng="gpsimd")
            ss = L.smp.get()
            k.reduce(ss[:], sq[:, 0:512].re("p (g d) -> p g d", d=64))
            rn = L.smp.get()
            k.act(rn[:], ss[:], AF.Sqrt, bias=C.eps_t[:])
            k.recip(rn[:], rn[:])
            k.ts(rn[:, 0:4], rn[:, 0:4], 0.125, ALU.mult)
            k.tt(qo[:, 0:512].re("p (g d) -> p g d", d=64), qo[:, 0:512].re("p (g d) -> p g d", d=64),
                 rn[:, :, None].bc([128, 8, 64]), ALU.mult)
            k.dma("sync", S["DNQKV"][r0:r0 + 128, :], qo[:])
            ba = sc_[:, 1024:1040].re("p (d a h) -> p d a h", d=2, a=2)
            gb = L.gbp.get()
            k.act(gb[:, 8:16].re("p (d h) -> p d h", d=2), ba[:, :, 0, :], AF.Sigmoid)
            x = L.smp.get()
            k.tt(x[:].re("p (d h) -> p d h", d=2), ba[:, :, 1, :], dtb[:].re("p (d h) -> p d h", d=2), ALU.add)
            k.act(x[:], x[:], AF.Exp)
            k.act(x[:], x[:], AF.Ln, bias=1.0)
            k.tt(gb[:, 0:8], x[:], nA[:], ALU.mult)
            k.dma("sync", S["DNGB"][r0:r0 + 128, :], gb[:])

        run_lanes(k, 3, mkpools, body, range(NT // 128))


def dn_scan_stage(k, C, l, need_ctx):
    S = C.S
    with k.stage():
        tri = k.sb("tri", [64, 2, 64], F32)
        mneg = k.sb("mneg", [64, 2, 8, 64], F32)
        id8 = k.sb("id8", [64, 8, 64], F32)
        k.dma("sync", tri[:], C.I("c_tri")[:])
        k.dma("sync", mneg[:], C.I("c_mneg")[:])
        k.dma("sync", id8[:], C.I("c_ident8")[:])
        idn = C.ident[0:64, 0:64]
        idbt = k.sb("idbt", [64, 64], BF16)
        k.copy(idbt[:], C.ident[0:64, 0:64])
        idb = idbt[:]
        St = k.sb("St", [64, 8, 64], F32)
        k.memset(St[:], 0.0)
        pps = Pool(k, "pp", [64, 8, 64], F32, 7, space="ps")
        psm = k.ps("psm", [64, 8], F32)
        sbp = Pool(k, "w", [64, 8, 64], F32, 40)
        inv_f = Pool(k, "ivf", [64, 8, 64], F32, 8)
        inv_b = Pool(k, "ivb", [64, 8, 64], BF16, 6)
        sbb = Pool(k, "wb", [64, 8, 64], BF16, 24)
        Stb = k.sb("Stb", [64, 8, 64], BF16)
        k.memset(Stb[:], 0.0)
        qkp = Pool(k, "qk8", [64, 2, 8, 64], F32, 2)
        v8p = Pool(k, "v8", [64, 8, 64], F32, 2)
        gbp = Pool(k, "gb8", [64, 2, 8], F32, 2)
        smp = Pool(k, "sm", [64, 8], F32, 8)
        import os
        nsteps = int(os.environ.get("DN_STEPS", str(NCH)))
        part = int(os.environ.get("DN_PART", "99"))
        for s in range(nsteps):
            ch = (FORD[s], BORD[s])
            qk8, v8, gb8 = qkp.get(), v8p.get(), gbp.get()
            for d in range(2):
                r0 = ch[d] * 64
                es = slice(d * 4, d * 4 + 4)
                k.dma("sync", qk8[:, :, es, :], V(S["DNQKV"].t[r0:r0 + 64, 0:512].rearrange("p (a h d) -> p a h d", a=2, h=4), [S["DNQKV"].tok]))
                k.dma("sync", v8[:, es, :], V(S["DNQKV"].t[r0:r0 + 64, 512:768].rearrange("p (h d) -> p h d", h=4), [S["DNQKV"].tok]))
                k.dma("sync", gb8[:, 0, es], S["DNGB"][r0:r0 + 64, d * 4:d * 4 + 4])
                k.dma("sync", gb8[:, 1, es], S["DNGB"][r0:r0 + 64, 8 + d * 4:8 + d * 4 + 4])
            beta_bc = gb8[:, 1, :, None].bc([64, 8, 64])
            if part < 1:
                continue
            for d in range(2):
                k.mm(psm[:, d * 4:d * 4 + 4], tri[:, d, :], gb8[:, 0, d * 4:d * 4 + 4])
            gcc = smp.get()
            k.copy(gcc[:], psm[:], eng="scalar")
            sub = int(os.environ.get("DN_SUB", "99"))
            if sub < 2:
                continue
            grep = sbp.get()
            k.copy(grep[:], gb8[:, 0, :, None].bc([64, 8, 64]))
            pgcb = pps.get()
            for e in range(8):
                k.mm(pgcb[:, e, :], grep[:, e, :], tri[:, e // 4, :])
            if sub < 3:
                continue
            gcb = sbp.get()
            k.copy(gcb[:], pgcb[:], eng="scalar")
            D1 = sbp.get()
            k.tt(D1[:], gcb[:], gcc[:, :, None].bc([64, 8, 64]), ALU.subtract)
            egcb = sbp.get()
            k.act(egcb[:], gcb[:], AF.Exp)
            if sub < 4:
                continue
            t = sbp.get()
            k.tt(t[:], D1[:], mneg[:, 0], ALU.add, eng="gpsimd")
            E1 = sbp.get()
            k.act(E1[:], t[:], AF.Exp)
            if sub < 5:
                continue
            t = sbp.get()
            k.stt(t[:], D1[:], -1.0, mneg[:, 1], ALU.mult, ALU.add)
            E2 = sbp.get()
            k.act(E2[:], t[:], AF.Exp)
            if sub < 6:
                continue
            egc = smp.get()
            k.act(egc[:], gcc[:], AF.Exp)
            ekd = smp.get()
            k.act(ekd[:, 0:4], D1[:, 0:4, 63], AF.Exp)
            k.act(ekd[:, 4:8], D1[:, 4:8, 0], AF.Exp)
            if part < 2:
                continue
            pKT, pQT = pps.get(), pps.get()
            for e in range(8):
                k.tr(pKT[:, e, :], qk8[:, 1, e, :], idn)
            for e in range(8):
                k.tr(pQT[:, e, :], qk8[:, 0, e, :], idn)
            KT, QT = sbb.get(), sbb.get()
            k.copy(KT[:], pKT[:], eng="scalar")
            k.copy(QT[:], pQT[:], eng="vector")
            if part < 3:
                continue
            pG, pQK = pps.get(), pps.get()
            for e in range(8):
                k.mm(pG[:, e, :], KT[:, e, :], KT[:, e, :])
            for e in range(8):
                k.mm(pQK[:, e, :], KT[:, e, :], QT[:, e, :])
            t = sbp.get()
            k.tt(t[:], pG[:], E2[:], ALU.mult)
            M = sbp.get()
            k.tt(M[:], t[:], beta_bc, ALU.mult, eng="gpsimd")
            attnT = sbb.get()
            k.tt(attnT[:], pQK[:], E1[:], ALU.mult)
            pA = pps.get()
            for e in range(8):
                k.tr(pA[:, e, :], M[:, e, :], idn)
            A = sbp.get()
            k.copy(A[:], pA[:], eng="scalar")
            if part < 4:
                continue
            X = tri_inverse(k, M, A, id8, pps, inv_f, inv_b)
            if part < 5:
                continue
            Vb = sbb.get()
            k.tt(Vb[:], v8[:], beta_bc, ALU.mult, eng="gpsimd")
            bg = smp.get()
            k.tt(bg[:], gb8[:, 1, :], egc[:], ALU.mult)
            KBG = sbb.get()
            k.tt(KBG[:], qk8[:, 1], bg[:, :, None].bc([64, 8, 64]), ALU.mult, eng="gpsimd")
            pU, pW = pps.get(), pps.get()
            for e in range(8):
                k.mm(pU[:, e, :], X[:, e, :], Vb[:, e, :])
            for e in range(8):
                k.mm(pW[:, e, :], KBG[:, e, :], X[:, e, :])
            U, WT = sbp.get(), sbb.get()
            k.copy(U[:], pU[:], eng="scalar")
            k.copy(WT[:], pW[:], eng="vector")
            QdT = sbb.get()
            k.tt(QdT[:], QT[:], egcb[:], ALU.mult, eng="gpsimd")
            Kd = sbb.get()
            k.tt(Kd[:], qk8[:, 1], ekd[:, :, None].bc([64, 8, 64]), ALU.mult, eng="gpsimd")
            if part < 6:
                continue
            pv = pps.get()
            for e in range(8):
                k.mm(pv[:, e, :], WT[:, e, :], Stb[:, e, :])
            vnew = sbb.get()
            k.tt(vnew[:], U[:], pv[:], ALU.subtract)
            po = pps.get()
            for e in range(8):
                k.mm(po[:, e, :], QdT[:, e, :], Stb[:, e, :], start=True, stop=False)
                k.mm(po[:, e, :], attnT[:, e, :], vnew[:, e, :], start=False, stop=True)
            o8 = sbp.get()
            k.copy(o8[:], po[:], eng="scalar")
            for d in range(2):
                if need_ctx or ch[d] >= 4:
                    k.dma("sync", S["DNO"][d, ch[d] * 64:(ch[d] + 1) * 64, :], o8[:, d * 4:d * 4 + 4, :].re("p h d -> p (h d)"))
            pS = pps.get()
            for e in range(8):
                k.mm(pS[:, e, :], Kd[:, e, :], vnew[:, e, :])
            glast = smp.get()
            k.copy(glast[:, 0:4], egcb[:, 0:4, 63])
            k.copy(glast[:, 4:8], egcb[:, 4:8, 0])
            t = sbp.get()
            k.tt(t[:], St[:], glast[:, :, None].bc([64, 8, 64]), ALU.mult)
            k.tt(St[:], t[:], pS[:], ALU.add)
            k.copy(Stb[:], St[:], eng="gpsimd")


def dn_post_stage(k, C, l, need_ctx):
    S = C.S
    with k.stage():
        nwt = k.sb("nwt", [128, 256], F32)
        bc_load(k, nwt[:], C.I("dn_normT").t[l], C.I("dn_normT").tok)
        op_ = Pool(k, "o", [128, 2, 256], F32, 2)
        zp = Pool(k, "z", [128, 256], F32, 2)
        tp = Pool(k, "t", [128, 256], F32, 6)
        smp = Pool(k, "sm", [128, 4], F32, 4)
        for tc in range(NT // 128):
            if tc < 2 and not need_ctx:
                continue
            r0 = tc * 128
            o = op_.get()
            k.dma("sync", o[:], V(S["DNO"].t[:, r0:r0 + 128, :].rearrange("d p c -> p d c"), [S["DNO"].tok]))
            zt = zp.get()
            k.dma("sync", zt[:], S["DN"][r0:r0 + 128, 768:1024])
            os_ = tp.get()
            k.tt(os_[:], o[:, 0, :], o[:, 1, :], ALU.add)
            sq = tp.get()
            k.tt(sq[:], os_[:], os_[:], ALU.mult, eng="gpsimd")
            ss = smp.get()
            k.reduce(ss[:], sq[:].re("p (h d) -> p h d", d=64))
            rn = smp.get()
            k.act(rn[:], ss[:], AF.Sqrt, scale=1.0 / 64, bias=C.eps_t[:])
            k.recip(rn[:], rn[:])
            sz = tp.get()
            k.act(sz[:], zt[:], AF.Silu)
            a = tp.get()
            k.tt(a[:].re("p (h d) -> p h d", d=64), os_[:].re("p (h d) -> p h d", d=64), rn[:, :, None].bc([128, 4, 64]), ALU.mult)
            k.tt(a[:], a[:], nwt[:], ALU.mult, eng="gpsimd")
            y = tp.get()
            k.tt(y[:], a[:], sz[:], ALU.mult)
            k.dma("sync", S["G"][r0:r0 + 128, 512:768], y[:])


def dn_stage(k, C, l, need_ctx):
    dn_pre_stage(k, C, l)
    scan_stage(k, C, l, need_ctx, do_dn=True, do_rw=False)
    dn_post_stage(k, C, l, need_ctx)


for _nm, _sh in (("rw_mu", [NL, 2, 896]), ("rw_w_up", [NL, 2, 32, 256]), ("rw_a_up", [NL, 2, 32, 256]),
                 ("rw_g_up", [NL, 64, 256]), ("rw_w0", [NL, 2, 256]), ("rw_a0", [NL, 2, 256]),
                 ("rw_k_k", [NL, 256]), ("rw_k_a", [NL, 256]), ("rw_ln_w", [NL, 256]), ("rw_ln_b", [NL, 256]),
                 ("rw_r_kT", [NL, 256]), ("rw_mulrT", [NL, 64, 3, 2])):
    IN_SHAPES[_nm] = _sh
HOST_LAYOUT["rw_r_kT"] = lambda inp, b: inp["rw_r_k"].reshape(NL, 256)


def _rw_mulrT(inp, b):
    mu = inp["rw_mu"]
    o = np.zeros((NL, 64, 3, 2), np.float32)
    o[:, 0:32, 0, :] = mu[:, :, 768:800].transpose(0, 2, 1)
    o[:, 0:32, 1, :] = mu[:, :, 800:832].transpose(0, 2, 1)
    o[:, :, 2, :] = mu[:, :, 832:896].transpose(0, 2, 1)
    return o


HOST_LAYOUT["rw_mulrT"] = _rw_mulrT
SEQS = ((0, CTX), (CTX, NT))
LW_SCALE = -math.exp(-0.5)


def rw_pre_stage(k, C, l):
    S = C.S
    with k.stage():
        mulr = k.sb("mulr", [64, 3, 2], F32)
        k.dma("sync", mulr[:], C.I("rw_mulrT")[l])
        c0lr = k.sb("c0lr", [64, 3], F32)
        k.tt(c0lr[:], mulr[:, :, 0], mulr[:, :, 1], ALU.add)
        k.ts(c0lr[:], c0lr[:], -1.0, ALU.mult, 1.0, ALU.add)
        lrT = []
        for g_, ncl, fn in ((0, 32, AF.Tanh), (1, 32, None), (2, 64, AF.Sigmoid)):
            u = k.sb(f"lru{g_}", [64, NT], F32)
            o = k.sb(f"lro{g_}", [64, NT], F32)
            k.dma("sync", u[0:ncl, :], S["RWLR"][g_, 0:ncl, :])
            k.ts(o[0:ncl, :], u[0:ncl, :], c0lr[0:ncl, g_:g_ + 1], ALU.mult)
            for (a, b_) in SEQS:
                k.stt(o[0:ncl, a + 1:b_], u[0:ncl, a:b_ - 1], mulr[0:ncl, g_, 0:1], o[0:ncl, a + 1:b_], ALU.mult, ALU.add)
                k.stt(o[0:ncl, a:b_ - 1], u[0:ncl, a + 1:b_], mulr[0:ncl, g_, 1:2], o[0:ncl, a:b_ - 1], ALU.mult, ALU.add)
            if fn is not None:
                k.act(o[0:ncl, :], o[0:ncl, :], fn)
            lrT.append(o)
        wup = k.sb("wup", [32, 2, 256], F32)
        aup = k.sb("aup", [32, 2, 256], F32)
        gup = k.sb("gup", [64, 256], F32)
        w0r = k.sb("w0r", [1, 2, 256], F32)
        a0r = k.sb("a0r", [1, 2, 256], F32)
        ones1 = k.sb("ones1", [1, 128], F32)
        k.memset(ones1[:], 1.0)
        for d in range(2):
            k.dma("sync", wup[:, d, :], C.I("rw_w_up")[l, d])
            k.dma("sync", aup[:, d, :], C.I("rw_a_up")[l, d])
            k.dma("sync", w0r[:, d, :], C.I("rw_w0")[l, d:d + 1])
            k.dma("sync", a0r[:, d, :], C.I("rw_a0")[l, d:d + 1])
        k.dma("sync", gup[:], C.I("rw_g_up")[l])
        mu = k.sb("mu", [128, 2, 768], F32)
        for i in range(2):
            bc_load(k, mu[:, i, :], C.I("rw_mu").t[l, i, 0:768], C.I("rw_mu").tok)
        c0 = k.sb("c0", [128, 768], F32)
        k.tt(c0[:], mu[:, 0, :], mu[:, 1, :], ALU.add)
        k.ts(c0[:], c0[:], -1.0, ALU.mult, 1.0, ALU.add)
        kkw = k.sb("kkw", [128, 256], F32)
        kaw = k.sb("kaw", [128, 256], F32)
        omka = k.sb("omka", [128, 256], F32)
        bc_load(k, kkw[:], C.I("rw_k_k").t[l], C.I("rw_k_k").tok)
        bc_load(k, kaw[:], C.I("rw_k_a").t[l], C.I("rw_k_a").tok)
        k.ts(omka[:], kaw[:], -1.0, ALU.mult, 1.0, ALU.add)
        def mkpools(i):
            L = Lane()
            L.pc = Pool(k, f"scur{i}", [128, 768], F32, 2)
            L.pp = Pool(k, f"sprev{i}", [128, 768], F32, 2)
            L.pn = Pool(k, f"snext{i}", [128, 768], F32, 2)
            for p_ in (L.pc, L.pp, L.pn):
                for b_ in p_.bufs:
                    b_.ap_shape = b_.t.shape
            L.tp = Pool(k, f"tt{i}", [128, 768], F32, 3)
            L.op_ = Pool(k, f"out{i}", [128, 10, 256], F32, 2)
            L.t2 = Pool(k, f"t2{i}", [128, 256], F32, 6)
            L.smp = Pool(k, f"sm{i}", [128, 4], F32, 4)
            L.plp = Pool(k, f"pl{i}", [128, 2, 256], F32, 3, space="ps")
            return L

        def body(tc, L):
            pc, pp, pn, tp, op_, t2, smp, plp = L.pc, L.pp, L.pn, L.tp, L.op_, L.t2, L.smp, L.plp
            r0 = tc * 128
            sc_, sp, sn = shifted_loads(k, S["RW"], r0, 768, tc in (0, 2), tc in (1, NT // 128 - 1), (pc, pp, pn))
            o = op_.get()
            s_ = tp.get()
            t1 = tp.get()
            k.tt(s_[:], sc_[:], c0[:], ALU.mult)
            k.tt(t1[:], sp[:], mu[:, 0, :], ALU.mult, eng="gpsimd")
            k.tt(s_[:], s_[:], t1[:], ALU.add)
            t1 = tp.get()
            k.tt(t1[:], sn[:], mu[:, 1, :], ALU.mult, eng="gpsimd")
            k.tt(s_[:, 0:256], s_[:, 0:256], t1[:, 0:256], ALU.add)
            k.tt(s_[:, 256:512], s_[:, 256:512], t1[:, 256:512], ALU.add)
            k.tt(o[:, 1, :], s_[:, 512:768], t1[:, 512:768], ALU.add)
            k.copy(o[:, 0, :], s_[:, 0:256], eng="scalar")
            kcur = s_[:, 256:512]
            tok = slice(r0, r0 + 128)
            pwt, pat, pgt = plp.get(), plp.get(), plp.get()
            pw = [pwt[:, 0, :], pwt[:, 1, :]]
            pa = [pat[:, 0, :], pat[:, 1, :]]
            pg = pgt[:, 0, :]
            for d in range(2):
                k.mm(pw[d], lrT[0][0:32, tok], wup[:, d, :], start=True, stop=False)
                k.mm(pw[d], ones1[:], w0r[:, d, :], start=False, stop=True)
            for d in range(2):
                k.mm(pa[d], lrT[1][0:32, tok], aup[:, d, :], start=True, stop=False)
                k.mm(pa[d], ones1[:], a0r[:, d, :], start=False, stop=True)
            k.mm(pg, lrT[2][0:64, tok], gup[:])
            k.copy(o[:, 9, :], pg, eng="scalar")
            kx = t2.get()
            k.tt(kx[:], kcur, kkw[:], ALU.mult, eng="gpsimd")
            sq = t2.get()
            k.tt(sq[:], kx[:], kx[:], ALU.mult, eng="gpsimd")
            ss = smp.get()
            k.reduce(ss[:], sq[:].re("p (h d) -> p h d", d=64))
            rn = smp.get()
            k.act(rn[:], ss[:], AF.Sqrt, bias=C.eps_t[:])
            k.recip(rn[:], rn[:])
            kk = t2.get()
            k.tt(kk[:].re("p (h d) -> p h d", d=64), kx[:].re("p (h d) -> p h d", d=64), rn[:, :, None].bc([128, 4, 64]), ALU.mult)
            k.ts(o[:, 2, :], kk[:], -1.0, ALU.mult)
            for d in range(2):
                sg = t2.get()
                k.act(sg[:], pw[d], AF.Sigmoid)
                k.ts(o[:, 3 + d, :], sg[:], LW_SCALE, ALU.mult, eng="gpsimd")
                ar = t2.get()
                k.act(ar[:], pa[d], AF.Sigmoid)
                k.tt(o[:, 7 + d, :], kk[:], ar[:], ALU.mult, eng="gpsimd")
                t = t2.get()
                k.tt(t[:], ar[:], kaw[:], ALU.mult)
                k.tt(t[:], t[:], omka[:], ALU.add)
                k.tt(o[:, 5 + d, :], kcur, t[:], ALU.mult)
            k.dma("sync", S["RWP"][r0:r0 + 128], o[:])


        run_lanes(k, 2, mkpools, body, range(NT // 128))


def rw_scan_stage(k, C, l, need_ctx):
    S = C.S
    with k.stage():
        tri = k.sb("tri", [64, 2, 64], F32)
        m01 = k.sb("m01", [64, 3, 8, 64], F32)
        nm = k.sb("nm", [64, 2, 8, 64], F32)
        id8 = k.sb("id8", [64, 8, 64], F32)
        ones64 = k.sb("ones64", [64, 64], F32)
        k.memset(ones64[:], 1.0)
        k.dma("sync", tri[:], C.I("c_tri")[:])
        k.dma("sync", m01[:], C.I("c_m01")[:])
        k.dma("sync", id8[:], C.I("c_ident8")[:])
        k.ts(nm[:], m01[:, 1:3], -1.0, ALU.mult)
        idn = C.ident[0:64, 0:64]
        St = k.sb("St", [64, 8, 64], F32)
        k.memset(St[:], 0.0)
        pps = Pool(k, "pp", [64, 8, 64], F32, 7, space="ps")
        psm = k.ps("psm", [64, 8, 2], F32)
        sbp = Pool(k, "w", [64, 8, 64], F32, 44)
        inv_sb = Pool(k, "iv", [64, 8, 64], F32, 8)
        inp_ = [Pool(k, f"in{i}", [64, 8, 64], F32, 2) for i in range(6)]
        smp = Pool(k, "sm", [64, 8], F32, 4)
        RWP = S["RWP"]
        slots = ((0, 0), (1, 1), (2, 2), (3, 4), (5, 6), (7, 8))
        for s in range(NCH):
            ch = (FORD[s], BORD[s])
            tl = [p.get() for p in inp_]
            for i, sl in enumerate(slots):
                for d in range(2):
                    r0 = ch[d] * 64
                    k.dma("sync", tl[i][:, d * 4:d * 4 + 4, :],
                          V(RWP.t[r0:r0 + 64, sl[d], :].rearrange("p (h d) -> p h d", h=4), [RWP.tok]))
            R8, V8, A8, LW8, K8, B8 = tl
            plc, ptot = pps.get(), pps.get()
            for d in range(2):
                k.mm(plc[:, d * 4:d * 4 + 4, :], tri[:, d, :], LW8[:, d * 4:d * 4 + 4, :])
            k.mm(ptot[:], ones64[:], LW8[:])
            for e in range(8):
                k.mm(psm[:, e, :], LW8[:, e, :], ones64[:, 0:2])
            lc, tot = sbp.get(), sbp.get()
            k.copy(lc[:], plc[:], eng="scalar")
            k.copy(tot[:], ptot[:], eng="vector")
            gCT = smp.get()
            k.act(gCT[:], psm[:, :, 0], AF.Exp)
            eg, egi, egp, ehat = sbp.get(), sbp.get(), sbp.get(), sbp.get()
            k.act(eg[:], lc[:], AF.Exp)
            k.act(egi[:], lc[:], AF.Exp, scale=-1.0)
            t = sbp.get()
            k.tt(t[:], lc[:], LW8[:], ALU.subtract, eng="gpsimd")
            k.act(egp[:], t[:], AF.Exp)
            t = sbp.get()
            k.tt(t[:], tot[:], lc[:], ALU.subtract)
            k.act(ehat[:], t[:], AF.Exp)
            At, Bt, Kt, Rt, Bh, Kh = [sbp.get() for _ in range(6)]
            k.tt(At[:], A8[:], egp[:], ALU.mult)
            k.tt(Bt[:], B8[:], egi[:], ALU.mult, eng="gpsimd")
            k.tt(Kt[:], K8[:], egi[:], ALU.mult)
            k.tt(Rt[:], R8[:], eg[:], ALU.mult, eng="gpsimd")
            k.tt(Bh[:], B8[:], ehat[:], ALU.mult)
            k.tt(Kh[:], K8[:], ehat[:], ALU.mult, eng="gpsimd")
            fmT = []
            for i, src in enumerate((At, Bt, Kt, Rt)):
                pT = pps.get()
                for e in range(8):
                    k.tr(pT[:, e, :], src[:, e, :], idn)
                dst = sbp.get()
                k.copy(dst[:], pT[:], eng=("scalar" if i % 2 == 0 else "vector"))
                fmT.append(dst)
            AtT, BtT, KtT, RtT = fmT
            def score(lhs, rhs, mask, eng):
                p_ = pps.get()
                for e in range(8):
                    k.mm(p_[:, e, :], lhs[:, e, :], rhs[:, e, :])
                o_ = sbp.get()
                k.tt(o_[:], p_[:], mask, ALU.mult, eng=eng)
                return o_
            M = score(AtT, BtT, nm[:, 1], "vector")
            A = score(BtT, AtT, nm[:, 0], "vector")
            AakT = score(KtT, AtT, m01[:, 1], "vector")
            ArbT = score(BtT, RtT, m01[:, 0], "vector")
            ArkT = score(KtT, RtT, m01[:, 0], "vector")
            X = tri_inverse(k, None, M, A, id8, pps, inv_sb)
            pW, pAkV = pps.get(), pps.get()
            for e in range(8):
                k.mm(pW[:, e, :], At[:, e, :], X[:, e, :])
            for e in range(8):
                k.mm(pAkV[:, e, :], AakT[:, e, :], V8[:, e, :])
            WT, AkV = sbp.get(), sbp.get()
            k.copy(WT[:], pW[:], eng="scalar")
            k.copy(AkV[:], pAkV[:], eng="vector")
            pUv = pps.get()
            for e in range(8):
                k.mm(pUv[:, e, :], X[:, e, :], AkV[:, e, :])
            Uv = sbp.get()
            k.copy(Uv[:], pUv[:], eng="scalar")
            pe = pps.get()
            for e in range(8):
                k.mm(pe[:, e, :], WT[:, e, :], St[:, e, :])
            E = sbp.get()
            k.tt(E[:], Uv[:], pe[:], ALU.add)
            py = pps.get()
            for e in range(8):
                k.mm(py[:, e, :], RtT[:, e, :], St[:, e, :], start=True, stop=False)
                k.mm(py[:, e, :], ArbT[:, e, :], E[:, e, :], start=False, stop=False)
                k.mm(py[:, e, :], ArkT[:, e, :], V8[:, e, :], start=False, stop=True)
            y8 = sbp.get()
            k.copy(y8[:], py[:], eng="scalar")
            for d in range(2):
                if need_ctx or ch[d] >= 4:
                    k.dma("sync", S["RWY"][d, ch[d] * 64:(ch[d] + 1) * 64, :], y8[:, d * 4:d * 4 + 4, :].re("p h d -> p (h d)"))
            pS = pps.get()
            for e in range(8):
                k.mm(pS[:, e, :], Bh[:, e, :], E[:, e, :], start=True, stop=False)
                k.mm(pS[:, e, :], Kh[:, e, :], V8[:, e, :], start=False, stop=True)
            t = sbp.get()
            k.tt(t[:], St[:], gCT[:, :, None].bc([64, 8, 64]), ALU.mult)
            k.tt(St[:], t[:], pS[:], ALU.add)


def rw_post_stage(k, C, l, need_ctx):
    S = C.S
    with k.stage():
        lnw = k.sb("lnw", [128, 256], F32)
        lnb = k.sb("lnb", [128, 256], F32)
        rkw = k.sb("rkw", [128, 256], F32)
        bc_load(k, lnw[:], C.I("rw_ln_w").t[l], C.I("rw_ln_w").tok)
        bc_load(k, lnb[:], C.I("rw_ln_b").t[l], C.I("rw_ln_b").tok)
        bc_load(k, rkw[:], C.I("rw_r_kT").t[l], C.I("rw_r_kT").tok)
        lneps = k.sb("lneps", [128, 1], F32)
        k.memset(lneps[:], 64e-5)
        yp = Pool(k, "y", [128, 2, 256], F32, 2)
        pp = Pool(k, "p", [128, 10, 256], F32, 2)
        tp = Pool(k, "t", [128, 256], F32, 8)
        smp = Pool(k, "sm", [128, 4], F32, 6)

        def hview(x):
            return x.re("p (h d) -> p h d", d=64)
        for tc in range(NT // 128):
            if tc < 2 and not need_ctx:
                continue
            r0 = tc * 128
            yy = yp.get()
            k.dma("sync", yy[:], V(S["RWY"].t[:, r0:r0 + 128, :].rearrange("d p c -> p d c"), [S["RWY"].tok]))
            pr = pp.get()
            k.dma("sync", pr[:], S["RWP"][r0:r0 + 128])
            y = tp.get()
            k.tt(y[:], yy[:, 0, :], yy[:, 1, :], ALU.add)
            s1 = smp.get()
            k.reduce(s1[:], hview(y[:]))
            k.ts(s1[:], s1[:], -1.0 / 64, ALU.mult)
            yc = tp.get()
            k.tt(hview(yc[:]), hview(y[:]), s1[:, :, None].bc([128, 4, 64]), ALU.add)
            sq = tp.get()
            k.tt(sq[:], yc[:], yc[:], ALU.mult, eng="gpsimd")
            s2 = smp.get()
            k.reduce(s2[:], hview(sq[:]))
            rstd = smp.get()
            k.act(rstd[:], s2[:], AF.Sqrt, scale=1.0 / 64, bias=lneps[:])
            k.recip(rstd[:], rstd[:])
            yn = tp.get()
            k.tt(hview(yn[:]), hview(yc[:]), rstd[:, :, None].bc([128, 4, 64]), ALU.mult)
            k.tt(yn[:], yn[:], lnw[:], ALU.mult, eng="gpsimd")
            k.tt(yn[:], yn[:], lnb[:], ALU.add, eng="gpsimd")
            ks = tp.get()
            k.tt(ks[:], pr[:, 5, :], pr[:, 6, :], ALU.add, eng="gpsimd")
            k.tt(ks[:], ks[:], pr[:, 0, :], ALU.mult, eng="gpsimd")
            k.tt(ks[:], ks[:], rkw[:], ALU.mult, eng="gpsimd")
            bs = smp.get()
            k.reduce(bs[:], hview(ks[:]))
            bon = tp.get()
            k.tt(hview(bon[:]), hview(pr[:, 1, :]), bs[:, :, None].bc([128, 4, 64]), ALU.mult)
            k.tt(yn[:], yn[:], bon[:], ALU.add)
            out = tp.get()
            k.tt(out[:], yn[:], pr[:, 9, :], ALU.mult)
            k.dma("sync", S["G"][r0:r0 + 128, 768:1024], out[:])


def rw_stage(k, C, l, need_ctx):
    rw_pre_stage(k, C, l)
    scan_stage(k, C, l, need_ctx, do_dn=False, do_rw=True)
    rw_post_stage(k, C, l, need_ctx)


def dnrw_stage(k, C, l, need_ctx):
    dn_pre_stage(k, C, l)
    rw_pre_stage(k, C, l)
    scan_stage(k, C, l, need_ctx)
    dn_post_stage(k, C, l, need_ctx)
    rw_post_stage(k, C, l, need_ctx)


def dn_scan_setup(k, C, need_ctx, sh):
    X = Lane()
    X.S = C.S
    X.need_ctx = need_ctx
    X.tri, X.id8, X.idn = sh.tri, sh.id8, sh.idn
    X.mneg = k.sb("mneg", [64, 2, 8, 64], F32)
    k.dma("sync", X.mneg[:], C.I("c_mneg")[:])
    X.St = k.sb("dSt", [64, 8, 64], F32)
    X.Stb = k.sb("dStb", [64, 8, 64], BF16)
    k.memset(X.St[:], 0.0)
    k.memset(X.Stb[:], 0.0)
    X.pps = Pool(k, "dpp", [64, 8, 64], F32, sh.npp_dn, space="ps")
    X.psm = sh.psm_dn
    X.sbp = Pool(k, "dw", [64, 8, 64], F32, 14)
    X.sbb = Pool(k, "dwb", [64, 8, 64], BF16, 9)
    X.inv_f = Pool(k, "divf", [64, 8, 64], F32, 8)
    X.inv_b = Pool(k, "divb", [64, 8, 64], BF16, 2)
    X.qkp = Pool(k, "qk8", [64, 2, 8, 64], F32, 2)
    X.v8p = Pool(k, "v8", [64, 8, 64], F32, 2)
    X.gbp = Pool(k, "gb8", [64, 2, 8], F32, 2)
    X.smp = Pool(k, "dsm", [64, 8], F32, 8)
    return X


def dn_step(k, X, s):
    S = X.S
    tri, mneg, id8, idn, St, Stb = X.tri, X.mneg, X.id8, X.idn, X.St, X.Stb
    pps, psm, sbp, sbb, smp = X.pps, X.psm, X.sbp, X.sbb, X.smp
    ch = (FORD[s], BORD[s])
    qk8, v8, gb8 = X.qkp.get(), X.v8p.get(), X.gbp.get()
    for d in range(2):
        r0 = ch[d] * 64
        es = slice(d * 4, d * 4 + 4)
        k.dma("sync", qk8[:, :, es, :], V(S["DNQKV"].t[r0:r0 + 64, 0:512].rearrange("p (a h d) -> p a h d", a=2, h=4), [S["DNQKV"].tok]))
        k.dma("sync", v8[:, es, :], V(S["DNQKV"].t[r0:r0 + 64, 512:768].rearrange("p (h d) -> p h d", h=4), [S["DNQKV"].tok]))
        k.dma("sync", gb8[:, 0, es], S["DNGB"][r0:r0 + 64, d * 4:d * 4 + 4])
        k.dma("sync", gb8[:, 1, es], S["DNGB"][r0:r0 + 64, 8 + d * 4:8 + d * 4 + 4])
    beta_bc = gb8[:, 1, :, None].bc([64, 8, 64])
    for d in range(2):
        k.mm(psm[:, d * 4:d * 4 + 4], tri[:, d, :], gb8[:, 0, d * 4:d * 4 + 4])
    gcc = smp.get()
    k.copy(gcc[:], psm[:], eng="scalar")
    grep = sbp.get()
    k.copy(grep[:], gb8[:, 0, :, None].bc([64, 8, 64]))
    pgcb = pps.get()
    for e in range(8):
        k.mm(pgcb[:, e, :], grep[:, e, :], tri[:, e // 4, :])
    gcb = sbp.get()
    k.copy(gcb[:], pgcb[:], eng="scalar")
    D1 = sbp.get()
    k.tt(D1[:], gcb[:], gcc[:, :, None].bc([64, 8, 64]), ALU.subtract)
    egcb = sbp.get()
    k.act(egcb[:], gcb[:], AF.Exp)
    t = sbp.get()
    k.tt(t[:], D1[:], mneg[:, 0], ALU.add, eng="gpsimd")
    E1 = sbp.get()
    k.act(E1[:], t[:], AF.Exp)
    t = sbp.get()
    k.stt(t[:], D1[:], -1.0, mneg[:, 1], ALU.mult, ALU.add)
    E2 = sbp.get()
    k.act(E2[:], t[:], AF.Exp)
    egc = smp.get()
    k.act(egc[:], gcc[:], AF.Exp)
    ekd = smp.get()
    k.act(ekd[:, 0:4], D1[:, 0:4, 63], AF.Exp)
    k.act(ekd[:, 4:8], D1[:, 4:8, 0], AF.Exp)
    pKT = pps.get()
    for e in range(8):
        k.tr(pKT[:, e, :], qk8[:, 1, e, :], idn)
    KT = sbb.get()
    k.copy(KT[:], pKT[:], eng="scalar")
    pQT = pps.get()
    for e in range(8):
        k.tr(pQT[:, e, :], qk8[:, 0, e, :], idn)
    QT = sbb.get()
    k.copy(QT[:], pQT[:], eng="vector")
    pG = pps.get()
    for e in range(8):
        k.mm(pG[:, e, :], KT[:, e, :], KT[:, e, :])
    t = sbp.get()
    k.tt(t[:], pG[:], E2[:], ALU.mult)
    M = sbp.get()
    k.tt(M[:], t[:], beta_bc, ALU.mult, eng="gpsimd")
    pQK = pps.get()
    for e in range(8):
        k.mm(pQK[:, e, :], KT[:, e, :], QT[:, e, :])
    attnT = sbb.get()
    k.tt(attnT[:], pQK[:], E1[:], ALU.mult)
    pA = pps.get()
    for e in range(8):
        k.tr(pA[:, e, :], M[:, e, :], idn)
    A = sbp.get()
    k.copy(A[:], pA[:], eng="scalar")
    _, Xi = tri_inverse(k, M, A, id8, pps, X.inv_f, X.inv_b)
    Vb = sbb.get()
    k.tt(Vb[:], v8[:], beta_bc, ALU.mult, eng="gpsimd")
    bg = smp.get()
    k.tt(bg[:], gb8[:, 1, :], egc[:], ALU.mult)
    KBG = sbb.get()
    k.tt(KBG[:], qk8[:, 1], bg[:, :, None].bc([64, 8, 64]), ALU.mult, eng="gpsimd")
    pU = pps.get()
    for e in range(8):
        k.mm(pU[:, e, :], Xi[:, e, :], Vb[:, e, :])
    U = sbp.get()
    k.copy(U[:], pU[:], eng="scalar")
    pW = pps.get()
    for e in range(8):
        k.mm(pW[:, e, :], KBG[:, e, :], Xi[:, e, :])
    WT = sbb.get()
    k.copy(WT[:], pW[:], eng="vector")
    QdT = sbb.get()
    k.tt(QdT[:], QT[:], egcb[:], ALU.mult, eng="gpsimd")
    Kd = sbb.get()
    k.tt(Kd[:], qk8[:, 1], ekd[:, :, None].bc([64, 8, 64]), ALU.mult, eng="gpsimd")
    glast = smp.get()
    k.copy(glast[:, 0:4], egcb[:, 0:4, 63], eng="gpsimd")
    k.copy(glast[:, 4:8], egcb[:, 4:8, 0], eng="gpsimd")
    pv = pps.get()
    for e in range(8):
        k.mm(pv[:, e, :], WT[:, e, :], Stb[:, e, :])
    vnew = sbb.get()
    k.tt(vnew[:], U[:], pv[:], ALU.subtract)
    po = pps.get()
    for e in range(8):
        k.mm(po[:, e, :], QdT[:, e, :], Stb[:, e, :], start=True, stop=False)
        k.mm(po[:, e, :], attnT[:, e, :], vnew[:, e, :], start=False, stop=True)
    pS = pps.get()
    for e in range(8):
        k.mm(pS[:, e, :], Kd[:, e, :], vnew[:, e, :])
    t = sbp.get()
    k.tt(t[:], St[:], glast[:, :, None].bc([64, 8, 64]), ALU.mult, eng="gpsimd")
    k.tt(St[:], t[:], pS[:], ALU.add)
    k.copy(Stb[:], St[:], eng="gpsimd")
    o8 = sbp.get()
    k.copy(o8[:], po[:], eng="scalar")
    for d in range(2):
        if X.need_ctx or ch[d] >= 4:
            k.dma("sync", S["DNO"][d, ch[d] * 64:(ch[d] + 1) * 64, :], o8[:, d * 4:d * 4 + 4, :].re("p h d -> p (h d)"))


def rw_scan_setup(k, C, need_ctx, sh):
    X = Lane()
    X.S = C.S
    X.need_ctx = need_ctx
    X.tri, X.id8, X.idn = sh.tri, sh.id8, sh.idn
    X.m01 = k.sb("m01", [64, 3, 8, 64], F32)
    X.nm = k.sb("nm", [64, 2, 8, 64], F32)
    X.ones64 = k.sb("ones64", [64, 64], F32)
    k.memset(X.ones64[:], 1.0)
    k.dma("sync", X.m01[:], C.I("c_m01")[:])
    k.ts(X.nm[:], X.m01[:, 1:3], -1.0, ALU.mult)
    X.St = k.sb("rSt", [64, 8, 64], F32)
    X.Stb = X.St
    k.memset(X.St[:], 0.0)
    X.pps = Pool(k, "rpp", [64, 8, 64], F32, sh.npp_rw, space="ps")
    X.psm = sh.psm_rw
    X.sbp = Pool(k, "rw", [64, 8, 64], F32, 17)
    X.sbb = Pool(k, "rwb", [64, 8, 64], F32, 12)
    X.inv_f = Pool(k, "rivf", [64, 8, 64], F32, 8)
    X.inp_ = [Pool(k, f"rin{i}", [64, 8, 64], F32, 2) for i in range(6)]
    X.smp = Pool(k, "rsm", [64, 8], F32, 4)
    return X


RW_SLOTS = ((0, 0), (1, 1), (2, 2), (3, 4), (5, 6), (7, 8))


def rw_step(k, X, s):
    S = X.S
    tri, m01, nm, id8, idn, ones64, St, Stb = X.tri, X.m01, X.nm, X.id8, X.idn, X.ones64, X.St, X.Stb
    pps, psm, sbp, sbb, smp = X.pps, X.psm, X.sbp, X.sbb, X.smp
    RWP = S["RWP"]
    ch = (FORD[s], BORD[s])
    tl = [p.get() for p in X.inp_]
    for i, sl in enumerate(RW_SLOTS):
        for d in range(2):
            r0 = ch[d] * 64
            k.dma("sync", tl[i][:, d * 4:d * 4 + 4, :],
                  V(RWP.t[r0:r0 + 64, sl[d], :].rearrange("p (h d) -> p h d", h=4), [RWP.tok]))
    R8, V8, A8, LW8, K8, B8 = tl
    plc = pps.get()
    for d in range(2):
        k.mm(plc[:, d * 4:d * 4 + 4, :], tri[:, d, :], LW8[:, d * 4:d * 4 + 4, :])
    lc = sbp.get()
    k.copy(lc[:], plc[:], eng="scalar")
    ptot = pps.get()
    k.mm(ptot[:], ones64[:], LW8[:])
    tot = sbp.get()
    k.copy(tot[:], ptot[:], eng="vector")
    for e in range(8):
        k.mm(psm[:, e, :], LW8[:, e, :], ones64[:, 0:2])
    gCT = smp.get()
    k.act(gCT[:], psm[:, :, 0], AF.Exp)
    V8b = V8
    eg, egi, egp, ehat = sbp.get(), sbp.get(), sbp.get(), sbp.get()
    k.act(eg[:], lc[:], AF.Exp)
    k.act(egi[:], lc[:], AF.Exp, scale=-1.0)
    t = sbp.get()
    k.tt(t[:], lc[:], LW8[:], ALU.subtract, eng="gpsimd")
    k.act(egp[:], t[:], AF.Exp)
    t = sbp.get()
    k.tt(t[:], tot[:], lc[:], ALU.subtract)
    k.act(ehat[:], t[:], AF.Exp)
    At, Bt, Kt, Rt = [sbp.get() for _ in range(4)]
    Bh, Kh, Atb = sbb.get(), sbb.get(), At
    k.tt(At[:], A8[:], egp[:], ALU.mult)
    k.tt(Bt[:], B8[:], egi[:], ALU.mult, eng="gpsimd")
    k.tt(Kt[:], K8[:], egi[:], ALU.mult)
    k.tt(Rt[:], R8[:], eg[:], ALU.mult, eng="gpsimd")
    k.tt(Bh[:], B8[:], ehat[:], ALU.mult)
    k.tt(Kh[:], K8[:], ehat[:], ALU.mult, eng="gpsimd")
    fmT = []
    for i, src in enumerate((At, Bt, Kt, Rt)):
        pT = pps.get()
        for e in range(8):
            k.tr(pT[:, e, :], src[:, e, :], idn)
        dst = sbb.get()
        k.copy(dst[:], pT[:], eng=("scalar" if i % 2 == 0 else "vector"))
        fmT.append(dst)
    AtT, BtT, KtT, RtT = fmT

    def score(lhs, rhs, mask, pool):
        p_ = pps.get()
        for e in range(8):
            k.mm(p_[:, e, :], lhs[:, e, :], rhs[:, e, :])
        o_ = pool.get()
        k.tt(o_[:], p_[:], mask, ALU.mult)
        return o_
    M = score(AtT, BtT, nm[:, 1], sbp)
    A = score(BtT, AtT, nm[:, 0], sbp)
    AakT = score(KtT, AtT, m01[:, 1], sbb)
    ArbT = score(BtT, RtT, m01[:, 0], sbb)
    ArkT = score(KtT, RtT, m01[:, 0], sbb)
    Xi, _ = tri_inverse(k, M, A, id8, pps, X.inv_f, None)
    pW = pps.get()
    for e in range(8):
        k.mm(pW[:, e, :], Atb[:, e, :], Xi[:, e, :])
    WT = sbb.get()
    k.copy(WT[:], pW[:], eng="scalar")
    pAkV = pps.get()
    for e in range(8):
        k.mm(pAkV[:, e, :], AakT[:, e, :], V8b[:, e, :])
    AkV = sbb.get()
    k.copy(AkV[:], pAkV[:], eng="vector")
    pUv = pps.get()
    for e in range(8):
        k.mm(pUv[:, e, :], Xi[:, e, :], AkV[:, e, :])
    Uv = sbp.get()
    k.copy(Uv[:], pUv[:], eng="scalar")
    pe = pps.get()
    for e in range(8):
        k.mm(pe[:, e, :], WT[:, e, :], Stb[:, e, :])
    E = sbb.get()
    k.tt(E[:], Uv[:], pe[:], ALU.add)
    py = pps.get()
    for e in range(8):
        k.mm(py[:, e, :], RtT[:, e, :], Stb[:, e, :], start=True, stop=False)
        k.mm(py[:, e, :], ArbT[:, e, :], E[:, e, :], start=False, stop=False)
        k.mm(py[:, e, :], ArkT[:, e, :], V8b[:, e, :], start=False, stop=True)
    pS = pps.get()
    for e in range(8):
        k.mm(pS[:, e, :], Bh[:, e, :], E[:, e, :], start=True, stop=False)
        k.mm(pS[:, e, :], Kh[:, e, :], V8b[:, e, :], start=False, stop=True)
    t = sbp.get()
    k.tt(t[:], St[:], gCT[:, :, None].bc([64, 8, 64]), ALU.mult, eng="gpsimd")
    k.tt(St[:], t[:], pS[:], ALU.add)
    y8 = sbp.get()
    k.copy(y8[:], py[:], eng="scalar")
    for d in range(2):
        if X.need_ctx or ch[d] >= 4:
            k.dma("sync", S["RWY"][d, ch[d] * 64:(ch[d] + 1) * 64, :], y8[:, d * 4:d * 4 + 4, :].re("p h d -> p (h d)"))


def scan_stage(k, C, l, need_ctx, do_dn=True, do_rw=True):
    with k.stage():
        sh = Lane()
        sh.tri = k.sb("tri", [64, 2, 64], F32)
        sh.id8 = k.sb("id8", [64, 8, 64], F32)
        k.dma("sync", sh.tri[:], C.I("c_tri")[:])
        k.dma("sync", sh.id8[:], C.I("c_ident8")[:])
        sh.idn = C.ident[0:64, 0:64]
        both = do_dn and do_rw
        sh.npp_dn = 3 if both else 7
        sh.npp_rw = 4 if both else 7
        psm = k.ps("psm", [64, 32], F32)
        sh.psm_dn = psm.sub("dn", (slice(None), slice(0, 8)))
        sh.psm_rw = psm.sub("rw", (slice(None), slice(8, 24))).re("p (e two) -> p e two", two=2)
        Xd = dn_scan_setup(k, C, need_ctx, sh) if do_dn else None
        Xr = rw_scan_setup(k, C, need_ctx, sh) if do_rw else None
        for s in range(NCH):
            lists = []
            if do_dn:
                with k.defer() as La:
                    dn_step(k, Xd, s)
                lists.append(La)
            if do_rw:
                with k.defer() as Lb:
                    rw_step(k, Xr, s)
                lists.append(Lb)
            k.replay(lists)
```

```python
import contextlib
import numpy as np
import concourse.bass as bass
import concourse.mybir as mybir
from concourse.bass_utils import run_bass_kernel_spmd

F32 = mybir.dt.float32
BF16 = mybir.dt.bfloat16
I32 = mybir.dt.int32
AF = mybir.ActivationFunctionType
ALU = mybir.AluOpType
AX = mybir.AxisListType

ENGS = ("tensor", "vector", "scalar", "gpsimd", "sync")
N_DSEM = 12
SEM_EPOCH = 1 << 40


class Tok:
    __slots__ = ("name", "w", "r")

    def __init__(self, name):
        self.name = name
        self.w = None
        self.r = []


class T:
    def __init__(self, k, name, t, space):
        self.k = k
        self.name = name
        self.t = t
        self.space = space
        self.tok = Tok(name)
        self.subs = {}

    def __getitem__(self, key):
        return V(self.t[key] if not isinstance(self.t, bass.AP) else self.t[key], [self.tok])

    def v(self):
        return self[:]

    def sub(self, key, idx):
        if key not in self.subs:
            self.subs[key] = Tok(f"{self.name}.{key}")
        return V(self.t[idx], [self.subs[key]])


class V:
    __slots__ = ("ap", "toks")

    def __init__(self, ap, toks):
        self.ap = ap
        self.toks = toks

    def __getitem__(self, key):
        return V(self.ap[key], self.toks)

    def re(self, s, **kw):
        return V(self.ap.rearrange(s, **kw), self.toks)

    def bc(self, shape):
        return V(self.ap.to_broadcast(shape), self.toks)

    def bitcast(self, dt):
        return V(self.ap.bitcast(dt), self.toks)

    def with_toks(self, toks):
        return V(self.ap, toks)


def _ap(x):
    return x.ap if isinstance(x, V) else x


class K:
    def __init__(self):
        self.nc = bass.Bass("TRN2", target_bir_lowering=False)
        self.stack = contextlib.ExitStack()
        self.q = {e: [] for e in ENGS}
        self.cnt = {e: 0 for e in ENGS}
        self.epoch = {e: 0 for e in ENGS}
        self.sems = {}
        self.seen = {e: {} for e in ENGS}
        self.dsem = {}
        self.dcnt = {}
        self.dq_i = {e: 0 for e in ENGS}
        self.uid = 0
        self.out_deps = []
        self.stage_stack = None

    def _name(self, n):
        self.uid += 1
        return f"{n}_{self.uid}"

    def sb(self, name, shape, dtype, stack=None):
        t = (stack or self.stage_stack or self.stack).enter_context(self.nc.sbuf_tensor(self._name(name), list(shape), dtype))
        return T(self, name, t, "sb")

    def ps(self, name, shape, dtype=F32, stack=None):
        t = (stack or self.stage_stack or self.stack).enter_context(self.nc.psum_tensor(self._name(name), list(shape), dtype))
        return T(self, name, t, "ps")

    def dram(self, name, shape, dtype, kind="Internal"):
        t = self.nc.dram_tensor(name, list(shape), dtype, kind=kind).ap()
        return T(self, name, t, "dram")

    def _sem(self, key):
        if key not in self.sems:
            self.sems[key] = self.stack.enter_context(self.nc.semaphore(self._name("s")))
        return self.sems[key]

    def _deps(self, reads, writes, pe_accum=False):
        deps = []
        for v in reads:
            for t in v.toks:
                if t.w is not None:
                    deps.append(t.w)
        for v in writes:
            for t in v.toks:
                if t.w is not None and not pe_accum:
                    deps.append(t.w)
                if not pe_accum:
                    deps.extend(t.r)
        return deps

    def _commit(self, me, reads, writes, pe_accum=False):
        for v in reads:
            for t in v.toks:
                t.r.append(me)
        for v in writes:
            for t in v.toks:
                t.w = me
                if not pe_accum:
                    t.r = []

    def _waits(self, eng, deps):
        need = {}
        for (sk, val) in deps:
            if self.seen[eng].get(sk, 0) >= val:
                continue
            if need.get(sk, 0) < val:
                need[sk] = val
        for sk, val in need.items():
            self.seen[eng][sk] = val
        return list(need.items())

    @contextlib.contextmanager
    def defer(self):
        prev = getattr(self, "_defer", None)
        lst = []
        self._defer = lst
        try:
            yield lst
        finally:
            self._defer = prev

    def replay(self, lists):
        lists = [l for l in lists if l]
        pos = [0] * len(lists)
        total = sum(len(l) for l in lists)
        last_pe = -1

        def is_pe(i):
            return pos[i] < len(lists[i]) and lists[i][pos[i]][0] == "op" and lists[i][pos[i]][1][0] == "tensor"
        for _ in range(total):
            live = [i for i in range(len(lists)) if pos[i] < len(lists[i])]
            pe = [i for i in live if is_pe(i)]
            if len(pe) >= 2:
                cand = [i for i in pe if i != last_pe]
                j = min(cand, key=lambda i: pos[i] / len(lists[i]))
            else:
                j = min(live, key=lambda i: pos[i] / len(lists[i]))
            kind, a, kw = lists[j][pos[j]]
            if kind == "op" and a[0] == "tensor":
                last_pe = j
            pos[j] += 1
            getattr(self, kind)(*a, **kw)

    def op(self, eng, fn, reads=(), writes=(), pe_accum=False, same_eng_sync=True):
        if getattr(self, "_defer", None) is not None:
            self._defer.append(("op", (eng, fn), dict(reads=reads, writes=writes, pe_accum=pe_accum, same_eng_sync=same_eng_sync)))
            return None
        reads = [r for r in reads if isinstance(r, V)]
        writes = [w for w in writes if isinstance(w, V)]
        deps = self._deps(reads, writes, pe_accum)
        if eng == "tensor" or not same_eng_sync:
            deps = [d for d in deps if d[0][0] != eng or d[0][0] == "dma"]
        if self.cnt[eng] >= SEM_EPOCH:
            self.epoch[eng] += 1
            self.cnt[eng] = 0
        sk = (eng, self.epoch[eng])
        self._sem(sk)
        self.cnt[eng] += 1
        me = (sk, self.cnt[eng])
        waits = self._waits(eng, deps)
        self.q[eng].append((fn, waits, (sk, 1), self.cnt[eng]))
        self._commit(me, reads, writes, pe_accum)
        return me

    def dma(self, eng, out, in_, **kw):
        if getattr(self, "_defer", None) is not None:
            self._defer.append(("dma", (eng, out, in_), dict(kw)))
            return None
        deps = self._deps([in_], [out])
        i = self.dq_i[eng]
        self.dq_i[eng] += 1
        sk = ("dma", eng, i % N_DSEM)
        self._sem(sk)
        prev = self.dcnt.get(sk, 0)
        if prev:
            deps.append((sk, prev))
        self.dcnt[sk] = prev + 16
        me = (sk, prev + 16)
        waits = self._waits(eng, deps)
        o, a = _ap(out), _ap(in_)
        self.q[eng].append((lambda e: e.dma_start(out=o, in_=a, **kw), waits, (sk, 16), None))
        self._commit(me, [in_], [out])
        return me

    def barrier(self):
        alld = []
        for e in ENGS:
            for ep in range(self.epoch[e] + 1):
                sk = (e, ep)
                if sk in self.sems:
                    alld.append((sk, self.cnt[e] if ep == self.epoch[e] else SEM_EPOCH))
        for sk, v in self.dcnt.items():
            alld.append((sk, v))
        for e in ENGS:
            w = self._waits(e, alld)
            if w:
                self.q[e].append((None, w, None, None))

    def flush(self):
        import bisect
        nc = self.nc
        q = self.q
        sems = self.sems
        if not hasattr(self, "base_idx"):
            self.base_idx, self.base_val = {}, {}
        targets = {}
        for ename in ENGS:
            for (fn, waits, inc, idx) in q[ename]:
                for (sk, val) in waits:
                    if sk[0] != "dma":
                        targets.setdefault(sk, set()).add(val)
        for ename in ENGS:
            sk = (ename, self.epoch[ename])
            if sk in sems:
                targets.setdefault(sk, set()).add(self.cnt[ename])
        tl = {}
        for sk, st in targets.items():
            b = self.base_idx.get(sk, 0)
            tl[sk] = sorted(v for v in st if v > b)

        def val_of(sk, v):
            b = self.base_idx.get(sk, 0)
            bv = self.base_val.get(sk, 0)
            if v <= b:
                return bv
            return bv + bisect.bisect_left(tl[sk], v) + 1

        with nc.Block() as block:
            for ename in ENGS:
                ops = q[ename]
                if not ops:
                    continue

                def body(e, ops=ops):
                    for (fn, waits, inc, idx) in ops:
                        for (sk, val) in waits:
                            if sk[0] == "dma":
                                e.wait_ge(sems[sk], val)
                            else:
                                e.wait_ge(sems[sk], val_of(sk, val))
                        if fn is not None:
                            ins = fn(e)
                            if inc is not None:
                                if inc[0][0] == "dma":
                                    ins.then_inc(sems[inc[0]], inc[1])
                                else:
                                    lst = tl.get(inc[0], ())
                                    j = bisect.bisect_left(lst, idx)
                                    if j < len(lst) and lst[j] == idx:
                                        ins.then_inc(sems[inc[0]], 1)
                getattr(block, ename)(body)
        for sk, lst in tl.items():
            self.base_val[sk] = self.base_val.get(sk, 0) + len(lst)
            if sk[0] != "dma":
                self.base_idx[sk] = self.cnt[sk[0]]
        self.q = {e: [] for e in ENGS}

    @contextlib.contextmanager
    def stage(self, name=None):
        st = contextlib.ExitStack()
        prev = self.stage_stack
        self.stage_stack = st
        self.stage_i = getattr(self, "stage_i", 0) + 1
        with st:
            yield st
            self.barrier()
            if getattr(self, "scopes", False):
                import inspect
                nm = name or inspect.stack()[2].function
                with self.nc.named_scope(f"s{self.stage_i:03d}_{nm}"):
                    self.flush()
            else:
                self.flush()
        self.stage_stack = prev

    def mm(self, out, lhsT, rhs, start=True, stop=True, **kw):
        o, l, r = _ap(out), _ap(lhsT), _ap(rhs)
        return self.op("tensor", lambda e: e.matmul(o, l, r, start=start, stop=stop, **kw),
                       reads=[lhsT, rhs], writes=[out], pe_accum=not start)

    def tr(self, out, in_, ident):
        o, i, d = _ap(out), _ap(in_), _ap(ident)
        return self.op("tensor", lambda e: e.transpose(o, i, d), reads=[in_, ident], writes=[out])

    def act(self, out, in_, func, bias=None, scale=None, accum_out=None, eng="scalar"):
        o, i = _ap(out), _ap(in_)
        kw = {}
        rd = [in_]
        if bias is not None:
            kw["bias"] = _ap(bias)
            rd.append(bias)
        if scale is not None:
            kw["scale"] = _ap(scale)
            rd.append(scale)
        wr = [out]
        if accum_out is not None:
            kw["accum_out"] = _ap(accum_out)
            wr.append(accum_out)
        return self.op("scalar", lambda e: e.activation(o, i, func, **kw), reads=rd, writes=wr)

    def tt(self, out, in0, in1, op, eng="vector"):
        o, a, b = _ap(out), _ap(in0), _ap(in1)
        return self.op(eng, lambda e: e.tensor_tensor(o, a, b, op), reads=[in0, in1], writes=[out])

    def ts(self, out, in0, s1, op0, s2=None, op1=None, eng="vector", accum_out=None):
        o, a = _ap(out), _ap(in0)
        x1, x2 = _ap(s1), _ap(s2)
        kw = {}
        if op1 is not None:
            kw["op1"] = op1
        wr = [out]
        if accum_out is not None:
            kw["accum_out"] = _ap(accum_out)
            wr.append(accum_out)
        return self.op(eng, lambda e: e.tensor_scalar(o, a, x1, x2, op0, **kw), reads=[in0, s1, s2], writes=wr)

    def stt(self, out, in0, scalar, in1, op0, op1):
        o, a, s, b = _ap(out), _ap(in0), _ap(scalar), _ap(in1)
        return self.op("vector", lambda e: e.scalar_tensor_tensor(o, a, s, b, op0, op1), reads=[in0, scalar, in1], writes=[out])

    def copy(self, out, in_, eng="vector"):
        o, i = _ap(out), _ap(in_)
        if eng == "scalar":
            return self.op("scalar", lambda e: e.copy(o, i), reads=[in_], writes=[out])
        return self.op(eng, lambda e: e.tensor_copy(o, i), reads=[in_], writes=[out])

    def memset(self, out, val, eng="vector"):
        o = _ap(out)
        return self.op(eng, lambda e: e.memset(o, val), reads=[], writes=[out])

    def recip(self, out, in_):
        o, i = _ap(out), _ap(in_)
        return self.op("vector", lambda e: e.reciprocal(o, i), reads=[in_], writes=[out])

    def reduce(self, out, in_, op=None, axis=None, **kw):
        o, i = _ap(out), _ap(in_)
        op = op or ALU.add
        axis = axis or AX.X
        return self.op("vector", lambda e: e.tensor_reduce(o, i, axis, op, **kw), reads=[in_], writes=[out])

D = 1024
SEQ = 2048
CTX = 256
NT = SEQ + CTX
DFF = 2816
NL = 2
EPS = 1e-6
TILES = [(0, 256, 1)] + [(256 + 512 * i, 512, 0) for i in range(4)]
QS = [(0, 6), (6, 6), (12, 5), (17, 5)]


class Pool:
    def __init__(self, k, name, shape, dtype, n, space="sb", stack=None):
        mk = k.sb if space == "sb" else k.ps
        self.bufs = [mk(f"{name}{i}", shape, dtype, stack=stack) for i in range(n)]
        self.i = 0

    def get(self):
        b = self.bufs[self.i % len(self.bufs)]
        self.i += 1
        return b


class Ctx:
    pass


class TV:
    def __init__(self, t, psl):
        self.t, self.psl = t, psl
        self.tok = t.tok

    def __getitem__(self, key):
        if not isinstance(key, tuple):
            key = (key,)
        assert key[0] == slice(None), key
        return V(self.t.t[(self.psl,) + tuple(key[1:])], [self.t.tok])


class PoolV:
    def __init__(self, k, name, shape, dtype, n, psl, space="sb"):
        mk = k.sb if space == "sb" else k.ps
        full = [128] + list(shape[1:]) if psl.start else list(shape)
        self.bufs = [TV(mk(f"{name}{i}", full, dtype), psl) for i in range(n)]
        self.i = 0

    def get(self):
        b = self.bufs[self.i % len(self.bufs)]
        self.i += 1
        return b


def xt_view(C, t0, n):
    toks = [C.XT.subtok(i) for i, (s, sz, w) in enumerate(TILES) if s < t0 + n and t0 < s + sz]
    ap = C.XT.t.rearrange("(c p) t -> p c t", p=128)[:, :, t0:t0 + n]
    return V(ap, toks)


IN_SHAPES = {
    "xT": [D, NT], "cT": [128, 8, 2], "b_modT": [NL, 128, 72], "norm_wT": [NL, 128, 24],
    "w_mod": [NL, D, 9 * D], "ffn_w_gu": [NL, 2, D, 2 * DFF], "ffn_w_down": [NL, 2, DFF, D],
    "w_in": [NL, D, 3472], "w_out": [NL, D, D], "identD": [128, 128],
}


class LazyIn:
    def __init__(self, k, C):
        self.k, self.C, self.d = k, C, {}

    def __call__(self, name):
        if name not in self.d:
            self.d[name] = self.k.dram(name, IN_SHAPES[name], IN_DTYPES.get(name, F32), kind="ExternalInput")
        return self.d[name]


def declare_io(k, C, debug_out=()):
    C.I = LazyIn(k, C)
    C.XT = C.I("xT")
    C.XT.subtok = lambda i: C.XT.subs.setdefault(i, Tok(f"XT.{i}"))
    declare_scratch(k, C)
    if debug_out == "gin":
        IN_SHAPES["g_in"] = [NT, 1024]
        C.S["G"] = C.I("g_in")
    C.OUT = k.dram("outT", [D, SEQ], F32, kind="ExternalOutput")
    C.DBG = k.dram("dbgT", [D, NT], F32, kind="ExternalOutput") if debug_out else None


def setup_consts(k, C):
    C.ones_bf = k.sb("ones_bf", [128, 128], BF16, stack=k.stack)
    k.memset(C.ones_bf[:], 1.0)
    C.eps_t = k.sb("eps_t", [128, 1], F32, stack=k.stack)
    k.memset(C.eps_t[:], EPS)
    C.ident = k.sb("ident", [128, 128], F32, stack=k.stack)
    k.dma("sync", C.ident[:], C.I("identD")[:])
    C.P = [k.sb(f"P{l}", [128, 9, 8, 2], F32, stack=k.stack) for l in range(NL)]


def mod_stage(k, C):
    with k.stage():
        ct = k.sb("ct", [128, 8, 2], F32)
        sc = k.sb("sc", [128, 8, 2], F32)
        k.dma("sync", ct[:], C.I("cT")[:])
        k.act(sc[:], ct[:], AF.Silu)
        wpool = Pool(k, "wm", [128, 8, 512], F32, 3)
        for l in range(NL):
            bm = k.sb(f"bm{l}", [128, 72], F32)
            nw = k.sb(f"nw{l}", [128, 24], F32)
            k.dma("sync", bm[:], C.I("b_modT")[l])
            k.dma("sync", nw[:], C.I("norm_wT")[l])
            pm = k.ps(f"pm{l}", [128, 72, 2], F32)
            wsrc = C.I("w_mod").t[l].rearrange("(kc p) n -> p kc n", p=128)
            for g in range(18):
                wm = wpool.get()
                k.dma("sync", wm[:], V(wsrc[:, :, g * 512:(g + 1) * 512], [C.I("w_mod").tok]))
                for c4 in range(4):
                    ci = g * 4 + c4
                    for kc in range(8):
                        k.mm(pm[:, ci, :], wm[:, kc, c4 * 128:(c4 + 1) * 128], sc[:, kc, :],
                             start=(kc == 0), stop=(kc == 7))
            P = C.P[l]
            Pv = P[:].re("p a c w -> p (a c) w")
            k.tt(Pv, pm[:], bm[:, :, None].bc([128, 72, 2]), ALU.add)
            for s in range(3):
                k.stt(P[:, 3 * s + 1], P[:, 3 * s + 1], 1.0,
                      nw[:, s * 8:(s + 1) * 8, None].bc([128, 8, 2]), ALU.add, ALU.mult)
                if s != 1:
                    k.ts(P[:, 3 * s + 2], P[:, 3 * s + 2], 0.5, ALU.mult)


def hv_(hall, ti, c, t0, n):
    return hall.sub(ti, (slice(None), c, slice(t0, t0 + n)))


def norm_pass(k, C, P, s, hall, xpool, sq, pss, rs, tmpp, skip_ctx=False):
    for ti, (t0, n, w) in enumerate(TILES):
        if w == 1 and skip_ctx:
            continue
        xt = xpool.get()
        k.dma("sync", xt[:, :, :n], xt_view(C, t0, n))
        k.act(sq[:, :, :n], xt[:, :, :n], AF.Square)
        for c in range(8):
            k.mm(pss[:, :n], C.ones_bf[:], sq[:, c, :n], start=(c == 0), stop=(c == 7))
        k.act(rs[:, :n], pss[:, :n], AF.Sqrt, scale=1.0 / D, bias=C.eps_t[:])
        k.recip(rs[:, :n], rs[:, :n])
        for c in range(8):
            tmp = tmpp.get()
            k.stt(tmp[:, :n], xt[:, c, :n], P[:, 3 * s + 1, c, w:w + 1], rs[:, :n], ALU.mult, ALU.mult)
            k.act(hv_(hall, ti, c, t0, n), tmp[:, :n], AF.Identity, bias=P[:, 3 * s, c, w:w + 1])


def ffn_stage(k, C, l, f, s, skip_ctx=False):
    P = C.P[l]
    Wgu = C.I("ffn_w_gu").t[l, f].rearrange("(kc p) n -> p kc n", p=128)
    Wdn = C.I("ffn_w_down").t[l, f].rearrange("(j p) n -> p j n", p=128)
    with k.stage():
        hall = k.sb("hall", [128, 8, NT], BF16)
        xpool = Pool(k, "xt", [128, 8, 512], F32, 2)
        sq = k.sb("sq", [128, 8, 512], BF16)
        tmpp = Pool(k, "tmp", [128, 512], F32, 2)
        rs = k.sb("rs", [128, 512], F32)
        pss = k.ps("pss", [128, 512], F32)
        wgp = Pool(k, "wg", [128, 8, 2, 6 * 128], BF16, 2)
        wdp = Pool(k, "wd", [128, 6, D], BF16, 2)
        actp = Pool(k, "act", [128, 6, 512], BF16, 2)
        sgp = Pool(k, "sg", [128, 512], F32, 2)
        pgp = Pool(k, "pg", [128, 512], F32, 2, space="ps")
        pup = Pool(k, "pu", [128, 512], F32, 2, space="ps")
        pdp = Pool(k, "pd", [128, 512], F32, 2, space="ps")

        def hv(ti, c, t0, n):
            return hall.sub(ti, (slice(None), c, slice(t0, t0 + n)))

        norm_pass(k, C, P, s, hall, xpool, sq, pss, rs, tmpp, skip_ctx)
        for qi, (j0, nj) in enumerate(QS):
            wg = wgp.get()
            wd = wdp.get()
            k.dma("gpsimd", wg[:, :, 0, :nj * 128], V(Wgu[:, :, j0 * 128:(j0 + nj) * 128], [C.I("ffn_w_gu").tok]))
            k.dma("gpsimd", wg[:, :, 1, :nj * 128], V(Wgu[:, :, DFF + j0 * 128:DFF + (j0 + nj) * 128], [C.I("ffn_w_gu").tok]))
            k.dma("gpsimd", wd[:, :nj, :], V(Wdn[:, j0:j0 + nj, :], [C.I("ffn_w_down").tok]))
            for ti, (t0, n, w) in enumerate(TILES):
                if w == 1 and skip_ctx:
                    continue
                act = actp.get()
                for jj in range(nj):
                    pg = pgp.get()
                    pu = pup.get()
                    for kc in range(8):
                        k.mm(pg[:, :n], wg[:, kc, 0, jj * 128:(jj + 1) * 128], hv(ti, kc, t0, n),
                             start=(kc == 0), stop=(kc == 7))
                    for kc in range(8):
                        k.mm(pu[:, :n], wg[:, kc, 1, jj * 128:(jj + 1) * 128], hv(ti, kc, t0, n),
                             start=(kc == 0), stop=(kc == 7))
                    sg = sgp.get()
                    k.act(sg[:, :n], pg[:, :n], AF.Silu)
                    k.tt(act[:, jj, :n], sg[:, :n], pu[:, :n], ALU.mult)
                xr = xpool.get()
                k.dma("sync", xr[:, :, :n], xt_view(C, t0, n))
                for m in range(8):
                    pd = pdp.get()
                    for jj in range(nj):
                        k.mm(pd[:, :n], wd[:, jj, m * 128:(m + 1) * 128], act[:, jj, :n],
                             start=(jj == 0), stop=(jj == nj - 1))
                    k.stt(xr[:, m, :n], pd[:, :n], P[:, 3 * s + 2, m, w:w + 1], xr[:, m, :n], ALU.mult, ALU.add)
                k.dma("sync", xt_view(C, t0, n), xr[:, :, :n])


PT = 3472
TM_GROUPS = [
    (0, 512, "HY", 0), (512, 256, "HY", 512),
    (1280, 256, "NAV", 0),
    (1536, 512, "DN", 0), (2048, 512, "DN", 512), (2560, 16, "DN", 1024),
    (2576, 512, "RW", 0), (3088, 256, "RW", 512),
]
FM_CHUNKS = [(768, 128, "NAQK", 0), (896, 128, "NAQK", 1), (1024, 128, "NAQK", 2), (1152, 128, "NAQK", 3),
             (3344, 32, "RWLR", 0), (3376, 32, "RWLR", 1), (3408, 64, "RWLR", 2)]


def declare_scratch(k, C):
    def mk(name, shape):
        kind = "ExternalOutput" if name in C.dbg_names else "Internal"
        return k.dram("s_" + name.lower(), shape, F32, kind=kind)
    C.S = {
        "HY": mk("HY", [NT, 768]),
        "NAV": mk("NAV", [NT, 256]),
        "DN": mk("DN", [NT, 1040]),
        "RW": mk("RW", [NT, 768]),
        "NAQK": mk("NAQK", [512, NT]),
        "RWLR": mk("RWLR", [3, 64, NT]),
        "RWP": mk("RWP", [NT, 10, 256]),
        "RWY": mk("RWY", [2, NT, 256]),
        "G": mk("G", [NT, 1024]),
        "DNQKV": mk("DNQKV", [NT, 768]),
        "DNGB": mk("DNGB", [NT, 16]),
        "DNO": mk("DNO", [2, NT, 256]),
    }


def proj_stage(k, C, l):
    P = C.P[l]
    Win = C.I("w_in").t[l].rearrange("(kc p) n -> p kc n", p=128)
    with k.stage():
        hall = k.sb("hall", [128, 8, NT], BF16)
        xpool = Pool(k, "xt", [128, 8, 512], F32, 2)
        sq = k.sb("sq", [128, 8, 512], BF16)
        tmpp = Pool(k, "tmp", [128, 512], F32, 2)
        rs = k.sb("rs", [128, 512], F32)
        pss = k.ps("pss", [128, 512], F32)
        win = k.sb("win", [128, 8, PT], BF16)
        for kc in range(8):
            k.dma("gpsimd", win[:, kc, :], V(Win[:, kc, :], [C.I("w_in").tok]))
        norm_pass(k, C, P, 1, hall, xpool, sq, pss, rs, tmpp)
        pp = Pool(k, "pp", [128, 512], F32, 4, space="ps")
        rowp = Pool(k, "row", [128, 2832], F32, 2)
        fmp = Pool(k, "fm", [128, 7, 512], F32, 2)
        ev = 0
        for tc in range(NT // 128):
            t0 = tc * 128
            ti = 0 if t0 < 256 else 1 + (t0 - 256) // 512
            row = rowp.get()
            off = 0
            offs = []
            for (c0, nc_, name, dcol) in TM_GROUPS:
                ps = pp.get()
                for kc in range(8):
                    k.mm(ps[:, :nc_], hv_(hall, ti, kc, t0, 128), win[:, kc, c0:c0 + nc_],
                         start=(kc == 0), stop=(kc == 7))
                k.copy(row[:, off:off + nc_], ps[:, :nc_], eng=("scalar" if ev % 2 else "vector"))
                ev += 1
                offs.append((off, nc_, name, dcol))
                off += nc_
            for name in ("HY", "NAV", "DN", "RW"):
                gs = [g for g in offs if g[2] == name]
                o0 = gs[0][0]
                tot = sum(g[1] for g in gs)
                k.dma("sync", C.S[name][t0:t0 + 128, 0:tot], row[:, o0:o0 + tot])
        for ti, (t0, n, w) in enumerate(TILES):
            fm = fmp.get()
            for i, (c0, ncl, name, di) in enumerate(FM_CHUNKS):
                ps = pp.get()
                for kc in range(8):
                    k.mm(ps[0:ncl, :n], win[:, kc, c0:c0 + ncl], hv_(hall, ti, kc, t0, n),
                         start=(kc == 0), stop=(kc == 7))
                k.copy(fm[0:ncl, i, :n], ps[0:ncl, :n], eng=("scalar" if ev % 2 else "vector"))
                ev += 1
            k.dma("sync", V(C.S["NAQK"].t.rearrange("(c p) t -> p c t", p=128)[:, :, t0:t0 + n], [C.S["NAQK"].tok]),
                  fm[:, 0:4, :n])
            for g_, ncl in ((0, 32), (1, 32), (2, 64)):
                k.dma("sync", C.S["RWLR"][g_, 0:ncl, t0:t0 + n], fm[0:ncl, 4 + g_, :n])


def outproj_stage(k, C, l, need_ctx):
    P = C.P[l]
    Wout = C.I("w_out").t[l].rearrange("(kc p) n -> p kc n", p=128)
    with k.stage():
        wo = k.sb("wo", [128, 8, D], BF16)
        k.dma("gpsimd", wo[:], V(Wout, [C.I("w_out").tok]))
        gpool = Pool(k, "gt", [128, D], F32, 3)
        gT = Pool(k, "gT", [128, 8, 512], BF16, 2)
        xpool = Pool(k, "xt", [128, 8, 512], F32, 2)
        ptp = Pool(k, "ptp", [128, 4, 128], F32, 2, space="ps")
        pyp = Pool(k, "py", [128, 512], F32, 2, space="ps")
        ev = 0
        for ti, (t0, n, w) in enumerate(TILES):
            if w == 1 and not need_ctx:
                continue
            g = gT.get()
            for sc_ in range(n // 128):
                gt = gpool.get()
                k.dma("sync", gt[:], C.S["G"][t0 + sc_ * 128:t0 + (sc_ + 1) * 128, :])
                for half in range(2):
                    pt = ptp.get()
                    for q in range(4):
                        c = half * 4 + q
                        k.tr(pt[:, q, :], gt[:, c * 128:(c + 1) * 128], C.ident[:])
                    k.copy(g[:, half * 4:(half + 1) * 4, sc_ * 128:(sc_ + 1) * 128], pt[:],
                           eng=("scalar" if ev % 2 else "vector"))
                    ev += 1
            xr = xpool.get()
            k.dma("sync", xr[:, :, :n], xt_view(C, t0, n))
            for m in range(8):
                py = pyp.get()
                for kc in range(8):
                    k.mm(py[:, :n], wo[:, kc, m * 128:(m + 1) * 128], g[:, kc, :n], start=(kc == 0), stop=(kc == 7))
                k.stt(xr[:, m, :n], py[:, :n], P[:, 5, m, w:w + 1], xr[:, m, :n], ALU.mult, ALU.add)
            k.dma("sync", xt_view(C, t0, n), xr[:, :, :n])


def out_stage(k, C):
    with k.stage():
        xpool = Pool(k, "xo", [128, 8, 512], F32, 2)
        last = []
        for i in range(4):
            xo = xpool.get()
            k.dma("sync", xo[:], xt_view(C, 256 + 512 * i, 512))
            dst = V(C.OUT.t.rearrange("(c p) t -> p c t", p=128)[:, :, 512 * i:512 * (i + 1)], [C.OUT.tok])
            k.dma("sync", dst, xo[:])
        if C.DBG is not None:
            for ti, (t0, n, w) in enumerate(TILES):
                xo = xpool.get()
                k.dma("sync", xo[:, :, :n], xt_view(C, t0, n))
                dst = V(C.DBG.t.rearrange("(c p) t -> p c t", p=128)[:, :, t0:t0 + n], [C.DBG.tok])
                k.dma("sync", dst, xo[:, :, :n])


def build(plan=None, dbg_names=(), scopes=False):
    k = K()
    k.scopes = scopes
    C = Ctx()
    C.dbg_names = set(dbg_names)
    declare_io(k, C, debug_out=(False if plan in (None, 'full') else ("gin" if plan == "m3" else True)))
    with k.stage():
        setup_consts(k, C)
    mod_stage(k, C)
    plan = plan or "full"
    if plan == "full":
        for l in range(NL):
            need_ctx = l < NL - 1
            ffn_stage(k, C, l, 0, 0)
            proj_stage(k, C, l)
            for nm in ("hy", "na", "dnrw"):
                globals()[nm + "_stage"](k, C, l, need_ctx)
            outproj_stage(k, C, l, need_ctx)
            ffn_stage(k, C, l, 1, 2, skip_ctx=not need_ctx)
    if plan == "ffn1":
        ffn_stage(k, C, 0, 0, 0)
    if plan == "m1":
        proj_stage(k, C, 0)
    if plan == "m3":
        outproj_stage(k, C, 0, True)
    if plan.startswith("mix:"):
        proj_stage(k, C, 0)
        for nm in plan[4:].split(","):
            globals()[nm + "_stage"](k, C, 0, True)
    out_stage(k, C)
    return k, C


def host_prep(inp, b, used):
    f = np.float32
    m = {}
    for name in used:
        if name == "xT":
            v = np.concatenate([inp["ctx"][b], inp["x"][b]], axis=0).T
        elif name == "cT":
            v = np.stack([inp["c"][b].reshape(8, 128).T, inp["c_ctx"].reshape(8, 128).T], axis=-1)
        elif name == "b_modT":
            v = inp["b_mod"].reshape(NL, 72, 128).transpose(0, 2, 1)
        elif name == "norm_wT":
            v = inp["norm_w"].reshape(NL, 24, 128).transpose(0, 2, 1)
        elif name == "identD":
            v = np.eye(128, dtype=f)
        elif name in HOST_LAYOUT:
            v = HOST_LAYOUT[name](inp, b)
        else:
            v = inp[name]
        m[name] = np.ascontiguousarray(np.asarray(v, dtype=(ml_dtypes.bfloat16 if name in IN_DTYPES else f)))
    return m


HOST_LAYOUT = {}


def run(inputs, plan=None, dbg_names=(), extra=None, ncores=8):
    k, C = build(plan, dbg_names)
    extra = extra or {}
    used = [n for n in C.I.d.keys() if n not in extra]
    in_maps = [host_prep(inputs, b, used) for b in range(ncores)]
    for m in in_maps:
        m.update(extra)
    res = run_bass_kernel_spmd(k.nc, in_maps, core_ids=list(range(ncores)))
    return res


def kernel(**inputs):
    inputs = {kk: np.asarray(v) for kk, v in inputs.items()}
    res = run(inputs)
    out = np.stack([np.ascontiguousarray(r["outT"].T) for r in res.results], axis=0)
    return out.astype(np.float32)


IN_SHAPES["na_nT"] = [NL, 128, 2]
IN_SHAPES["na_biasT"] = [NL, 128, 8, 4, 4, 64]


def _na_nT(inp, b):
    return np.stack([np.tile(inp["na_q_norm"], (1, 2)), np.tile(inp["na_k_norm"], (1, 2))], axis=-1)


def _na_biasT(inp, b):
    a = np.arange(2)[:, None, None, None, None]
    kcol = np.arange(64)[None, :, None, None, None]
    rho = np.arange(8)[None, None, :, None, None]
    i = np.arange(4)[None, None, None, :, None]
    qcol = np.arange(64)[None, None, None, None, :]
    drow = 2 * i + a - rho + 0 * kcol + 0 * qcol
    dcol = np.clip(kcol - qcol, -15, 15) + 0 * drow
    cs = np.clip(qcol - 8, 0, 48)
    inwin = ((kcol >= cs) & (kcol < cs + 16)) & (drow > -100)
    rpb = inp["na_rpb"]
    g = rpb[:, :, drow + 7, dcol + 15]
    g = np.where(inwin[None, None], g, np.float32(-30000.0))
    g = g.transpose(0, 2, 3, 4, 1, 5, 6).reshape(NL, 128, 8, 4, 4, 64)
    return g


HOST_LAYOUT["na_nT"] = _na_nT
HOST_LAYOUT["na_biasT"] = _na_biasT


def na_stage(k, C, l, need_ctx):
    with contextlib.ExitStack() as outer:
        na_stage_(k, C, l, need_ctx, outer)


def na_stage_(k, C, l, need_ctx, outer):
    E = k.sb("E", [128, 8, 4, 4, 64], BF16, stack=outer)
    QK = k.sb("QK", [128, 4, NT], BF16, stack=outer)
    QM = k.sb("QM", [128, 2, 2, NT], BF16, stack=outer)
    Ve = k.sb("Ve", [128, 18, 4, 65], BF16, stack=outer)
    Vo = k.sb("Vo", [128, 17, 4, 65], BF16, stack=outer)
    with k.stage():
        bd = k.sb("bd", [128, 128], BF16)
        k.memset(bd[:], 0.0)
        k.memset(bd[0:64, 0:64], 1.0)
        k.memset(bd[64:128, 64:128], 1.0)
        nw = k.sb("nw", [128, 2], F32)
        k.dma("sync", nw[:], C.I("na_nT")[l])
        k.memset(QM[:], 0.0, eng="gpsimd")
        for half in range(2):
            bt = k.sb(f"bt{half}", [128, 4, 4, 4, 64], F32)
            k.dma("sync", bt[:], C.I("na_biasT")[l, :, half * 4:(half + 1) * 4])
            k.act(E[:, half * 4:(half + 1) * 4], bt[:], AF.Exp)
        qkp = Pool(k, "qk", [128, 4, 512], F32, 2)
        sqp = Pool(k, "sq", [128, 4, 512], BF16, 2)
        rnp = Pool(k, "rn", [128, 512], F32, 2)
        pss = Pool(k, "pss", [128, 512], F32, 2, space="ps")
        src = C.S["NAQK"].t.rearrange("(c p) t -> p c t", p=128)
        for ti, (t0, n, w) in enumerate(TILES):
            qk = qkp.get()
            k.dma("sync", qk[:, :, :n], V(src[:, :, t0:t0 + n], [C.S["NAQK"].tok]))
            sq = sqp.get()
            k.act(sq[:, :, :n], qk[:, :, :n], AF.Square)
            for c in range(4):
                ps = pss.get()
                k.mm(ps[:, :n], bd[:], sq[:, c, :n])
                rn = rnp.get()
                k.act(rn[:, :n], ps[:, :n], AF.Sqrt, scale=1.0 / 64, bias=C.eps_t[:])
                k.recip(rn[:, :n], rn[:, :n])
                if c >= 2:
                    k.stt(QK[:, c, t0:t0 + n], qk[:, c, :n], nw[:, 1:2], rn[:, :n], ALU.mult, ALU.mult)
                else:
                    for par in range(2):
                        pp_ = slice(par * 64, par * 64 + 64)
                        k.stt(QM[pp_, c, par, t0:t0 + n], qk[pp_, c, :n], nw[pp_, 0:1], rn[pp_, :n], ALU.mult, ALU.mult)
        k.memset(Ve[:], 1.0)
        k.memset(Vo[:], 1.0, eng="gpsimd")
        nav = C.S["NAV"]
        vst = k.sb("vst", [128, 18, 256], F32)
        vso = k.sb("vso", [128, 17, 256], F32)
        for c0 in range(0, 18, 6):
            c1 = min(18, c0 + 6)
            k.dma("sync", vst[:, c0:c1, :], V(nav.t[c0 * 128:c1 * 128, :].rearrange("(c p) d -> p c d", p=128), [nav.tok]))
            c1o = min(17, c0 + 6)
            k.dma("sync", vso[:, c0:c1o, :], V(nav.t[64 + c0 * 128:64 + c1o * 128, :].rearrange("(c p) d -> p c d", p=128), [nav.tok]))
        k.copy(Ve[:, :, :, 0:64], vst[:].re("p c (h d) -> p c h d", d=64), eng="vector")
        k.copy(Vo[:, :, :, 0:64], vso[:].re("p c (h d) -> p c h d", d=64), eng="gpsimd")
    import os
    if os.environ.get("NA_STOP") == "1":
        return
    with k.stage():
        psp = Pool(k, "ps", [128, 4, 6, 64], F32, 2, space="ps")
        pop = Pool(k, "po", [64, 4, 65], F32, 2, space="ps")
        Pp = Pool(k, "P", [128, 4, 6, 64], BF16, 2)
        rdp = Pool(k, "rd", [64, 4], F32, 2)
        orp = Pool(k, "orow", [64, 256], F32, 3)
        G = C.S["G"]
        for r in range(32):
            start = min(max(r - 4, 0), 24)
            rho = r - start
            qt0 = 256 + 64 * r
            ps = psp.get()
            for h in range(4):
                hp = slice((h % 2) * 64, (h % 2) * 64 + 64)
                for i in range(6):
                    kt0 = (256 + 64 * (start + 2 * i)) if i < 4 else (i - 4) * 128
                    k.mm(ps[:, h, i, :], QK[:, 2 + h // 2, kt0:kt0 + 128], QM[:, h // 2, h % 2, qt0:qt0 + 64])
            Pt = Pp.get()
            k.act(Pt[:], ps[:], AF.Exp, scale=0.125)
            k.tt(Pt[:, :, 0:4, :], Pt[:, :, 0:4, :], E[:, rho], ALU.mult)
            po = pop.get()
            for h in range(4):
                for i in range(6):
                    if i < 4:
                        r0 = start + 2 * i
                        vv = Ve[:, 2 + r0 // 2, h, :] if r0 % 2 == 0 else Vo[:, (r0 + 3) // 2, h, :]
                    else:
                        vv = Ve[:, i - 4, h, :]
                    k.mm(po[:, h, :], Pt[:, h, i, :], vv, start=(i == 0), stop=(i == 5))
            rd = rdp.get()
            k.recip(rd[:], po[:, :, 64])
            orow = orp.get()
            k.tt(orow[:].re("p (h d) -> p h d", d=64), po[:, :, 0:64], rd[:, :, None].bc([64, 4, 64]), ALU.mult)
            k.dma("sync", G[qt0:qt0 + 64, 256:512], orow[:])
    if os.environ.get("NA_STOP") == "2":
        return
    with k.stage():
        G = C.S["G"]
        if need_ctx:
            pcp = Pool(k, "pc", [128, 2, 256], F32, 2, space="ps")
            pocp = Pool(k, "poc", [128, 65], F32, 2, space="ps")
            Pcp = Pool(k, "Pc", [128, 2, 256], BF16, 2)
            oc = k.sb("oc", [128, 2, 256], F32)
            rdc = Pool(k, "rdc", [128, 1], F32, 2)
            for h in range(4):
                hp = slice((h % 2) * 64, (h % 2) * 64 + 64)
                pc = pcp.get()
                for kc in range(2):
                    k.mm(pc[:, kc, :], QK[:, 2 + h // 2, kc * 128:(kc + 1) * 128], QM[:, h // 2, h % 2, 0:256])
                Pc = Pcp.get()
                k.act(Pc[:], pc[:], AF.Exp, scale=0.125)
                for qc in range(2):
                    poc = pocp.get()
                    for kc in range(2):
                        k.mm(poc[:], Pc[:, kc, qc * 128:(qc + 1) * 128], Ve[:, kc, h, :], start=(kc == 0), stop=(kc == 1))
                    rd = rdc.get()
                    k.recip(rd[:], poc[:, 64:65])
                    k.ts(oc[:, qc, h * 64:(h + 1) * 64], poc[:, 0:64], rd[:, 0:1], ALU.mult)
            k.dma("sync", V(G.t[0:256, 256:512].rearrange("(c p) n -> p c n", p=128), [G.tok]), oc[:])


import math
import ml_dtypes

HY_BANDS = 16


def _hy_consts(n):
    f64 = np.float64
    N = 2 * n
    nch = n // 128
    t = np.linspace(0.0, 1.0, n, dtype=np.float32).astype(f64)
    ang = 2.0 * math.pi * np.arange(n, dtype=f64) / n
    bands = np.linspace(1e-4, HY_BANDS - 1, HY_BANDS, dtype=np.float32).astype(f64)[None]
    z = np.concatenate([t[:, None], np.cos(bands * ang[:, None]), -np.sin(bands * ang[:, None])], axis=-1)
    max_decay = math.log(1e-2) / 0.3
    min_decay = math.log(1e-2) / 1.5
    deltas = np.abs(np.linspace(min_decay, max_decay, 512, dtype=np.float32).astype(f64))
    idx = np.arange(n, dtype=f64) + 0.5
    ph = 2.0 * math.pi * np.outer(idx, idx) / N
    C4 = np.cos(ph)
    S4 = np.sin(ph)

    def tile_(M):
        return np.ascontiguousarray(M.reshape(nch, 128, nch, 128).transpose(2, 1, 0, 3).reshape(nch, 128, nch * 128))
    w = 2.0 * math.pi * idx / N
    cs = np.stack([(2.0 / N) * np.cos(w / 2), (2.0 / N) * np.sin(w / 2), -(2.0 / N) * np.cos(w / 2)], axis=-1)
    return {
        "zT": np.ascontiguousarray(z.T.astype(np.float32)),
        "ntcol": np.ascontiguousarray((-t).reshape(nch, 128).T.astype(np.float32)),
        "absdelta": np.ascontiguousarray(np.tile(deltas[None, :], (128, 1)).astype(np.float32)),
        "C4t": tile_(C4).astype(ml_dtypes.bfloat16),
        "S4t": tile_(S4).astype(ml_dtypes.bfloat16),
        "cs": np.ascontiguousarray(cs.reshape(nch, 128, 3).transpose(1, 0, 2).astype(np.float32)),
    }


_HYC = {}


def hy_const(n, name):
    if n not in _HYC:
        _HYC[n] = _hy_consts(n)
    return _HYC[n][name]


IN_DTYPES = {}
for _n in (SEQ, CTX):
    _nch = _n // 128
    IN_SHAPES[f"hy_zT_{_n}"] = [33, _n]
    IN_SHAPES[f"hy_ntcol_{_n}"] = [128, _nch]
    IN_SHAPES[f"hy_cs_{_n}"] = [128, _nch, 3]
    IN_SHAPES[f"hy_C4t_{_n}"] = [_nch, 128, _nch * 128]
    IN_SHAPES[f"hy_S4t_{_n}"] = [_nch, 128, _nch * 128]
    IN_DTYPES[f"hy_C4t_{_n}"] = BF16
    IN_DTYPES[f"hy_S4t_{_n}"] = BF16
    for _nm in ("zT", "ntcol", "cs", "C4t", "S4t"):
        HOST_LAYOUT[f"hy_{_nm}_{_n}"] = (lambda inp, b, _n=_n, _nm=_nm: hy_const(_n, _nm))
IN_SHAPES["hy_absdelta"] = [128, 512]
HOST_LAYOUT["hy_absdelta"] = lambda inp, b: hy_const(CTX, "absdelta")
IN_SHAPES["hy_fcol"] = [NL, 64, 3]
HOST_LAYOUT["hy_fcol"] = lambda inp, b: np.stack([inp["hy_f_b1"], inp["hy_f_b2"], inp["hy_f_freq"]], axis=-1)
for _nm, _sh in (("hy_f_w1", [NL, 33, 64]), ("hy_f_w2", [NL, 64, 64]), ("hy_f_w3", [NL, 64, 1024]),
                 ("hy_conv", [NL, 3, 768]), ("hy_bias", [NL, 2, 256])):
    IN_SHAPES[_nm] = _sh


def bc_load(k, dst, src_ap, tok, eng="sync"):
    P = dst.ap.shape[0]
    k.dma(eng, dst, V(src_ap.partition_broadcast(P), [tok]))


def sin_rr(k, out, in_ps, scale_col, bias_col, pools, npi, nfree):
    a1, a2, ai = pools
    t1 = a1.get()
    t2 = a2.get()
    ti = ai.get()
    P = out.ap.shape[0]
    k.act(t1[:P, :nfree], in_ps, AF.Identity, scale=scale_col, bias=bias_col)
    k.ts(t2[:P, :nfree], t1[:P, :nfree], 1.0 / (2.0 * math.pi), ALU.mult, 64.5, ALU.add)
    k.copy(ti[:P, :nfree], t2[:P, :nfree])
    k.tt(t1[:P, :nfree], t2[:P, :nfree], ti[:P, :nfree], ALU.subtract)
    k.stt(t2[:P, :nfree], t1[:P, :nfree], 0.0, t1[:P, :nfree], ALU.is_lt, ALU.add)
    k.act(out, t2[:P, :nfree], AF.Sin, scale=2.0 * math.pi, bias=npi[:P, 0:1])


def hyf_stage(k, C, l, n, KRI):
    nch = n // 128
    TW = min(512, n)
    with contextlib.ExitStack() as outer:
        hyf_stage_(k, C, l, n, KRI, nch, TW, outer)


def hyf_stage_(k, C, l, n, KRI, nch, TW, outer):
    Pb = k.sb("Pb", [128, nch, 512], BF16, stack=outer)
    Qb = k.sb("Qb", [128, nch, 512], BF16, stack=outer)
    cs = k.sb("cs", [128, nch, 3], F32, stack=outer)
    with k.stage():
        w1 = k.sb("w1", [33, 64], F32)
        w2 = k.sb("w2", [64, 64], F32)
        w3 = k.sb("w3", [64, 1024], F32)
        fcol = k.sb("fcol", [64, 3], F32)
        fb = k.sb("fb", [64, 2], F32)
        zT = k.sb("zT", [33, n], F32)
        npi = k.sb("npi", [128, 1], F32)
        ones_f = k.sb("ones_f", [128, 128], F32)
        k.memset(npi[:], -math.pi)
        k.memset(ones_f[:], 1.0)
        k.dma("sync", w1[:], C.I("hy_f_w1")[l])
        k.dma("sync", w2[:], C.I("hy_f_w2")[l])
        k.dma("sync", w3[:], C.I("hy_f_w3")[l])
        k.dma("sync", fcol[:], C.I("hy_fcol")[l])
        k.dma("sync", zT[:], C.I(f"hy_zT_{n}")[:])
        k.tt(fb[:], fcol[:, 0:2], fcol[:, 2:3].bc([64, 2]), ALU.mult)
        hid1 = k.sb("hid1", [64, n], F32)
        hid2 = k.sb("hid2", [64, n], F32)
        pools = (Pool(k, "a1", [64, 512], F32, 2), Pool(k, "a2", [64, 512], F32, 2), Pool(k, "ai", [64, 512], I32, 2))
        pmp = Pool(k, "pm", [64, 512], F32, 2, space="ps")
        for t0 in range(0, n, TW):
            ps = pmp.get()
            k.mm(ps[:, :TW], w1[:], zT[:, t0:t0 + TW])
            sin_rr(k, hid1[:, t0:t0 + TW], ps[:, :TW], fcol[:, 2:3], fb[:, 0:1], pools, npi, TW)
        for t0 in range(0, n, TW):
            ps = pmp.get()
            k.mm(ps[:, :TW], w2[:], hid1[:, t0:t0 + TW])
            sin_rr(k, hid2[:, t0:t0 + TW], ps[:, :TW], fcol[:, 2:3], fb[:, 1:2], pools, npi, TW)
        absd = k.sb("absd", [128, 512], F32)
        ntc = k.sb("ntc", [128, nch], F32)
        k.dma("sync", absd[:], C.I("hy_absdelta")[:])
        k.dma("sync", ntc[:], C.I(f"hy_ntcol_{n}")[:])
        k.dma("sync", cs[:], C.I(f"hy_cs_{n}")[:])
        HF = k.sb("HF", [128, nch, 512], F32)
        HB = k.sb("HB", [128, nch, 512], F32)
        php = Pool(k, "ph", [128, 512], F32, 3, space="ps")
        pl1 = k.ps("pl1", [128, 512], F32)
        winp = Pool(k, "win", [128, 512], F32, 2)
        absp = Pool(k, "abs", [128, 512], F32, 3)
        for c in range(nch):
            ph0 = php.get()
            ph1 = php.get()
            k.mm(ph0[:], hid2[:, c * 128:(c + 1) * 128], w3[:, 0:512])
            k.mm(ph1[:], hid2[:, c * 128:(c + 1) * 128], w3[:, 512:1024])
            win = winp.get()
            k.act(win[:], absd[:], AF.Exp, scale=ntc[:, c:c + 1])
            k.tt(HF[:, c, :], ph0[:], win[:], ALU.mult)
            k.tt(HB[:, c, :], ph1[:], win[:], ALU.mult)
            if c == 0:
                k.memset(HB[0:1, 0, :], 0.0)
            for j, src in enumerate((HF, HB)):
                ab = absp.get()
                k.act(ab[:], src[:, c, :], AF.Abs)
                k.mm(pl1[:], ones_f[:], ab[:], start=(c == 0 and j == 0), stop=(c == nch - 1 and j == 1))
        rn = k.sb("rn", [128, 512], F32)
        k.recip(rn[:], pl1[:])
        for c in range(nch):
            t = absp.get()
            k.tt(t[:], HF[:, c, :], HB[:, c, :], ALU.add, eng="gpsimd")
            k.tt(Pb[:, c, :], t[:], rn[:], ALU.mult, eng="gpsimd")
            t = absp.get()
            k.tt(t[:], HF[:, c, :], HB[:, c, :], ALU.subtract)
            k.tt(Qb[:, c, :], t[:], rn[:], ALU.mult)
    with k.stage():
        absp = Pool(k, "abs2", [128, 512], F32, 3)
        cbp = Pool(k, "cb", [128, nch, 128], BF16, 2)
        sbp = Pool(k, "sbk", [128, nch, 128], BF16, 2)
        psp = Pool(k, "psq", [128, 512], F32, 4, space="ps")
        kop = Pool(k, "ko", [128, 2, 512], F32, 2)
        C4 = C.I(f"hy_C4t_{n}")
        S4 = C.I(f"hy_S4t_{n}")
        for fc in range(nch):
            cb = cbp.get()
            sb_ = sbp.get()
            k.dma("sync", cb[:].re("p c m -> p (c m)"), C4[fc])
            k.dma("sync", sb_[:].re("p c m -> p (c m)"), S4[fc])
            pPc, pPs, pQc, pQs = psp.get(), psp.get(), psp.get(), psp.get()
            for c in range(nch):
                st, sp = (c == 0), (c == nch - 1)
                k.mm(pPc[:], cb[:, c, :], Pb[:, c, :], start=st, stop=sp)
                k.mm(pPs[:], sb_[:, c, :], Pb[:, c, :], start=st, stop=sp)
                k.mm(pQc[:], cb[:, c, :], Qb[:, c, :], start=st, stop=sp)
                k.mm(pQs[:], sb_[:, c, :], Qb[:, c, :], start=st, stop=sp)
            ko = kop.get()
            t = absp.get()
            k.ts(t[:], pPc[:], cs[:, fc, 0:1], ALU.mult)
            k.stt(ko[:, 0, :], pPs[:], cs[:, fc, 1:2], t[:], ALU.mult, ALU.add)
            t = absp.get()
            k.ts(t[:], pQc[:], cs[:, fc, 1:2], ALU.mult)
            k.stt(ko[:, 1, :], pQs[:], cs[:, fc, 2:3], t[:], ALU.mult, ALU.add)
            k.dma("sync", KRI[fc * 128:(fc + 1) * 128], ko[:])


def hyc_stage(k, C, l, n, base, KRI):
    nch = n // 128
    HYs = C.S["HY"]
    G = C.S["G"]
    with k.stage():
        cw = k.sb("cw", [128, 3, 768], F32)
        hb = k.sb("hb", [128, 2, 256], F32)
        for i in range(3):
            bc_load(k, cw[:, i, :], C.I("hy_conv").t[l, i], C.I("hy_conv").tok)
        for i in range(2):
            bc_load(k, hb[:, i, :], C.I("hy_bias").t[l, i], C.I("hy_bias").tok)
        v = k.sb("v", [128, nch, 256], F32)
        x1 = k.sb("x1", [128, nch, 256], F32)
        x2 = k.sb("x2", [128, nch, 256], F32)
        vb = k.sb("vb", [128, nch, 256], BF16)
        z = k.sb("z", [128, nch, 256], F32)
        zb = k.sb("zb", [128, nch, 256], BF16)
        Yc = k.sb("Yc", [128, nch, 256], BF16)
        Ys = k.sb("Ys", [128, nch, 256], BF16)
        sp_ = Pool(k, "scur", [128, 768], F32, 2)
        pp_ = Pool(k, "sprev", [128, 768], F32, 2)
        np_ = Pool(k, "snext", [128, 768], F32, 2)
        up = Pool(k, "u", [128, 768], F32, 2)
        tp = Pool(k, "tt", [128, 768], F32, 2)
        for c in range(nch):
            r0 = base + c * 128
            sc_, sp, sn = sp_.get(), pp_.get(), np_.get()
            k.dma("sync", sc_[:], HYs[r0:r0 + 128, :])
            if c == 0:
                k.memset(sp[0:1, :], 0.0)
                k.dma("sync", sp[1:128, :], HYs[r0:r0 + 127, :])
            else:
                k.dma("sync", sp[:], HYs[r0 - 1:r0 + 127, :])
            if c == nch - 1:
                k.memset(sn[:], 0.0)
                k.dma("sync", sn[0:127, :], HYs[r0 + 1:r0 + 128, :])
            else:
                k.dma("sync", sn[:], HYs[r0 + 1:r0 + 129, :])
            u = up.get()
            t1 = tp.get()
            t2 = tp.get()
            k.tt(u[:], sc_[:], cw[:, 1, :], ALU.mult)
            k.tt(t1[:], sp[:], cw[:, 0, :], ALU.mult, eng="gpsimd")
            k.tt(t2[:], sn[:], cw[:, 2, :], ALU.mult, eng="gpsimd")
            k.tt(u[:], u[:], t1[:], ALU.add)
            k.tt(v[:, c, :], u[:, 0:256], t2[:, 0:256], ALU.add)
            k.tt(x1[:, c, :], u[:, 256:512], t2[:, 256:512], ALU.add)
            k.tt(x2[:, c, :], u[:, 512:768], t2[:, 512:768], ALU.add)
            k.copy(vb[:, c, :], v[:, c, :], eng="scalar")
        cbp = Pool(k, "cb", [128, nch, 128], BF16, 3)
        sbp = Pool(k, "sbk", [128, nch, 128], BF16, 3)
        pup = Pool(k, "pU", [128, 256], F32, 4, space="ps")
        pyp = Pool(k, "pY", [128, 256], F32, 2, space="ps")
        krp = Pool(k, "kr", [128, 2, 512], F32, 2)
        usp = Pool(k, "us", [128, 2, 256], F32, 2)
        tp2 = Pool(k, "t2", [128, 256], F32, 6)
        outp = Pool(k, "yo", [128, 256], F32, 2)
        C4 = C.I(f"hy_C4t_{n}")
        S4 = C.I(f"hy_S4t_{n}")
        for o in range(2):
            src = vb if o == 0 else zb
            for fc in range(nch):
                cb = cbp.get()
                sb_ = sbp.get()
                k.dma("sync", cb[:].re("p c m -> p (c m)"), C4[fc])
                k.dma("sync", sb_[:].re("p c m -> p (c m)"), S4[fc])
                pUc, pUs = pup.get(), pup.get()
                for c in range(nch):
                    st, sp = (c == 0), (c == nch - 1)
                    k.mm(pUc[:], cb[:, c, :], src[:, c, :], start=st, stop=sp)
                    k.mm(pUs[:], sb_[:, c, :], src[:, c, :], start=st, stop=sp)
                kr = krp.get()
                k.dma("sync", kr[:], KRI[fc * 128:(fc + 1) * 128])
                us = usp.get()
                k.copy(us[:, 0, :], pUc[:], eng="scalar")
                k.copy(us[:, 1, :], pUs[:], eng="scalar")
                KR = kr[:, 0, o * 256:(o + 1) * 256]
                KI = kr[:, 1, o * 256:(o + 1) * 256]
                a, b_, c_, d_ = tp2.get(), tp2.get(), tp2.get(), tp2.get()
                k.tt(a[:], us[:, 0, :], KR, ALU.mult)
                k.tt(b_[:], us[:, 1, :], KI, ALU.mult, eng="gpsimd")
                k.tt(Yc[:, fc, :], a[:], b_[:], ALU.add)
                k.tt(c_[:], us[:, 1, :], KR, ALU.mult, eng="gpsimd")
                k.tt(d_[:], us[:, 0, :], KI, ALU.mult)
                k.tt(Ys[:, fc, :], c_[:], d_[:], ALU.subtract, eng="gpsimd")
            for tc in range(nch):
                cb = cbp.get()
                sb_ = sbp.get()
                k.dma("sync", cb[:].re("p c m -> p (c m)"), C4[tc])
                k.dma("sync", sb_[:].re("p c m -> p (c m)"), S4[tc])
                py = pyp.get()
                for f in range(nch):
                    k.mm(py[:], cb[:, f, :], Yc[:, f, :], start=(f == 0), stop=False)
                    k.mm(py[:], sb_[:, f, :], Ys[:, f, :], start=False, stop=(f == nch - 1))
                a, b_ = tp2.get(), tp2.get()
                if o == 0:
                    k.tt(a[:], v[:, tc, :], hb[:, 0, :], ALU.mult, eng="gpsimd")
                    k.tt(b_[:], py[:], a[:], ALU.add)
                    k.tt(z[:, tc, :], b_[:], x1[:, tc, :], ALU.mult)
                    k.copy(zb[:, tc, :], z[:, tc, :], eng="scalar")
                else:
                    k.tt(a[:], z[:, tc, :], hb[:, 1, :], ALU.mult, eng="gpsimd")
                    k.tt(b_[:], py[:], a[:], ALU.add)
                    yo = outp.get()
                    k.tt(yo[:], b_[:], x2[:, tc, :], ALU.mult)
                    k.dma("sync", G[base + tc * 128:base + (tc + 1) * 128, 0:256], yo[:])


def hy_stage(k, C, l, need_ctx):
    if not hasattr(C, "KRI"):
        C.KRI = {n: k.dram(f"s_kri{n}", [n, 2, 512], F32) for n in (SEQ, CTX)}
    hyf_stage(k, C, l, SEQ, C.KRI[SEQ])
    hyc_stage(k, C, l, SEQ, CTX, C.KRI[SEQ])
    if need_ctx:
        hyf_stage(k, C, l, CTX, C.KRI[CTX])
        hyc_stage(k, C, l, CTX, 0, C.KRI[CTX])


def _chunk_consts():
    i = np.arange(64)
    tri = np.zeros((64, 2, 64), np.float32)
    tri[:, 0, :] = (i[:, None] <= i[None, :])
    tri[:, 1, :] = (i[:, None] >= i[None, :])
    after_eq = np.zeros((64, 8, 64), bool)
    after_st = np.zeros((64, 8, 64), bool)
    before_st = np.zeros((64, 8, 64), bool)
    for e in range(8):
        if e < 4:
            after_eq[:, e, :] = i[None, :] >= i[:, None]
            after_st[:, e, :] = i[None, :] > i[:, None]
            before_st[:, e, :] = i[None, :] < i[:, None]
        else:
            after_eq[:, e, :] = i[None, :] <= i[:, None]
            after_st[:, e, :] = i[None, :] < i[:, None]
            before_st[:, e, :] = i[None, :] > i[:, None]
    NEG = np.float32(-30000.0)
    mneg = np.stack([np.where(after_eq, 0, NEG), np.where(before_st, 0, NEG)], axis=1).astype(np.float32)
    m01 = np.stack([after_eq, after_st, before_st], axis=1).astype(np.float32)
    ident8 = np.tile(np.eye(64, dtype=np.float32)[:, None, :], (1, 8, 1))
    return {"c_tri": tri, "c_mneg": np.ascontiguousarray(mneg), "c_m01": np.ascontiguousarray(m01), "c_ident8": ident8}


_CC = {}


def _cc(name):
    if not _CC:
        _CC.update(_chunk_consts())
    return _CC[name]


for _nm, _sh in (("c_tri", [64, 2, 64]), ("c_mneg", [64, 2, 8, 64]), ("c_m01", [64, 3, 8, 64]), ("c_ident8", [64, 8, 64])):
    IN_SHAPES[_nm] = _sh
    HOST_LAYOUT[_nm] = (lambda inp, b, _nm=_nm: _cc(_nm))

NCH = NT // 64
FORD = list(range(NCH))
BORD = [3, 2, 1, 0] + list(range(NCH - 1, 3, -1))


def tri_inverse(k, M, A, ident8, pool_ps, pool_f, pool_b, levels=5):
    X = pool_f.get()
    k.tt(X[:], ident8[:], A[:], ALU.subtract)
    Mk, Ak = M, A
    for lev in range(1, levels + 1):
        pM = pool_ps.get()
        for e in range(8):
            k.mm(pM[:, e, :], Ak[:, e, :], Mk[:, e, :])
        if lev < levels:
            pA = pool_ps.get()
            for e in range(8):
                k.mm(pA[:, e, :], Mk[:, e, :], Ak[:, e, :])
            An = pool_f.get()
            k.copy(An[:], pA[:], eng="vector")
        Mn = pool_f.get()
        k.copy(Mn[:], pM[:], eng="scalar")
        pX = pool_ps.get()
        for e in range(8):
            k.mm(pX[:, e, :], Mn[:, e, :], X[:, e, :])
        Xn = pool_f.get()
        k.tt(Xn[:], X[:], pX[:], ALU.add)
        X = Xn
        Mk = Mn
        if lev < levels:
            Ak = An
    Xb = None
    if pool_b is not None:
        Xb = pool_b.get()
        k.copy(Xb[:], X[:], eng="gpsimd")
    return X, Xb


IN_SHAPES["dn_conv"] = [NL, 3, 768]
IN_SHAPES["dn_dtb8"] = [NL, 8]
IN_SHAPES["dn_alog8"] = [NL, 8]
IN_SHAPES["dn_normT"] = [NL, 256]
HOST_LAYOUT["dn_dtb8"] = lambda inp, b: inp["dn_dt_bias"].reshape(NL, 8)
HOST_LAYOUT["dn_alog8"] = lambda inp, b: inp["dn_a_log"].reshape(NL, 8)
HOST_LAYOUT["dn_normT"] = lambda inp, b: np.tile(inp["dn_norm"], (1, 4))


class Lane:
    pass


def run_lanes(k, nlanes, mkpools, body, items):
    lanes = [mkpools(i) for i in range(nlanes)]
    items = list(items)
    for i0 in range(0, len(items), nlanes):
        lists = []
        for j, it in enumerate(items[i0:i0 + nlanes]):
            with k.defer() as L:
                body(it, lanes[j])
            lists.append(L)
        k.replay(lists)


def shifted_loads(k, src, r0, ncols, seq_first, seq_last, pools):
    pc, pp, pn = pools
    sc_, sp, sn = pc.get(), pp.get(), pn.get()
    W = sc_.ap_shape[1]
    k.dma("sync", sc_[:], src[r0:r0 + 128, 0:W])
    if seq_first:
        k.memset(sp[0:1, :], 0.0)
        k.dma("sync", sp[1:128, :], src[r0:r0 + 127, 0:ncols])
    else:
        k.dma("sync", sp[:], src[r0 - 1:r0 + 127, 0:ncols])
    if seq_last:
        k.memset(sn[:], 0.0)
        k.dma("sync", sn[0:127, :], src[r0 + 1:r0 + 128, 0:ncols])
    else:
        k.dma("sync", sn[:], src[r0 + 1:r0 + 129, 0:ncols])
    return sc_, sp, sn


def dn_pre_stage(k, C, l):
    S = C.S
    with k.stage():
        cw = k.sb("cw", [128, 3, 768], F32)
        for i in range(3):
            bc_load(k, cw[:, i, :], C.I("dn_conv").t[l, i], C.I("dn_conv").tok)
        dtb = k.sb("dtb", [128, 8], F32)
        nA = k.sb("nA", [128, 8], F32)
        bc_load(k, dtb[:], C.I("dn_dtb8").t[l], C.I("dn_dtb8").tok)
        bc_load(k, nA[:], C.I("dn_alog8").t[l], C.I("dn_alog8").tok)
        k.act(nA[:], nA[:], AF.Exp)
        k.ts(nA[:], nA[:], -1.0, ALU.mult)
        def mkpools(i):
            L = Lane()
            L.pc = Pool(k, f"scur{i}", [128, 1040], F32, 2)
            L.pp = Pool(k, f"sprev{i}", [128, 768], F32, 2)
            L.pn = Pool(k, f"snext{i}", [128, 768], F32, 2)
            for p_ in (L.pc, L.pp, L.pn):
                for b_ in p_.bufs:
                    b_.ap_shape = b_.t.shape
            L.up = Pool(k, f"u{i}", [128, 768], F32, 1)
            L.tp = Pool(k, f"tt{i}", [128, 768], F32, 3)
            L.qp = Pool(k, f"qo{i}", [128, 768], F32, 2)
            L.gbp = Pool(k, f"gb{i}", [128, 16], F32, 2)
            L.smp = Pool(k, f"sm{i}", [128, 8], F32, 4)
            return L

        def body(tc, L):
            r0 = tc * 128
            sc_, sp, sn = shifted_loads(k, S["DN"], r0, 768, tc in (0, 2), tc in (1, NT // 128 - 1), (L.pc, L.pp, L.pn))
            u, t1, t2 = L.up.get(), L.tp.get(), L.tp.get()
            k.tt(u[:], sc_[:, 0:768], cw[:, 1, :], ALU.mult)
            k.tt(t1[:], sp[:], cw[:, 0, :], ALU.mult, eng="gpsimd")
            k.tt(t2[:], sn[:], cw[:, 2, :], ALU.mult, eng="gpsimd")
            k.tt(u[:], u[:], t1[:], ALU.add)
            k.tt(u[:], u[:], t2[:], ALU.add)
            qo = L.qp.get()
            k.act(qo[:], u[:], AF.Silu)
            sq = L.tp.get()
            k.tt(sq[:, 0:512], qo[:, 0:512], qo[:, 0:512], ALU.mult, eng="gpsimd")
            ss = L.smp.get()
            k.reduce(ss[:], sq[:, 0:512].re("p (g d) -> p g d", d=64))
            rn = L.smp.get()
            k.act(rn[:], ss[:], AF.Sqrt, bias=C.eps_t[:])
            k.recip(rn[:], rn[:])
            k.ts(rn[:, 0:4], rn[:, 0:4], 0.125, ALU.mult)
            k.tt(qo[:, 0:512].re("p (g d) -> p g d", d=64), qo[:, 0:512].re("p (g d) -> p g d", d=64),
                 rn[:, :, None].bc([128, 8, 64]), ALU.mult)
            k.dma("sync", S["DNQKV"][r0:r0 + 128, :], qo[:])
            ba = sc_[:, 1024:1040].re("p (d a h) -> p d a h", d=2, a=2)
            gb = L.gbp.get()
            k.act(gb[:, 8:16].re("p (d h) -> p d h", d=2), ba[:, :, 0, :], AF.Sigmoid)
            x = L.smp.get()
            k.tt(x[:].re("p (d h) -> p d h", d=2), ba[:, :, 1, :], dtb[:].re("p (d h) -> p d h", d=2), ALU.add)
            k.act(x[:], x[:], AF.Exp)
            k.act(x[:], x[:], AF.Ln, bias=1.0)
            k.tt(gb[:, 0:8], x[:], nA[:], ALU.mult)
            k.dma("sync", S["DNGB"][r0:r0 + 128, :], gb[:])

        run_lanes(k, 3, mkpools, body, range(NT // 128))


def dn_scan_stage(k, C, l, need_ctx):
    S = C.S
    with k.stage():
        tri = k.sb("tri", [64, 2, 64], F32)
        mneg = k.sb("mneg", [64, 2, 8, 64], F32)
        id8 = k.sb("id8", [64, 8, 64], F32)
        k.dma("sync", tri[:], C.I("c_tri")[:])
        k.dma("sync", mneg[:], C.I("c_mneg")[:])
        k.dma("sync", id8[:], C.I("c_ident8")[:])
        idn = C.ident[0:64, 0:64]
        idbt = k.sb("idbt", [64, 64], BF16)
        k.copy(idbt[:], C.ident[0:64, 0:64])
        idb = idbt[:]
        St = k.sb("St", [64, 8, 64], F32)
        k.memset(St[:], 0.0)
        pps = Pool(k, "pp", [64, 8, 64], F32, 7, space="ps")
        psm = k.ps("psm", [64, 8], F32)
        sbp = Pool(k, "w", [64, 8, 64], F32, 40)
        inv_f = Pool(k, "ivf", [64, 8, 64], F32, 8)
        inv_b = Pool(k, "ivb", [64, 8, 64], BF16, 6)
        sbb = Pool(k, "wb", [64, 8, 64], BF16, 24)
        Stb = k.sb("Stb", [64, 8, 64], BF16)
        k.memset(Stb[:], 0.0)
        qkp = Pool(k, "qk8", [64, 2, 8, 64], F32, 2)
        v8p = Pool(k, "v8", [64, 8, 64], F32, 2)
        gbp = Pool(k, "gb8", [64, 2, 8], F32, 2)
        smp = Pool(k, "sm", [64, 8], F32, 8)
        import os
        nsteps = int(os.environ.get("DN_STEPS", str(NCH)))
        part = int(os.environ.get("DN_PART", "99"))
        for s in range(nsteps):
            ch = (FORD[s], BORD[s])
            qk8, v8, gb8 = qkp.get(), v8p.get(), gbp.get()
            for d in range(2):
                r0 = ch[d] * 64
                es = slice(d * 4, d * 4 + 4)
                k.dma("sync", qk8[:, :, es, :], V(S["DNQKV"].t[r0:r0 + 64, 0:512].rearrange("p (a h d) -> p a h d", a=2, h=4), [S["DNQKV"].tok]))
                k.dma("sync", v8[:, es, :], V(S["DNQKV"].t[r0:r0 + 64, 512:768].rearrange("p (h d) -> p h d", h=4), [S["DNQKV"].tok]))
                k.dma("sync", gb8[:, 0, es], S["DNGB"][r0:r0 + 64, d * 4:d * 4 + 4])
                k.dma("sync", gb8[:, 1, es], S["DNGB"][r0:r0 + 64, 8 + d * 4:8 + d * 4 + 4])
            beta_bc = gb8[:, 1, :, None].bc([64, 8, 64])
            if part < 1:
                continue
            for d in range(2):
                k.mm(psm[:, d * 4:d * 4 + 4], tri[:, d, :], gb8[:, 0, d * 4:d * 4 + 4])
            gcc = smp.get()
            k.copy(gcc[:], psm[:], eng="scalar")
            sub = int(os.environ.get("DN_SUB", "99"))
            if sub < 2:
                continue
            grep = sbp.get()
            k.copy(grep[:], gb8[:, 0, :, None].bc([64, 8, 64]))
            pgcb = pps.get()
            for e in range(8):
                k.mm(pgcb[:, e, :], grep[:, e, :], tri[:, e // 4, :])
            if sub < 3:
                continue
            gcb = sbp.get()
            k.copy(gcb[:], pgcb[:], eng="scalar")
            D1 = sbp.get()
            k.tt(D1[:], gcb[:], gcc[:, :, None].bc([64, 8, 64]), ALU.subtract)
            egcb = sbp.get()
            k.act(egcb[:], gcb[:], AF.Exp)
            if sub < 4:
                continue
            t = sbp.get()
            k.tt(t[:], D1[:], mneg[:, 0], ALU.add, eng="gpsimd")
            E1 = sbp.get()
            k.act(E1[:], t[:], AF.Exp)
            if sub < 5:
                continue
            t = sbp.get()
            k.stt(t[:], D1[:], -1.0, mneg[:, 1], ALU.mult, ALU.add)
            E2 = sbp.get()
            k.act(E2[:], t[:], AF.Exp)
            if sub < 6:
                continue
            egc = smp.get()
            k.act(egc[:], gcc[:], AF.Exp)
            ekd = smp.get()
            k.act(ekd[:, 0:4], D1[:, 0:4, 63], AF.Exp)
            k.act(ekd[:, 4:8], D1[:, 4:8, 0], AF.Exp)
            if part < 2:
                continue
            pKT, pQT = pps.get(), pps.get()
            for e in range(8):
                k.tr(pKT[:, e, :], qk8[:, 1, e, :], idn)
            for e in range(8):
                k.tr(pQT[:, e, :], qk8[:, 0, e, :], idn)
            KT, QT = sbb.get(), sbb.get()
            k.copy(KT[:], pKT[:], eng="scalar")
            k.copy(QT[:], pQT[:], eng="vector")
            if part < 3:
                continue
            pG, pQK = pps.get(), pps.get()
            for e in range(8):
                k.mm(pG[:, e, :], KT[:, e, :], KT[:, e, :])
            for e in range(8):
                k.mm(pQK[:, e, :], KT[:, e, :], QT[:, e, :])
            t = sbp.get()
            k.tt(t[:], pG[:], E2[:], ALU.mult)
            M = sbp.get()
            k.tt(M[:], t[:], beta_bc, ALU.mult, eng="gpsimd")
            attnT = sbb.get()
            k.tt(attnT[:], pQK[:], E1[:], ALU.mult)
            pA = pps.get()
            for e in range(8):
                k.tr(pA[:, e, :], M[:, e, :], idn)
            A = sbp.get()
            k.copy(A[:], pA[:], eng="scalar")
            if part < 4:
                continue
            X = tri_inverse(k, M, A, id8, pps, inv_f, inv_b)
            if part < 5:
                continue
            Vb = sbb.get()
            k.tt(Vb[:], v8[:], beta_bc, ALU.mult, eng="gpsimd")
            bg = smp.get()
            k.tt(bg[:], gb8[:, 1, :], egc[:], ALU.mult)
            KBG = sbb.get()
            k.tt(KBG[:], qk8[:, 1], bg[:, :, None].bc([64, 8, 64]), ALU.mult, eng="gpsimd")
            pU, pW = pps.get(), pps.get()
            for e in range(8):
                k.mm(pU[:, e, :], X[:, e, :], Vb[:, e, :])
            for e in range(8):
                k.mm(pW[:, e, :], KBG[:, e, :], X[:, e, :])
            U, WT = sbp.get(), sbb.get()
            k.copy(U[:], pU[:], eng="scalar")
            k.copy(WT[:], pW[:], eng="vector")
            QdT = sbb.get()
            k.tt(QdT[:], QT[:], egcb[:], ALU.mult, eng="gpsimd")
            Kd = sbb.get()
            k.tt(Kd[:], qk8[:, 1], ekd[:, :, None].bc([64, 8, 64]), ALU.mult, eng="gpsimd")
            if part < 6:
                continue
            pv = pps.get()
            for e in range(8):
                k.mm(pv[:, e, :], WT[:, e, :], Stb[:, e, :])
            vnew = sbb.get()
            k.tt(vnew[:], U[:], pv[:], ALU.subtract)
            po = pps.get()
            for e in range(8):
                k.mm(po[:, e, :], QdT[:, e, :], Stb[:, e, :], start=True, stop=False)
                k.mm(po[:, e, :], attnT[:, e, :], vnew[:, e, :], start=False, stop=True)
            o8 = sbp.get()
            k.copy(o8[:], po[:], eng="scalar")
            for d in range(2):
                if need_ctx or ch[d] >= 4:
                    k.dma("sync", S["DNO"][d, ch[d] * 64:(ch[d] + 1) * 64, :], o8[:, d * 4:d * 4 + 4, :].re("p h d -> p (h d)"))
            pS = pps.get()
            for e in range(8):
                k.mm(pS[:, e, :], Kd[:, e, :], vnew[:, e, :])
            glast = smp.get()
            k.copy(glast[:, 0:4], egcb[:, 0:4, 63])
            k.copy(glast[:, 4:8], egcb[:, 4:8, 0])
            t = sbp.get()
            k.tt(t[:], St[:], glast[:, :, None].bc([64, 8, 64]), ALU.mult)
            k.tt(St[:], t[:], pS[:], ALU.add)
            k.copy(Stb[:], St[:], eng="gpsimd")


def dn_post_stage(k, C, l, need_ctx):
    S = C.S
    with k.stage():
        nwt = k.sb("nwt", [128, 256], F32)
        bc_load(k, nwt[:], C.I("dn_normT").t[l], C.I("dn_normT").tok)
        op_ = Pool(k, "o", [128, 2, 256], F32, 2)
        zp = Pool(k, "z", [128, 256], F32, 2)
        tp = Pool(k, "t", [128, 256], F32, 6)
        smp = Pool(k, "sm", [128, 4], F32, 4)
        for tc in range(NT // 128):
            if tc < 2 and not need_ctx:
                continue
            r0 = tc * 128
            o = op_.get()
            k.dma("sync", o[:], V(S["DNO"].t[:, r0:r0 + 128, :].rearrange("d p c -> p d c"), [S["DNO"].tok]))
            zt = zp.get()
            k.dma("sync", zt[:], S["DN"][r0:r0 + 128, 768:1024])
            os_ = tp.get()
            k.tt(os_[:], o[:, 0, :], o[:, 1, :], ALU.add)
            sq = tp.get()
            k.tt(sq[:], os_[:], os_[:], ALU.mult, eng="gpsimd")
            ss = smp.get()
            k.reduce(ss[:], sq[:].re("p (h d) -> p h d", d=64))
            rn = smp.get()
            k.act(rn[:], ss[:], AF.Sqrt, scale=1.0 / 64, bias=C.eps_t[:])
            k.recip(rn[:], rn[:])
            sz = tp.get()
            k.act(sz[:], zt[:], AF.Silu)
            a = tp.get()
            k.tt(a[:].re("p (h d) -> p h d", d=64), os_[:].re("p (h d) -> p h d", d=64), rn[:, :, None].bc([128, 4, 64]), ALU.mult)
            k.tt(a[:], a[:], nwt[:], ALU.mult, eng="gpsimd")
            y = tp.get()
            k.tt(y[:], a[:], sz[:], ALU.mult)
            k.dma("sync", S["G"][r0:r0 + 128, 512:768], y[:])


def dn_stage(k, C, l, need_ctx):
    dn_pre_stage(k, C, l)
    scan_stage(k, C, l, need_ctx, do_dn=True, do_rw=False)
    dn_post_stage(k, C, l, need_ctx)


for _nm, _sh in (("rw_mu", [NL, 2, 896]), ("rw_w_up", [NL, 2, 32, 256]), ("rw_a_up", [NL, 2, 32, 256]),
                 ("rw_g_up", [NL, 64, 256]), ("rw_w0", [NL, 2, 256]), ("rw_a0", [NL, 2, 256]),
                 ("rw_k_k", [NL, 256]), ("rw_k_a", [NL, 256]), ("rw_ln_w", [NL, 256]), ("rw_ln_b", [NL, 256]),
                 ("rw_r_kT", [NL, 256]), ("rw_mulrT", [NL, 64, 3, 2])):
    IN_SHAPES[_nm] = _sh
HOST_LAYOUT["rw_r_kT"] = lambda inp, b: inp["rw_r_k"].reshape(NL, 256)


def _rw_mulrT(inp, b):
    mu = inp["rw_mu"]
    o = np.zeros((NL, 64, 3, 2), np.float32)
    o[:, 0:32, 0, :] = mu[:, :, 768:800].transpose(0, 2, 1)
    o[:, 0:32, 1, :] = mu[:, :, 800:832].transpose(0, 2, 1)
    o[:, :, 2, :] = mu[:, :, 832:896].transpose(0, 2, 1)
    return o


HOST_LAYOUT["rw_mulrT"] = _rw_mulrT
SEQS = ((0, CTX), (CTX, NT))
LW_SCALE = -math.exp(-0.5)


def rw_pre_stage(k, C, l):
    S = C.S
    with k.stage():
        mulr = k.sb("mulr", [64, 3, 2], F32)
        k.dma("sync", mulr[:], C.I("rw_mulrT")[l])
        c0lr = k.sb("c0lr", [64, 3], F32)
        k.tt(c0lr[:], mulr[:, :, 0], mulr[:, :, 1], ALU.add)
        k.ts(c0lr[:], c0lr[:], -1.0, ALU.mult, 1.0, ALU.add)
        lrT = []
        for g_, ncl, fn in ((0, 32, AF.Tanh), (1, 32, None), (2, 64, AF.Sigmoid)):
            u = k.sb(f"lru{g_}", [64, NT], F32)
            o = k.sb(f"lro{g_}", [64, NT], F32)
            k.dma("sync", u[0:ncl, :], S["RWLR"][g_, 0:ncl, :])
            k.ts(o[0:ncl, :], u[0:ncl, :], c0lr[0:ncl, g_:g_ + 1], ALU.mult)
            for (a, b_) in SEQS:
                k.stt(o[0:ncl, a + 1:b_], u[0:ncl, a:b_ - 1], mulr[0:ncl, g_, 0:1], o[0:ncl, a + 1:b_], ALU.mult, ALU.add)
                k.stt(o[0:ncl, a:b_ - 1], u[0:ncl, a + 1:b_], mulr[0:ncl, g_, 1:2], o[0:ncl, a:b_ - 1], ALU.mult, ALU.add)
            if fn is not None:
                k.act(o[0:ncl, :], o[0:ncl, :], fn)
            lrT.append(o)
        wup = k.sb("wup", [32, 2, 256], F32)
        aup = k.sb("aup", [32, 2, 256], F32)
        gup = k.sb("gup", [64, 256], F32)
        w0r = k.sb("w0r", [1, 2, 256], F32)
        a0r = k.sb("a0r", [1, 2, 256], F32)
        ones1 = k.sb("ones1", [1, 128], F32)
        k.memset(ones1[:], 1.0)
        for d in range(2):
            k.dma("sync", wup[:, d, :], C.I("rw_w_up")[l, d])
            k.dma("sync", aup[:, d, :], C.I("rw_a_up")[l, d])
            k.dma("sync", w0r[:, d, :], C.I("rw_w0")[l, d:d + 1])
            k.dma("sync", a0r[:, d, :], C.I("rw_a0")[l, d:d + 1])
        k.dma("sync", gup[:], C.I("rw_g_up")[l])
        mu = k.sb("mu", [128, 2, 768], F32)
        for i in range(2):
            bc_load(k, mu[:, i, :], C.I("rw_mu").t[l, i, 0:768], C.I("rw_mu").tok)
        c0 = k.sb("c0", [128, 768], F32)
        k.tt(c0[:], mu[:, 0, :], mu[:, 1, :], ALU.add)
        k.ts(c0[:], c0[:], -1.0, ALU.mult, 1.0, ALU.add)
        kkw = k.sb("kkw", [128, 256], F32)
        kaw = k.sb("kaw", [128, 256], F32)
        omka = k.sb("omka", [128, 256], F32)
        bc_load(k, kkw[:], C.I("rw_k_k").t[l], C.I("rw_k_k").tok)
        bc_load(k, kaw[:], C.I("rw_k_a").t[l], C.I("rw_k_a").tok)
        k.ts(omka[:], kaw[:], -1.0, ALU.mult, 1.0, ALU.add)
        def mkpools(i):
            L = Lane()
            L.pc = Pool(k, f"scur{i}", [128, 768], F32, 2)
            L.pp = Pool(k, f"sprev{i}", [128, 768], F32, 2)
            L.pn = Pool(k, f"snext{i}", [128, 768], F32, 2)
            for p_ in (L.pc, L.pp, L.pn):
                for b_ in p_.bufs:
                    b_.ap_shape = b_.t.shape
            L.tp = Pool(k, f"tt{i}", [128, 768], F32, 3)
            L.op_ = Pool(k, f"out{i}", [128, 10, 256], F32, 2)
            L.t2 = Pool(k, f"t2{i}", [128, 256], F32, 6)
            L.smp = Pool(k, f"sm{i}", [128, 4], F32, 4)
            L.plp = Pool(k, f"pl{i}", [128, 2, 256], F32, 3, space="ps")
            return L

        def body(tc, L):
            pc, pp, pn, tp, op_, t2, smp, plp = L.pc, L.pp, L.pn, L.tp, L.op_, L.t2, L.smp, L.plp
            r0 = tc * 128
            sc_, sp, sn = shifted_loads(k, S["RW"], r0, 768, tc in (0, 2), tc in (1, NT // 128 - 1), (pc, pp, pn))
            o = op_.get()
            s_ = tp.get()
            t1 = tp.get()
            k.tt(s_[:], sc_[:], c0[:], ALU.mult)
            k.tt(t1[:], sp[:], mu[:, 0, :], ALU.mult, eng="gpsimd")
            k.tt(s_[:], s_[:], t1[:], ALU.add)
            t1 = tp.get()
            k.tt(t1[:], sn[:], mu[:, 1, :], ALU.mult, eng="gpsimd")
            k.tt(s_[:, 0:256], s_[:, 0:256], t1[:, 0:256], ALU.add)
            k.tt(s_[:, 256:512], s_[:, 256:512], t1[:, 256:512], ALU.add)
            k.tt(o[:, 1, :], s_[:, 512:768], t1[:, 512:768], ALU.add)
            k.copy(o[:, 0, :], s_[:, 0:256], eng="scalar")
            kcur = s_[:, 256:512]
            tok = slice(r0, r0 + 128)
            pwt, pat, pgt = plp.get(), plp.get(), plp.get()
            pw = [pwt[:, 0, :], pwt[:, 1, :]]
            pa = [pat[:, 0, :], pat[:, 1, :]]
            pg = pgt[:, 0, :]
            for d in range(2):
                k.mm(pw[d], lrT[0][0:32, tok], wup[:, d, :], start=True, stop=False)
                k.mm(pw[d], ones1[:], w0r[:, d, :], start=False, stop=True)
            for d in range(2):
                k.mm(pa[d], lrT[1][0:32, tok], aup[:, d, :], start=True, stop=False)
                k.mm(pa[d], ones1[:], a0r[:, d, :], start=False, stop=True)
            k.mm(pg, lrT[2][0:64, tok], gup[:])
            k.copy(o[:, 9, :], pg, eng="scalar")
            kx = t2.get()
            k.tt(kx[:], kcur, kkw[:], ALU.mult, eng="gpsimd")
            sq = t2.get()
            k.tt(sq[:], kx[:], kx[:], ALU.mult, eng="gpsimd")
            ss = smp.get()
            k.reduce(ss[:], sq[:].re("p (h d) -> p h d", d=64))
            rn = smp.get()
            k.act(rn[:], ss[:], AF.Sqrt, bias=C.eps_t[:])
            k.recip(rn[:], rn[:])
            kk = t2.get()
            k.tt(kk[:].re("p (h d) -> p h d", d=64), kx[:].re("p (h d) -> p h d", d=64), rn[:, :, None].bc([128, 4, 64]), ALU.mult)
            k.ts(o[:, 2, :], kk[:], -1.0, ALU.mult)
            for d in range(2):
                sg = t2.get()
                k.act(sg[:], pw[d], AF.Sigmoid)
                k.ts(o[:, 3 + d, :], sg[:], LW_SCALE, ALU.mult, eng="gpsimd")
                ar = t2.get()
                k.act(ar[:], pa[d], AF.Sigmoid)
                k.tt(o[:, 7 + d, :], kk[:], ar[:], ALU.mult, eng="gpsimd")
                t = t2.get()
                k.tt(t[:], ar[:], kaw[:], ALU.mult)
                k.tt(t[:], t[:], omka[:], ALU.add)
                k.tt(o[:, 5 + d, :], kcur, t[:], ALU.mult)
            k.dma("sync", S["RWP"][r0:r0 + 128], o[:])


        run_lanes(k, 2, mkpools, body, range(NT // 128))


def rw_scan_stage(k, C, l, need_ctx):
    S = C.S
    with k.stage():
        tri = k.sb("tri", [64, 2, 64], F32)
        m01 = k.sb("m01", [64, 3, 8, 64], F32)
        nm = k.sb("nm", [64, 2, 8, 64], F32)
        id8 = k.sb("id8", [64, 8, 64], F32)
        ones64 = k.sb("ones64", [64, 64], F32)
        k.memset(ones64[:], 1.0)
        k.dma("sync", tri[:], C.I("c_tri")[:])
        k.dma("sync", m01[:], C.I("c_m01")[:])
        k.dma("sync", id8[:], C.I("c_ident8")[:])
        k.ts(nm[:], m01[:, 1:3], -1.0, ALU.mult)
        idn = C.ident[0:64, 0:64]
        St = k.sb("St", [64, 8, 64], F32)
        k.memset(St[:], 0.0)
        pps = Pool(k, "pp", [64, 8, 64], F32, 7, space="ps")
        psm = k.ps("psm", [64, 8, 2], F32)
        sbp = Pool(k, "w", [64, 8, 64], F32, 44)
        inv_sb = Pool(k, "iv", [64, 8, 64], F32, 8)
        inp_ = [Pool(k, f"in{i}", [64, 8, 64], F32, 2) for i in range(6)]
        smp = Pool(k, "sm", [64, 8], F32, 4)
        RWP = S["RWP"]
        slots = ((0, 0), (1, 1), (2, 2), (3, 4), (5, 6), (7, 8))
        for s in range(NCH):
            ch = (FORD[s], BORD[s])
            tl = [p.get() for p in inp_]
            for i, sl in enumerate(slots):
                for d in range(2):
                    r0 = ch[d] * 64
                    k.dma("sync", tl[i][:, d * 4:d * 4 + 4, :],
                          V(RWP.t[r0:r0 + 64, sl[d], :].rearrange("p (h d) -> p h d", h=4), [RWP.tok]))
            R8, V8, A8, LW8, K8, B8 = tl
            plc, ptot = pps.get(), pps.get()
            for d in range(2):
                k.mm(plc[:, d * 4:d * 4 + 4, :], tri[:, d, :], LW8[:, d * 4:d * 4 + 4, :])
            k.mm(ptot[:], ones64[:], LW8[:])
            for e in range(8):
                k.mm(psm[:, e, :], LW8[:, e, :], ones64[:, 0:2])
            lc, tot = sbp.get(), sbp.get()
            k.copy(lc[:], plc[:], eng="scalar")
            k.copy(tot[:], ptot[:], eng="vector")
            gCT = smp.get()
            k.act(gCT[:], psm[:, :, 0], AF.Exp)
            eg, egi, egp, ehat = sbp.get(), sbp.get(), sbp.get(), sbp.get()
            k.act(eg[:], lc[:], AF.Exp)
            k.act(egi[:], lc[:], AF.Exp, scale=-1.0)
            t = sbp.get()
            k.tt(t[:], lc[:], LW8[:], ALU.subtract, eng="gpsimd")
            k.act(egp[:], t[:], AF.Exp)
            t = sbp.get()
            k.tt(t[:], tot[:], lc[:], ALU.subtract)
            k.act(ehat[:], t[:], AF.Exp)
            At, Bt, Kt, Rt, Bh, Kh = [sbp.get() for _ in range(6)]
            k.tt(At[:], A8[:], egp[:], ALU.mult)
            k.tt(Bt[:], B8[:], egi[:], ALU.mult, eng="gpsimd")
            k.tt(Kt[:], K8[:], egi[:], ALU.mult)
            k.tt(Rt[:], R8[:], eg[:], ALU.mult, eng="gpsimd")
            k.tt(Bh[:], B8[:], ehat[:], ALU.mult)
            k.tt(Kh[:], K8[:], ehat[:], ALU.mult, eng="gpsimd")
            fmT = []
            for i, src in enumerate((At, Bt, Kt, Rt)):
                pT = pps.get()
                for e in range(8):
                    k.tr(pT[:, e, :], src[:, e, :], idn)
                dst = sbp.get()
                k.copy(dst[:], pT[:], eng=("scalar" if i % 2 == 0 else "vector"))
                fmT.append(dst)
            AtT, BtT, KtT, RtT = fmT
            def score(lhs, rhs, mask, eng):
                p_ = pps.get()
                for e in range(8):
                    k.mm(p_[:, e, :], lhs[:, e, :], rhs[:, e, :])
                o_ = sbp.get()
                k.tt(o_[:], p_[:], mask, ALU.mult, eng=eng)
                return o_
            M = score(AtT, BtT, nm[:, 1], "vector")
            A = score(BtT, AtT, nm[:, 0], "vector")
            AakT = score(KtT, AtT, m01[:, 1], "vector")
            ArbT = score(BtT, RtT, m01[:, 0], "vector")
            ArkT = score(KtT, RtT, m01[:, 0], "vector")
            X = tri_inverse(k, None, M, A, id8, pps, inv_sb)
            pW, pAkV = pps.get(), pps.get()
            for e in range(8):
                k.mm(pW[:, e, :], At[:, e, :], X[:, e, :])
            for e in range(8):
                k.mm(pAkV[:, e, :], AakT[:, e, :], V8[:, e, :])
            WT, AkV = sbp.get(), sbp.get()
            k.copy(WT[:], pW[:], eng="scalar")
            k.copy(AkV[:], pAkV[:], eng="vector")
            pUv = pps.get()
            for e in range(8):
                k.mm(pUv[:, e, :], X[:, e, :], AkV[:, e, :])
            Uv = sbp.get()
            k.copy(Uv[:], pUv[:], eng="scalar")
            pe = pps.get()
            for e in range(8):
                k.mm(pe[:, e, :], WT[:, e, :], St[:, e, :])
            E = sbp.get()
            k.tt(E[:], Uv[:], pe[:], ALU.add)
            py = pps.get()
            for e in range(8):
                k.mm(py[:, e, :], RtT[:, e, :], St[:, e, :], start=True, stop=False)
                k.mm(py[:, e, :], ArbT[:, e, :], E[:, e, :], start=False, stop=False)
                k.mm(py[:, e, :], ArkT[:, e, :], V8[:, e, :], start=False, stop=True)
            y8 = sbp.get()
            k.copy(y8[:], py[:], eng="scalar")
            for d in range(2):
                if need_ctx or ch[d] >= 4:
                    k.dma("sync", S["RWY"][d, ch[d] * 64:(ch[d] + 1) * 64, :], y8[:, d * 4:d * 4 + 4, :].re("p h d -> p (h d)"))
            pS = pps.get()
            for e in range(8):
                k.mm(pS[:, e, :], Bh[:, e, :], E[:, e, :], start=True, stop=False)
                k.mm(pS[:, e, :], Kh[:, e, :], V8[:, e, :], start=False, stop=True)
            t = sbp.get()
            k.tt(t[:], St[:], gCT[:, :, None].bc([64, 8, 64]), ALU.mult)
            k.tt(St[:], t[:], pS[:], ALU.add)


def rw_post_stage(k, C, l, need_ctx):
    S = C.S
    with k.stage():
        lnw = k.sb("lnw", [128, 256], F32)
        lnb = k.sb("lnb", [128, 256], F32)
        rkw = k.sb("rkw", [128, 256], F32)
        bc_load(k, lnw[:], C.I("rw_ln_w").t[l], C.I("rw_ln_w").tok)
        bc_load(k, lnb[:], C.I("rw_ln_b").t[l], C.I("rw_ln_b").tok)
        bc_load(k, rkw[:], C.I("rw_r_kT").t[l], C.I("rw_r_kT").tok)
        lneps = k.sb("lneps", [128, 1], F32)
        k.memset(lneps[:], 64e-5)
        yp = Pool(k, "y", [128, 2, 256], F32, 2)
        pp = Pool(k, "p", [128, 10, 256], F32, 2)
        tp = Pool(k, "t", [128, 256], F32, 8)
        smp = Pool(k, "sm", [128, 4], F32, 6)

        def hview(x):
            return x.re("p (h d) -> p h d", d=64)
        for tc in range(NT // 128):
            if tc < 2 and not need_ctx:
                continue
            r0 = tc * 128
            yy = yp.get()
            k.dma("sync", yy[:], V(S["RWY"].t[:, r0:r0 + 128, :].rearrange("d p c -> p d c"), [S["RWY"].tok]))
            pr = pp.get()
            k.dma("sync", pr[:], S["RWP"][r0:r0 + 128])
            y = tp.get()
            k.tt(y[:], yy[:, 0, :], yy[:, 1, :], ALU.add)
            s1 = smp.get()
            k.reduce(s1[:], hview(y[:]))
            k.ts(s1[:], s1[:], -1.0 / 64, ALU.mult)
            yc = tp.get()
            k.tt(hview(yc[:]), hview(y[:]), s1[:, :, None].bc([128, 4, 64]), ALU.add)
            sq = tp.get()
            k.tt(sq[:], yc[:], yc[:], ALU.mult, eng="gpsimd")
            s2 = smp.get()
            k.reduce(s2[:], hview(sq[:]))
            rstd = smp.get()
            k.act(rstd[:], s2[:], AF.Sqrt, scale=1.0 / 64, bias=lneps[:])
            k.recip(rstd[:], rstd[:])
            yn = tp.get()
            k.tt(hview(yn[:]), hview(yc[:]), rstd[:, :, None].bc([128, 4, 64]), ALU.mult)
            k.tt(yn[:], yn[:], lnw[:], ALU.mult, eng="gpsimd")
            k.tt(yn[:], yn[:], lnb[:], ALU.add, eng="gpsimd")
            ks = tp.get()
            k.tt(ks[:], pr[:, 5, :], pr[:, 6, :], ALU.add, eng="gpsimd")
            k.tt(ks[:], ks[:], pr[:, 0, :], ALU.mult, eng="gpsimd")
            k.tt(ks[:], ks[:], rkw[:], ALU.mult, eng="gpsimd")
            bs = smp.get()
            k.reduce(bs[:], hview(ks[:]))
            bon = tp.get()
            k.tt(hview(bon[:]), hview(pr[:, 1, :]), bs[:, :, None].bc([128, 4, 64]), ALU.mult)
            k.tt(yn[:], yn[:], bon[:], ALU.add)
            out = tp.get()
            k.tt(out[:], yn[:], pr[:, 9, :], ALU.mult)
            k.dma("sync", S["G"][r0:r0 + 128, 768:1024], out[:])


def rw_stage(k, C, l, need_ctx):
    rw_pre_stage(k, C, l)
    scan_stage(k, C, l, need_ctx, do_dn=False, do_rw=True)
    rw_post_stage(k, C, l, need_ctx)


def dnrw_stage(k, C, l, need_ctx):
    dn_pre_stage(k, C, l)
    rw_pre_stage(k, C, l)
    scan_stage(k, C, l, need_ctx)
    dn_post_stage(k, C, l, need_ctx)
    rw_post_stage(k, C, l, need_ctx)


def dn_scan_setup(k, C, need_ctx, sh):
    X = Lane()
    X.S = C.S
    X.need_ctx = need_ctx
    X.tri, X.id8, X.idn = sh.tri, sh.id8, sh.idn
    X.mneg = k.sb("mneg", [64, 2, 8, 64], F32)
    k.dma("sync", X.mneg[:], C.I("c_mneg")[:])
    X.St = k.sb("dSt", [64, 8, 64], F32)
    X.Stb = k.sb("dStb", [64, 8, 64], BF16)
    k.memset(X.St[:], 0.0)
    k.memset(X.Stb[:], 0.0)
    X.pps = Pool(k, "dpp", [64, 8, 64], F32, sh.npp_dn, space="ps")
    X.psm = sh.psm_dn
    X.sbp = Pool(k, "dw", [64, 8, 64], F32, 14)
    X.sbb = Pool(k, "dwb", [64, 8, 64], BF16, 9)
    X.inv_f = Pool(k, "divf", [64, 8, 64], F32, 8)
    X.inv_b = Pool(k, "divb", [64, 8, 64], BF16, 2)
    X.qkp = Pool(k, "qk8", [64, 2, 8, 64], F32, 2)
    X.v8p = Pool(k, "v8", [64, 8, 64], F32, 2)
    X.gbp = Pool(k, "gb8", [64, 2, 8], F32, 2)
    X.smp = Pool(k, "dsm", [64, 8], F32, 8)
    return X


def dn_step(k, X, s):
    S = X.S
    tri, mneg, id8, idn, St, Stb = X.tri, X.mneg, X.id8, X.idn, X.St, X.Stb
    pps, psm, sbp, sbb, smp = X.pps, X.psm, X.sbp, X.sbb, X.smp
    ch = (FORD[s], BORD[s])
    qk8, v8, gb8 = X.qkp.get(), X.v8p.get(), X.gbp.get()
    for d in range(2):
        r0 = ch[d] * 64
        es = slice(d * 4, d * 4 + 4)
        k.dma("sync", qk8[:, :, es, :], V(S["DNQKV"].t[r0:r0 + 64, 0:512].rearrange("p (a h d) -> p a h d", a=2, h=4), [S["DNQKV"].tok]))
        k.dma("sync", v8[:, es, :], V(S["DNQKV"].t[r0:r0 + 64, 512:768].rearrange("p (h d) -> p h d", h=4), [S["DNQKV"].tok]))
        k.dma("sync", gb8[:, 0, es], S["DNGB"][r0:r0 + 64, d * 4:d * 4 + 4])
        k.dma("sync", gb8[:, 1, es], S["DNGB"][r0:r0 + 64, 8 + d * 4:8 + d * 4 + 4])
    beta_bc = gb8[:, 1, :, None].bc([64, 8, 64])
    for d in range(2):
        k.mm(psm[:, d * 4:d * 4 + 4], tri[:, d, :], gb8[:, 0, d * 4:d * 4 + 4])
    gcc = smp.get()
    k.copy(gcc[:], psm[:], eng="scalar")
    grep = sbp.get()
    k.copy(grep[:], gb8[:, 0, :, None].bc([64, 8, 64]))
    pgcb = pps.get()
    for e in range(8):
        k.mm(pgcb[:, e, :], grep[:, e, :], tri[:, e // 4, :])
    gcb = sbp.get()
    k.copy(gcb[:], pgcb[:], eng="scalar")
    D1 = sbp.get()
    k.tt(D1[:], gcb[:], gcc[:, :, None].bc([64, 8, 64]), ALU.subtract)
    egcb = sbp.get()
    k.act(egcb[:], gcb[:], AF.Exp)
    t = sbp.get()
    k.tt(t[:], D1[:], mneg[:, 0], ALU.add, eng="gpsimd")
    E1 = sbp.get()
    k.act(E1[:], t[:], AF.Exp)
    t = sbp.get()
    k.stt(t[:], D1[:], -1.0, mneg[:, 1], ALU.mult, ALU.add)
    E2 = sbp.get()
    k.act(E2[:], t[:], AF.Exp)
    egc = smp.get()
    k.act(egc[:], gcc[:], AF.Exp)
    ekd = smp.get()
    k.act(ekd[:, 0:4], D1[:, 0:4, 63], AF.Exp)
    k.act(ekd[:, 4:8], D1[:, 4:8, 0], AF.Exp)
    pKT = pps.get()
    for e in range(8):
        k.tr(pKT[:, e, :], qk8[:, 1, e, :], idn)
    KT = sbb.get()
    k.copy(KT[:], pKT[:], eng="scalar")
    pQT = pps.get()
    for e in range(8):
        k.tr(pQT[:, e, :], qk8[:, 0, e, :], idn)
    QT = sbb.get()
    k.copy(QT[:], pQT[:], eng="vector")
    pG = pps.get()
    for e in range(8):
        k.mm(pG[:, e, :], KT[:, e, :], KT[:, e, :])
    t = sbp.get()
    k.tt(t[:], pG[:], E2[:], ALU.mult)
    M = sbp.get()
    k.tt(M[:], t[:], beta_bc, ALU.mult, eng="gpsimd")
    pQK = pps.get()
    for e in range(8):
        k.mm(pQK[:, e, :], KT[:, e, :], QT[:, e, :])
    attnT = sbb.get()
    k.tt(attnT[:], pQK[:], E1[:], ALU.mult)
    pA = pps.get()
    for e in range(8):
        k.tr(pA[:, e, :], M[:, e, :], idn)
    A = sbp.get()
    k.copy(A[:], pA[:], eng="scalar")
    _, Xi = tri_inverse(k, M, A, id8, pps, X.inv_f, X.inv_b)
    Vb = sbb.get()
    k.tt(Vb[:], v8[:], beta_bc, ALU.mult, eng="gpsimd")
    bg = smp.get()
    k.tt(bg[:], gb8[:, 1, :], egc[:], ALU.mult)
    KBG = sbb.get()
    k.tt(KBG[:], qk8[:, 1], bg[:, :, None].bc([64, 8, 64]), ALU.mult, eng="gpsimd")
    pU = pps.get()
    for e in range(8):
        k.mm(pU[:, e, :], Xi[:, e, :], Vb[:, e, :])
    U = sbp.get()
    k.copy(U[:], pU[:], eng="scalar")
    pW = pps.get()
    for e in range(8):
        k.mm(pW[:, e, :], KBG[:, e, :], Xi[:, e, :])
    WT = sbb.get()
    k.copy(WT[:], pW[:], eng="vector")
    QdT = sbb.get()
    k.tt(QdT[:], QT[:], egcb[:], ALU.mult, eng="gpsimd")
    Kd = sbb.get()
    k.tt(Kd[:], qk8[:, 1], ekd[:, :, None].bc([64, 8, 64]), ALU.mult, eng="gpsimd")
    glast = smp.get()
    k.copy(glast[:, 0:4], egcb[:, 0:4, 63], eng="gpsimd")
    k.copy(glast[:, 4:8], egcb[:, 4:8, 0], eng="gpsimd")
    pv = pps.get()
    for e in range(8):
        k.mm(pv[:, e, :], WT[:, e, :], Stb[:, e, :])
    vnew = sbb.get()
    k.tt(vnew[:], U[:], pv[:], ALU.subtract)
    po = pps.get()
    for e in range(8):
        k.mm(po[:, e, :], QdT[:, e, :], Stb[:, e, :], start=True, stop=False)
        k.mm(po[:, e, :], attnT[:, e, :], vnew[:, e, :], start=False, stop=True)
    pS = pps.get()
    for e in range(8):
        k.mm(pS[:, e, :], Kd[:, e, :], vnew[:, e, :])
    t = sbp.get()
    k.tt(t[:], St[:], glast[:, :, None].bc([64, 8, 64]), ALU.mult, eng="gpsimd")
    k.tt(St[:], t[:], pS[:], ALU.add)
    k.copy(Stb[:], St[:], eng="gpsimd")
    o8 = sbp.get()
    k.copy(o8[:], po[:], eng="scalar")
    for d in range(2):
        if X.need_ctx or ch[d] >= 4:
            k.dma("sync", S["DNO"][d, ch[d] * 64:(ch[d] + 1) * 64, :], o8[:, d * 4:d * 4 + 4, :].re("p h d -> p (h d)"))


def rw_scan_setup(k, C, need_ctx, sh):
    X = Lane()
    psl = sh.rw_psl
    X.psl = psl
    X.S = C.S
    X.need_ctx = need_ctx

    def const(name, shape, dtype=F32):
        full = [128] + list(shape[1:]) if psl.start else list(shape)
        return TV(k.sb(name, full, dtype), psl)
    X.tri = const("rtri", [64, 2, 64])
    X.id8 = const("rid8", [64, 8, 64])
    X.m01 = const("m01", [64, 3, 8, 64])
    X.nm = const("nm", [64, 2, 8, 64])
    X.ones64 = const("ones64", [64, 64])
    idf = const("ridf", [64, 64])
    X.idb = const("ridb", [64, 64], BF16)
    k.dma("sync", X.tri[:], C.I("c_tri")[:])
    k.dma("sync", X.id8[:], C.I("c_ident8")[:])
    k.dma("sync", X.m01[:], C.I("c_m01")[:])
    k.dma("sync", idf[:], C.I("identD")[0:64, 0:64])
    k.copy(X.idb[:], idf[:])
    k.memset(X.ones64[:], 1.0)
    k.ts(X.nm[:], X.m01[:, 1:3], -1.0, ALU.mult)
    X.St = const("rSt", [64, 8, 64])
    X.Stb = const("rStb", [64, 8, 64], BF16)
    k.memset(X.St[:], 0.0)
    k.memset(X.Stb[:], 0.0)
    X.pps = PoolV(k, "rpp", [64, 8, 64], F32, sh.npp_rw, psl, space="ps")
    X.ppb = PoolV(k, "rppb", [64, 16, 64], BF16, 1, psl, space="ps")
    X.psm = PoolV(k, "rpsm", [64, 8, 2], F32, 1, psl, space="ps").get()
    X.sbp = PoolV(k, "rw", [64, 8, 64], F32, 24, psl)
    X.sbb = PoolV(k, "rwb", [64, 8, 64], BF16, 9, psl)
    X.inv_f = PoolV(k, "rivf", [64, 8, 64], F32, 8, psl)
    X.inp_ = [PoolV(k, f"rin{i}", [64, 8, 64], F32, 2, psl) for i in range(6)]
    X.smp = PoolV(k, "rsm", [64, 8], F32, 4, psl)
    return X


RW_SLOTS = ((0, 0), (1, 1), (2, 2), (3, 4), (5, 6), (7, 8))


def rw_step(k, X, s):
    S = X.S
    tri, m01, nm, id8, idb, ones64, St, Stb = X.tri, X.m01, X.nm, X.id8, X.idb, X.ones64, X.St, X.Stb
    pps, ppb, psm, sbp, sbb, smp = X.pps, X.ppb, X.psm, X.sbp, X.sbb, X.smp
    RWP = S["RWP"]
    ch = (FORD[s], BORD[s])
    tl = [p.get() for p in X.inp_]
    for i, sl in enumerate(RW_SLOTS):
        for d in range(2):
            r0 = ch[d] * 64
            k.dma("sync", tl[i][:, d * 4:d * 4 + 4, :],
                  V(RWP.t[r0:r0 + 64, sl[d], :].rearrange("p (h d) -> p h d", h=4), [RWP.tok]))
    R8, V8, A8, LW8, K8, B8 = tl
    plc = pps.get()
    for d in range(2):
        k.mm(plc[:, d * 4:d * 4 + 4, :], tri[:, d, :], LW8[:, d * 4:d * 4 + 4, :])
    lc = sbp.get()
    k.copy(lc[:], plc[:], eng="scalar")
    ptot = pps.get()
    k.mm(ptot[:], ones64[:], LW8[:])
    tot = sbp.get()
    k.copy(tot[:], ptot[:], eng="vector")
    for e in range(8):
        k.mm(psm[:, e, :], LW8[:, e, :], ones64[:, 0:2])
    gCT = smp.get()
    k.act(gCT[:], psm[:, :, 0], AF.Exp)
    eg, egi, egp, ehat = sbp.get(), sbp.get(), sbp.get(), sbp.get()
    k.act(eg[:], lc[:], AF.Exp)
    k.act(egi[:], lc[:], AF.Exp, scale=-1.0)
    t = sbp.get()
    k.tt(t[:], lc[:], LW8[:], ALU.subtract, eng="gpsimd")
    k.act(egp[:], t[:], AF.Exp)
    t = sbp.get()
    k.tt(t[:], tot[:], lc[:], ALU.subtract)
    k.act(ehat[:], t[:], AF.Exp)
    At, Bh, Kh = sbp.get(), sbp.get(), sbp.get()
    Atb, Btb, Ktb, Rtb = sbb.get(), sbb.get(), sbb.get(), sbb.get()
    k.tt(At[:], A8[:], egp[:], ALU.mult)
    k.copy(Atb[:], At[:], eng="gpsimd")
    k.tt(Btb[:], B8[:], egi[:], ALU.mult, eng="gpsimd")
    k.tt(Ktb[:], K8[:], egi[:], ALU.mult)
    k.tt(Rtb[:], R8[:], eg[:], ALU.mult, eng="gpsimd")
    k.tt(Bh[:], B8[:], ehat[:], ALU.mult)
    k.tt(Kh[:], K8[:], ehat[:], ALU.mult, eng="gpsimd")
    fmT = []
    for i, src in enumerate((Atb, Btb, Ktb, Rtb)):
        pT = ppb.get()
        for e in range(8):
            k.tr(pT[:, e, :], src[:, e, :], idb[:])
        dst = sbb.get()
        k.copy(dst[:], pT[:, 0:8, :], eng=("scalar" if i % 2 == 0 else "vector"))
        fmT.append(dst)
    AtT, BtT, KtT, RtT = fmT

    def score(lhs, rhs, mask):
        p_ = pps.get()
        for e in range(8):
            k.mm(p_[:, e, :], lhs[:, e, :], rhs[:, e, :])
        o_ = sbp.get()
        k.tt(o_[:], p_[:], mask, ALU.mult)
        return o_
    M = score(AtT, BtT, nm[:, 1])
    A = score(BtT, AtT, nm[:, 0])
    AakT = score(KtT, AtT, m01[:, 1])
    ArbT = score(BtT, RtT, m01[:, 0])
    ArkT = score(KtT, RtT, m01[:, 0])
    Xi, _ = tri_inverse(k, M, A, id8, pps, X.inv_f, None)
    pW = pps.get()
    for e in range(8):
        k.mm(pW[:, e, :], At[:, e, :], Xi[:, e, :])
    WT = sbp.get()
    k.copy(WT[:], pW[:], eng="scalar")
    pAkV = pps.get()
    for e in range(8):
        k.mm(pAkV[:, e, :], AakT[:, e, :], V8[:, e, :])
    AkV = sbp.get()
    k.copy(AkV[:], pAkV[:], eng="vector")
    pUv = pps.get()
    for e in range(8):
        k.mm(pUv[:, e, :], Xi[:, e, :], AkV[:, e, :])
    Uv = sbp.get()
    k.copy(Uv[:], pUv[:], eng="scalar")
    pe = pps.get()
    for e in range(8):
        k.mm(pe[:, e, :], WT[:, e, :], St[:, e, :])
    E = sbp.get()
    k.tt(E[:], Uv[:], pe[:], ALU.add)
    py = pps.get()
    for e in range(8):
        k.mm(py[:, e, :], RtT[:, e, :], Stb[:, e, :], start=True, stop=False)
        k.mm(py[:, e, :], ArbT[:, e, :], E[:, e, :], start=False, stop=False)
        k.mm(py[:, e, :], ArkT[:, e, :], V8[:, e, :], start=False, stop=True)
    y8 = sbp.get()
    k.copy(y8[:], py[:], eng="scalar")
    for d in range(2):
        if X.need_ctx or ch[d] >= 4:
            k.dma("sync", S["RWY"][d, ch[d] * 64:(ch[d] + 1) * 64, :], y8[:, d * 4:d * 4 + 4, :].re("p h d -> p (h d)"))
    pS = pps.get()
    for e in range(8):
        k.mm(pS[:, e, :], Bh[:, e, :], E[:, e, :], start=True, stop=False)
        k.mm(pS[:, e, :], Kh[:, e, :], V8[:, e, :], start=False, stop=True)
    t = sbp.get()
    k.tt(t[:], St[:], gCT[:, :, None].bc([64, 8, 64]), ALU.mult, eng="gpsimd")
    k.tt(St[:], t[:], pS[:], ALU.add)
    k.copy(Stb[:], St[:], eng="gpsimd")


def scan_stage(k, C, l, need_ctx, do_dn=True, do_rw=True):
    with k.stage():
        sh = Lane()
        sh.tri = k.sb("tri", [64, 2, 64], F32)
        sh.id8 = k.sb("id8", [64, 8, 64], F32)
        k.dma("sync", sh.tri[:], C.I("c_tri")[:])
        k.dma("sync", sh.id8[:], C.I("c_ident8")[:])
        sh.idn = C.ident[0:64, 0:64]
        import os
        both = do_dn and do_rw
        sh.npp_dn = 3 if both else 7
        sh.npp_rw = 2 if both else 5
        sh.rw_psl = slice(64, 128) if os.environ.get("RW_HI", "1") == "1" else slice(0, 64)
        psm = k.ps("psm", [64, 8], F32)
        sh.psm_dn = psm[:]
        Xd = dn_scan_setup(k, C, need_ctx, sh) if do_dn else None
        Xr = rw_scan_setup(k, C, need_ctx, sh) if do_rw else None
        for s in range(NCH):
            lists = []
            if do_dn:
                with k.defer() as La:
                    dn_step(k, Xd, s)
                lists.append(La)
            if do_rw:
                with k.defer() as Lb:
                    rw_step(k, Xr, s)
                lists.append(Lb)
            k.replay(lists)
```

```python
import contextlib
import numpy as np
import concourse.bass as bass
import concourse.mybir as mybir
from concourse.bass_utils import run_bass_kernel_spmd

F32 = mybir.dt.float32
BF16 = mybir.dt.bfloat16
I32 = mybir.dt.int32
AF = mybir.ActivationFunctionType
ALU = mybir.AluOpType
AX = mybir.AxisListType

ENGS = ("tensor", "vector", "scalar", "gpsimd", "sync")
N_DSEM = 12
SEM_EPOCH = 1 << 40


class Tok:
    __slots__ = ("name", "w", "r")

    def __init__(self, name):
        self.name = name
        self.w = None
        self.r = []


class T:
    def __init__(self, k, name, t, space):
        self.k = k
        self.name = name
        self.t = t
        self.space = space
        self.tok = Tok(name)
        self.subs = {}

    def __getitem__(self, key):
        return V(self.t[key] if not isinstance(self.t, bass.AP) else self.t[key], [self.tok])

    def v(self):
        return self[:]

    def sub(self, key, idx):
        if key not in self.subs:
            self.subs[key] = Tok(f"{self.name}.{key}")
        return V(self.t[idx], [self.subs[key]])


class V:
    __slots__ = ("ap", "toks")

    def __init__(self, ap, toks):
        self.ap = ap
        self.toks = toks

    def __getitem__(self, key):
        return V(self.ap[key], self.toks)

    def re(self, s, **kw):
        return V(self.ap.rearrange(s, **kw), self.toks)

    def bc(self, shape):
        return V(self.ap.to_broadcast(shape), self.toks)

    def bitcast(self, dt):
        return V(self.ap.bitcast(dt), self.toks)

    def with_toks(self, toks):
        return V(self.ap, toks)


def _ap(x):
    return x.ap if isinstance(x, V) else x


class K:
    def __init__(self):
        self.nc = bass.Bass("TRN2", target_bir_lowering=False)
        self.stack = contextlib.ExitStack()
        self.q = {e: [] for e in ENGS}
        self.cnt = {e: 0 for e in ENGS}
        self.epoch = {e: 0 for e in ENGS}
        self.sems = {}
        self.seen = {e: {} for e in ENGS}
        self.dsem = {}
        self.dcnt = {}
        self.dq_i = {e: 0 for e in ENGS}
        self.uid = 0
        self.out_deps = []
        self.stage_stack = None

    def _name(self, n):
        self.uid += 1
        return f"{n}_{self.uid}"

    def sb(self, name, shape, dtype, stack=None):
        t = (stack or self.stage_stack or self.stack).enter_context(self.nc.sbuf_tensor(self._name(name), list(shape), dtype))
        return T(self, name, t, "sb")

    def ps(self, name, shape, dtype=F32, stack=None):
        t = (stack or self.stage_stack or self.stack).enter_context(self.nc.psum_tensor(self._name(name), list(shape), dtype))
        return T(self, name, t, "ps")

    def dram(self, name, shape, dtype, kind="Internal"):
        t = self.nc.dram_tensor(name, list(shape), dtype, kind=kind).ap()
        return T(self, name, t, "dram")

    def _sem(self, key):
        if key not in self.sems:
            self.sems[key] = self.stack.enter_context(self.nc.semaphore(self._name("s")))
        return self.sems[key]

    def _deps(self, reads, writes, pe_accum=False):
        deps = []
        for v in reads:
            for t in v.toks:
                if t.w is not None:
                    deps.append(t.w)
        for v in writes:
            for t in v.toks:
                if t.w is not None and not pe_accum:
                    deps.append(t.w)
                if not pe_accum:
                    deps.extend(t.r)
        return deps

    def _commit(self, me, reads, writes, pe_accum=False):
        for v in reads:
            for t in v.toks:
                t.r.append(me)
        for v in writes:
            for t in v.toks:
                t.w = me
                if not pe_accum:
                    t.r = []

    def _waits(self, eng, deps):
        need = {}
        for (sk, val) in deps:
            if self.seen[eng].get(sk, 0) >= val:
                continue
            if need.get(sk, 0) < val:
                need[sk] = val
        for sk, val in need.items():
            self.seen[eng][sk] = val
        return list(need.items())

    @contextlib.contextmanager
    def defer(self):
        prev = getattr(self, "_defer", None)
        lst = []
        self._defer = lst
        try:
            yield lst
        finally:
            self._defer = prev

    def replay(self, lists):
        lists = [l for l in lists if l]
        pos = [0] * len(lists)
        total = sum(len(l) for l in lists)
        last_pe = -1

        def is_pe(i):
            return pos[i] < len(lists[i]) and lists[i][pos[i]][0] == "op" and lists[i][pos[i]][1][0] == "tensor"
        for _ in range(total):
            live = [i for i in range(len(lists)) if pos[i] < len(lists[i])]
            pe = [i for i in live if is_pe(i)]
            if len(pe) >= 2:
                cand = [i for i in pe if i != last_pe]
                j = min(cand, key=lambda i: pos[i] / len(lists[i]))
            else:
                j = min(live, key=lambda i: pos[i] / len(lists[i]))
            kind, a, kw = lists[j][pos[j]]
            if kind == "op" and a[0] == "tensor":
                last_pe = j
            pos[j] += 1
            getattr(self, kind)(*a, **kw)

    def op(self, eng, fn, reads=(), writes=(), pe_accum=False, same_eng_sync=True):
        if getattr(self, "_defer", None) is not None:
            self._defer.append(("op", (eng, fn), dict(reads=reads, writes=writes, pe_accum=pe_accum, same_eng_sync=same_eng_sync)))
            return None
        reads = [r for r in reads if isinstance(r, V)]
        writes = [w for w in writes if isinstance(w, V)]
        deps = self._deps(reads, writes, pe_accum)
        if eng == "tensor" or not same_eng_sync:
            deps = [d for d in deps if d[0][0] != eng or d[0][0] == "dma"]
        if self.cnt[eng] >= SEM_EPOCH:
            self.epoch[eng] += 1
            self.cnt[eng] = 0
        sk = (eng, self.epoch[eng])
        self._sem(sk)
        self.cnt[eng] += 1
        me = (sk, self.cnt[eng])
        waits = self._waits(eng, deps)
        self.q[eng].append((fn, waits, (sk, 1), self.cnt[eng]))
        self._commit(me, reads, writes, pe_accum)
        return me

    def dma(self, eng, out, in_, **kw):
        if getattr(self, "_defer", None) is not None:
            self._defer.append(("dma", (eng, out, in_), dict(kw)))
            return None
        deps = self._deps([in_], [out])
        i = self.dq_i[eng]
        self.dq_i[eng] += 1
        sk = ("dma", eng, i % N_DSEM)
        self._sem(sk)
        prev = self.dcnt.get(sk, 0)
        if prev:
            deps.append((sk, prev))
        self.dcnt[sk] = prev + 16
        me = (sk, prev + 16)
        waits = self._waits(eng, deps)
        o, a = _ap(out), _ap(in_)
        self.q[eng].append((lambda e: e.dma_start(out=o, in_=a, **kw), waits, (sk, 16), None))
        self._commit(me, [in_], [out])
        return me

    def barrier(self):
        alld = []
        for e in ENGS:
            for ep in range(self.epoch[e] + 1):
                sk = (e, ep)
                if sk in self.sems:
                    alld.append((sk, self.cnt[e] if ep == self.epoch[e] else SEM_EPOCH))
        for sk, v in self.dcnt.items():
            alld.append((sk, v))
        for e in ENGS:
            w = self._waits(e, alld)
            if w:
                self.q[e].append((None, w, None, None))

    def flush(self):
        import bisect
        nc = self.nc
        q = self.q
        sems = self.sems
        if not hasattr(self, "base_idx"):
            self.base_idx, self.base_val = {}, {}
        targets = {}
        for ename in ENGS:
            for (fn, waits, inc, idx) in q[ename]:
                for (sk, val) in waits:
                    if sk[0] != "dma":
                        targets.setdefault(sk, set()).add(val)
        for ename in ENGS:
            sk = (ename, self.epoch[ename])
            if sk in sems:
                targets.setdefault(sk, set()).add(self.cnt[ename])
        tl = {}
        for sk, st in targets.items():
            b = self.base_idx.get(sk, 0)
            tl[sk] = sorted(v for v in st if v > b)

        def val_of(sk, v):
            b = self.base_idx.get(sk, 0)
            bv = self.base_val.get(sk, 0)
            if v <= b:
                return bv
            return bv + bisect.bisect_left(tl[sk], v) + 1

        with nc.Block() as block:
            for ename in ENGS:
                ops = q[ename]
                if not ops:
                    continue

                def body(e, ops=ops):
                    for (fn, waits, inc, idx) in ops:
                        for (sk, val) in waits:
                            if sk[0] == "dma":
                                e.wait_ge(sems[sk], val)
                            else:
                                e.wait_ge(sems[sk], val_of(sk, val))
                        if fn is not None:
                            ins = fn(e)
                            if inc is not None:
                                if inc[0][0] == "dma":
                                    ins.then_inc(sems[inc[0]], inc[1])
                                else:
                                    lst = tl.get(inc[0], ())
                                    j = bisect.bisect_left(lst, idx)
                                    if j < len(lst) and lst[j] == idx:
                                        ins.then_inc(sems[inc[0]], 1)
                getattr(block, ename)(body)
        for sk, lst in tl.items():
            self.base_val[sk] = self.base_val.get(sk, 0) + len(lst)
            if sk[0] != "dma":
                self.base_idx[sk] = self.cnt[sk[0]]
        self.q = {e: [] for e in ENGS}

    @contextlib.contextmanager
    def stage(self, name=None):
        st = contextlib.ExitStack()
        prev = self.stage_stack
        self.stage_stack = st
        self.stage_i = getattr(self, "stage_i", 0) + 1
        with st:
            yield st
            self.barrier()
            if getattr(self, "scopes", False):
                import inspect
                nm = name or inspect.stack()[2].function
                with self.nc.named_scope(f"s{self.stage_i:03d}_{nm}"):
                    self.flush()
            else:
                self.flush()
        self.stage_stack = prev

    def mm(self, out, lhsT, rhs, start=True, stop=True, **kw):
        o, l, r = _ap(out), _ap(lhsT), _ap(rhs)
        return self.op("tensor", lambda e: e.matmul(o, l, r, start=start, stop=stop, **kw),
                       reads=[lhsT, rhs], writes=[out], pe_accum=not start)

    def tr(self, out, in_, ident):
        o, i, d = _ap(out), _ap(in_), _ap(ident)
        return self.op("tensor", lambda e: e.transpose(o, i, d), reads=[in_, ident], writes=[out])

    def act(self, out, in_, func, bias=None, scale=None, accum_out=None, eng="scalar"):
        o, i = _ap(out), _ap(in_)
        kw = {}
        rd = [in_]
        if bias is not None:
            kw["bias"] = _ap(bias)
            rd.append(bias)
        if scale is not None:
            kw["scale"] = _ap(scale)
            rd.append(scale)
        wr = [out]
        if accum_out is not None:
            kw["accum_out"] = _ap(accum_out)
            wr.append(accum_out)
        return self.op("scalar", lambda e: e.activation(o, i, func, **kw), reads=rd, writes=wr)

    def tt(self, out, in0, in1, op, eng="vector"):
        o, a, b = _ap(out), _ap(in0), _ap(in1)
        return self.op(eng, lambda e: e.tensor_tensor(o, a, b, op), reads=[in0, in1], writes=[out])

    def ts(self, out, in0, s1, op0, s2=None, op1=None, eng="vector", accum_out=None):
        o, a = _ap(out), _ap(in0)
        x1, x2 = _ap(s1), _ap(s2)
        kw = {}
        if op1 is not None:
            kw["op1"] = op1
        wr = [out]
        if accum_out is not None:
            kw["accum_out"] = _ap(accum_out)
            wr.append(accum_out)
        return self.op(eng, lambda e: e.tensor_scalar(o, a, x1, x2, op0, **kw), reads=[in0, s1, s2], writes=wr)

    def stt(self, out, in0, scalar, in1, op0, op1):
        o, a, s, b = _ap(out), _ap(in0), _ap(scalar), _ap(in1)
        return self.op("vector", lambda e: e.scalar_tensor_tensor(o, a, s, b, op0, op1), reads=[in0, scalar, in1], writes=[out])

    def copy(self, out, in_, eng="vector"):
        o, i = _ap(out), _ap(in_)
        if eng == "scalar":
            return self.op("scalar", lambda e: e.copy(o, i), reads=[in_], writes=[out])
        return self.op(eng, lambda e: e.tensor_copy(o, i), reads=[in_], writes=[out])

    def memset(self, out, val, eng="vector"):
        o = _ap(out)
        return self.op(eng, lambda e: e.memset(o, val), reads=[], writes=[out])

    def recip(self, out, in_):
        o, i = _ap(out), _ap(in_)
        return self.op("vector", lambda e: e.reciprocal(o, i), reads=[in_], writes=[out])

    def reduce(self, out, in_, op=None, axis=None, **kw):
        o, i = _ap(out), _ap(in_)
        op = op or ALU.add
        axis = axis or AX.X
        return self.op("vector", lambda e: e.tensor_reduce(o, i, axis, op, **kw), reads=[in_], writes=[out])

D = 1024
SEQ = 2048
CTX = 256
NT = SEQ + CTX
DFF = 2816
NL = 2
EPS = 1e-6
TILES = [(0, 256, 1)] + [(256 + 512 * i, 512, 0) for i in range(4)]
QS = [(0, 6), (6, 6), (12, 5), (17, 5)]


class Pool:
    def __init__(self, k, name, shape, dtype, n, space="sb", stack=None):
        mk = k.sb if space == "sb" else k.ps
        self.bufs = [mk(f"{name}{i}", shape, dtype, stack=stack) for i in range(n)]
        self.i = 0

    def get(self):
        b = self.bufs[self.i % len(self.bufs)]
        self.i += 1
        return b


class Ctx:
    pass


class TV:
    def __init__(self, t, psl):
        self.t, self.psl = t, psl
        self.tok = t.tok

    def __getitem__(self, key):
        if not isinstance(key, tuple):
            key = (key,)
        assert key[0] == slice(None), key
        return V(self.t.t[(self.psl,) + tuple(key[1:])], [self.t.tok])


class PoolV:
    def __init__(self, k, name, shape, dtype, n, psl, space="sb"):
        mk = k.sb if space == "sb" else k.ps
        full = [128] + list(shape[1:]) if psl.start else list(shape)
        self.bufs = [TV(mk(f"{name}{i}", full, dtype), psl) for i in range(n)]
        self.i = 0

    def get(self):
        b = self.bufs[self.i % len(self.bufs)]
        self.i += 1
        return b


def xt_view(C, t0, n):
    toks = [C.XT.subtok(i) for i, (s, sz, w) in enumerate(TILES) if s < t0 + n and t0 < s + sz]
    ap = C.XT.t.rearrange("(c p) t -> p c t", p=128)[:, :, t0:t0 + n]
    return V(ap, toks)


IN_SHAPES = {
    "xT": [D, NT], "cT": [128, 8, 2], "b_modT": [NL, 128, 72], "norm_wT": [NL, 128, 24],
    "w_mod": [NL, D, 9 * D], "ffn_w_gu": [NL, 2, D, 2 * DFF], "ffn_w_down": [NL, 2, DFF, D],
    "w_in": [NL, D, 3472], "w_out": [NL, D, D], "identD": [128, 128],
}


class LazyIn:
    def __init__(self, k, C):
        self.k, self.C, self.d = k, C, {}

    def __call__(self, name):
        if name not in self.d:
            self.d[name] = self.k.dram(name, IN_SHAPES[name], IN_DTYPES.get(name, F32), kind="ExternalInput")
        return self.d[name]


def declare_io(k, C, debug_out=()):
    C.I = LazyIn(k, C)
    C.XT = C.I("xT")
    C.XT.subtok = lambda i: C.XT.subs.setdefault(i, Tok(f"XT.{i}"))
    declare_scratch(k, C)
    if debug_out == "gin":
        IN_SHAPES["g_in"] = [NT, 1024]
        C.S["G"] = C.I("g_in")
    C.OUT = k.dram("outT", [D, SEQ], F32, kind="ExternalOutput")
    C.DBG = k.dram("dbgT", [D, NT], F32, kind="ExternalOutput") if debug_out else None


def setup_consts(k, C):
    C.ones_bf = k.sb("ones_bf", [128, 128], BF16, stack=k.stack)
    k.memset(C.ones_bf[:], 1.0)
    C.eps_t = k.sb("eps_t", [128, 1], F32, stack=k.stack)
    k.memset(C.eps_t[:], EPS)
    C.ident = k.sb("ident", [128, 128], F32, stack=k.stack)
    k.dma("sync", C.ident[:], C.I("identD")[:])
    C.P = [k.sb(f"P{l}", [128, 9, 8, 2], F32, stack=k.stack) for l in range(NL)]


def mod_stage(k, C):
    with k.stage():
        ct = k.sb("ct", [128, 8, 2], F32)
        sc = k.sb("sc", [128, 8, 2], F32)
        k.dma("sync", ct[:], C.I("cT")[:])
        k.act(sc[:], ct[:], AF.Silu)
        wpool = Pool(k, "wm", [128, 8, 512], F32, 3)
        for l in range(NL):
            bm = k.sb(f"bm{l}", [128, 72], F32)
            nw = k.sb(f"nw{l}", [128, 24], F32)
            k.dma("sync", bm[:], C.I("b_modT")[l])
            k.dma("sync", nw[:], C.I("norm_wT")[l])
            pm = k.ps(f"pm{l}", [128, 72, 2], F32)
            wsrc = C.I("w_mod").t[l].rearrange("(kc p) n -> p kc n", p=128)
            for g in range(18):
                wm = wpool.get()
                k.dma("sync", wm[:], V(wsrc[:, :, g * 512:(g + 1) * 512], [C.I("w_mod").tok]))
                for c4 in range(4):
                    ci = g * 4 + c4
                    for kc in range(8):
                        k.mm(pm[:, ci, :], wm[:, kc, c4 * 128:(c4 + 1) * 128], sc[:, kc, :],
                             start=(kc == 0), stop=(kc == 7))
            P = C.P[l]
            Pv = P[:].re("p a c w -> p (a c) w")
            k.tt(Pv, pm[:], bm[:, :, None].bc([128, 72, 2]), ALU.add)
            for s in range(3):
                k.stt(P[:, 3 * s + 1], P[:, 3 * s + 1], 1.0,
                      nw[:, s * 8:(s + 1) * 8, None].bc([128, 8, 2]), ALU.add, ALU.mult)
                if s != 1:
                    k.ts(P[:, 3 * s + 2], P[:, 3 * s + 2], 0.5, ALU.mult)


def hv_(hall, ti, c, t0, n):
    return hall.sub(ti, (slice(None), c, slice(t0, t0 + n)))


def norm_pass(k, C, P, s, hall, xpool, sq, pss, rs, tmpp, skip_ctx=False, only=None):
    for ti, (t0, n, w) in enumerate(TILES):
        if w == 1 and skip_ctx:
            continue
        if only is not None and ti != only:
            continue
        xt = xpool.get()
        k.dma("sync", xt[:, :, :n], xt_view(C, t0, n))
        k.act(sq[:, :, :n], xt[:, :, :n], AF.Square)
        for c in range(8):
            k.mm(pss[:, :n], C.ones_bf[:], sq[:, c, :n], start=(c == 0), stop=(c == 7))
        k.act(rs[:, :n], pss[:, :n], AF.Sqrt, scale=1.0 / D, bias=C.eps_t[:])
        k.recip(rs[:, :n], rs[:, :n])
        for c in range(8):
            tmp = tmpp.get()
            k.stt(tmp[:, :n], xt[:, c, :n], P[:, 3 * s + 1, c, w:w + 1], rs[:, :n], ALU.mult, ALU.mult)
            k.act(hv_(hall, ti, c, t0, n), tmp[:, :n], AF.Identity, bias=P[:, 3 * s, c, w:w + 1])


def ffn_stage(k, C, l, f, s, skip_ctx=False):
    P = C.P[l]
    Wgu = C.I("ffn_w_gu").t[l, f].rearrange("(kc p) n -> p kc n", p=128)
    Wdn = C.I("ffn_w_down").t[l, f].rearrange("(j p) n -> p j n", p=128)
    with k.stage():
        hall = k.sb("hall", [128, 8, NT], BF16)
        xpool = Pool(k, "xt", [128, 8, 512], F32, 2)
        sq = k.sb("sq", [128, 8, 512], BF16)
        tmpp = Pool(k, "tmp", [128, 512], F32, 2)
        rs = k.sb("rs", [128, 512], F32)
        pss = k.ps("pss", [128, 512], F32)
        wgp = Pool(k, "wg", [128, 8, 2, 6 * 128], BF16, 2)
        wdp = Pool(k, "wd", [128, 6, D], BF16, 2)
        actp = Pool(k, "act", [128, 6, 512], BF16, 2)
        sgp = Pool(k, "sg", [128, 512], F32, 2)
        pgp = Pool(k, "pg", [128, 512], F32, 2, space="ps")
        pup = Pool(k, "pu", [128, 512], F32, 2, space="ps")
        pdp = Pool(k, "pd", [128, 512], F32, 2, space="ps")

        def hv(ti, c, t0, n):
            return hall.sub(ti, (slice(None), c, slice(t0, t0 + n)))

        norm_pass(k, C, P, s, hall, xpool, sq, pss, rs, tmpp, skip_ctx)
        for qi, (j0, nj) in enumerate(QS):
            wg = wgp.get()
            wd = wdp.get()
            k.dma("gpsimd", wg[:, :, 0, :nj * 128], V(Wgu[:, :, j0 * 128:(j0 + nj) * 128], [C.I("ffn_w_gu").tok]))
            k.dma("gpsimd", wg[:, :, 1, :nj * 128], V(Wgu[:, :, DFF + j0 * 128:DFF + (j0 + nj) * 128], [C.I("ffn_w_gu").tok]))
            k.dma("gpsimd", wd[:, :nj, :], V(Wdn[:, j0:j0 + nj, :], [C.I("ffn_w_down").tok]))
            for ti, (t0, n, w) in enumerate(TILES):
                if w == 1 and skip_ctx:
                    continue
                act = actp.get()
                for jj in range(nj):
                    pg = pgp.get()
                    pu = pup.get()
                    for kc in range(8):
                        k.mm(pg[:, :n], wg[:, kc, 0, jj * 128:(jj + 1) * 128], hv(ti, kc, t0, n),
                             start=(kc == 0), stop=(kc == 7))
                    for kc in range(8):
                        k.mm(pu[:, :n], wg[:, kc, 1, jj * 128:(jj + 1) * 128], hv(ti, kc, t0, n),
                             start=(kc == 0), stop=(kc == 7))
                    sg = sgp.get()
                    k.act(sg[:, :n], pg[:, :n], AF.Silu)
                    k.tt(act[:, jj, :n], sg[:, :n], pu[:, :n], ALU.mult)
                xr = xpool.get()
                k.dma("sync", xr[:, :, :n], xt_view(C, t0, n))
                for m in range(8):
                    pd = pdp.get()
                    for jj in range(nj):
                        k.mm(pd[:, :n], wd[:, jj, m * 128:(m + 1) * 128], act[:, jj, :n],
                             start=(jj == 0), stop=(jj == nj - 1))
                    k.stt(xr[:, m, :n], pd[:, :n], P[:, 3 * s + 2, m, w:w + 1], xr[:, m, :n], ALU.mult, ALU.add)
                k.dma("sync", xt_view(C, t0, n), xr[:, :, :n])


PT = 3472
TM_GROUPS = [
    (0, 512, "HY", 0), (512, 256, "HY", 512),
    (1280, 256, "NAV", 0),
    (1536, 512, "DN", 0), (2048, 512, "DN", 512), (2560, 16, "DN", 1024),
    (2576, 512, "RW", 0), (3088, 256, "RW", 512),
]
FM_CHUNKS = [(768, 128, "NAQK", 0), (896, 128, "NAQK", 1), (1024, 128, "NAQK", 2), (1152, 128, "NAQK", 3),
             (3344, 32, "RWLR", 0), (3376, 32, "RWLR", 1), (3408, 64, "RWLR", 2)]


def declare_scratch(k, C):
    def mk(name, shape):
        kind = "ExternalOutput" if name in C.dbg_names else "Internal"
        return k.dram("s_" + name.lower(), shape, F32, kind=kind)
    C.S = {
        "HY": mk("HY", [NT, 768]),
        "NAV": mk("NAV", [NT, 256]),
        "DN": mk("DN", [NT, 1040]),
        "RW": mk("RW", [NT, 768]),
        "NAQK": mk("NAQK", [512, NT]),
        "RWLR": mk("RWLR", [3, 64, NT]),
        "RWP": mk("RWP", [NT, 10, 256]),
        "RWY": mk("RWY", [2, NT, 256]),
        "G": mk("G", [NT, 1024]),
        "DNQKV": mk("DNQKV", [NT, 768]),
        "DNGB": mk("DNGB", [NT, 16]),
        "DNO": mk("DNO", [2, NT, 256]),
    }


def proj_stage(k, C, l):
    P = C.P[l]
    Win = C.I("w_in").t[l].rearrange("(kc p) n -> p kc n", p=128)
    with k.stage():
        hall = k.sb("hall", [128, 8, NT], BF16)
        xpool = Pool(k, "xt", [128, 8, 512], F32, 2)
        sq = k.sb("sq", [128, 8, 512], BF16)
        tmpp = Pool(k, "tmp", [128, 512], F32, 2)
        rs = k.sb("rs", [128, 512], F32)
        pss = k.ps("pss", [128, 512], F32)
        win = k.sb("win", [128, 8, PT], BF16)
        for kc in range(8):
            k.dma("gpsimd", win[:, kc, :], V(Win[:, kc, :], [C.I("w_in").tok]))
        norm_pass(k, C, P, 1, hall, xpool, sq, pss, rs, tmpp)
        pp = Pool(k, "pp", [128, 512], F32, 4, space="ps")
        rowp = Pool(k, "row", [128, 2832], F32, 2)
        fmp = Pool(k, "fm", [128, 7, 512], F32, 2)
        ev = 0
        for tc in range(NT // 128):
            t0 = tc * 128
            ti = 0 if t0 < 256 else 1 + (t0 - 256) // 512
            row = rowp.get()
            off = 0
            offs = []
            for (c0, nc_, name, dcol) in TM_GROUPS:
                ps = pp.get()
                for kc in range(8):
                    k.mm(ps[:, :nc_], hv_(hall, ti, kc, t0, 128), win[:, kc, c0:c0 + nc_],
                         start=(kc == 0), stop=(kc == 7))
                k.copy(row[:, off:off + nc_], ps[:, :nc_], eng=("scalar" if ev % 2 else "vector"))
                ev += 1
                offs.append((off, nc_, name, dcol))
                off += nc_
            for name in ("HY", "NAV", "DN", "RW"):
                gs = [g for g in offs if g[2] == name]
                o0 = gs[0][0]
                tot = sum(g[1] for g in gs)
                k.dma("sync", C.S[name][t0:t0 + 128, 0:tot], row[:, o0:o0 + tot])
        for ti, (t0, n, w) in enumerate(TILES):
            fm = fmp.get()
            for i, (c0, ncl, name, di) in enumerate(FM_CHUNKS):
                ps = pp.get()
                for kc in range(8):
                    k.mm(ps[0:ncl, :n], win[:, kc, c0:c0 + ncl], hv_(hall, ti, kc, t0, n),
                         start=(kc == 0), stop=(kc == 7))
                k.copy(fm[0:ncl, i, :n], ps[0:ncl, :n], eng=("scalar" if ev % 2 else "vector"))
                ev += 1
            k.dma("sync", V(C.S["NAQK"].t.rearrange("(c p) t -> p c t", p=128)[:, :, t0:t0 + n], [C.S["NAQK"].tok]),
                  fm[:, 0:4, :n])
            for g_, ncl in ((0, 32), (1, 32), (2, 64)):
                k.dma("sync", C.S["RWLR"][g_, 0:ncl, t0:t0 + n], fm[0:ncl, 4 + g_, :n])


def outproj_stage(k, C, l, need_ctx):
    P = C.P[l]
    Wout = C.I("w_out").t[l].rearrange("(kc p) n -> p kc n", p=128)
    with k.stage():
        wo = k.sb("wo", [128, 8, D], BF16)
        k.dma("gpsimd", wo[:], V(Wout, [C.I("w_out").tok]))
        gpool = Pool(k, "gt", [128, D], F32, 3)
        gT = Pool(k, "gT", [128, 8, 512], BF16, 2)
        xpool = Pool(k, "xt", [128, 8, 512], F32, 2)
        ptp = Pool(k, "ptp", [128, 4, 128], F32, 2, space="ps")
        pyp = Pool(k, "py", [128, 512], F32, 2, space="ps")
        ev = 0
        for ti, (t0, n, w) in enumerate(TILES):
            if w == 1 and not need_ctx:
                continue
            g = gT.get()
            for sc_ in range(n // 128):
                gt = gpool.get()
                k.dma("sync", gt[:], C.S["G"][t0 + sc_ * 128:t0 + (sc_ + 1) * 128, :])
                for half in range(2):
                    pt = ptp.get()
                    for q in range(4):
                        c = half * 4 + q
                        k.tr(pt[:, q, :], gt[:, c * 128:(c + 1) * 128], C.ident[:])
                    k.copy(g[:, half * 4:(half + 1) * 4, sc_ * 128:(sc_ + 1) * 128], pt[:],
                           eng=("scalar" if ev % 2 else "vector"))
                    ev += 1
            xr = xpool.get()
            k.dma("sync", xr[:, :, :n], xt_view(C, t0, n))
            for m in range(8):
                py = pyp.get()
                for kc in range(8):
                    k.mm(py[:, :n], wo[:, kc, m * 128:(m + 1) * 128], g[:, kc, :n], start=(kc == 0), stop=(kc == 7))
                k.stt(xr[:, m, :n], py[:, :n], P[:, 5, m, w:w + 1], xr[:, m, :n], ALU.mult, ALU.add)
            k.dma("gpsimd", xt_view(C, t0, n), xr[:, :, :n])


def out_stage(k, C):
    with k.stage():
        xpool = Pool(k, "xo", [128, 8, 512], F32, 2)
        last = []
        for i in range(4):
            xo = xpool.get()
            k.dma("sync", xo[:], xt_view(C, 256 + 512 * i, 512))
            dst = V(C.OUT.t.rearrange("(c p) t -> p c t", p=128)[:, :, 512 * i:512 * (i + 1)], [C.OUT.tok])
            k.dma("sync", dst, xo[:])
        if C.DBG is not None:
            for ti, (t0, n, w) in enumerate(TILES):
                xo = xpool.get()
                k.dma("sync", xo[:, :, :n], xt_view(C, t0, n))
                dst = V(C.DBG.t.rearrange("(c p) t -> p c t", p=128)[:, :, t0:t0 + n], [C.DBG.tok])
                k.dma("sync", dst, xo[:, :, :n])


def build(plan=None, dbg_names=(), scopes=False):
    k = K()
    k.scopes = scopes
    C = Ctx()
    C.dbg_names = set(dbg_names)
    declare_io(k, C, debug_out=(False if plan in (None, 'full') else ("gin" if plan == "m3" else True)))
    with k.stage():
        setup_consts(k, C)
    mod_stage(k, C)
    plan = plan or "full"
    if plan == "full":
        for l in range(NL):
            need_ctx = l < NL - 1
            ffn_stage(k, C, l, 0, 0)
            proj_stage(k, C, l)
            for nm in ("hy", "na", "dnrw"):
                globals()[nm + "_stage"](k, C, l, need_ctx)
            outproj_stage(k, C, l, need_ctx)
            ffn_stage(k, C, l, 1, 2, skip_ctx=not need_ctx)
    if plan == "ffn1":
        ffn_stage(k, C, 0, 0, 0)
    if plan == "m1":
        proj_stage(k, C, 0)
    if plan == "m3":
        outproj_stage(k, C, 0, True)
    if plan.startswith("mix:"):
        proj_stage(k, C, 0)
        for nm in plan[4:].split(","):
            globals()[nm + "_stage"](k, C, 0, True)
    out_stage(k, C)
    return k, C


def host_prep(inp, b, used):
    f = np.float32
    m = {}
    for name in used:
        if name == "xT":
            v = np.concatenate([inp["ctx"][b], inp["x"][b]], axis=0).T
        elif name == "cT":
            v = np.stack([inp["c"][b].reshape(8, 128).T, inp["c_ctx"].reshape(8, 128).T], axis=-1)
        elif name == "b_modT":
            v = inp["b_mod"].reshape(NL, 72, 128).transpose(0, 2, 1)
        elif name == "norm_wT":
            v = inp["norm_w"].reshape(NL, 24, 128).transpose(0, 2, 1)
        elif name == "identD":
            v = np.eye(128, dtype=f)
        elif name in HOST_LAYOUT:
            v = HOST_LAYOUT[name](inp, b)
        else:
            v = inp[name]
        m[name] = np.ascontiguousarray(np.asarray(v, dtype=(ml_dtypes.bfloat16 if name in IN_DTYPES else f)))
    return m


HOST_LAYOUT = {}


def run(inputs, plan=None, dbg_names=(), extra=None, ncores=8):
    k, C = build(plan, dbg_names)
    extra = extra or {}
    used = [n for n in C.I.d.keys() if n not in extra]
    in_maps = [host_prep(inputs, b, used) for b in range(ncores)]
    for m in in_maps:
        m.update(extra)
    res = run_bass_kernel_spmd(k.nc, in_maps, core_ids=list(range(ncores)))
    return res


def kernel(**inputs):
    inputs = {kk: np.asarray(v) for kk, v in inputs.items()}
    res = run(inputs)
    out = np.stack([np.ascontiguousarray(r["outT"].T) for r in res.results], axis=0)
    return out.astype(np.float32)


IN_SHAPES["na_nT"] = [NL, 128, 2]
IN_SHAPES["na_biasT"] = [NL, 128, 8, 4, 4, 64]


def _na_nT(inp, b):
    return np.stack([np.tile(inp["na_q_norm"], (1, 2)), np.tile(inp["na_k_norm"], (1, 2))], axis=-1)


def _na_biasT(inp, b):
    a = np.arange(2)[:, None, None, None, None]
    kcol = np.arange(64)[None, :, None, None, None]
    rho = np.arange(8)[None, None, :, None, None]
    i = np.arange(4)[None, None, None, :, None]
    qcol = np.arange(64)[None, None, None, None, :]
    drow = 2 * i + a - rho + 0 * kcol + 0 * qcol
    dcol = np.clip(kcol - qcol, -15, 15) + 0 * drow
    cs = np.clip(qcol - 8, 0, 48)
    inwin = ((kcol >= cs) & (kcol < cs + 16)) & (drow > -100)
    rpb = inp["na_rpb"]
    g = rpb[:, :, drow + 7, dcol + 15]
    g = np.where(inwin[None, None], g, np.float32(-30000.0))
    g = g.transpose(0, 2, 3, 4, 1, 5, 6).reshape(NL, 128, 8, 4, 4, 64)
    return g


HOST_LAYOUT["na_nT"] = _na_nT
HOST_LAYOUT["na_biasT"] = _na_biasT


def na_stage(k, C, l, need_ctx):
    with contextlib.ExitStack() as outer:
        na_stage_(k, C, l, need_ctx, outer)


def na_stage_(k, C, l, need_ctx, outer):
    E = k.sb("E", [128, 8, 4, 4, 64], BF16, stack=outer)
    QK = k.sb("QK", [128, 4, NT], BF16, stack=outer)
    QM = k.sb("QM", [128, 2, 2, NT], BF16, stack=outer)
    Ve = k.sb("Ve", [128, 18, 4, 65], BF16, stack=outer)
    Vo = k.sb("Vo", [128, 17, 4, 65], BF16, stack=outer)
    with k.stage():
        bd = k.sb("bd", [128, 128], BF16)
        k.memset(bd[:], 0.0)
        k.memset(bd[0:64, 0:64], 1.0)
        k.memset(bd[64:128, 64:128], 1.0)
        nw = k.sb("nw", [128, 2], F32)
        k.dma("sync", nw[:], C.I("na_nT")[l])
        k.memset(QM[:], 0.0, eng="gpsimd")
        for half in range(2):
            bt = k.sb(f"bt{half}", [128, 4, 4, 4, 64], F32)
            k.dma("sync", bt[:], C.I("na_biasT")[l, :, half * 4:(half + 1) * 4])
            k.act(E[:, half * 4:(half + 1) * 4], bt[:], AF.Exp)
        qkp = Pool(k, "qk", [128, 4, 512], F32, 2)
        sqp = Pool(k, "sq", [128, 4, 512], BF16, 2)
        rnp = Pool(k, "rn", [128, 512], F32, 2)
        pss = Pool(k, "pss", [128, 512], F32, 2, space="ps")
        src = C.S["NAQK"].t.rearrange("(c p) t -> p c t", p=128)
        for ti, (t0, n, w) in enumerate(TILES):
            qk = qkp.get()
            k.dma("sync", qk[:, :, :n], V(src[:, :, t0:t0 + n], [C.S["NAQK"].tok]))
            sq = sqp.get()
            k.act(sq[:, :, :n], qk[:, :, :n], AF.Square)
            for c in range(4):
                ps = pss.get()
                k.mm(ps[:, :n], bd[:], sq[:, c, :n])
                rn = rnp.get()
                k.act(rn[:, :n], ps[:, :n], AF.Sqrt, scale=1.0 / 64, bias=C.eps_t[:])
                k.recip(rn[:, :n], rn[:, :n])
                if c >= 2:
                    k.stt(QK[:, c, t0:t0 + n], qk[:, c, :n], nw[:, 1:2], rn[:, :n], ALU.mult, ALU.mult)
                else:
                    for par in range(2):
                        pp_ = slice(par * 64, par * 64 + 64)
                        k.stt(QM[pp_, c, par, t0:t0 + n], qk[pp_, c, :n], nw[pp_, 0:1], rn[pp_, :n], ALU.mult, ALU.mult)
        k.memset(Ve[:], 1.0)
        k.memset(Vo[:], 1.0, eng="gpsimd")
        nav = C.S["NAV"]
        vst = k.sb("vst", [128, 18, 256], F32)
        vso = k.sb("vso", [128, 17, 256], F32)
        for c0 in range(0, 18, 6):
            c1 = min(18, c0 + 6)
            k.dma("sync", vst[:, c0:c1, :], V(nav.t[c0 * 128:c1 * 128, :].rearrange("(c p) d -> p c d", p=128), [nav.tok]))
            c1o = min(17, c0 + 6)
            k.dma("sync", vso[:, c0:c1o, :], V(nav.t[64 + c0 * 128:64 + c1o * 128, :].rearrange("(c p) d -> p c d", p=128), [nav.tok]))
        k.copy(Ve[:, :, :, 0:64], vst[:].re("p c (h d) -> p c h d", d=64), eng="vector")
        k.copy(Vo[:, :, :, 0:64], vso[:].re("p c (h d) -> p c h d", d=64), eng="gpsimd")
    import os
    if os.environ.get("NA_STOP") == "1":
        return
    with k.stage():
        psp = Pool(k, "ps", [128, 4, 6, 64], F32, 2, space="ps")
        pop = Pool(k, "po", [64, 4, 65], F32, 2, space="ps")
        Pp = Pool(k, "P", [128, 4, 6, 64], BF16, 2)
        rdp = Pool(k, "rd", [64, 4], F32, 2)
        orp = Pool(k, "orow", [64, 256], F32, 3)
        G = C.S["G"]
        for r in range(32):
            start = min(max(r - 4, 0), 24)
            rho = r - start
            qt0 = 256 + 64 * r
            ps = psp.get()
            for h in range(4):
                hp = slice((h % 2) * 64, (h % 2) * 64 + 64)
                for i in range(6):
                    kt0 = (256 + 64 * (start + 2 * i)) if i < 4 else (i - 4) * 128
                    k.mm(ps[:, h, i, :], QK[:, 2 + h // 2, kt0:kt0 + 128], QM[:, h // 2, h % 2, qt0:qt0 + 64])
            Pt = Pp.get()
            k.act(Pt[:], ps[:], AF.Exp, scale=0.125)
            k.tt(Pt[:, :, 0:4, :], Pt[:, :, 0:4, :], E[:, rho], ALU.mult)
            po = pop.get()
            for h in range(4):
                for i in range(6):
                    if i < 4:
                        r0 = start + 2 * i
                        vv = Ve[:, 2 + r0 // 2, h, :] if r0 % 2 == 0 else Vo[:, (r0 + 3) // 2, h, :]
                    else:
                        vv = Ve[:, i - 4, h, :]
                    k.mm(po[:, h, :], Pt[:, h, i, :], vv, start=(i == 0), stop=(i == 5))
            rd = rdp.get()
            k.recip(rd[:], po[:, :, 64])
            orow = orp.get()
            k.tt(orow[:].re("p (h d) -> p h d", d=64), po[:, :, 0:64], rd[:, :, None].bc([64, 4, 64]), ALU.mult)
            k.dma("sync", G[qt0:qt0 + 64, 256:512], orow[:])
    if os.environ.get("NA_STOP") == "2":
        return
    with k.stage():
        G = C.S["G"]
        if need_ctx:
            pcp = Pool(k, "pc", [128, 2, 256], F32, 2, space="ps")
            pocp = Pool(k, "poc", [128, 65], F32, 2, space="ps")
            Pcp = Pool(k, "Pc", [128, 2, 256], BF16, 2)
            oc = k.sb("oc", [128, 2, 256], F32)
            rdc = Pool(k, "rdc", [128, 1], F32, 2)
            for h in range(4):
                hp = slice((h % 2) * 64, (h % 2) * 64 + 64)
                pc = pcp.get()
                for kc in range(2):
                    k.mm(pc[:, kc, :], QK[:, 2 + h // 2, kc * 128:(kc + 1) * 128], QM[:, h // 2, h % 2, 0:256])
                Pc = Pcp.get()
                k.act(Pc[:], pc[:], AF.Exp, scale=0.125)
                for qc in range(2):
                    poc = pocp.get()
                    for kc in range(2):
                        k.mm(poc[:], Pc[:, kc, qc * 128:(qc + 1) * 128], Ve[:, kc, h, :], start=(kc == 0), stop=(kc == 1))
                    rd = rdc.get()
                    k.recip(rd[:], poc[:, 64:65])
                    k.ts(oc[:, qc, h * 64:(h + 1) * 64], poc[:, 0:64], rd[:, 0:1], ALU.mult)
            k.dma("sync", V(G.t[0:256, 256:512].rearrange("(c p) n -> p c n", p=128), [G.tok]), oc[:])


import math
import ml_dtypes

HY_BANDS = 16


def _hy_consts(n):
    f64 = np.float64
    N = 2 * n
    nch = n // 128
    t = np.linspace(0.0, 1.0, n, dtype=np.float32).astype(f64)
    ang = 2.0 * math.pi * np.arange(n, dtype=f64) / n
    bands = np.linspace(1e-4, HY_BANDS - 1, HY_BANDS, dtype=np.float32).astype(f64)[None]
    z = np.concatenate([t[:, None], np.cos(bands * ang[:, None]), -np.sin(bands * ang[:, None])], axis=-1)
    max_decay = math.log(1e-2) / 0.3
    min_decay = math.log(1e-2) / 1.5
    deltas = np.abs(np.linspace(min_decay, max_decay, 512, dtype=np.float32).astype(f64))
    idx = np.arange(n, dtype=f64) + 0.5
    ph = 2.0 * math.pi * np.outer(idx, idx) / N
    C4 = np.cos(ph)
    S4 = np.sin(ph)

    def tile_(M):
        return np.ascontiguousarray(M.reshape(nch, 128, nch, 128).transpose(2, 1, 0, 3).reshape(nch, 128, nch * 128))
    w = 2.0 * math.pi * idx / N
    cs = np.stack([(2.0 / N) * np.cos(w / 2), (2.0 / N) * np.sin(w / 2), -(2.0 / N) * np.cos(w / 2)], axis=-1)
    return {
        "zT": np.ascontiguousarray(z.T.astype(np.float32)),
        "ntcol": np.ascontiguousarray((-t).reshape(nch, 128).T.astype(np.float32)),
        "absdelta": np.ascontiguousarray(np.tile(deltas[None, :], (128, 1)).astype(np.float32)),
        "C4t": tile_(C4).astype(ml_dtypes.bfloat16),
        "S4t": tile_(S4).astype(ml_dtypes.bfloat16),
        "cs": np.ascontiguousarray(cs.reshape(nch, 128, 3).transpose(1, 0, 2).astype(np.float32)),
    }


_HYC = {}


def hy_const(n, name):
    if n not in _HYC:
        _HYC[n] = _hy_consts(n)
    return _HYC[n][name]


IN_DTYPES = {}
for _n in (SEQ, CTX):
    _nch = _n // 128
    IN_SHAPES[f"hy_zT_{_n}"] = [33, _n]
    IN_SHAPES[f"hy_ntcol_{_n}"] = [128, _nch]
    IN_SHAPES[f"hy_cs_{_n}"] = [128, _nch, 3]
    IN_SHAPES[f"hy_C4t_{_n}"] = [_nch, 128, _nch * 128]
    IN_SHAPES[f"hy_S4t_{_n}"] = [_nch, 128, _nch * 128]
    IN_DTYPES[f"hy_C4t_{_n}"] = BF16
    IN_DTYPES[f"hy_S4t_{_n}"] = BF16
    for _nm in ("zT", "ntcol", "cs", "C4t", "S4t"):
        HOST_LAYOUT[f"hy_{_nm}_{_n}"] = (lambda inp, b, _n=_n, _nm=_nm: hy_const(_n, _nm))
IN_SHAPES["hy_absdelta"] = [128, 512]
HOST_LAYOUT["hy_absdelta"] = lambda inp, b: hy_const(CTX, "absdelta")
IN_SHAPES["hy_fcol"] = [NL, 64, 3]
HOST_LAYOUT["hy_fcol"] = lambda inp, b: np.stack([inp["hy_f_b1"], inp["hy_f_b2"], inp["hy_f_freq"]], axis=-1)
for _nm, _sh in (("hy_f_w1", [NL, 33, 64]), ("hy_f_w2", [NL, 64, 64]), ("hy_f_w3", [NL, 64, 1024]),
                 ("hy_conv", [NL, 3, 768]), ("hy_bias", [NL, 2, 256])):
    IN_SHAPES[_nm] = _sh


def bc_load(k, dst, src_ap, tok, eng="sync"):
    P = dst.ap.shape[0]
    k.dma(eng, dst, V(src_ap.partition_broadcast(P), [tok]))


def sin_rr(k, out, in_ps, scale_col, bias_col, pools, npi, nfree):
    a1, a2, ai = pools
    t1 = a1.get()
    t2 = a2.get()
    ti = ai.get()
    P = out.ap.shape[0]
    k.act(t1[:P, :nfree], in_ps, AF.Identity, scale=scale_col, bias=bias_col)
    k.ts(t2[:P, :nfree], t1[:P, :nfree], 1.0 / (2.0 * math.pi), ALU.mult, 64.5, ALU.add)
    k.copy(ti[:P, :nfree], t2[:P, :nfree])
    k.tt(t1[:P, :nfree], t2[:P, :nfree], ti[:P, :nfree], ALU.subtract)
    k.stt(t2[:P, :nfree], t1[:P, :nfree], 0.0, t1[:P, :nfree], ALU.is_lt, ALU.add)
    k.act(out, t2[:P, :nfree], AF.Sin, scale=2.0 * math.pi, bias=npi[:P, 0:1])


def hyf_stage(k, C, l, n, KRI):
    nch = n // 128
    TW = min(512, n)
    with contextlib.ExitStack() as outer:
        hyf_stage_(k, C, l, n, KRI, nch, TW, outer)


def hyf_stage_(k, C, l, n, KRI, nch, TW, outer):
    Pb = k.sb("Pb", [128, nch, 512], BF16, stack=outer)
    Qb = k.sb("Qb", [128, nch, 512], BF16, stack=outer)
    cs = k.sb("cs", [128, nch, 3], F32, stack=outer)
    with k.stage():
        w1 = k.sb("w1", [33, 64], F32)
        w2 = k.sb("w2", [64, 64], F32)
        w3 = k.sb("w3", [64, 1024], F32)
        fcol = k.sb("fcol", [64, 3], F32)
        fb = k.sb("fb", [64, 2], F32)
        zT = k.sb("zT", [33, n], F32)
        npi = k.sb("npi", [128, 1], F32)
        ones_f = k.sb("ones_f", [128, 128], F32)
        k.memset(npi[:], -math.pi)
        k.memset(ones_f[:], 1.0)
        k.dma("sync", w1[:], C.I("hy_f_w1")[l])
        k.dma("sync", w2[:], C.I("hy_f_w2")[l])
        k.dma("sync", w3[:], C.I("hy_f_w3")[l])
        k.dma("sync", fcol[:], C.I("hy_fcol")[l])
        k.dma("sync", zT[:], C.I(f"hy_zT_{n}")[:])
        k.tt(fb[:], fcol[:, 0:2], fcol[:, 2:3].bc([64, 2]), ALU.mult)
        hid1 = k.sb("hid1", [64, n], F32)
        hid2 = k.sb("hid2", [64, n], F32)
        pools = (Pool(k, "a1", [64, 512], F32, 2), Pool(k, "a2", [64, 512], F32, 2), Pool(k, "ai", [64, 512], I32, 2))
        pmp = Pool(k, "pm", [64, 512], F32, 2, space="ps")
        for t0 in range(0, n, TW):
            ps = pmp.get()
            k.mm(ps[:, :TW], w1[:], zT[:, t0:t0 + TW])
            sin_rr(k, hid1[:, t0:t0 + TW], ps[:, :TW], fcol[:, 2:3], fb[:, 0:1], pools, npi, TW)
        for t0 in range(0, n, TW):
            ps = pmp.get()
            k.mm(ps[:, :TW], w2[:], hid1[:, t0:t0 + TW])
            sin_rr(k, hid2[:, t0:t0 + TW], ps[:, :TW], fcol[:, 2:3], fb[:, 1:2], pools, npi, TW)
        absd = k.sb("absd", [128, 512], F32)
        ntc = k.sb("ntc", [128, nch], F32)
        k.dma("sync", absd[:], C.I("hy_absdelta")[:])
        k.dma("sync", ntc[:], C.I(f"hy_ntcol_{n}")[:])
        k.dma("sync", cs[:], C.I(f"hy_cs_{n}")[:])
        HF = k.sb("HF", [128, nch, 512], F32)
        HB = k.sb("HB", [128, nch, 512], F32)
        php = Pool(k, "ph", [128, 512], F32, 3, space="ps")
        pl1 = k.ps("pl1", [128, 512], F32)
        winp = Pool(k, "win", [128, 512], F32, 2)
        absp = Pool(k, "abs", [128, 512], F32, 3)
        for c in range(nch):
            ph0 = php.get()
            ph1 = php.get()
            k.mm(ph0[:], hid2[:, c * 128:(c + 1) * 128], w3[:, 0:512])
            k.mm(ph1[:], hid2[:, c * 128:(c + 1) * 128], w3[:, 512:1024])
            win = winp.get()
            k.act(win[:], absd[:], AF.Exp, scale=ntc[:, c:c + 1])
            k.tt(HF[:, c, :], ph0[:], win[:], ALU.mult)
            k.tt(HB[:, c, :], ph1[:], win[:], ALU.mult)
            if c == 0:
                k.memset(HB[0:1, 0, :], 0.0)
            for j, src in enumerate((HF, HB)):
                ab = absp.get()
                k.act(ab[:], src[:, c, :], AF.Abs)
                k.mm(pl1[:], ones_f[:], ab[:], start=(c == 0 and j == 0), stop=(c == nch - 1 and j == 1))
        rn = k.sb("rn", [128, 512], F32)
        k.recip(rn[:], pl1[:])
        for c in range(nch):
            t = absp.get()
            k.tt(t[:], HF[:, c, :], HB[:, c, :], ALU.add, eng="gpsimd")
            k.tt(Pb[:, c, :], t[:], rn[:], ALU.mult, eng="gpsimd")
            t = absp.get()
            k.tt(t[:], HF[:, c, :], HB[:, c, :], ALU.subtract)
            k.tt(Qb[:, c, :], t[:], rn[:], ALU.mult)
    with k.stage():
        absp = Pool(k, "abs2", [128, 512], F32, 3)
        cbp = Pool(k, "cb", [128, nch, 128], BF16, 2)
        sbp = Pool(k, "sbk", [128, nch, 128], BF16, 2)
        psp = Pool(k, "psq", [128, 512], F32, 8, space="ps")
        kop = Pool(k, "ko", [128, 2, 512], F32, 2)
        C4 = C.I(f"hy_C4t_{n}")
        S4 = C.I(f"hy_S4t_{n}")
        for fc in range(nch):
            cb = cbp.get()
            sb_ = sbp.get()
            k.dma("sync", cb[:].re("p c m -> p (c m)"), C4[fc])
            k.dma("sync", sb_[:].re("p c m -> p (c m)"), S4[fc])
            pPc, pPs, pQc, pQs = psp.get(), psp.get(), psp.get(), psp.get()
            for c in range(nch):
                st, sp = (c == 0), (c == nch - 1)
                k.mm(pPc[:], cb[:, c, :], Pb[:, c, :], start=st, stop=sp)
                k.mm(pPs[:], sb_[:, c, :], Pb[:, c, :], start=st, stop=sp)
                k.mm(pQc[:], cb[:, c, :], Qb[:, c, :], start=st, stop=sp)
                k.mm(pQs[:], sb_[:, c, :], Qb[:, c, :], start=st, stop=sp)
            ko = kop.get()
            t = absp.get()
            k.ts(t[:], pPc[:], cs[:, fc, 0:1], ALU.mult)
            k.stt(ko[:, 0, :], pPs[:], cs[:, fc, 1:2], t[:], ALU.mult, ALU.add)
            t = absp.get()
            k.ts(t[:], pQc[:], cs[:, fc, 1:2], ALU.mult)
            k.stt(ko[:, 1, :], pQs[:], cs[:, fc, 2:3], t[:], ALU.mult, ALU.add)
            k.dma("gpsimd", KRI[fc * 128:(fc + 1) * 128], ko[:])


def hyc_stage(k, C, l, n, base, KRI):
    nch = n // 128
    HYs = C.S["HY"]
    G = C.S["G"]
    with k.stage():
        cw = k.sb("cw", [128, 3, 768], F32)
        hb = k.sb("hb", [128, 2, 256], F32)
        for i in range(3):
            bc_load(k, cw[:, i, :], C.I("hy_conv").t[l, i], C.I("hy_conv").tok)
        for i in range(2):
            bc_load(k, hb[:, i, :], C.I("hy_bias").t[l, i], C.I("hy_bias").tok)
        v = k.sb("v", [128, nch, 256], F32)
        x1 = k.sb("x1", [128, nch, 256], F32)
        x2 = k.sb("x2", [128, nch, 256], F32)
        vb = k.sb("vb", [128, nch, 256], BF16)
        z = k.sb("z", [128, nch, 256], F32)
        zb = k.sb("zb", [128, nch, 256], BF16)
        Yc = k.sb("Yc", [128, nch, 256], BF16)
        Ys = k.sb("Ys", [128, nch, 256], BF16)
        sp_ = Pool(k, "scur", [128, 768], F32, 2)
        pp_ = Pool(k, "sprev", [128, 768], F32, 2)
        np_ = Pool(k, "snext", [128, 768], F32, 2)
        up = Pool(k, "u", [128, 768], F32, 2)
        tp = Pool(k, "tt", [128, 768], F32, 2)
        for c in range(nch):
            r0 = base + c * 128
            sc_, sp, sn = sp_.get(), pp_.get(), np_.get()
            k.dma("sync", sc_[:], HYs[r0:r0 + 128, :])
            if c == 0:
                k.memset(sp[0:1, :], 0.0)
                k.dma("sync", sp[1:128, :], HYs[r0:r0 + 127, :])
            else:
                k.dma("sync", sp[:], HYs[r0 - 1:r0 + 127, :])
            if c == nch - 1:
                k.memset(sn[:], 0.0)
                k.dma("sync", sn[0:127, :], HYs[r0 + 1:r0 + 128, :])
            else:
                k.dma("sync", sn[:], HYs[r0 + 1:r0 + 129, :])
            u = up.get()
            t1 = tp.get()
            t2 = tp.get()
            k.tt(u[:], sc_[:], cw[:, 1, :], ALU.mult)
            k.tt(t1[:], sp[:], cw[:, 0, :], ALU.mult, eng="gpsimd")
            k.tt(t2[:], sn[:], cw[:, 2, :], ALU.mult, eng="gpsimd")
            k.tt(u[:], u[:], t1[:], ALU.add)
            k.tt(v[:, c, :], u[:, 0:256], t2[:, 0:256], ALU.add)
            k.tt(x1[:, c, :], u[:, 256:512], t2[:, 256:512], ALU.add)
            k.tt(x2[:, c, :], u[:, 512:768], t2[:, 512:768], ALU.add)
            k.copy(vb[:, c, :], v[:, c, :], eng="scalar")
        cbp = Pool(k, "cb", [128, nch, 128], BF16, 3)
        sbp = Pool(k, "sbk", [128, nch, 128], BF16, 3)
        pup = Pool(k, "pU", [128, 256], F32, 4, space="ps")
        pyp = Pool(k, "pY", [128, 256], F32, 2, space="ps")
        krp = Pool(k, "kr", [128, 2, 512], F32, 2)
        usp = Pool(k, "us", [128, 2, 256], F32, 2)
        tp2 = Pool(k, "t2", [128, 256], F32, 6)
        outp = Pool(k, "yo", [128, 256], F32, 2)
        C4 = C.I(f"hy_C4t_{n}")
        S4 = C.I(f"hy_S4t_{n}")
        for o in range(2):
            src = vb if o == 0 else zb
            for fc in range(nch):
                cb = cbp.get()
                sb_ = sbp.get()
                k.dma("sync", cb[:].re("p c m -> p (c m)"), C4[fc])
                k.dma("sync", sb_[:].re("p c m -> p (c m)"), S4[fc])
                pUc, pUs = pup.get(), pup.get()
                for c in range(nch):
                    st, sp = (c == 0), (c == nch - 1)
                    k.mm(pUc[:], cb[:, c, :], src[:, c, :], start=st, stop=sp)
                    k.mm(pUs[:], sb_[:, c, :], src[:, c, :], start=st, stop=sp)
                kr = krp.get()
                k.dma("sync", kr[:], KRI[fc * 128:(fc + 1) * 128])
                us = usp.get()
                k.copy(us[:, 0, :], pUc[:], eng="scalar")
                k.copy(us[:, 1, :], pUs[:], eng="scalar")
                KR = kr[:, 0, o * 256:(o + 1) * 256]
                KI = kr[:, 1, o * 256:(o + 1) * 256]
                a, b_, c_, d_ = tp2.get(), tp2.get(), tp2.get(), tp2.get()
                k.tt(a[:], us[:, 0, :], KR, ALU.mult)
                k.tt(b_[:], us[:, 1, :], KI, ALU.mult, eng="gpsimd")
                k.tt(Yc[:, fc, :], a[:], b_[:], ALU.add)
                k.tt(c_[:], us[:, 1, :], KR, ALU.mult, eng="gpsimd")
                k.tt(d_[:], us[:, 0, :], KI, ALU.mult)
                k.tt(Ys[:, fc, :], c_[:], d_[:], ALU.subtract, eng="gpsimd")
            for tc in range(nch):
                cb = cbp.get()
                sb_ = sbp.get()
                k.dma("sync", cb[:].re("p c m -> p (c m)"), C4[tc])
                k.dma("sync", sb_[:].re("p c m -> p (c m)"), S4[tc])
                py = pyp.get()
                for f in range(nch):
                    k.mm(py[:], cb[:, f, :], Yc[:, f, :], start=(f == 0), stop=False)
                    k.mm(py[:], sb_[:, f, :], Ys[:, f, :], start=False, stop=(f == nch - 1))
                a, b_ = tp2.get(), tp2.get()
                if o == 0:
                    k.tt(a[:], v[:, tc, :], hb[:, 0, :], ALU.mult, eng="gpsimd")
                    k.tt(b_[:], py[:], a[:], ALU.add)
                    k.tt(z[:, tc, :], b_[:], x1[:, tc, :], ALU.mult)
                    k.copy(zb[:, tc, :], z[:, tc, :], eng="scalar")
                else:
                    k.tt(a[:], z[:, tc, :], hb[:, 1, :], ALU.mult, eng="gpsimd")
                    k.tt(b_[:], py[:], a[:], ALU.add)
                    yo = outp.get()
                    k.tt(yo[:], b_[:], x2[:, tc, :], ALU.mult)
                    k.dma("scalar", G[base + tc * 128:base + (tc + 1) * 128, 0:256], yo[:])


def hy_stage(k, C, l, need_ctx):
    if not hasattr(C, "KRI"):
        C.KRI = {n: k.dram(f"s_kri{n}", [n, 2, 512], F32) for n in (SEQ, CTX)}
    hyf_stage(k, C, l, SEQ, C.KRI[SEQ])
    hyc_stage(k, C, l, SEQ, CTX, C.KRI[SEQ])
    if need_ctx:
        hyf_stage(k, C, l, CTX, C.KRI[CTX])
        hyc_stage(k, C, l, CTX, 0, C.KRI[CTX])


def _chunk_consts():
    i = np.arange(64)
    tri = np.zeros((64, 2, 64), np.float32)
    tri[:, 0, :] = (i[:, None] <= i[None, :])
    tri[:, 1, :] = (i[:, None] >= i[None, :])
    after_eq = np.zeros((64, 8, 64), bool)
    after_st = np.zeros((64, 8, 64), bool)
    before_st = np.zeros((64, 8, 64), bool)
    for e in range(8):
        if e < 4:
            after_eq[:, e, :] = i[None, :] >= i[:, None]
            after_st[:, e, :] = i[None, :] > i[:, None]
            before_st[:, e, :] = i[None, :] < i[:, None]
        else:
            after_eq[:, e, :] = i[None, :] <= i[:, None]
            after_st[:, e, :] = i[None, :] < i[:, None]
            before_st[:, e, :] = i[None, :] > i[:, None]
    NEG = np.float32(-30000.0)
    mneg = np.stack([np.where(after_eq, 0, NEG), np.where(before_st, 0, NEG)], axis=1).astype(np.float32)
    m01 = np.stack([after_eq, after_st, before_st], axis=1).astype(np.float32)
    ident8 = np.tile(np.eye(64, dtype=np.float32)[:, None, :], (1, 8, 1))
    return {"c_tri": tri, "c_mneg": np.ascontiguousarray(mneg), "c_m01": np.ascontiguousarray(m01), "c_ident8": ident8}


_CC = {}


def _cc(name):
    if not _CC:
        _CC.update(_chunk_consts())
    return _CC[name]


for _nm, _sh in (("c_tri", [64, 2, 64]), ("c_mneg", [64, 2, 8, 64]), ("c_m01", [64, 3, 8, 64]), ("c_ident8", [64, 8, 64])):
    IN_SHAPES[_nm] = _sh
    HOST_LAYOUT[_nm] = (lambda inp, b, _nm=_nm: _cc(_nm))

NCH = NT // 64
FORD = list(range(NCH))
BORD = [3, 2, 1, 0] + list(range(NCH - 1, 3, -1))


def tri_inverse(k, M, A, ident8, pool_ps, pool_f, pool_b, levels=5):
    X = pool_f.get()
    k.tt(X[:], ident8[:], A[:], ALU.subtract)
    Mk, Ak = M, A
    for lev in range(1, levels + 1):
        pM = pool_ps.get()
        for e in range(8):
            k.mm(pM[:, e, :], Ak[:, e, :], Mk[:, e, :])
        if lev < levels:
            pA = pool_ps.get()
            for e in range(8):
                k.mm(pA[:, e, :], Mk[:, e, :], Ak[:, e, :])
            An = pool_f.get()
            k.copy(An[:], pA[:], eng="vector")
        Mn = pool_f.get()
        k.copy(Mn[:], pM[:], eng="scalar")
        pX = pool_ps.get()
        for e in range(8):
            k.mm(pX[:, e, :], Mn[:, e, :], X[:, e, :])
        Xn = pool_f.get()
        k.tt(Xn[:], X[:], pX[:], ALU.add)
        X = Xn
        Mk = Mn
        if lev < levels:
            Ak = An
    Xb = None
    if pool_b is not None:
        Xb = pool_b.get()
        k.copy(Xb[:], X[:], eng="gpsimd")
    return X, Xb


IN_SHAPES["dn_conv"] = [NL, 3, 768]
IN_SHAPES["dn_dtb8"] = [NL, 8]
IN_SHAPES["dn_alog8"] = [NL, 8]
IN_SHAPES["dn_normT"] = [NL, 256]
HOST_LAYOUT["dn_dtb8"] = lambda inp, b: inp["dn_dt_bias"].reshape(NL, 8)
HOST_LAYOUT["dn_alog8"] = lambda inp, b: inp["dn_a_log"].reshape(NL, 8)
HOST_LAYOUT["dn_normT"] = lambda inp, b: np.tile(inp["dn_norm"], (1, 4))


class Lane:
    pass


def run_lanes(k, nlanes, mkpools, body, items):
    lanes = [mkpools(i) for i in range(nlanes)]
    items = list(items)
    for i0 in range(0, len(items), nlanes):
        lists = []
        for j, it in enumerate(items[i0:i0 + nlanes]):
            with k.defer() as L:
                body(it, lanes[j])
            lists.append(L)
        k.replay(lists)


def shifted_loads(k, src, r0, ncols, seq_first, seq_last, pools):
    pc, pp, pn = pools
    sc_, sp, sn = pc.get(), pp.get(), pn.get()
    W = sc_.ap_shape[1]
    k.dma("sync", sc_[:], src[r0:r0 + 128, 0:W])
    if seq_first:
        k.memset(sp[0:1, :], 0.0)
        k.dma("sync", sp[1:128, :], src[r0:r0 + 127, 0:ncols])
    else:
        k.dma("sync", sp[:], src[r0 - 1:r0 + 127, 0:ncols])
    if seq_last:
        k.memset(sn[:], 0.0)
        k.dma("sync", sn[0:127, :], src[r0 + 1:r0 + 128, 0:ncols])
    else:
        k.dma("sync", sn[:], src[r0 + 1:r0 + 129, 0:ncols])
    return sc_, sp, sn


def dn_pre_stage(k, C, l):
    S = C.S
    with k.stage():
        cw = k.sb("cw", [128, 3, 768], F32)
        for i in range(3):
            bc_load(k, cw[:, i, :], C.I("dn_conv").t[l, i], C.I("dn_conv").tok)
        dtb = k.sb("dtb", [128, 8], F32)
        nA = k.sb("nA", [128, 8], F32)
        bc_load(k, dtb[:], C.I("dn_dtb8").t[l], C.I("dn_dtb8").tok)
        bc_load(k, nA[:], C.I("dn_alog8").t[l], C.I("dn_alog8").tok)
        k.act(nA[:], nA[:], AF.Exp)
        k.ts(nA[:], nA[:], -1.0, ALU.mult)
        def mkpools(i):
            L = Lane()
            L.pc = Pool(k, f"scur{i}", [128, 1040], F32, 2)
            L.pp = Pool(k, f"sprev{i}", [128, 768], F32, 2)
            L.pn = Pool(k, f"snext{i}", [128, 768], F32, 2)
            for p_ in (L.pc, L.pp, L.pn):
                for b_ in p_.bufs:
                    b_.ap_shape = b_.t.shape
            L.up = Pool(k, f"u{i}", [128, 768], F32, 1)
            L.tp = Pool(k, f"tt{i}", [128, 768], F32, 3)
            L.qp = Pool(k, f"qo{i}", [128, 768], F32, 2)
            L.gbp = Pool(k, f"gb{i}", [128, 16], F32, 2)
            L.smp = Pool(k, f"sm{i}", [128, 8], F32, 4)
            return L

        def body(tc, L):
            r0 = tc * 128
            sc_, sp, sn = shifted_loads(k, S["DN"], r0, 768, tc in (0, 2), tc in (1, NT // 128 - 1), (L.pc, L.pp, L.pn))
            u, t1, t2 = L.up.get(), L.tp.get(), L.tp.get()
            k.tt(u[:], sc_[:, 0:768], cw[:, 1, :], ALU.mult)
            k.tt(t1[:], sp[:], cw[:, 0, :], ALU.mult, eng="gpsimd")
            k.tt(t2[:], sn[:], cw[:, 2, :], ALU.mult, eng="gpsimd")
            k.tt(u[:], u[:], t1[:], ALU.add)
            k.tt(u[:], u[:], t2[:], ALU.add)
            qo = L.qp.get()
            k.act(qo[:], u[:], AF.Silu)
            sq = L.tp.get()
            k.tt(sq[:, 0:512], qo[:, 0:512], qo[:, 0:512], ALU.mult, eng="gpsimd")
            ss = L.smp.get()
            k.reduce(ss[:], sq[:, 0:512].re("p (g d) -> p g d", d=64))
            rn = L.smp.get()
            k.act(rn[:], ss[:], AF.Sqrt, bias=C.eps_t[:])
            k.recip(rn[:], rn[:])
            k.ts(rn[:, 0:4], rn[:, 0:4], 0.125, ALU.mult)
            k.tt(qo[:, 0:512].re("p (g d) -> p g d", d=64), qo[:, 0:512].re("p (g d) -> p g d", d=64),
                 rn[:, :, None].bc([128, 8, 64]), ALU.mult)
            k.dma("scalar", S["DNQKV"][r0:r0 + 128, :], qo[:])
            ba = sc_[:, 1024:1040].re("p (d a h) -> p d a h", d=2, a=2)
            gb = L.gbp.get()
            k.act(gb[:, 8:16].re("p (d h) -> p d h", d=2), ba[:, :, 0, :], AF.Sigmoid)
            x = L.smp.get()
            k.tt(x[:].re("p (d h) -> p d h", d=2), ba[:, :, 1, :], dtb[:].re("p (d h) -> p d h", d=2), ALU.add)
            k.act(x[:], x[:], AF.Exp)
            k.act(x[:], x[:], AF.Ln, bias=1.0)
            k.tt(gb[:, 0:8], x[:], nA[:], ALU.mult)
            k.dma("scalar", S["DNGB"][r0:r0 + 128, :], gb[:])

        run_lanes(k, 3, mkpools, body, range(NT // 128))


def dn_scan_stage(k, C, l, need_ctx):
    S = C.S
    with k.stage():
        tri = k.sb("tri", [64, 2, 64], F32)
        mneg = k.sb("mneg", [64, 2, 8, 64], F32)
        id8 = k.sb("id8", [64, 8, 64], F32)
        k.dma("sync", tri[:], C.I("c_tri")[:])
        k.dma("sync", mneg[:], C.I("c_mneg")[:])
        k.dma("sync", id8[:], C.I("c_ident8")[:])
        idn = C.ident[0:64, 0:64]
        idbt = k.sb("idbt", [64, 64], BF16)
        k.copy(idbt[:], C.ident[0:64, 0:64])
        idb = idbt[:]
        St = k.sb("St", [64, 8, 64], F32)
        k.memset(St[:], 0.0)
        pps = Pool(k, "pp", [64, 8, 64], F32, 7, space="ps")
        psm = k.ps("psm", [64, 8], F32)
        sbp = Pool(k, "w", [64, 8, 64], F32, 40)
        inv_f = Pool(k, "ivf", [64, 8, 64], F32, 8)
        inv_b = Pool(k, "ivb", [64, 8, 64], BF16, 6)
        sbb = Pool(k, "wb", [64, 8, 64], BF16, 24)
        Stb = k.sb("Stb", [64, 8, 64], BF16)
        k.memset(Stb[:], 0.0)
        qkp = Pool(k, "qk8", [64, 2, 8, 64], F32, 2)
        v8p = Pool(k, "v8", [64, 8, 64], F32, 2)
        gbp = Pool(k, "gb8", [64, 2, 8], F32, 2)
        smp = Pool(k, "sm", [64, 8], F32, 8)
        import os
        nsteps = int(os.environ.get("DN_STEPS", str(NCH)))
        part = int(os.environ.get("DN_PART", "99"))
        for s in range(nsteps):
            ch = (FORD[s], BORD[s])
            qk8, v8, gb8 = qkp.get(), v8p.get(), gbp.get()
            for d in range(2):
                r0 = ch[d] * 64
                es = slice(d * 4, d * 4 + 4)
                k.dma("sync", qk8[:, :, es, :], V(S["DNQKV"].t[r0:r0 + 64, 0:512].rearrange("p (a h d) -> p a h d", a=2, h=4), [S["DNQKV"].tok]))
                k.dma("sync", v8[:, es, :], V(S["DNQKV"].t[r0:r0 + 64, 512:768].rearrange("p (h d) -> p h d", h=4), [S["DNQKV"].tok]))
                k.dma("sync", gb8[:, 0, es], S["DNGB"][r0:r0 + 64, d * 4:d * 4 + 4])
                k.dma("sync", gb8[:, 1, es], S["DNGB"][r0:r0 + 64, 8 + d * 4:8 + d * 4 + 4])
            beta_bc = gb8[:, 1, :, None].bc([64, 8, 64])
            if part < 1:
                continue
            for d in range(2):
                k.mm(psm[:, d * 4:d * 4 + 4], tri[:, d, :], gb8[:, 0, d * 4:d * 4 + 4])
            gcc = smp.get()
            k.copy(gcc[:], psm[:], eng="scalar")
            sub = int(os.environ.get("DN_SUB", "99"))
            if sub < 2:
                continue
            grep = sbp.get()
            k.copy(grep[:], gb8[:, 0, :, None].bc([64, 8, 64]))
            pgcb = pps.get()
            for e in range(8):
                k.mm(pgcb[:, e, :], grep[:, e, :], tri[:, e // 4, :])
            if sub < 3:
                continue
            gcb = sbp.get()
            k.copy(gcb[:], pgcb[:], eng="scalar")
            D1 = sbp.get()
            k.tt(D1[:], gcb[:], gcc[:, :, None].bc([64, 8, 64]), ALU.subtract)
            egcb = sbp.get()
            k.act(egcb[:], gcb[:], AF.Exp)
            if sub < 4:
                continue
            t = sbp.get()
            k.tt(t[:], D1[:], mneg[:, 0], ALU.add, eng="gpsimd")
            E1 = sbp.get()
            k.act(E1[:], t[:], AF.Exp)
            if sub < 5:
                continue
            t = sbp.get()
            k.stt(t[:], D1[:], -1.0, mneg[:, 1], ALU.mult, ALU.add)
            E2 = sbp.get()
            k.act(E2[:], t[:], AF.Exp)
            if sub < 6:
                continue
            egc = smp.get()
            k.act(egc[:], gcc[:], AF.Exp)
            ekd = smp.get()
            k.act(ekd[:, 0:4], D1[:, 0:4, 63], AF.Exp)
            k.act(ekd[:, 4:8], D1[:, 4:8, 0], AF.Exp)
            if part < 2:
                continue
            pKT, pQT = pps.get(), pps.get()
            for e in range(8):
                k.tr(pKT[:, e, :], qk8[:, 1, e, :], idn)
            for e in range(8):
                k.tr(pQT[:, e, :], qk8[:, 0, e, :], idn)
            KT, QT = sbb.get(), sbb.get()
            k.copy(KT[:], pKT[:], eng="scalar")
            k.copy(QT[:], pQT[:], eng="vector")
            if part < 3:
                continue
            pG, pQK = pps.get(), pps.get()
            for e in range(8):
                k.mm(pG[:, e, :], KT[:, e, :], KT[:, e, :])
            for e in range(8):
                k.mm(pQK[:, e, :], KT[:, e, :], QT[:, e, :])
            t = sbp.get()
            k.tt(t[:], pG[:], E2[:], ALU.mult)
            M = sbp.get()
            k.tt(M[:], t[:], beta_bc, ALU.mult, eng="gpsimd")
            attnT = sbb.get()
            k.tt(attnT[:], pQK[:], E1[:], ALU.mult)
            pA = pps.get()
            for e in range(8):
                k.tr(pA[:, e, :], M[:, e, :], idn)
            A = sbp.get()
            k.copy(A[:], pA[:], eng="scalar")
            if part < 4:
                continue
            X = tri_inverse(k, M, A, id8, pps, inv_f, inv_b)
            if part < 5:
                continue
            Vb = sbb.get()
            k.tt(Vb[:], v8[:], beta_bc, ALU.mult, eng="gpsimd")
            bg = smp.get()
            k.tt(bg[:], gb8[:, 1, :], egc[:], ALU.mult)
            KBG = sbb.get()
            k.tt(KBG[:], qk8[:, 1], bg[:, :, None].bc([64, 8, 64]), ALU.mult, eng="gpsimd")
            pU, pW = pps.get(), pps.get()
            for e in range(8):
                k.mm(pU[:, e, :], X[:, e, :], Vb[:, e, :])
            for e in range(8):
                k.mm(pW[:, e, :], KBG[:, e, :], X[:, e, :])
            U, WT = sbp.get(), sbb.get()
            k.copy(U[:], pU[:], eng="scalar")
            k.copy(WT[:], pW[:], eng="vector")
            QdT = sbb.get()
            k.tt(QdT[:], QT[:], egcb[:], ALU.mult, eng="gpsimd")
            Kd = sbb.get()
            k.tt(Kd[:], qk8[:, 1], ekd[:, :, None].bc([64, 8, 64]), ALU.mult, eng="gpsimd")
            if part < 6:
                continue
            pv = pps.get()
            for e in range(8):
                k.mm(pv[:, e, :], WT[:, e, :], Stb[:, e, :])
            vnew = sbb.get()
            k.tt(vnew[:], U[:], pv[:], ALU.subtract)
            po = pps.get()
            for e in range(8):
                k.mm(po[:, e, :], QdT[:, e, :], Stb[:, e, :], start=True, stop=False)
                k.mm(po[:, e, :], attnT[:, e, :], vnew[:, e, :], start=False, stop=True)
            o8 = sbp.get()
            k.copy(o8[:], po[:], eng="scalar")
            for d in range(2):
                if need_ctx or ch[d] >= 4:
                    k.dma("scalar", S["DNO"][d, ch[d] * 64:(ch[d] + 1) * 64, :], o8[:, d * 4:d * 4 + 4, :].re("p h d -> p (h d)"))
            pS = pps.get()
            for e in range(8):
                k.mm(pS[:, e, :], Kd[:, e, :], vnew[:, e, :])
            glast = smp.get()
            k.copy(glast[:, 0:4], egcb[:, 0:4, 63])
            k.copy(glast[:, 4:8], egcb[:, 4:8, 0])
            t = sbp.get()
            k.tt(t[:], St[:], glast[:, :, None].bc([64, 8, 64]), ALU.mult)
            k.tt(St[:], t[:], pS[:], ALU.add)
            k.copy(Stb[:], St[:], eng="gpsimd")


def dn_post_stage(k, C, l, need_ctx):
    S = C.S
    with k.stage():
        nwt = k.sb("nwt", [128, 256], F32)
        bc_load(k, nwt[:], C.I("dn_normT").t[l], C.I("dn_normT").tok)

        def mkpools(i):
            L = Lane()
            L.op_ = Pool(k, f"o{i}", [128, 2, 256], F32, 2)
            L.zp = Pool(k, f"z{i}", [128, 256], F32, 2)
            L.tp = Pool(k, f"t{i}", [128, 256], F32, 6)
            L.smp = Pool(k, f"sm{i}", [128, 4], F32, 4)
            return L

        def body(tc, L):
            r0 = tc * 128
            o = L.op_.get()
            k.dma("sync", o[:], V(S["DNO"].t[:, r0:r0 + 128, :].rearrange("d p c -> p d c"), [S["DNO"].tok]))
            zt = L.zp.get()
            k.dma("sync", zt[:], S["DN"][r0:r0 + 128, 768:1024])
            os_ = L.tp.get()
            k.tt(os_[:], o[:, 0, :], o[:, 1, :], ALU.add)
            sq = L.tp.get()
            k.tt(sq[:], os_[:], os_[:], ALU.mult, eng="gpsimd")
            ss = L.smp.get()
            k.reduce(ss[:], sq[:].re("p (h d) -> p h d", d=64))
            rn = L.smp.get()
            k.act(rn[:], ss[:], AF.Sqrt, scale=1.0 / 64, bias=C.eps_t[:])
            k.recip(rn[:], rn[:])
            sz = L.tp.get()
            k.act(sz[:], zt[:], AF.Silu)
            a = L.tp.get()
            k.tt(a[:].re("p (h d) -> p h d", d=64), os_[:].re("p (h d) -> p h d", d=64), rn[:, :, None].bc([128, 4, 64]), ALU.mult)
            k.tt(a[:], a[:], nwt[:], ALU.mult, eng="gpsimd")
            y = L.tp.get()
            k.tt(y[:], a[:], sz[:], ALU.mult)
            k.dma("scalar", S["G"][r0:r0 + 128, 512:768], y[:])

        run_lanes(k, 3, mkpools, body, [tc for tc in range(NT // 128) if not (tc < 2 and not need_ctx)])


def dn_stage(k, C, l, need_ctx):
    dn_pre_stage(k, C, l)
    scan_stage(k, C, l, need_ctx, do_dn=True, do_rw=False)
    dn_post_stage(k, C, l, need_ctx)


for _nm, _sh in (("rw_mu", [NL, 2, 896]), ("rw_w_up", [NL, 2, 32, 256]), ("rw_a_up", [NL, 2, 32, 256]),
                 ("rw_g_up", [NL, 64, 256]), ("rw_w0", [NL, 2, 256]), ("rw_a0", [NL, 2, 256]),
                 ("rw_k_k", [NL, 256]), ("rw_k_a", [NL, 256]), ("rw_ln_w", [NL, 256]), ("rw_ln_b", [NL, 256]),
                 ("rw_r_kT", [NL, 256]), ("rw_mulrT", [NL, 64, 3, 2])):
    IN_SHAPES[_nm] = _sh
HOST_LAYOUT["rw_r_kT"] = lambda inp, b: inp["rw_r_k"].reshape(NL, 256)


def _rw_mulrT(inp, b):
    mu = inp["rw_mu"]
    o = np.zeros((NL, 64, 3, 2), np.float32)
    o[:, 0:32, 0, :] = mu[:, :, 768:800].transpose(0, 2, 1)
    o[:, 0:32, 1, :] = mu[:, :, 800:832].transpose(0, 2, 1)
    o[:, :, 2, :] = mu[:, :, 832:896].transpose(0, 2, 1)
    return o


HOST_LAYOUT["rw_mulrT"] = _rw_mulrT
SEQS = ((0, CTX), (CTX, NT))
LW_SCALE = -math.exp(-0.5)


def rw_pre_stage(k, C, l):
    S = C.S
    with k.stage():
        mulr = k.sb("mulr", [64, 3, 2], F32)
        k.dma("sync", mulr[:], C.I("rw_mulrT")[l])
        c0lr = k.sb("c0lr", [64, 3], F32)
        k.tt(c0lr[:], mulr[:, :, 0], mulr[:, :, 1], ALU.add)
        k.ts(c0lr[:], c0lr[:], -1.0, ALU.mult, 1.0, ALU.add)
        lrT = []
        for g_, ncl, fn in ((0, 32, AF.Tanh), (1, 32, None), (2, 64, AF.Sigmoid)):
            u = k.sb(f"lru{g_}", [64, NT], F32)
            o = k.sb(f"lro{g_}", [64, NT], F32)
            k.dma("sync", u[0:ncl, :], S["RWLR"][g_, 0:ncl, :])
            k.ts(o[0:ncl, :], u[0:ncl, :], c0lr[0:ncl, g_:g_ + 1], ALU.mult)
            for (a, b_) in SEQS:
                k.stt(o[0:ncl, a + 1:b_], u[0:ncl, a:b_ - 1], mulr[0:ncl, g_, 0:1], o[0:ncl, a + 1:b_], ALU.mult, ALU.add)
                k.stt(o[0:ncl, a:b_ - 1], u[0:ncl, a + 1:b_], mulr[0:ncl, g_, 1:2], o[0:ncl, a:b_ - 1], ALU.mult, ALU.add)
            if fn is not None:
                k.act(o[0:ncl, :], o[0:ncl, :], fn)
            lrT.append(o)
        wup = k.sb("wup", [32, 2, 256], F32)
        aup = k.sb("aup", [32, 2, 256], F32)
        gup = k.sb("gup", [64, 256], F32)
        w0r = k.sb("w0r", [1, 2, 256], F32)
        a0r = k.sb("a0r", [1, 2, 256], F32)
        ones1 = k.sb("ones1", [1, 128], F32)
        k.memset(ones1[:], 1.0)
        for d in range(2):
            k.dma("sync", wup[:, d, :], C.I("rw_w_up")[l, d])
            k.dma("sync", aup[:, d, :], C.I("rw_a_up")[l, d])
            k.dma("sync", w0r[:, d, :], C.I("rw_w0")[l, d:d + 1])
            k.dma("sync", a0r[:, d, :], C.I("rw_a0")[l, d:d + 1])
        k.dma("sync", gup[:], C.I("rw_g_up")[l])
        mu = k.sb("mu", [128, 2, 768], F32)
        for i in range(2):
            bc_load(k, mu[:, i, :], C.I("rw_mu").t[l, i, 0:768], C.I("rw_mu").tok)
        c0 = k.sb("c0", [128, 768], F32)
        k.tt(c0[:], mu[:, 0, :], mu[:, 1, :], ALU.add)
        k.ts(c0[:], c0[:], -1.0, ALU.mult, 1.0, ALU.add)
        kkw = k.sb("kkw", [128, 256], F32)
        kaw = k.sb("kaw", [128, 256], F32)
        omka = k.sb("omka", [128, 256], F32)
        bc_load(k, kkw[:], C.I("rw_k_k").t[l], C.I("rw_k_k").tok)
        bc_load(k, kaw[:], C.I("rw_k_a").t[l], C.I("rw_k_a").tok)
        k.ts(omka[:], kaw[:], -1.0, ALU.mult, 1.0, ALU.add)
        def mkpools(i):
            L = Lane()
            L.pc = Pool(k, f"scur{i}", [128, 768], F32, 2)
            L.pp = Pool(k, f"sprev{i}", [128, 768], F32, 2)
            L.pn = Pool(k, f"snext{i}", [128, 768], F32, 2)
            for p_ in (L.pc, L.pp, L.pn):
                for b_ in p_.bufs:
                    b_.ap_shape = b_.t.shape
            L.tp = Pool(k, f"tt{i}", [128, 768], F32, 3)
            L.op_ = Pool(k, f"out{i}", [128, 10, 256], F32, 2)
            L.t2 = Pool(k, f"t2{i}", [128, 256], F32, 6)
            L.smp = Pool(k, f"sm{i}", [128, 4], F32, 4)
            L.plp = Pool(k, f"pl{i}", [128, 2, 256], F32, 3, space="ps")
            return L

        def body(tc, L):
            pc, pp, pn, tp, op_, t2, smp, plp = L.pc, L.pp, L.pn, L.tp, L.op_, L.t2, L.smp, L.plp
            r0 = tc * 128
            sc_, sp, sn = shifted_loads(k, S["RW"], r0, 768, tc in (0, 2), tc in (1, NT // 128 - 1), (pc, pp, pn))
            o = op_.get()
            s_ = tp.get()
            t1 = tp.get()
            k.tt(s_[:], sc_[:], c0[:], ALU.mult)
            k.tt(t1[:], sp[:], mu[:, 0, :], ALU.mult, eng="gpsimd")
            k.tt(s_[:], s_[:], t1[:], ALU.add)
            t1 = tp.get()
            k.tt(t1[:], sn[:], mu[:, 1, :], ALU.mult, eng="gpsimd")
            k.tt(s_[:, 0:256], s_[:, 0:256], t1[:, 0:256], ALU.add)
            k.tt(s_[:, 256:512], s_[:, 256:512], t1[:, 256:512], ALU.add)
            k.tt(o[:, 1, :], s_[:, 512:768], t1[:, 512:768], ALU.add)
            k.copy(o[:, 0, :], s_[:, 0:256], eng="scalar")
            kcur = s_[:, 256:512]
            tok = slice(r0, r0 + 128)
            pwt, pat, pgt = plp.get(), plp.get(), plp.get()
            pw = [pwt[:, 0, :], pwt[:, 1, :]]
            pa = [pat[:, 0, :], pat[:, 1, :]]
            pg = pgt[:, 0, :]
            for d in range(2):
                k.mm(pw[d], lrT[0][0:32, tok], wup[:, d, :], start=True, stop=False)
                k.mm(pw[d], ones1[:], w0r[:, d, :], start=False, stop=True)
            for d in range(2):
                k.mm(pa[d], lrT[1][0:32, tok], aup[:, d, :], start=True, stop=False)
                k.mm(pa[d], ones1[:], a0r[:, d, :], start=False, stop=True)
            k.mm(pg, lrT[2][0:64, tok], gup[:])
            k.copy(o[:, 9, :], pg, eng="scalar")
            kx = t2.get()
            k.tt(kx[:], kcur, kkw[:], ALU.mult, eng="gpsimd")
            sq = t2.get()
            k.tt(sq[:], kx[:], kx[:], ALU.mult, eng="gpsimd")
            ss = smp.get()
            k.reduce(ss[:], sq[:].re("p (h d) -> p h d", d=64))
            rn = smp.get()
            k.act(rn[:], ss[:], AF.Sqrt, bias=C.eps_t[:])
            k.recip(rn[:], rn[:])
            kk = t2.get()
            k.tt(kk[:].re("p (h d) -> p h d", d=64), kx[:].re("p (h d) -> p h d", d=64), rn[:, :, None].bc([128, 4, 64]), ALU.mult)
            k.ts(o[:, 2, :], kk[:], -1.0, ALU.mult)
            for d in range(2):
                sg = t2.get()
                k.act(sg[:], pw[d], AF.Sigmoid)
                k.ts(o[:, 3 + d, :], sg[:], LW_SCALE, ALU.mult, eng="gpsimd")
                ar = t2.get()
                k.act(ar[:], pa[d], AF.Sigmoid)
                k.tt(o[:, 7 + d, :], kk[:], ar[:], ALU.mult, eng="gpsimd")
                t = t2.get()
                k.tt(t[:], ar[:], kaw[:], ALU.mult)
                k.tt(t[:], t[:], omka[:], ALU.add)
                k.tt(o[:, 5 + d, :], kcur, t[:], ALU.mult)
            k.dma("scalar", S["RWP"][r0:r0 + 128], o[:])


        run_lanes(k, 2, mkpools, body, range(NT // 128))


def rw_scan_stage(k, C, l, need_ctx):
    S = C.S
    with k.stage():
        tri = k.sb("tri", [64, 2, 64], F32)
        m01 = k.sb("m01", [64, 3, 8, 64], F32)
        nm = k.sb("nm", [64, 2, 8, 64], F32)
        id8 = k.sb("id8", [64, 8, 64], F32)
        ones64 = k.sb("ones64", [64, 64], F32)
        k.memset(ones64[:], 1.0)
        k.dma("sync", tri[:], C.I("c_tri")[:])
        k.dma("sync", m01[:], C.I("c_m01")[:])
        k.dma("sync", id8[:], C.I("c_ident8")[:])
        k.ts(nm[:], m01[:, 1:3], -1.0, ALU.mult)
        idn = C.ident[0:64, 0:64]
        St = k.sb("St", [64, 8, 64], F32)
        k.memset(St[:], 0.0)
        pps = Pool(k, "pp", [64, 8, 64], F32, 7, space="ps")
        psm = k.ps("psm", [64, 8, 2], F32)
        sbp = Pool(k, "w", [64, 8, 64], F32, 44)
        inv_sb = Pool(k, "iv", [64, 8, 64], F32, 8)
        inp_ = [Pool(k, f"in{i}", [64, 8, 64], F32, 2) for i in range(6)]
        smp = Pool(k, "sm", [64, 8], F32, 4)
        RWP = S["RWP"]
        slots = ((0, 0), (1, 1), (2, 2), (3, 4), (5, 6), (7, 8))
        for s in range(NCH):
            ch = (FORD[s], BORD[s])
            tl = [p.get() for p in inp_]
            for i, sl in enumerate(slots):
                for d in range(2):
                    r0 = ch[d] * 64
                    k.dma("sync", tl[i][:, d * 4:d * 4 + 4, :],
                          V(RWP.t[r0:r0 + 64, sl[d], :].rearrange("p (h d) -> p h d", h=4), [RWP.tok]))
            R8, V8, A8, LW8, K8, B8 = tl
            plc, ptot = pps.get(), pps.get()
            for d in range(2):
                k.mm(plc[:, d * 4:d * 4 + 4, :], tri[:, d, :], LW8[:, d * 4:d * 4 + 4, :])
            k.mm(ptot[:], ones64[:], LW8[:])
            for e in range(8):
                k.mm(psm[:, e, :], LW8[:, e, :], ones64[:, 0:2])
            lc, tot = sbp.get(), sbp.get()
            k.copy(lc[:], plc[:], eng="scalar")
            k.copy(tot[:], ptot[:], eng="vector")
            gCT = smp.get()
            k.act(gCT[:], psm[:, :, 0], AF.Exp)
            eg, egi, egp, ehat = sbp.get(), sbp.get(), sbp.get(), sbp.get()
            k.act(eg[:], lc[:], AF.Exp)
            k.act(egi[:], lc[:], AF.Exp, scale=-1.0)
            t = sbp.get()
            k.tt(t[:], lc[:], LW8[:], ALU.subtract, eng="gpsimd")
            k.act(egp[:], t[:], AF.Exp)
            t = sbp.get()
            k.tt(t[:], tot[:], lc[:], ALU.subtract)
            k.act(ehat[:], t[:], AF.Exp)
            At, Bt, Kt, Rt, Bh, Kh = [sbp.get() for _ in range(6)]
            k.tt(At[:], A8[:], egp[:], ALU.mult)
            k.tt(Bt[:], B8[:], egi[:], ALU.mult, eng="gpsimd")
            k.tt(Kt[:], K8[:], egi[:], ALU.mult)
            k.tt(Rt[:], R8[:], eg[:], ALU.mult, eng="gpsimd")
            k.tt(Bh[:], B8[:], ehat[:], ALU.mult)
            k.tt(Kh[:], K8[:], ehat[:], ALU.mult, eng="gpsimd")
            fmT = []
            for i, src in enumerate((At, Bt, Kt, Rt)):
                pT = pps.get()
                for e in range(8):
                    k.tr(pT[:, e, :], src[:, e, :], idn)
                dst = sbp.get()
                k.copy(dst[:], pT[:], eng=("scalar" if i % 2 == 0 else "vector"))
                fmT.append(dst)
            AtT, BtT, KtT, RtT = fmT
            def score(lhs, rhs, mask, eng):
                p_ = pps.get()
                for e in range(8):
                    k.mm(p_[:, e, :], lhs[:, e, :], rhs[:, e, :])
                o_ = sbp.get()
                k.tt(o_[:], p_[:], mask, ALU.mult, eng=eng)
                return o_
            M = score(AtT, BtT, nm[:, 1], "vector")
            A = score(BtT, AtT, nm[:, 0], "vector")
            AakT = score(KtT, AtT, m01[:, 1], "vector")
            ArbT = score(BtT, RtT, m01[:, 0], "vector")
            ArkT = score(KtT, RtT, m01[:, 0], "vector")
            X = tri_inverse(k, None, M, A, id8, pps, inv_sb)
            pW, pAkV = pps.get(), pps.get()
            for e in range(8):
                k.mm(pW[:, e, :], At[:, e, :], X[:, e, :])
            for e in range(8):
                k.mm(pAkV[:, e, :], AakT[:, e, :], V8[:, e, :])
            WT, AkV = sbp.get(), sbp.get()
            k.copy(WT[:], pW[:], eng="scalar")
            k.copy(AkV[:], pAkV[:], eng="vector")
            pUv = pps.get()
            for e in range(8):
                k.mm(pUv[:, e, :], X[:, e, :], AkV[:, e, :])
            Uv = sbp.get()
            k.copy(Uv[:], pUv[:], eng="scalar")
            pe = pps.get()
            for e in range(8):
                k.mm(pe[:, e, :], WT[:, e, :], St[:, e, :])
            E = sbp.get()
            k.tt(E[:], Uv[:], pe[:], ALU.add)
            py = pps.get()
            for e in range(8):
                k.mm(py[:, e, :], RtT[:, e, :], St[:, e, :], start=True, stop=False)
                k.mm(py[:, e, :], ArbT[:, e, :], E[:, e, :], start=False, stop=False)
                k.mm(py[:, e, :], ArkT[:, e, :], V8[:, e, :], start=False, stop=True)
            y8 = sbp.get()
            k.copy(y8[:], py[:], eng="scalar")
            for d in range(2):
                if need_ctx or ch[d] >= 4:
                    k.dma("scalar", S["RWY"][d, ch[d] * 64:(ch[d] + 1) * 64, :], y8[:, d * 4:d * 4 + 4, :].re("p h d -> p (h d)"))
            pS = pps.get()
            for e in range(8):
                k.mm(pS[:, e, :], Bh[:, e, :], E[:, e, :], start=True, stop=False)
                k.mm(pS[:, e, :], Kh[:, e, :], V8[:, e, :], start=False, stop=True)
            t = sbp.get()
            k.tt(t[:], St[:], gCT[:, :, None].bc([64, 8, 64]), ALU.mult)
            k.tt(St[:], t[:], pS[:], ALU.add)


def rw_post_stage(k, C, l, need_ctx):
    S = C.S
    with k.stage():
        lnw = k.sb("lnw", [128, 256], F32)
        lnb = k.sb("lnb", [128, 256], F32)
        rkw = k.sb("rkw", [128, 256], F32)
        bc_load(k, lnw[:], C.I("rw_ln_w").t[l], C.I("rw_ln_w").tok)
        bc_load(k, lnb[:], C.I("rw_ln_b").t[l], C.I("rw_ln_b").tok)
        bc_load(k, rkw[:], C.I("rw_r_kT").t[l], C.I("rw_r_kT").tok)
        lneps = k.sb("lneps", [128, 1], F32)
        k.memset(lneps[:], 64e-5)
        yp = Pool(k, "y", [128, 2, 256], F32, 2)
        pp = Pool(k, "p", [128, 10, 256], F32, 2)
        tp = Pool(k, "t", [128, 256], F32, 8)
        smp = Pool(k, "sm", [128, 4], F32, 6)

        def hview(x):
            return x.re("p (h d) -> p h d", d=64)
        for tc in range(NT // 128):
            if tc < 2 and not need_ctx:
                continue
            r0 = tc * 128
            yy = yp.get()
            k.dma("sync", yy[:], V(S["RWY"].t[:, r0:r0 + 128, :].rearrange("d p c -> p d c"), [S["RWY"].tok]))
            pr = pp.get()
            k.dma("sync", pr[:], S["RWP"][r0:r0 + 128])
            y = tp.get()
            k.tt(y[:], yy[:, 0, :], yy[:, 1, :], ALU.add)
            s1 = smp.get()
            k.reduce(s1[:], hview(y[:]))
            k.ts(s1[:], s1[:], -1.0 / 64, ALU.mult)
            yc = tp.get()
            k.tt(hview(yc[:]), hview(y[:]), s1[:, :, None].bc([128, 4, 64]), ALU.add)
            sq = tp.get()
            k.tt(sq[:], yc[:], yc[:], ALU.mult, eng="gpsimd")
            s2 = smp.get()
            k.reduce(s2[:], hview(sq[:]))
            rstd = smp.get()
            k.act(rstd[:], s2[:], AF.Sqrt, scale=1.0 / 64, bias=lneps[:])
            k.recip(rstd[:], rstd[:])
            yn = tp.get()
            k.tt(hview(yn[:]), hview(yc[:]), rstd[:, :, None].bc([128, 4, 64]), ALU.mult)
            k.tt(yn[:], yn[:], lnw[:], ALU.mult, eng="gpsimd")
            k.tt(yn[:], yn[:], lnb[:], ALU.add, eng="gpsimd")
            ks = tp.get()
            k.tt(ks[:], pr[:, 5, :], pr[:, 6, :], ALU.add, eng="gpsimd")
            k.tt(ks[:], ks[:], pr[:, 0, :], ALU.mult, eng="gpsimd")
            k.tt(ks[:], ks[:], rkw[:], ALU.mult, eng="gpsimd")
            bs = smp.get()
            k.reduce(bs[:], hview(ks[:]))
            bon = tp.get()
            k.tt(hview(bon[:]), hview(pr[:, 1, :]), bs[:, :, None].bc([128, 4, 64]), ALU.mult)
            k.tt(yn[:], yn[:], bon[:], ALU.add)
            out = tp.get()
            k.tt(out[:], yn[:], pr[:, 9, :], ALU.mult)
            k.dma("scalar", S["G"][r0:r0 + 128, 768:1024], out[:])


def rw_stage(k, C, l, need_ctx):
    rw_pre_stage(k, C, l)
    scan_stage(k, C, l, need_ctx, do_dn=False, do_rw=True)
    rw_post_stage(k, C, l, need_ctx)


def dnrw_stage(k, C, l, need_ctx):
    dn_pre_stage(k, C, l)
    rw_pre_stage(k, C, l)
    scan_stage(k, C, l, need_ctx)
    dn_post_stage(k, C, l, need_ctx)
    rw_post_stage(k, C, l, need_ctx)


def dn_scan_setup(k, C, need_ctx, sh):
    X = Lane()
    X.S = C.S
    X.need_ctx = need_ctx
    X.tri, X.id8, X.idn = sh.tri, sh.id8, sh.idn
    X.mneg = k.sb("mneg", [64, 2, 8, 64], F32)
    k.dma("sync", X.mneg[:], C.I("c_mneg")[:])
    X.St = k.sb("dSt", [64, 8, 64], F32)
    X.Stb = k.sb("dStb", [64, 8, 64], BF16)
    k.memset(X.St[:], 0.0)
    k.memset(X.Stb[:], 0.0)
    X.pps = Pool(k, "dpp", [64, 8, 64], F32, sh.npp_dn, space="ps")
    X.psm = sh.psm_dn
    X.sbp = Pool(k, "dw", [64, 8, 64], F32, 14)
    X.sbb = Pool(k, "dwb", [64, 8, 64], BF16, 9)
    X.inv_f = Pool(k, "divf", [64, 8, 64], F32, 8)
    X.inv_b = Pool(k, "divb", [64, 8, 64], BF16, 2)
    X.qkp = Pool(k, "qk8", [64, 2, 8, 64], F32, 2)
    X.v8p = Pool(k, "v8", [64, 8, 64], F32, 2)
    X.gbp = Pool(k, "gb8", [64, 2, 8], F32, 2)
    X.smp = Pool(k, "dsm", [64, 8], F32, 8)
    return X


def dn_step(k, X, s):
    S = X.S
    tri, mneg, id8, idn, St, Stb = X.tri, X.mneg, X.id8, X.idn, X.St, X.Stb
    pps, psm, sbp, sbb, smp = X.pps, X.psm, X.sbp, X.sbb, X.smp
    ch = (FORD[s], BORD[s])
    qk8, v8, gb8 = X.qkp.get(), X.v8p.get(), X.gbp.get()
    for d in range(2):
        r0 = ch[d] * 64
        es = slice(d * 4, d * 4 + 4)
        k.dma("sync", qk8[:, :, es, :], V(S["DNQKV"].t[r0:r0 + 64, 0:512].rearrange("p (a h d) -> p a h d", a=2, h=4), [S["DNQKV"].tok]))
        k.dma("sync", v8[:, es, :], V(S["DNQKV"].t[r0:r0 + 64, 512:768].rearrange("p (h d) -> p h d", h=4), [S["DNQKV"].tok]))
        k.dma("sync", gb8[:, 0, es], S["DNGB"][r0:r0 + 64, d * 4:d * 4 + 4])
        k.dma("sync", gb8[:, 1, es], S["DNGB"][r0:r0 + 64, 8 + d * 4:8 + d * 4 + 4])
    beta_bc = gb8[:, 1, :, None].bc([64, 8, 64])
    for d in range(2):
        k.mm(psm[:, d * 4:d * 4 + 4], tri[:, d, :], gb8[:, 0, d * 4:d * 4 + 4])
    gcc = smp.get()
    k.copy(gcc[:], psm[:], eng="scalar")
    grep = sbp.get()
    k.copy(grep[:], gb8[:, 0, :, None].bc([64, 8, 64]))
    pgcb = pps.get()
    for e in range(8):
        k.mm(pgcb[:, e, :], grep[:, e, :], tri[:, e // 4, :])
    gcb = sbp.get()
    k.copy(gcb[:], pgcb[:], eng="scalar")
    D1 = sbp.get()
    k.tt(D1[:], gcb[:], gcc[:, :, None].bc([64, 8, 64]), ALU.subtract)
    egcb = sbp.get()
    k.act(egcb[:], gcb[:], AF.Exp)
    t = sbp.get()
    k.tt(t[:], D1[:], mneg[:, 0], ALU.add, eng="gpsimd")
    E1 = sbp.get()
    k.act(E1[:], t[:], AF.Exp)
    t = sbp.get()
    k.stt(t[:], D1[:], -1.0, mneg[:, 1], ALU.mult, ALU.add)
    E2 = sbp.get()
    k.act(E2[:], t[:], AF.Exp)
    egc = smp.get()
    k.act(egc[:], gcc[:], AF.Exp)
    ekd = smp.get()
    k.act(ekd[:, 0:4], D1[:, 0:4, 63], AF.Exp)
    k.act(ekd[:, 4:8], D1[:, 4:8, 0], AF.Exp)
    pKT = pps.get()
    for e in range(8):
        k.tr(pKT[:, e, :], qk8[:, 1, e, :], idn)
    KT = sbb.get()
    k.copy(KT[:], pKT[:], eng="scalar")
    pQT = pps.get()
    for e in range(8):
        k.tr(pQT[:, e, :], qk8[:, 0, e, :], idn)
    QT = sbb.get()
    k.copy(QT[:], pQT[:], eng="vector")
    pG = pps.get()
    for e in range(8):
        k.mm(pG[:, e, :], KT[:, e, :], KT[:, e, :])
    t = sbp.get()
    k.tt(t[:], pG[:], E2[:], ALU.mult)
    M = sbp.get()
    k.tt(M[:], t[:], beta_bc, ALU.mult, eng="gpsimd")
    pQK = pps.get()
    for e in range(8):
        k.mm(pQK[:, e, :], KT[:, e, :], QT[:, e, :])
    attnT = sbb.get()
    k.tt(attnT[:], pQK[:], E1[:], ALU.mult)
    pA = pps.get()
    for e in range(8):
        k.tr(pA[:, e, :], M[:, e, :], idn)
    A = sbp.get()
    k.copy(A[:], pA[:], eng="scalar")
    _, Xi = tri_inverse(k, M, A, id8, pps, X.inv_f, X.inv_b)
    Vb = sbb.get()
    k.tt(Vb[:], v8[:], beta_bc, ALU.mult, eng="gpsimd")
    bg = smp.get()
    k.tt(bg[:], gb8[:, 1, :], egc[:], ALU.mult)
    KBG = sbb.get()
    k.tt(KBG[:], qk8[:, 1], bg[:, :, None].bc([64, 8, 64]), ALU.mult, eng="gpsimd")
    pU = pps.get()
    for e in range(8):
        k.mm(pU[:, e, :], Xi[:, e, :], Vb[:, e, :])
    U = sbp.get()
    k.copy(U[:], pU[:], eng="scalar")
    pW = pps.get()
    for e in range(8):
        k.mm(pW[:, e, :], KBG[:, e, :], Xi[:, e, :])
    WT = sbb.get()
    k.copy(WT[:], pW[:], eng="vector")
    QdT = sbb.get()
    k.tt(QdT[:], QT[:], egcb[:], ALU.mult, eng="gpsimd")
    Kd = sbb.get()
    k.tt(Kd[:], qk8[:, 1], ekd[:, :, None].bc([64, 8, 64]), ALU.mult, eng="gpsimd")
    glast = smp.get()
    k.copy(glast[:, 0:4], egcb[:, 0:4, 63], eng="gpsimd")
    k.copy(glast[:, 4:8], egcb[:, 4:8, 0], eng="gpsimd")
    pv = pps.get()
    for e in range(8):
        k.mm(pv[:, e, :], WT[:, e, :], Stb[:, e, :])
    vnew = sbb.get()
    k.tt(vnew[:], U[:], pv[:], ALU.subtract)
    po = pps.get()
    for e in range(8):
        k.mm(po[:, e, :], QdT[:, e, :], Stb[:, e, :], start=True, stop=False)
        k.mm(po[:, e, :], attnT[:, e, :], vnew[:, e, :], start=False, stop=True)
    pS = pps.get()
    for e in range(8):
        k.mm(pS[:, e, :], Kd[:, e, :], vnew[:, e, :])
    t = sbp.get()
    k.tt(t[:], St[:], glast[:, :, None].bc([64, 8, 64]), ALU.mult, eng="gpsimd")
    k.tt(St[:], t[:], pS[:], ALU.add)
    k.copy(Stb[:], St[:], eng="gpsimd")
    o8 = sbp.get()
    k.copy(o8[:], po[:], eng="scalar")
    for d in range(2):
        if X.need_ctx or ch[d] >= 4:
            k.dma("scalar", S["DNO"][d, ch[d] * 64:(ch[d] + 1) * 64, :], o8[:, d * 4:d * 4 + 4, :].re("p h d -> p (h d)"))


def rw_scan_setup(k, C, need_ctx, sh):
    X = Lane()
    psl = sh.rw_psl
    X.psl = psl
    X.S = C.S
    X.need_ctx = need_ctx

    def const(name, shape, dtype=F32):
        full = [128] + list(shape[1:]) if psl.start else list(shape)
        return TV(k.sb(name, full, dtype), psl)
    X.tri = const("rtri", [64, 2, 64])
    X.id8 = const("rid8", [64, 8, 64])
    X.m01 = const("m01", [64, 3, 8, 64])
    X.nm = const("nm", [64, 2, 8, 64])
    X.ones64 = const("ones64", [64, 64])
    idf = const("ridf", [64, 64])
    X.idb = const("ridb", [64, 64], BF16)
    k.dma("sync", X.tri[:], C.I("c_tri")[:])
    k.dma("sync", X.id8[:], C.I("c_ident8")[:])
    k.dma("sync", X.m01[:], C.I("c_m01")[:])
    k.dma("sync", idf[:], C.I("identD")[0:64, 0:64])
    k.copy(X.idb[:], idf[:])
    k.memset(X.ones64[:], 1.0)
    k.ts(X.nm[:], X.m01[:, 1:3], -1.0, ALU.mult)
    X.St = const("rSt", [64, 8, 64])
    X.Stb = const("rStb", [64, 8, 64], BF16)
    k.memset(X.St[:], 0.0)
    k.memset(X.Stb[:], 0.0)
    X.pps = PoolV(k, "rpp", [64, 8, 64], F32, sh.npp_rw, psl, space="ps")
    X.ppb = PoolV(k, "rppb", [64, 16, 64], BF16, 1, psl, space="ps")
    X.psm = PoolV(k, "rpsm", [64, 8, 2], F32, 1, psl, space="ps").get()
    X.sbp = PoolV(k, "rw", [64, 8, 64], F32, 24, psl)
    X.sbb = PoolV(k, "rwb", [64, 8, 64], BF16, 9, psl)
    X.inv_f = PoolV(k, "rivf", [64, 8, 64], F32, 8, psl)
    X.inp_ = [PoolV(k, f"rin{i}", [64, 8, 64], F32, 2, psl) for i in range(6)]
    X.smp = PoolV(k, "rsm", [64, 8], F32, 4, psl)
    return X


RW_SLOTS = ((0, 0), (1, 1), (2, 2), (3, 4), (5, 6), (7, 8))


def rw_step(k, X, s):
    S = X.S
    tri, m01, nm, id8, idb, ones64, St, Stb = X.tri, X.m01, X.nm, X.id8, X.idb, X.ones64, X.St, X.Stb
    pps, ppb, psm, sbp, sbb, smp = X.pps, X.ppb, X.psm, X.sbp, X.sbb, X.smp
    RWP = S["RWP"]
    ch = (FORD[s], BORD[s])
    tl = [p.get() for p in X.inp_]
    for i, sl in enumerate(RW_SLOTS):
        for d in range(2):
            r0 = ch[d] * 64
            k.dma("sync", tl[i][:, d * 4:d * 4 + 4, :],
                  V(RWP.t[r0:r0 + 64, sl[d], :].rearrange("p (h d) -> p h d", h=4), [RWP.tok]))
    R8, V8, A8, LW8, K8, B8 = tl
    plc = pps.get()
    for d in range(2):
        k.mm(plc[:, d * 4:d * 4 + 4, :], tri[:, d, :], LW8[:, d * 4:d * 4 + 4, :])
    lc = sbp.get()
    k.copy(lc[:], plc[:], eng="scalar")
    ptot = pps.get()
    k.mm(ptot[:], ones64[:], LW8[:])
    tot = sbp.get()
    k.copy(tot[:], ptot[:], eng="vector")
    for e in range(8):
        k.mm(psm[:, e, :], LW8[:, e, :], ones64[:, 0:2])
    gCT = smp.get()
    k.act(gCT[:], psm[:, :, 0], AF.Exp)
    eg, egi, egp, ehat = sbp.get(), sbp.get(), sbp.get(), sbp.get()
    k.act(eg[:], lc[:], AF.Exp)
    k.act(egi[:], lc[:], AF.Exp, scale=-1.0)
    t = sbp.get()
    k.tt(t[:], lc[:], LW8[:], ALU.subtract, eng="gpsimd")
    k.act(egp[:], t[:], AF.Exp)
    t = sbp.get()
    k.tt(t[:], tot[:], lc[:], ALU.subtract)
    k.act(ehat[:], t[:], AF.Exp)
    At, Bh, Kh = sbp.get(), sbp.get(), sbp.get()
    Atb, Btb, Ktb, Rtb = sbb.get(), sbb.get(), sbb.get(), sbb.get()
    k.tt(At[:], A8[:], egp[:], ALU.mult)
    k.copy(Atb[:], At[:], eng="gpsimd")
    k.tt(Btb[:], B8[:], egi[:], ALU.mult, eng="gpsimd")
    k.tt(Ktb[:], K8[:], egi[:], ALU.mult)
    k.tt(Rtb[:], R8[:], eg[:], ALU.mult, eng="gpsimd")
    k.tt(Bh[:], B8[:], ehat[:], ALU.mult)
    k.tt(Kh[:], K8[:], ehat[:], ALU.mult, eng="gpsimd")
    fmT = []
    for i, src in enumerate((Atb, Btb, Ktb, Rtb)):
        pT = ppb.get()
        for e in range(8):
            k.tr(pT[:, e, :], src[:, e, :], idb[:])
        dst = sbb.get()
        k.copy(dst[:], pT[:, 0:8, :], eng=("scalar" if i % 2 == 0 else "vector"))
        fmT.append(dst)
    AtT, BtT, KtT, RtT = fmT

    def score(lhs, rhs, mask):
        p_ = pps.get()
        for e in range(8):
            k.mm(p_[:, e, :], lhs[:, e, :], rhs[:, e, :])
        o_ = sbp.get()
        k.tt(o_[:], p_[:], mask, ALU.mult)
        return o_
    M = score(AtT, BtT, nm[:, 1])
    A = score(BtT, AtT, nm[:, 0])
    AakT = score(KtT, AtT, m01[:, 1])
    ArbT = score(BtT, RtT, m01[:, 0])
    ArkT = score(KtT, RtT, m01[:, 0])
    Xi, _ = tri_inverse(k, M, A, id8, pps, X.inv_f, None)
    pW = pps.get()
    for e in range(8):
        k.mm(pW[:, e, :], At[:, e, :], Xi[:, e, :])
    WT = sbp.get()
    k.copy(WT[:], pW[:], eng="scalar")
    pAkV = pps.get()
    for e in range(8):
        k.mm(pAkV[:, e, :], AakT[:, e, :], V8[:, e, :])
    AkV = sbp.get()
    k.copy(AkV[:], pAkV[:], eng="vector")
    pUv = pps.get()
    for e in range(8):
        k.mm(pUv[:, e, :], Xi[:, e, :], AkV[:, e, :])
    Uv = sbp.get()
    k.copy(Uv[:], pUv[:], eng="scalar")
    pe = pps.get()
    for e in range(8):
        k.mm(pe[:, e, :], WT[:, e, :], St[:, e, :])
    E = sbp.get()
    k.tt(E[:], Uv[:], pe[:], ALU.add)
    py = pps.get()
    for e in range(8):
        k.mm(py[:, e, :], RtT[:, e, :], Stb[:, e, :], start=True, stop=False)
        k.mm(py[:, e, :], ArbT[:, e, :], E[:, e, :], start=False, stop=False)
        k.mm(py[:, e, :], ArkT[:, e, :], V8[:, e, :], start=False, stop=True)
    y8 = sbp.get()
    k.copy(y8[:], py[:], eng="scalar")
    for d in range(2):
        if X.need_ctx or ch[d] >= 4:
            k.dma("scalar", S["RWY"][d, ch[d] * 64:(ch[d] + 1) * 64, :], y8[:, d * 4:d * 4 + 4, :].re("p h d -> p (h d)"))
    pS = pps.get()
    for e in range(8):
        k.mm(pS[:, e, :], Bh[:, e, :], E[:, e, :], start=True, stop=False)
        k.mm(pS[:, e, :], Kh[:, e, :], V8[:, e, :], start=False, stop=True)
    t = sbp.get()
    k.tt(t[:], St[:], gCT[:, :, None].bc([64, 8, 64]), ALU.mult, eng="gpsimd")
    k.tt(St[:], t[:], pS[:], ALU.add)
    k.copy(Stb[:], St[:], eng="gpsimd")


def scan_stage(k, C, l, need_ctx, do_dn=True, do_rw=True):
    with k.stage():
        sh = Lane()
        sh.tri = k.sb("tri", [64, 2, 64], F32)
        sh.id8 = k.sb("id8", [64, 8, 64], F32)
        k.dma("sync", sh.tri[:], C.I("c_tri")[:])
        k.dma("sync", sh.id8[:], C.I("c_ident8")[:])
        sh.idn = C.ident[0:64, 0:64]
        import os
        both = do_dn and do_rw
        sh.npp_dn = 3 if both else 7
        sh.npp_rw = 2 if both else 5
        sh.rw_psl = slice(64, 128) if os.environ.get("RW_HI", "1") == "1" else slice(0, 64)
        psm = k.ps("psm", [64, 8], F32)
        sh.psm_dn = psm[:]
        Xd = dn_scan_setup(k, C, need_ctx, sh) if do_dn else None
        Xr = rw_scan_setup(k, C, need_ctx, sh) if do_rw else None
        for s in range(NCH):
            lists = []
            if do_dn:
                with k.defer() as La:
                    dn_step(k, Xd, s)
                lists.append(La)
            if do_rw:
                with k.defer() as Lb:
                    rw_step(k, Xr, s)
                lists.append(Lb)
            k.replay(lists)
```

```python
import contextlib
import numpy as np
import concourse.bass as bass
import concourse.mybir as mybir
from concourse.bass_utils import run_bass_kernel_spmd

F32 = mybir.dt.float32
BF16 = mybir.dt.bfloat16
I32 = mybir.dt.int32
AF = mybir.ActivationFunctionType
ALU = mybir.AluOpType
AX = mybir.AxisListType

ENGS = ("tensor", "vector", "scalar", "gpsimd", "sync")
N_DSEM = 12
SEM_EPOCH = 1 << 40


class Tok:
    __slots__ = ("name", "w", "r")

    def __init__(self, name):
        self.name = name
        self.w = None
        self.r = []


class T:
    def __init__(self, k, name, t, space):
        self.k = k
        self.name = name
        self.t = t
        self.space = space
        self.tok = Tok(name)
        self.subs = {}

    def __getitem__(self, key):
        return V(self.t[key] if not isinstance(self.t, bass.AP) else self.t[key], [self.tok])

    def v(self):
        return self[:]

    def sub(self, key, idx):
        if key not in self.subs:
            self.subs[key] = Tok(f"{self.name}.{key}")
        return V(self.t[idx], [self.subs[key]])


class V:
    __slots__ = ("ap", "toks")

    def __init__(self, ap, toks):
        self.ap = ap
        self.toks = toks

    def __getitem__(self, key):
        return V(self.ap[key], self.toks)

    def re(self, s, **kw):
        return V(self.ap.rearrange(s, **kw), self.toks)

    def bc(self, shape):
        return V(self.ap.to_broadcast(shape), self.toks)

    def bitcast(self, dt):
        return V(self.ap.bitcast(dt), self.toks)

    def with_toks(self, toks):
        return V(self.ap, toks)


def _ap(x):
    return x.ap if isinstance(x, V) else x


class K:
    def __init__(self):
        self.nc = bass.Bass("TRN2", target_bir_lowering=False)
        self.stack = contextlib.ExitStack()
        self.q = {e: [] for e in ENGS}
        self.cnt = {e: 0 for e in ENGS}
        self.epoch = {e: 0 for e in ENGS}
        self.sems = {}
        self.seen = {e: {} for e in ENGS}
        self.dsem = {}
        self.dcnt = {}
        self.dq_i = {e: 0 for e in ENGS}
        self.uid = 0
        self.out_deps = []
        self.stage_stack = None

    def _name(self, n):
        self.uid += 1
        return f"{n}_{self.uid}"

    def sb(self, name, shape, dtype, stack=None):
        t = (stack or self.stage_stack or self.stack).enter_context(self.nc.sbuf_tensor(self._name(name), list(shape), dtype))
        return T(self, name, t, "sb")

    def ps(self, name, shape, dtype=F32, stack=None):
        t = (stack or self.stage_stack or self.stack).enter_context(self.nc.psum_tensor(self._name(name), list(shape), dtype))
        return T(self, name, t, "ps")

    def dram(self, name, shape, dtype, kind="Internal"):
        t = self.nc.dram_tensor(name, list(shape), dtype, kind=kind).ap()
        return T(self, name, t, "dram")

    def _sem(self, key):
        if key not in self.sems:
            self.sems[key] = self.stack.enter_context(self.nc.semaphore(self._name("s")))
        return self.sems[key]

    def _deps(self, reads, writes, pe_accum=False):
        deps = []
        for v in reads:
            for t in v.toks:
                if t.w is not None:
                    deps.append(t.w)
        for v in writes:
            for t in v.toks:
                if t.w is not None and not pe_accum:
                    deps.append(t.w)
                if not pe_accum:
                    deps.extend(t.r)
        return deps

    def _commit(self, me, reads, writes, pe_accum=False):
        for v in reads:
            for t in v.toks:
                t.r.append(me)
        for v in writes:
            for t in v.toks:
                t.w = me
                if not pe_accum:
                    t.r = []

    def _waits(self, eng, deps):
        need = {}
        for (sk, val) in deps:
            if self.seen[eng].get(sk, 0) >= val:
                continue
            if need.get(sk, 0) < val:
                need[sk] = val
        for sk, val in need.items():
            self.seen[eng][sk] = val
        return list(need.items())

    @contextlib.contextmanager
    def defer(self):
        prev = getattr(self, "_defer", None)
        lst = []
        self._defer = lst
        try:
            yield lst
        finally:
            self._defer = prev

    def replay(self, lists):
        lists = [l for l in lists if l]
        pos = [0] * len(lists)
        total = sum(len(l) for l in lists)
        last_pe = -1

        def is_pe(i):
            return pos[i] < len(lists[i]) and lists[i][pos[i]][0] == "op" and lists[i][pos[i]][1][0] == "tensor"
        for _ in range(total):
            live = [i for i in range(len(lists)) if pos[i] < len(lists[i])]
            pe = [i for i in live if is_pe(i)]
            if len(pe) >= 2:
                cand = [i for i in pe if i != last_pe]
                j = min(cand, key=lambda i: pos[i] / len(lists[i]))
            else:
                j = min(live, key=lambda i: pos[i] / len(lists[i]))
            kind, a, kw = lists[j][pos[j]]
            if kind == "op" and a[0] == "tensor":
                last_pe = j
            pos[j] += 1
            getattr(self, kind)(*a, **kw)

    def op(self, eng, fn, reads=(), writes=(), pe_accum=False, same_eng_sync=True):
        if getattr(self, "_defer", None) is not None:
            self._defer.append(("op", (eng, fn), dict(reads=reads, writes=writes, pe_accum=pe_accum, same_eng_sync=same_eng_sync)))
            return None
        reads = [r for r in reads if isinstance(r, V)]
        writes = [w for w in writes if isinstance(w, V)]
        deps = self._deps(reads, writes, pe_accum)
        if eng == "tensor" or not same_eng_sync:
            deps = [d for d in deps if d[0][0] != eng or d[0][0] == "dma"]
        if self.cnt[eng] >= SEM_EPOCH:
            self.epoch[eng] += 1
            self.cnt[eng] = 0
        sk = (eng, self.epoch[eng])
        self._sem(sk)
        self.cnt[eng] += 1
        me = (sk, self.cnt[eng])
        waits = self._waits(eng, deps)
        self.q[eng].append((fn, waits, (sk, 1), self.cnt[eng]))
        self._commit(me, reads, writes, pe_accum)
        return me

    def dma(self, eng, out, in_, **kw):
        if getattr(self, "_defer", None) is not None:
            self._defer.append(("dma", (eng, out, in_), dict(kw)))
            return None
        deps = self._deps([in_], [out])
        i = self.dq_i[eng]
        self.dq_i[eng] += 1
        sk = ("dma", eng, i % N_DSEM)
        self._sem(sk)
        prev = self.dcnt.get(sk, 0)
        if prev:
            deps.append((sk, prev))
        self.dcnt[sk] = prev + 16
        me = (sk, prev + 16)
        waits = self._waits(eng, deps)
        o, a = _ap(out), _ap(in_)
        self.q[eng].append((lambda e: e.dma_start(out=o, in_=a, **kw), waits, (sk, 16), None))
        self._commit(me, [in_], [out])
        return me

    def barrier(self):
        alld = []
        for e in ENGS:
            for ep in range(self.epoch[e] + 1):
                sk = (e, ep)
                if sk in self.sems:
                    alld.append((sk, self.cnt[e] if ep == self.epoch[e] else SEM_EPOCH))
        for sk, v in self.dcnt.items():
            alld.append((sk, v))
        for e in ENGS:
            w = self._waits(e, alld)
            if w:
                self.q[e].append((None, w, None, None))

    def flush(self):
        import bisect
        nc = self.nc
        q = self.q
        sems = self.sems
        if not hasattr(self, "base_idx"):
            self.base_idx, self.base_val = {}, {}
        targets = {}
        for ename in ENGS:
            for (fn, waits, inc, idx) in q[ename]:
                for (sk, val) in waits:
                    if sk[0] != "dma":
                        targets.setdefault(sk, set()).add(val)
        for ename in ENGS:
            sk = (ename, self.epoch[ename])
            if sk in sems:
                targets.setdefault(sk, set()).add(self.cnt[ename])
        tl = {}
        for sk, st in targets.items():
            b = self.base_idx.get(sk, 0)
            tl[sk] = sorted(v for v in st if v > b)

        def val_of(sk, v):
            b = self.base_idx.get(sk, 0)
            bv = self.base_val.get(sk, 0)
            if v <= b:
                return bv
            return bv + bisect.bisect_left(tl[sk], v) + 1

        with nc.Block() as block:
            for ename in ENGS:
                ops = q[ename]
                if not ops:
                    continue

                def body(e, ops=ops):
                    for (fn, waits, inc, idx) in ops:
                        for (sk, val) in waits:
                            if sk[0] == "dma":
                                e.wait_ge(sems[sk], val)
                            else:
                                e.wait_ge(sems[sk], val_of(sk, val))
                        if fn is not None:
                            ins = fn(e)
                            if inc is not None:
                                if inc[0][0] == "dma":
                                    ins.then_inc(sems[inc[0]], inc[1])
                                else:
                                    lst = tl.get(inc[0], ())
                                    j = bisect.bisect_left(lst, idx)
                                    if j < len(lst) and lst[j] == idx:
                                        ins.then_inc(sems[inc[0]], 1)
                getattr(block, ename)(body)
        for sk, lst in tl.items():
            self.base_val[sk] = self.base_val.get(sk, 0) + len(lst)
            if sk[0] != "dma":
                self.base_idx[sk] = self.cnt[sk[0]]
        self.q = {e: [] for e in ENGS}

    @contextlib.contextmanager
    def stage(self, name=None):
        st = contextlib.ExitStack()
        prev = self.stage_stack
        self.stage_stack = st
        self.stage_i = getattr(self, "stage_i", 0) + 1
        with st:
            yield st
            self.barrier()
            if getattr(self, "scopes", False):
                import inspect
                nm = name or inspect.stack()[2].function
                with self.nc.named_scope(f"s{self.stage_i:03d}_{nm}"):
                    self.flush()
            else:
                self.flush()
        self.stage_stack = prev

    def mm(self, out, lhsT, rhs, start=True, stop=True, **kw):
        o, l, r = _ap(out), _ap(lhsT), _ap(rhs)
        return self.op("tensor", lambda e: e.matmul(o, l, r, start=start, stop=stop, **kw),
                       reads=[lhsT, rhs], writes=[out], pe_accum=not start)

    def tr(self, out, in_, ident):
        o, i, d = _ap(out), _ap(in_), _ap(ident)
        return self.op("tensor", lambda e: e.transpose(o, i, d), reads=[in_, ident], writes=[out])

    def act(self, out, in_, func, bias=None, scale=None, accum_out=None, eng="scalar"):
        o, i = _ap(out), _ap(in_)
        kw = {}
        rd = [in_]
        if bias is not None:
            kw["bias"] = _ap(bias)
            rd.append(bias)
        if scale is not None:
            kw["scale"] = _ap(scale)
            rd.append(scale)
        wr = [out]
        if accum_out is not None:
            kw["accum_out"] = _ap(accum_out)
            wr.append(accum_out)
        return self.op("scalar", lambda e: e.activation(o, i, func, **kw), reads=rd, writes=wr)

    def tt(self, out, in0, in1, op, eng="vector"):
        o, a, b = _ap(out), _ap(in0), _ap(in1)
        return self.op(eng, lambda e: e.tensor_tensor(o, a, b, op), reads=[in0, in1], writes=[out])

    def ts(self, out, in0, s1, op0, s2=None, op1=None, eng="vector", accum_out=None):
        o, a = _ap(out), _ap(in0)
        x1, x2 = _ap(s1), _ap(s2)
        kw = {}
        if op1 is not None:
            kw["op1"] = op1
        wr = [out]
        if accum_out is not None:
            kw["accum_out"] = _ap(accum_out)
            wr.append(accum_out)
        return self.op(eng, lambda e: e.tensor_scalar(o, a, x1, x2, op0, **kw), reads=[in0, s1, s2], writes=wr)

    def stt(self, out, in0, scalar, in1, op0, op1):
        o, a, s, b = _ap(out), _ap(in0), _ap(scalar), _ap(in1)
        return self.op("vector", lambda e: e.scalar_tensor_tensor(o, a, s, b, op0, op1), reads=[in0, scalar, in1], writes=[out])

    def copy(self, out, in_, eng="vector"):
        o, i = _ap(out), _ap(in_)
        if eng == "scalar":
            return self.op("scalar", lambda e: e.copy(o, i), reads=[in_], writes=[out])
        return self.op(eng, lambda e: e.tensor_copy(o, i), reads=[in_], writes=[out])

    def memset(self, out, val, eng="vector"):
        o = _ap(out)
        return self.op(eng, lambda e: e.memset(o, val), reads=[], writes=[out])

    def recip(self, out, in_):
        o, i = _ap(out), _ap(in_)
        return self.op("vector", lambda e: e.reciprocal(o, i), reads=[in_], writes=[out])

    def reduce(self, out, in_, op=None, axis=None, **kw):
        o, i = _ap(out), _ap(in_)
        op = op or ALU.add
        axis = axis or AX.X
        return self.op("vector", lambda e: e.tensor_reduce(o, i, axis, op, **kw), reads=[in_], writes=[out])

D = 1024
SEQ = 2048
CTX = 256
NT = SEQ + CTX
DFF = 2816
NL = 2
EPS = 1e-6
TILES = [(0, 256, 1)] + [(256 + 512 * i, 512, 0) for i in range(4)]
QS = [(0, 6), (6, 6), (12, 5), (17, 5)]


class Pool:
    def __init__(self, k, name, shape, dtype, n, space="sb", stack=None):
        mk = k.sb if space == "sb" else k.ps
        self.bufs = [mk(f"{name}{i}", shape, dtype, stack=stack) for i in range(n)]
        self.i = 0

    def get(self):
        b = self.bufs[self.i % len(self.bufs)]
        self.i += 1
        return b


class Ctx:
    pass


class TV:
    def __init__(self, t, psl):
        self.t, self.psl = t, psl
        self.tok = t.tok

    def __getitem__(self, key):
        if not isinstance(key, tuple):
            key = (key,)
        assert key[0] == slice(None), key
        return V(self.t.t[(self.psl,) + tuple(key[1:])], [self.t.tok])


class PoolV:
    def __init__(self, k, name, shape, dtype, n, psl, space="sb"):
        mk = k.sb if space == "sb" else k.ps
        full = [128] + list(shape[1:]) if psl.start else list(shape)
        self.bufs = [TV(mk(f"{name}{i}", full, dtype), psl) for i in range(n)]
        self.i = 0

    def get(self):
        b = self.bufs[self.i % len(self.bufs)]
        self.i += 1
        return b


def xt_view(C, t0, n):
    toks = [C.XT.subtok(i) for i, (s, sz, w) in enumerate(TILES) if s < t0 + n and t0 < s + sz]
    ap = C.XT.t.rearrange("(c p) t -> p c t", p=128)[:, :, t0:t0 + n]
    return V(ap, toks)


IN_SHAPES = {
    "xT": [D, NT], "cT": [128, 8, 2], "b_modT": [NL, 128, 72], "norm_wT": [NL, 128, 24],
    "w_mod": [NL, D, 9 * D], "ffn_w_gu": [NL, 2, D, 2 * DFF], "ffn_w_down": [NL, 2, DFF, D],
    "w_in": [NL, D, 3472], "w_out": [NL, D, D], "identD": [128, 128],
}


class LazyIn:
    def __init__(self, k, C):
        self.k, self.C, self.d = k, C, {}

    def __call__(self, name):
        if name not in self.d:
            self.d[name] = self.k.dram(name, IN_SHAPES[name], IN_DTYPES.get(name, F32), kind="ExternalInput")
        return self.d[name]


def declare_io(k, C, debug_out=()):
    C.I = LazyIn(k, C)
    C.XT = C.I("xT")
    C.XT.subtok = lambda i: C.XT.subs.setdefault(i, Tok(f"XT.{i}"))
    declare_scratch(k, C)
    if debug_out == "gin":
        IN_SHAPES["g_in"] = [NT, 1024]
        C.S["G"] = C.I("g_in")
    C.OUT = k.dram("outT", [D, SEQ], F32, kind="ExternalOutput")
    C.DBG = k.dram("dbgT", [D, NT], F32, kind="ExternalOutput") if debug_out else None


def setup_consts(k, C):
    C.ones_bf = k.sb("ones_bf", [128, 128], BF16, stack=k.stack)
    k.memset(C.ones_bf[:], 1.0)
    C.eps_t = k.sb("eps_t", [128, 1], F32, stack=k.stack)
    k.memset(C.eps_t[:], EPS)
    C.ident = k.sb("ident", [128, 128], F32, stack=k.stack)
    k.dma("sync", C.ident[:], C.I("identD")[:])
    C.P = [k.sb(f"P{l}", [128, 9, 8, 2], F32, stack=k.stack) for l in range(NL)]


def mod_stage(k, C):
    with k.stage():
        ct = k.sb("ct", [128, 8, 2], F32)
        sc = k.sb("sc", [128, 8, 2], F32)
        k.dma("sync", ct[:], C.I("cT")[:])
        k.act(sc[:], ct[:], AF.Silu)
        wpool = Pool(k, "wm", [128, 8, 512], F32, 3)
        for l in range(NL):
            bm = k.sb(f"bm{l}", [128, 72], F32)
            nw = k.sb(f"nw{l}", [128, 24], F32)
            k.dma("sync", bm[:], C.I("b_modT")[l])
            k.dma("sync", nw[:], C.I("norm_wT")[l])
            pm = k.ps(f"pm{l}", [128, 72, 2], F32)
            wsrc = C.I("w_mod").t[l].rearrange("(kc p) n -> p kc n", p=128)
            for g in range(18):
                wm = wpool.get()
                k.dma("sync", wm[:], V(wsrc[:, :, g * 512:(g + 1) * 512], [C.I("w_mod").tok]))
                for c4 in range(4):
                    ci = g * 4 + c4
                    for kc in range(8):
                        k.mm(pm[:, ci, :], wm[:, kc, c4 * 128:(c4 + 1) * 128], sc[:, kc, :],
                             start=(kc == 0), stop=(kc == 7))
            P = C.P[l]
            Pv = P[:].re("p a c w -> p (a c) w")
            k.tt(Pv, pm[:], bm[:, :, None].bc([128, 72, 2]), ALU.add)
            for s in range(3):
                k.stt(P[:, 3 * s + 1], P[:, 3 * s + 1], 1.0,
                      nw[:, s * 8:(s + 1) * 8, None].bc([128, 8, 2]), ALU.add, ALU.mult)
                if s != 1:
                    k.ts(P[:, 3 * s + 2], P[:, 3 * s + 2], 0.5, ALU.mult)


def hv_(hall, ti, c, t0, n):
    return hall.sub(ti, (slice(None), c, slice(t0, t0 + n)))


def norm_pass(k, C, P, s, hall, xpool, sq, pss, rs, tmpp, skip_ctx=False, only=None):
    for ti, (t0, n, w) in enumerate(TILES):
        if w == 1 and skip_ctx:
            continue
        if only is not None and ti != only:
            continue
        xt = xpool.get()
        k.dma("sync", xt[:, :, :n], xt_view(C, t0, n))
        k.act(sq[:, :, :n], xt[:, :, :n], AF.Square)
        for c in range(8):
            k.mm(pss[:, :n], C.ones_bf[:], sq[:, c, :n], start=(c == 0), stop=(c == 7))
        k.act(rs[:, :n], pss[:, :n], AF.Sqrt, scale=1.0 / D, bias=C.eps_t[:])
        k.recip(rs[:, :n], rs[:, :n])
        for c in range(8):
            tmp = tmpp.get()
            k.stt(tmp[:, :n], xt[:, c, :n], P[:, 3 * s + 1, c, w:w + 1], rs[:, :n], ALU.mult, ALU.mult)
            k.act(hv_(hall, ti, c, t0, n), tmp[:, :n], AF.Identity, bias=P[:, 3 * s, c, w:w + 1])


def ffn_tile(k, C, P, s, hall, ti, t0, n, w, nj, wg, wd, actp, pgp, pup, pdp, sgp, xpool):
    act = actp.get()
    for jj in range(nj):
        pg = pgp.get()
        pu = pup.get()
        for kc in range(8):
            k.mm(pg[:, :n], wg[:, kc, 0, jj * 128:(jj + 1) * 128], hv_(hall, ti, kc, t0, n),
                 start=(kc == 0), stop=(kc == 7))
        for kc in range(8):
            k.mm(pu[:, :n], wg[:, kc, 1, jj * 128:(jj + 1) * 128], hv_(hall, ti, kc, t0, n),
                 start=(kc == 0), stop=(kc == 7))
        sg = sgp.get()
        k.act(sg[:, :n], pg[:, :n], AF.Silu)
        k.tt(act[:, jj, :n], sg[:, :n], pu[:, :n], ALU.mult)
    xr = xpool.get()
    k.dma("sync", xr[:, :, :n], xt_view(C, t0, n))
    for m in range(8):
        pd = pdp.get()
        for jj in range(nj):
            k.mm(pd[:, :n], wd[:, jj, m * 128:(m + 1) * 128], act[:, jj, :n],
                 start=(jj == 0), stop=(jj == nj - 1))
        k.stt(xr[:, m, :n], pd[:, :n], P[:, 3 * s + 2, m, w:w + 1], xr[:, m, :n], ALU.mult, ALU.add)
    k.dma("sync", xt_view(C, t0, n), xr[:, :, :n])


def ffn_stage(k, C, l, f, s, skip_ctx=False):
    P = C.P[l]
    Wgu = C.I("ffn_w_gu").t[l, f].rearrange("(kc p) n -> p kc n", p=128)
    Wdn = C.I("ffn_w_down").t[l, f].rearrange("(j p) n -> p j n", p=128)
    with k.stage():
        hall = k.sb("hall", [128, 8, NT], BF16)
        xpool = Pool(k, "xt", [128, 8, 512], F32, 2)
        sq = k.sb("sq", [128, 8, 512], BF16)
        tmpp = Pool(k, "tmp", [128, 512], F32, 2)
        rs = k.sb("rs", [128, 512], F32)
        pss = k.ps("pss", [128, 512], F32)
        wgp = Pool(k, "wg", [128, 8, 2, 6 * 128], BF16, 2)
        wdp = Pool(k, "wd", [128, 6, D], BF16, 2)
        actp = Pool(k, "act", [128, 6, 512], BF16, 2)
        sgp = Pool(k, "sg", [128, 512], F32, 2)
        pgp = Pool(k, "pg", [128, 512], F32, 2, space="ps")
        pup = Pool(k, "pu", [128, 512], F32, 2, space="ps")
        pdp = Pool(k, "pd", [128, 512], F32, 2, space="ps")

        def hv(ti, c, t0, n):
            return hall.sub(ti, (slice(None), c, slice(t0, t0 + n)))

        norm_pass(k, C, P, s, hall, xpool, sq, pss, rs, tmpp, skip_ctx)
        for qi, (j0, nj) in enumerate(QS):
            wg = wgp.get()
            wd = wdp.get()
            k.dma("gpsimd", wg[:, :, 0, :nj * 128], V(Wgu[:, :, j0 * 128:(j0 + nj) * 128], [C.I("ffn_w_gu").tok]))
            k.dma("gpsimd", wg[:, :, 1, :nj * 128], V(Wgu[:, :, DFF + j0 * 128:DFF + (j0 + nj) * 128], [C.I("ffn_w_gu").tok]))
            k.dma("gpsimd", wd[:, :nj, :], V(Wdn[:, j0:j0 + nj, :], [C.I("ffn_w_down").tok]))
            for ti, (t0, n, w) in enumerate(TILES):
                if w == 1 and skip_ctx:
                    continue
                ffn_tile(k, C, P, s, hall, ti, t0, n, w, nj, wg, wd, actp, pgp, pup, pdp, sgp, xpool)
                continue
                act = actp.get()
                for jj in range(nj):
                    pg = pgp.get()
                    pu = pup.get()
                    for kc in range(8):
                        k.mm(pg[:, :n], wg[:, kc, 0, jj * 128:(jj + 1) * 128], hv(ti, kc, t0, n),
                             start=(kc == 0), stop=(kc == 7))
                    for kc in range(8):
                        k.mm(pu[:, :n], wg[:, kc, 1, jj * 128:(jj + 1) * 128], hv(ti, kc, t0, n),
                             start=(kc == 0), stop=(kc == 7))
                    sg = sgp.get()
                    k.act(sg[:, :n], pg[:, :n], AF.Silu)
                    k.tt(act[:, jj, :n], sg[:, :n], pu[:, :n], ALU.mult)
                xr = xpool.get()
                k.dma("sync", xr[:, :, :n], xt_view(C, t0, n))
                for m in range(8):
                    pd = pdp.get()
                    for jj in range(nj):
                        k.mm(pd[:, :n], wd[:, jj, m * 128:(m + 1) * 128], act[:, jj, :n],
                             start=(jj == 0), stop=(jj == nj - 1))
                    k.stt(xr[:, m, :n], pd[:, :n], P[:, 3 * s + 2, m, w:w + 1], xr[:, m, :n], ALU.mult, ALU.add)
                k.dma("sync", xt_view(C, t0, n), xr[:, :, :n])


PT = 3472
TM_GROUPS = [
    (0, 512, "HY", 0), (512, 256, "HY", 512),
    (1280, 256, "NAV", 0),
    (1536, 512, "DN", 0), (2048, 512, "DN", 512), (2560, 16, "DN", 1024),
    (2576, 512, "RW", 0), (3088, 256, "RW", 512),
]
FM_CHUNKS = [(768, 128, "NAQK", 0), (896, 128, "NAQK", 1), (1024, 128, "NAQK", 2), (1152, 128, "NAQK", 3),
             (3344, 32, "RWLR", 0), (3376, 32, "RWLR", 1), (3408, 64, "RWLR", 2)]


def declare_scratch(k, C):
    def mk(name, shape):
        kind = "ExternalOutput" if name in C.dbg_names else "Internal"
        return k.dram("s_" + name.lower(), shape, F32, kind=kind)
    C.S = {
        "HY": mk("HY", [NT, 768]),
        "NAV": mk("NAV", [NT, 256]),
        "DN": mk("DN", [NT, 1040]),
        "RW": mk("RW", [NT, 768]),
        "NAQK": mk("NAQK", [512, NT]),
        "RWLR": mk("RWLR", [3, 64, NT]),
        "RWP": mk("RWP", [NT, 10, 256]),
        "RWY": mk("RWY", [2, NT, 256]),
        "G": mk("G", [NT, 1024]),
        "DNQKV": mk("DNQKV", [NT, 768]),
        "DNGB": mk("DNGB", [NT, 16]),
        "DNO": mk("DNO", [2, NT, 256]),
    }


def proj_stage(k, C, l):
    P = C.P[l]
    Win = C.I("w_in").t[l].rearrange("(kc p) n -> p kc n", p=128)
    with k.stage():
        hall = k.sb("hall", [128, 8, NT], BF16)
        xpool = Pool(k, "xt", [128, 8, 512], F32, 2)
        sq = k.sb("sq", [128, 8, 512], BF16)
        tmpp = Pool(k, "tmp", [128, 512], F32, 2)
        rs = k.sb("rs", [128, 512], F32)
        pss = k.ps("pss", [128, 512], F32)
        win = k.sb("win", [128, 8, PT], BF16)
        for kc in range(8):
            k.dma("gpsimd", win[:, kc, :], V(Win[:, kc, :], [C.I("w_in").tok]))
        norm_pass(k, C, P, 1, hall, xpool, sq, pss, rs, tmpp)
        pp = Pool(k, "pp", [128, 512], F32, 4, space="ps")
        rowp = Pool(k, "row", [128, 2832], F32, 2)
        fmp = Pool(k, "fm", [128, 7, 512], F32, 2)
        ev = 0
        for tc in range(NT // 128):
            t0 = tc * 128
            ti = 0 if t0 < 256 else 1 + (t0 - 256) // 512
            row = rowp.get()
            off = 0
            offs = []
            for (c0, nc_, name, dcol) in TM_GROUPS:
                ps = pp.get()
                for kc in range(8):
                    k.mm(ps[:, :nc_], hv_(hall, ti, kc, t0, 128), win[:, kc, c0:c0 + nc_],
                         start=(kc == 0), stop=(kc == 7))
                k.copy(row[:, off:off + nc_], ps[:, :nc_], eng=("scalar" if ev % 2 else "vector"))
                ev += 1
                offs.append((off, nc_, name, dcol))
                off += nc_
            for name in ("HY", "NAV", "DN", "RW"):
                gs = [g for g in offs if g[2] == name]
                o0 = gs[0][0]
                tot = sum(g[1] for g in gs)
                k.dma("sync", C.S[name][t0:t0 + 128, 0:tot], row[:, o0:o0 + tot])
        for ti, (t0, n, w) in enumerate(TILES):
            fm = fmp.get()
            for i, (c0, ncl, name, di) in enumerate(FM_CHUNKS):
                ps = pp.get()
                for kc in range(8):
                    k.mm(ps[0:ncl, :n], win[:, kc, c0:c0 + ncl], hv_(hall, ti, kc, t0, n),
                         start=(kc == 0), stop=(kc == 7))
                k.copy(fm[0:ncl, i, :n], ps[0:ncl, :n], eng=("scalar" if ev % 2 else "vector"))
                ev += 1
            k.dma("sync", V(C.S["NAQK"].t.rearrange("(c p) t -> p c t", p=128)[:, :, t0:t0 + n], [C.S["NAQK"].tok]),
                  fm[:, 0:4, :n])
            for g_, ncl in ((0, 32), (1, 32), (2, 64)):
                k.dma("sync", C.S["RWLR"][g_, 0:ncl, t0:t0 + n], fm[0:ncl, 4 + g_, :n])


def outproj_stage(k, C, l, need_ctx):
    P = C.P[l]
    Wout = C.I("w_out").t[l].rearrange("(kc p) n -> p kc n", p=128)
    with k.stage():
        wo = k.sb("wo", [128, 8, D], BF16)
        k.dma("gpsimd", wo[:], V(Wout, [C.I("w_out").tok]))
        gpool = Pool(k, "gt", [128, D], F32, 3)
        gT = Pool(k, "gT", [128, 8, 512], BF16, 2)
        xpool = Pool(k, "xt", [128, 8, 512], F32, 2)
        ptp = Pool(k, "ptp", [128, 4, 128], F32, 2, space="ps")
        pyp = Pool(k, "py", [128, 512], F32, 2, space="ps")
        ev = 0
        for ti, (t0, n, w) in enumerate(TILES):
            if w == 1 and not need_ctx:
                continue
            g = gT.get()
            for sc_ in range(n // 128):
                gt = gpool.get()
                k.dma("sync", gt[:], C.S["G"][t0 + sc_ * 128:t0 + (sc_ + 1) * 128, :])
                for half in range(2):
                    pt = ptp.get()
                    for q in range(4):
                        c = half * 4 + q
                        k.tr(pt[:, q, :], gt[:, c * 128:(c + 1) * 128], C.ident[:])
                    k.copy(g[:, half * 4:(half + 1) * 4, sc_ * 128:(sc_ + 1) * 128], pt[:],
                           eng=("scalar" if ev % 2 else "vector"))
                    ev += 1
            xr = xpool.get()
            k.dma("sync", xr[:, :, :n], xt_view(C, t0, n))
            for m in range(8):
                py = pyp.get()
                for kc in range(8):
                    k.mm(py[:, :n], wo[:, kc, m * 128:(m + 1) * 128], g[:, kc, :n], start=(kc == 0), stop=(kc == 7))
                k.stt(xr[:, m, :n], py[:, :n], P[:, 5, m, w:w + 1], xr[:, m, :n], ALU.mult, ALU.add)
            k.dma("gpsimd", xt_view(C, t0, n), xr[:, :, :n])


def out_stage(k, C):
    with k.stage():
        xpool = Pool(k, "xo", [128, 8, 512], F32, 2)
        last = []
        for i in range(4):
            xo = xpool.get()
            k.dma("sync", xo[:], xt_view(C, 256 + 512 * i, 512))
            dst = V(C.OUT.t.rearrange("(c p) t -> p c t", p=128)[:, :, 512 * i:512 * (i + 1)], [C.OUT.tok])
            k.dma("sync", dst, xo[:])
        if C.DBG is not None:
            for ti, (t0, n, w) in enumerate(TILES):
                xo = xpool.get()
                k.dma("sync", xo[:, :, :n], xt_view(C, t0, n))
                dst = V(C.DBG.t.rearrange("(c p) t -> p c t", p=128)[:, :, t0:t0 + n], [C.DBG.tok])
                k.dma("sync", dst, xo[:, :, :n])


def build(plan=None, dbg_names=(), scopes=False):
    k = K()
    k.scopes = scopes
    C = Ctx()
    C.dbg_names = set(dbg_names)
    declare_io(k, C, debug_out=(False if plan in (None, 'full') else ("gin" if plan == "m3" else True)))
    with k.stage():
        setup_consts(k, C)
    mod_stage(k, C)
    plan = plan or "full"
    if plan == "full":
        for l in range(NL):
            need_ctx = l < NL - 1
            ffn_stage(k, C, l, 0, 0)
            proj_stage(k, C, l)
            for nm in ("hy", "na", "dnrw"):
                globals()[nm + "_stage"](k, C, l, need_ctx)
            outproj_stage(k, C, l, need_ctx)
            ffn_stage(k, C, l, 1, 2, skip_ctx=not need_ctx)
    if plan == "ffn1":
        ffn_stage(k, C, 0, 0, 0)
    if plan == "m1":
        proj_stage(k, C, 0)
    if plan == "m3":
        outproj_stage(k, C, 0, True)
    if plan.startswith("mix:"):
        proj_stage(k, C, 0)
        for nm in plan[4:].split(","):
            globals()[nm + "_stage"](k, C, 0, True)
    out_stage(k, C)
    return k, C


def host_prep(inp, b, used):
    f = np.float32
    m = {}
    for name in used:
        if name == "xT":
            v = np.concatenate([inp["ctx"][b], inp["x"][b]], axis=0).T
        elif name == "cT":
            v = np.stack([inp["c"][b].reshape(8, 128).T, inp["c_ctx"].reshape(8, 128).T], axis=-1)
        elif name == "b_modT":
            v = inp["b_mod"].reshape(NL, 72, 128).transpose(0, 2, 1)
        elif name == "norm_wT":
            v = inp["norm_w"].reshape(NL, 24, 128).transpose(0, 2, 1)
        elif name == "identD":
            v = np.eye(128, dtype=f)
        elif name in HOST_LAYOUT:
            v = HOST_LAYOUT[name](inp, b)
        else:
            v = inp[name]
        m[name] = np.ascontiguousarray(np.asarray(v, dtype=(ml_dtypes.bfloat16 if name in IN_DTYPES else f)))
    return m


HOST_LAYOUT = {}


def run(inputs, plan=None, dbg_names=(), extra=None, ncores=8):
    k, C = build(plan, dbg_names)
    extra = extra or {}
    used = [n for n in C.I.d.keys() if n not in extra]
    in_maps = [host_prep(inputs, b, used) for b in range(ncores)]
    for m in in_maps:
        m.update(extra)
    res = run_bass_kernel_spmd(k.nc, in_maps, core_ids=list(range(ncores)))
    return res


def kernel(**inputs):
    inputs = {kk: np.asarray(v) for kk, v in inputs.items()}
    res = run(inputs)
    out = np.stack([np.ascontiguousarray(r["outT"].T) for r in res.results], axis=0)
    return out.astype(np.float32)


IN_SHAPES["na_nT"] = [NL, 128, 2]
IN_SHAPES["na_biasT"] = [NL, 128, 8, 4, 4, 64]


def _na_nT(inp, b):
    return np.stack([np.tile(inp["na_q_norm"], (1, 2)), np.tile(inp["na_k_norm"], (1, 2))], axis=-1)


def _na_biasT(inp, b):
    a = np.arange(2)[:, None, None, None, None]
    kcol = np.arange(64)[None, :, None, None, None]
    rho = np.arange(8)[None, None, :, None, None]
    i = np.arange(4)[None, None, None, :, None]
    qcol = np.arange(64)[None, None, None, None, :]
    drow = 2 * i + a - rho + 0 * kcol + 0 * qcol
    dcol = np.clip(kcol - qcol, -15, 15) + 0 * drow
    cs = np.clip(qcol - 8, 0, 48)
    inwin = ((kcol >= cs) & (kcol < cs + 16)) & (drow > -100)
    rpb = inp["na_rpb"]
    g = rpb[:, :, drow + 7, dcol + 15]
    g = np.where(inwin[None, None], g, np.float32(-30000.0))
    g = g.transpose(0, 2, 3, 4, 1, 5, 6).reshape(NL, 128, 8, 4, 4, 64)
    return g


HOST_LAYOUT["na_nT"] = _na_nT
HOST_LAYOUT["na_biasT"] = _na_biasT


def na_stage(k, C, l, need_ctx):
    with contextlib.ExitStack() as outer:
        na_stage_(k, C, l, need_ctx, outer)


def na_stage_(k, C, l, need_ctx, outer):
    E = k.sb("E", [128, 8, 4, 4, 64], BF16, stack=outer)
    QK = k.sb("QK", [128, 4, NT], BF16, stack=outer)
    QM = k.sb("QM", [128, 2, 2, NT], BF16, stack=outer)
    Ve = k.sb("Ve", [128, 18, 4, 65], BF16, stack=outer)
    Vo = k.sb("Vo", [128, 17, 4, 65], BF16, stack=outer)
    with k.stage():
        bd = k.sb("bd", [128, 128], BF16)
        k.memset(bd[:], 0.0)
        k.memset(bd[0:64, 0:64], 1.0)
        k.memset(bd[64:128, 64:128], 1.0)
        nw = k.sb("nw", [128, 2], F32)
        k.dma("sync", nw[:], C.I("na_nT")[l])
        k.memset(QM[:], 0.0, eng="gpsimd")
        for half in range(2):
            bt = k.sb(f"bt{half}", [128, 4, 4, 4, 64], F32)
            k.dma("sync", bt[:], C.I("na_biasT")[l, :, half * 4:(half + 1) * 4])
            k.act(E[:, half * 4:(half + 1) * 4], bt[:], AF.Exp)
        qkp = Pool(k, "qk", [128, 4, 512], F32, 2)
        sqp = Pool(k, "sq", [128, 4, 512], BF16, 2)
        rnp = Pool(k, "rn", [128, 512], F32, 2)
        pss = Pool(k, "pss", [128, 512], F32, 2, space="ps")
        src = C.S["NAQK"].t.rearrange("(c p) t -> p c t", p=128)
        for ti, (t0, n, w) in enumerate(TILES):
            qk = qkp.get()
            k.dma("sync", qk[:, :, :n], V(src[:, :, t0:t0 + n], [C.S["NAQK"].tok]))
            sq = sqp.get()
            k.act(sq[:, :, :n], qk[:, :, :n], AF.Square)
            for c in range(4):
                ps = pss.get()
                k.mm(ps[:, :n], bd[:], sq[:, c, :n])
                rn = rnp.get()
                k.act(rn[:, :n], ps[:, :n], AF.Sqrt, scale=1.0 / 64, bias=C.eps_t[:])
                k.recip(rn[:, :n], rn[:, :n])
                if c >= 2:
                    k.stt(QK[:, c, t0:t0 + n], qk[:, c, :n], nw[:, 1:2], rn[:, :n], ALU.mult, ALU.mult)
                else:
                    for par in range(2):
                        pp_ = slice(par * 64, par * 64 + 64)
                        k.stt(QM[pp_, c, par, t0:t0 + n], qk[pp_, c, :n], nw[pp_, 0:1], rn[pp_, :n], ALU.mult, ALU.mult)
        k.memset(Ve[:], 1.0)
        k.memset(Vo[:], 1.0, eng="gpsimd")
        nav = C.S["NAV"]
        vst = k.sb("vst", [128, 18, 256], F32)
        vso = k.sb("vso", [128, 17, 256], F32)
        for c0 in range(0, 18, 6):
            c1 = min(18, c0 + 6)
            k.dma("sync", vst[:, c0:c1, :], V(nav.t[c0 * 128:c1 * 128, :].rearrange("(c p) d -> p c d", p=128), [nav.tok]))
            c1o = min(17, c0 + 6)
            k.dma("sync", vso[:, c0:c1o, :], V(nav.t[64 + c0 * 128:64 + c1o * 128, :].rearrange("(c p) d -> p c d", p=128), [nav.tok]))
        k.copy(Ve[:, :, :, 0:64], vst[:].re("p c (h d) -> p c h d", d=64), eng="vector")
        k.copy(Vo[:, :, :, 0:64], vso[:].re("p c (h d) -> p c h d", d=64), eng="gpsimd")
    import os
    if os.environ.get("NA_STOP") == "1":
        return
    with k.stage():
        psp = Pool(k, "ps", [128, 4, 6, 64], F32, 2, space="ps")
        pop = Pool(k, "po", [64, 4, 65], F32, 2, space="ps")
        Pp = Pool(k, "P", [128, 4, 6, 64], BF16, 2)
        rdp = Pool(k, "rd", [64, 4], F32, 2)
        orp = Pool(k, "orow", [64, 256], F32, 3)
        G = C.S["G"]
        for r in range(32):
            start = min(max(r - 4, 0), 24)
            rho = r - start
            qt0 = 256 + 64 * r
            ps = psp.get()
            for h in range(4):
                hp = slice((h % 2) * 64, (h % 2) * 64 + 64)
                for i in range(6):
                    kt0 = (256 + 64 * (start + 2 * i)) if i < 4 else (i - 4) * 128
                    k.mm(ps[:, h, i, :], QK[:, 2 + h // 2, kt0:kt0 + 128], QM[:, h // 2, h % 2, qt0:qt0 + 64])
            Pt = Pp.get()
            k.act(Pt[:], ps[:], AF.Exp, scale=0.125)
            k.tt(Pt[:, :, 0:4, :], Pt[:, :, 0:4, :], E[:, rho], ALU.mult)
            po = pop.get()
            for h in range(4):
                for i in range(6):
                    if i < 4:
                        r0 = start + 2 * i
                        vv = Ve[:, 2 + r0 // 2, h, :] if r0 % 2 == 0 else Vo[:, (r0 + 3) // 2, h, :]
                    else:
                        vv = Ve[:, i - 4, h, :]
                    k.mm(po[:, h, :], Pt[:, h, i, :], vv, start=(i == 0), stop=(i == 5))
            rd = rdp.get()
            k.recip(rd[:], po[:, :, 64])
            orow = orp.get()
            k.tt(orow[:].re("p (h d) -> p h d", d=64), po[:, :, 0:64], rd[:, :, None].bc([64, 4, 64]), ALU.mult)
            k.dma("sync", G[qt0:qt0 + 64, 256:512], orow[:])
    if os.environ.get("NA_STOP") == "2":
        return
    with k.stage():
        G = C.S["G"]
        if need_ctx:
            pcp = Pool(k, "pc", [128, 2, 256], F32, 2, space="ps")
            pocp = Pool(k, "poc", [128, 65], F32, 2, space="ps")
            Pcp = Pool(k, "Pc", [128, 2, 256], BF16, 2)
            oc = k.sb("oc", [128, 2, 256], F32)
            rdc = Pool(k, "rdc", [128, 1], F32, 2)
            for h in range(4):
                hp = slice((h % 2) * 64, (h % 2) * 64 + 64)
                pc = pcp.get()
                for kc in range(2):
                    k.mm(pc[:, kc, :], QK[:, 2 + h // 2, kc * 128:(kc + 1) * 128], QM[:, h // 2, h % 2, 0:256])
                Pc = Pcp.get()
                k.act(Pc[:], pc[:], AF.Exp, scale=0.125)
                for qc in range(2):
                    poc = pocp.get()
                    for kc in range(2):
                        k.mm(poc[:], Pc[:, kc, qc * 128:(qc + 1) * 128], Ve[:, kc, h, :], start=(kc == 0), stop=(kc == 1))
                    rd = rdc.get()
                    k.recip(rd[:], poc[:, 64:65])
                    k.ts(oc[:, qc, h * 64:(h + 1) * 64], poc[:, 0:64], rd[:, 0:1], ALU.mult)
            k.dma("sync", V(G.t[0:256, 256:512].rearrange("(c p) n -> p c n", p=128), [G.tok]), oc[:])


import math
import ml_dtypes

HY_BANDS = 16


def _hy_consts(n):
    f64 = np.float64
    N = 2 * n
    nch = n // 128
    t = np.linspace(0.0, 1.0, n, dtype=np.float32).astype(f64)
    ang = 2.0 * math.pi * np.arange(n, dtype=f64) / n
    bands = np.linspace(1e-4, HY_BANDS - 1, HY_BANDS, dtype=np.float32).astype(f64)[None]
    z = np.concatenate([t[:, None], np.cos(bands * ang[:, None]), -np.sin(bands * ang[:, None])], axis=-1)
    max_decay = math.log(1e-2) / 0.3
    min_decay = math.log(1e-2) / 1.5
    deltas = np.abs(np.linspace(min_decay, max_decay, 512, dtype=np.float32).astype(f64))
    idx = np.arange(n, dtype=f64) + 0.5
    ph = 2.0 * math.pi * np.outer(idx, idx) / N
    C4 = np.cos(ph)
    S4 = np.sin(ph)

    def tile_(M):
        return np.ascontiguousarray(M.reshape(nch, 128, nch, 128).transpose(2, 1, 0, 3).reshape(nch, 128, nch * 128))
    w = 2.0 * math.pi * idx / N
    cs = np.stack([(2.0 / N) * np.cos(w / 2), (2.0 / N) * np.sin(w / 2), -(2.0 / N) * np.cos(w / 2)], axis=-1)
    return {
        "zT": np.ascontiguousarray(z.T.astype(np.float32)),
        "ntcol": np.ascontiguousarray((-t).reshape(nch, 128).T.astype(np.float32)),
        "absdelta": np.ascontiguousarray(np.tile(deltas[None, :], (128, 1)).astype(np.float32)),
        "C4t": tile_(C4).astype(ml_dtypes.bfloat16),
        "S4t": tile_(S4).astype(ml_dtypes.bfloat16),
        "cs": np.ascontiguousarray(cs.reshape(nch, 128, 3).transpose(1, 0, 2).astype(np.float32)),
    }


_HYC = {}


def hy_const(n, name):
    if n not in _HYC:
        _HYC[n] = _hy_consts(n)
    return _HYC[n][name]


IN_DTYPES = {}
for _n in (SEQ, CTX):
    _nch = _n // 128
    IN_SHAPES[f"hy_zT_{_n}"] = [33, _n]
    IN_SHAPES[f"hy_ntcol_{_n}"] = [128, _nch]
    IN_SHAPES[f"hy_cs_{_n}"] = [128, _nch, 3]
    IN_SHAPES[f"hy_C4t_{_n}"] = [_nch, 128, _nch * 128]
    IN_SHAPES[f"hy_S4t_{_n}"] = [_nch, 128, _nch * 128]
    IN_DTYPES[f"hy_C4t_{_n}"] = BF16
    IN_DTYPES[f"hy_S4t_{_n}"] = BF16
    for _nm in ("zT", "ntcol", "cs", "C4t", "S4t"):
        HOST_LAYOUT[f"hy_{_nm}_{_n}"] = (lambda inp, b, _n=_n, _nm=_nm: hy_const(_n, _nm))
IN_SHAPES["hy_absdelta"] = [128, 512]
HOST_LAYOUT["hy_absdelta"] = lambda inp, b: hy_const(CTX, "absdelta")
IN_SHAPES["hy_fcol"] = [NL, 64, 3]
HOST_LAYOUT["hy_fcol"] = lambda inp, b: np.stack([inp["hy_f_b1"], inp["hy_f_b2"], inp["hy_f_freq"]], axis=-1)
for _nm, _sh in (("hy_f_w1", [NL, 33, 64]), ("hy_f_w2", [NL, 64, 64]), ("hy_f_w3", [NL, 64, 1024]),
                 ("hy_conv", [NL, 3, 768]), ("hy_bias", [NL, 2, 256])):
    IN_SHAPES[_nm] = _sh


def bc_load(k, dst, src_ap, tok, eng="sync"):
    P = dst.ap.shape[0]
    k.dma(eng, dst, V(src_ap.partition_broadcast(P), [tok]))


def sin_rr(k, out, in_ps, scale_col, bias_col, pools, npi, nfree):
    a1, a2, ai = pools
    t1 = a1.get()
    t2 = a2.get()
    ti = ai.get()
    P = out.ap.shape[0]
    k.act(t1[:P, :nfree], in_ps, AF.Identity, scale=scale_col, bias=bias_col)
    k.ts(t2[:P, :nfree], t1[:P, :nfree], 1.0 / (2.0 * math.pi), ALU.mult, 64.5, ALU.add)
    k.copy(ti[:P, :nfree], t2[:P, :nfree])
    k.tt(t1[:P, :nfree], t2[:P, :nfree], ti[:P, :nfree], ALU.subtract)
    k.stt(t2[:P, :nfree], t1[:P, :nfree], 0.0, t1[:P, :nfree], ALU.is_lt, ALU.add)
    k.act(out, t2[:P, :nfree], AF.Sin, scale=2.0 * math.pi, bias=npi[:P, 0:1])


def hyf_stage(k, C, l, n, KRI):
    nch = n // 128
    TW = min(512, n)
    with contextlib.ExitStack() as outer:
        hyf_stage_(k, C, l, n, KRI, nch, TW, outer)


def hyf_stage_(k, C, l, n, KRI, nch, TW, outer):
    Pb = k.sb("Pb", [128, nch, 512], BF16, stack=outer)
    Qb = k.sb("Qb", [128, nch, 512], BF16, stack=outer)
    cs = k.sb("cs", [128, nch, 3], F32, stack=outer)
    with k.stage():
        w1 = k.sb("w1", [33, 64], F32)
        w2 = k.sb("w2", [64, 64], F32)
        w3 = k.sb("w3", [64, 1024], F32)
        fcol = k.sb("fcol", [64, 3], F32)
        fb = k.sb("fb", [64, 2], F32)
        zT = k.sb("zT", [33, n], F32)
        npi = k.sb("npi", [128, 1], F32)
        ones_f = k.sb("ones_f", [128, 128], F32)
        k.memset(npi[:], -math.pi)
        k.memset(ones_f[:], 1.0)
        k.dma("sync", w1[:], C.I("hy_f_w1")[l])
        k.dma("sync", w2[:], C.I("hy_f_w2")[l])
        k.dma("sync", w3[:], C.I("hy_f_w3")[l])
        k.dma("sync", fcol[:], C.I("hy_fcol")[l])
        k.dma("sync", zT[:], C.I(f"hy_zT_{n}")[:])
        k.tt(fb[:], fcol[:, 0:2], fcol[:, 2:3].bc([64, 2]), ALU.mult)
        hid1 = k.sb("hid1", [64, n], F32)
        hid2 = k.sb("hid2", [64, n], F32)
        pools = (Pool(k, "a1", [64, 512], F32, 2), Pool(k, "a2", [64, 512], F32, 2), Pool(k, "ai", [64, 512], I32, 2))
        pmp = Pool(k, "pm", [64, 512], F32, 2, space="ps")
        for t0 in range(0, n, TW):
            ps = pmp.get()
            k.mm(ps[:, :TW], w1[:], zT[:, t0:t0 + TW])
            sin_rr(k, hid1[:, t0:t0 + TW], ps[:, :TW], fcol[:, 2:3], fb[:, 0:1], pools, npi, TW)
        for t0 in range(0, n, TW):
            ps = pmp.get()
            k.mm(ps[:, :TW], w2[:], hid1[:, t0:t0 + TW])
            sin_rr(k, hid2[:, t0:t0 + TW], ps[:, :TW], fcol[:, 2:3], fb[:, 1:2], pools, npi, TW)
        absd = k.sb("absd", [128, 512], F32)
        ntc = k.sb("ntc", [128, nch], F32)
        k.dma("sync", absd[:], C.I("hy_absdelta")[:])
        k.dma("sync", ntc[:], C.I(f"hy_ntcol_{n}")[:])
        k.dma("sync", cs[:], C.I(f"hy_cs_{n}")[:])
        HF = k.sb("HF", [128, nch, 512], F32)
        HB = k.sb("HB", [128, nch, 512], F32)
        php = Pool(k, "ph", [128, 512], F32, 3, space="ps")
        pl1 = k.ps("pl1", [128, 512], F32)
        winp = Pool(k, "win", [128, 512], F32, 2)
        absp = Pool(k, "abs", [128, 512], F32, 3)
        for c in range(nch):
            ph0 = php.get()
            ph1 = php.get()
            k.mm(ph0[:], hid2[:, c * 128:(c + 1) * 128], w3[:, 0:512])
            k.mm(ph1[:], hid2[:, c * 128:(c + 1) * 128], w3[:, 512:1024])
            win = winp.get()
            k.act(win[:], absd[:], AF.Exp, scale=ntc[:, c:c + 1])
            k.tt(HF[:, c, :], ph0[:], win[:], ALU.mult)
            k.tt(HB[:, c, :], ph1[:], win[:], ALU.mult)
            if c == 0:
                k.memset(HB[0:1, 0, :], 0.0)
            for j, src in enumerate((HF, HB)):
                ab = absp.get()
                k.act(ab[:], src[:, c, :], AF.Abs)
                k.mm(pl1[:], ones_f[:], ab[:], start=(c == 0 and j == 0), stop=(c == nch - 1 and j == 1))
        rn = k.sb("rn", [128, 512], F32)
        k.recip(rn[:], pl1[:])
        for c in range(nch):
            t = absp.get()
            k.tt(t[:], HF[:, c, :], HB[:, c, :], ALU.add, eng="gpsimd")
            k.tt(Pb[:, c, :], t[:], rn[:], ALU.mult, eng="gpsimd")
            t = absp.get()
            k.tt(t[:], HF[:, c, :], HB[:, c, :], ALU.subtract)
            k.tt(Qb[:, c, :], t[:], rn[:], ALU.mult)
    with k.stage():
        absp = Pool(k, "abs2", [128, 512], F32, 3)
        cbp = Pool(k, "cb", [128, nch, 128], BF16, 2)
        sbp = Pool(k, "sbk", [128, nch, 128], BF16, 2)
        psp = Pool(k, "psq", [128, 512], F32, 8, space="ps")
        kop = Pool(k, "ko", [128, 2, 512], F32, 2)
        C4 = C.I(f"hy_C4t_{n}")
        S4 = C.I(f"hy_S4t_{n}")
        for fc in range(nch):
            cb = cbp.get()
            sb_ = sbp.get()
            k.dma("sync", cb[:].re("p c m -> p (c m)"), C4[fc])
            k.dma("sync", sb_[:].re("p c m -> p (c m)"), S4[fc])
            pPc, pPs, pQc, pQs = psp.get(), psp.get(), psp.get(), psp.get()
            for c in range(nch):
                st, sp = (c == 0), (c == nch - 1)
                k.mm(pPc[:], cb[:, c, :], Pb[:, c, :], start=st, stop=sp)
                k.mm(pPs[:], sb_[:, c, :], Pb[:, c, :], start=st, stop=sp)
                k.mm(pQc[:], cb[:, c, :], Qb[:, c, :], start=st, stop=sp)
                k.mm(pQs[:], sb_[:, c, :], Qb[:, c, :], start=st, stop=sp)
            ko = kop.get()
            t = absp.get()
            k.ts(t[:], pPc[:], cs[:, fc, 0:1], ALU.mult)
            k.stt(ko[:, 0, :], pPs[:], cs[:, fc, 1:2], t[:], ALU.mult, ALU.add)
            t = absp.get()
            k.ts(t[:], pQc[:], cs[:, fc, 1:2], ALU.mult)
            k.stt(ko[:, 1, :], pQs[:], cs[:, fc, 2:3], t[:], ALU.mult, ALU.add)
            k.dma("gpsimd", KRI[fc * 128:(fc + 1) * 128], ko[:])


def hyc_stage(k, C, l, n, base, KRI):
    nch = n // 128
    HYs = C.S["HY"]
    G = C.S["G"]
    with k.stage():
        cw = k.sb("cw", [128, 3, 768], F32)
        hb = k.sb("hb", [128, 2, 256], F32)
        for i in range(3):
            bc_load(k, cw[:, i, :], C.I("hy_conv").t[l, i], C.I("hy_conv").tok)
        for i in range(2):
            bc_load(k, hb[:, i, :], C.I("hy_bias").t[l, i], C.I("hy_bias").tok)
        v = k.sb("v", [128, nch, 256], F32)
        x1 = k.sb("x1", [128, nch, 256], F32)
        x2 = k.sb("x2", [128, nch, 256], F32)
        vb = k.sb("vb", [128, nch, 256], BF16)
        z = k.sb("z", [128, nch, 256], F32)
        zb = k.sb("zb", [128, nch, 256], BF16)
        Yc = k.sb("Yc", [128, nch, 256], BF16)
        Ys = k.sb("Ys", [128, nch, 256], BF16)
        sp_ = Pool(k, "scur", [128, 768], F32, 2)
        pp_ = Pool(k, "sprev", [128, 768], F32, 2)
        np_ = Pool(k, "snext", [128, 768], F32, 2)
        up = Pool(k, "u", [128, 768], F32, 2)
        tp = Pool(k, "tt", [128, 768], F32, 2)
        for c in range(nch):
            r0 = base + c * 128
            sc_, sp, sn = sp_.get(), pp_.get(), np_.get()
            k.dma("sync", sc_[:], HYs[r0:r0 + 128, :])
            if c == 0:
                k.memset(sp[0:1, :], 0.0)
                k.dma("sync", sp[1:128, :], HYs[r0:r0 + 127, :])
            else:
                k.dma("sync", sp[:], HYs[r0 - 1:r0 + 127, :])
            if c == nch - 1:
                k.memset(sn[:], 0.0)
                k.dma("sync", sn[0:127, :], HYs[r0 + 1:r0 + 128, :])
            else:
                k.dma("sync", sn[:], HYs[r0 + 1:r0 + 129, :])
            u = up.get()
            t1 = tp.get()
            t2 = tp.get()
            k.tt(u[:], sc_[:], cw[:, 1, :], ALU.mult)
            k.tt(t1[:], sp[:], cw[:, 0, :], ALU.mult, eng="gpsimd")
            k.tt(t2[:], sn[:], cw[:, 2, :], ALU.mult, eng="gpsimd")
            k.tt(u[:], u[:], t1[:], ALU.add)
            k.tt(v[:, c, :], u[:, 0:256], t2[:, 0:256], ALU.add)
            k.tt(x1[:, c, :], u[:, 256:512], t2[:, 256:512], ALU.add)
            k.tt(x2[:, c, :], u[:, 512:768], t2[:, 512:768], ALU.add)
            k.copy(vb[:, c, :], v[:, c, :], eng="scalar")
        cbp = Pool(k, "cb", [128, nch, 128], BF16, 3)
        sbp = Pool(k, "sbk", [128, nch, 128], BF16, 3)
        pup = Pool(k, "pU", [128, 256], F32, 4, space="ps")
        pyp = Pool(k, "pY", [128, 256], F32, 2, space="ps")
        krp = Pool(k, "kr", [128, 2, 512], F32, 2)
        usp = Pool(k, "us", [128, 2, 256], F32, 2)
        tp2 = Pool(k, "t2", [128, 256], F32, 6)
        outp = Pool(k, "yo", [128, 256], F32, 2)
        C4 = C.I(f"hy_C4t_{n}")
        S4 = C.I(f"hy_S4t_{n}")
        for o in range(2):
            src = vb if o == 0 else zb
            for fc in range(nch):
                cb = cbp.get()
                sb_ = sbp.get()
                k.dma("sync", cb[:].re("p c m -> p (c m)"), C4[fc])
                k.dma("sync", sb_[:].re("p c m -> p (c m)"), S4[fc])
                pUc, pUs = pup.get(), pup.get()
                for c in range(nch):
                    st, sp = (c == 0), (c == nch - 1)
                    k.mm(pUc[:], cb[:, c, :], src[:, c, :], start=st, stop=sp)
                    k.mm(pUs[:], sb_[:, c, :], src[:, c, :], start=st, stop=sp)
                kr = krp.get()
                k.dma("sync", kr[:], KRI[fc * 128:(fc + 1) * 128])
                us = usp.get()
                k.copy(us[:, 0, :], pUc[:], eng="scalar")
                k.copy(us[:, 1, :], pUs[:], eng="scalar")
                KR = kr[:, 0, o * 256:(o + 1) * 256]
                KI = kr[:, 1, o * 256:(o + 1) * 256]
                a, b_, c_, d_ = tp2.get(), tp2.get(), tp2.get(), tp2.get()
                k.tt(a[:], us[:, 0, :], KR, ALU.mult)
                k.tt(b_[:], us[:, 1, :], KI, ALU.mult, eng="gpsimd")
                k.tt(Yc[:, fc, :], a[:], b_[:], ALU.add)
                k.tt(c_[:], us[:, 1, :], KR, ALU.mult, eng="gpsimd")
                k.tt(d_[:], us[:, 0, :], KI, ALU.mult)
                k.tt(Ys[:, fc, :], c_[:], d_[:], ALU.subtract, eng="gpsimd")
            for tc in range(nch):
                cb = cbp.get()
                sb_ = sbp.get()
                k.dma("sync", cb[:].re("p c m -> p (c m)"), C4[tc])
                k.dma("sync", sb_[:].re("p c m -> p (c m)"), S4[tc])
                py = pyp.get()
                for f in range(nch):
                    k.mm(py[:], cb[:, f, :], Yc[:, f, :], start=(f == 0), stop=False)
                    k.mm(py[:], sb_[:, f, :], Ys[:, f, :], start=False, stop=(f == nch - 1))
                a, b_ = tp2.get(), tp2.get()
                if o == 0:
                    k.tt(a[:], v[:, tc, :], hb[:, 0, :], ALU.mult, eng="gpsimd")
                    k.tt(b_[:], py[:], a[:], ALU.add)
                    k.tt(z[:, tc, :], b_[:], x1[:, tc, :], ALU.mult)
                    k.copy(zb[:, tc, :], z[:, tc, :], eng="scalar")
                else:
                    k.tt(a[:], z[:, tc, :], hb[:, 1, :], ALU.mult, eng="gpsimd")
                    k.tt(b_[:], py[:], a[:], ALU.add)
                    yo = outp.get()
                    k.tt(yo[:], b_[:], x2[:, tc, :], ALU.mult)
                    k.dma("scalar", G[base + tc * 128:base + (tc + 1) * 128, 0:256], yo[:])


def hy_stage(k, C, l, need_ctx):
    if not hasattr(C, "KRI"):
        C.KRI = {n: k.dram(f"s_kri{n}", [n, 2, 512], F32) for n in (SEQ, CTX)}
    hyf_stage(k, C, l, SEQ, C.KRI[SEQ])
    hyc_stage(k, C, l, SEQ, CTX, C.KRI[SEQ])
    if need_ctx:
        hyf_stage(k, C, l, CTX, C.KRI[CTX])
        hyc_stage(k, C, l, CTX, 0, C.KRI[CTX])


def _chunk_consts():
    i = np.arange(64)
    tri = np.zeros((64, 2, 64), np.float32)
    tri[:, 0, :] = (i[:, None] <= i[None, :])
    tri[:, 1, :] = (i[:, None] >= i[None, :])
    after_eq = np.zeros((64, 8, 64), bool)
    after_st = np.zeros((64, 8, 64), bool)
    before_st = np.zeros((64, 8, 64), bool)
    for e in range(8):
        if e < 4:
            after_eq[:, e, :] = i[None, :] >= i[:, None]
            after_st[:, e, :] = i[None, :] > i[:, None]
            before_st[:, e, :] = i[None, :] < i[:, None]
        else:
            after_eq[:, e, :] = i[None, :] <= i[:, None]
            after_st[:, e, :] = i[None, :] < i[:, None]
            before_st[:, e, :] = i[None, :] > i[:, None]
    NEG = np.float32(-30000.0)
    mneg = np.stack([np.where(after_eq, 0, NEG), np.where(before_st, 0, NEG)], axis=1).astype(np.float32)
    m01 = np.stack([after_eq, after_st, before_st], axis=1).astype(np.float32)
    ident8 = np.tile(np.eye(64, dtype=np.float32)[:, None, :], (1, 8, 1))
    return {"c_tri": tri, "c_mneg": np.ascontiguousarray(mneg), "c_m01": np.ascontiguousarray(m01), "c_ident8": ident8}


_CC = {}


def _cc(name):
    if not _CC:
        _CC.update(_chunk_consts())
    return _CC[name]


for _nm, _sh in (("c_tri", [64, 2, 64]), ("c_mneg", [64, 2, 8, 64]), ("c_m01", [64, 3, 8, 64]), ("c_ident8", [64, 8, 64])):
    IN_SHAPES[_nm] = _sh
    HOST_LAYOUT[_nm] = (lambda inp, b, _nm=_nm: _cc(_nm))

NCH = NT // 64
FORD = list(range(NCH))
BORD = [3, 2, 1, 0] + list(range(NCH - 1, 3, -1))


def tri_inverse(k, M, A, ident8, pool_ps, pool_f, pool_b, levels=5):
    X = pool_f.get()
    k.tt(X[:], ident8[:], A[:], ALU.subtract)
    Mk, Ak = M, A
    for lev in range(1, levels + 1):
        pM = pool_ps.get()
        for e in range(8):
            k.mm(pM[:, e, :], Ak[:, e, :], Mk[:, e, :])
        if lev < levels:
            pA = pool_ps.get()
            for e in range(8):
                k.mm(pA[:, e, :], Mk[:, e, :], Ak[:, e, :])
            An = pool_f.get()
            k.copy(An[:], pA[:], eng="vector")
        Mn = pool_f.get()
        k.copy(Mn[:], pM[:], eng="scalar")
        pX = pool_ps.get()
        for e in range(8):
            k.mm(pX[:, e, :], Mn[:, e, :], X[:, e, :])
        Xn = pool_f.get()
        k.tt(Xn[:], X[:], pX[:], ALU.add)
        X = Xn
        Mk = Mn
        if lev < levels:
            Ak = An
    Xb = None
    if pool_b is not None:
        Xb = pool_b.get()
        k.copy(Xb[:], X[:], eng="gpsimd")
    return X, Xb


IN_SHAPES["dn_conv"] = [NL, 3, 768]
IN_SHAPES["dn_dtb8"] = [NL, 8]
IN_SHAPES["dn_alog8"] = [NL, 8]
IN_SHAPES["dn_normT"] = [NL, 256]
HOST_LAYOUT["dn_dtb8"] = lambda inp, b: inp["dn_dt_bias"].reshape(NL, 8)
HOST_LAYOUT["dn_alog8"] = lambda inp, b: inp["dn_a_log"].reshape(NL, 8)
HOST_LAYOUT["dn_normT"] = lambda inp, b: np.tile(inp["dn_norm"], (1, 4))


class Lane:
    pass


def run_lanes(k, nlanes, mkpools, body, items):
    lanes = [mkpools(i) for i in range(nlanes)]
    items = list(items)
    for i0 in range(0, len(items), nlanes):
        lists = []
        for j, it in enumerate(items[i0:i0 + nlanes]):
            with k.defer() as L:
                body(it, lanes[j])
            lists.append(L)
        k.replay(lists)


def shifted_loads(k, src, r0, ncols, seq_first, seq_last, pools):
    pc, pp, pn = pools
    sc_, sp, sn = pc.get(), pp.get(), pn.get()
    W = sc_.ap_shape[1]
    k.dma("sync", sc_[:], src[r0:r0 + 128, 0:W])
    if seq_first:
        k.memset(sp[0:1, :], 0.0)
        k.dma("sync", sp[1:128, :], src[r0:r0 + 127, 0:ncols])
    else:
        k.dma("sync", sp[:], src[r0 - 1:r0 + 127, 0:ncols])
    if seq_last:
        k.memset(sn[:], 0.0)
        k.dma("sync", sn[0:127, :], src[r0 + 1:r0 + 128, 0:ncols])
    else:
        k.dma("sync", sn[:], src[r0 + 1:r0 + 129, 0:ncols])
    return sc_, sp, sn


def dn_pre_stage(k, C, l):
    S = C.S
    with k.stage():
        cw = k.sb("cw", [128, 3, 768], F32)
        for i in range(3):
            bc_load(k, cw[:, i, :], C.I("dn_conv").t[l, i], C.I("dn_conv").tok)
        dtb = k.sb("dtb", [128, 8], F32)
        nA = k.sb("nA", [128, 8], F32)
        bc_load(k, dtb[:], C.I("dn_dtb8").t[l], C.I("dn_dtb8").tok)
        bc_load(k, nA[:], C.I("dn_alog8").t[l], C.I("dn_alog8").tok)
        k.act(nA[:], nA[:], AF.Exp)
        k.ts(nA[:], nA[:], -1.0, ALU.mult)
        def mkpools(i):
            L = Lane()
            L.pc = Pool(k, f"scur{i}", [128, 1040], F32, 2)
            L.pp = Pool(k, f"sprev{i}", [128, 768], F32, 2)
            L.pn = Pool(k, f"snext{i}", [128, 768], F32, 2)
            for p_ in (L.pc, L.pp, L.pn):
                for b_ in p_.bufs:
                    b_.ap_shape = b_.t.shape
            L.up = Pool(k, f"u{i}", [128, 768], F32, 1)
            L.tp = Pool(k, f"tt{i}", [128, 768], F32, 3)
            L.qp = Pool(k, f"qo{i}", [128, 768], F32, 2)
            L.gbp = Pool(k, f"gb{i}", [128, 16], F32, 2)
            L.smp = Pool(k, f"sm{i}", [128, 8], F32, 4)
            return L

        def body(tc, L):
            r0 = tc * 128
            sc_, sp, sn = shifted_loads(k, S["DN"], r0, 768, tc in (0, 2), tc in (1, NT // 128 - 1), (L.pc, L.pp, L.pn))
            u, t1, t2 = L.up.get(), L.tp.get(), L.tp.get()
            k.tt(u[:], sc_[:, 0:768], cw[:, 1, :], ALU.mult)
            k.tt(t1[:], sp[:], cw[:, 0, :], ALU.mult, eng="gpsimd")
            k.tt(t2[:], sn[:], cw[:, 2, :], ALU.mult, eng="gpsimd")
            k.tt(u[:], u[:], t1[:], ALU.add)
            k.tt(u[:], u[:], t2[:], ALU.add)
            qo = L.qp.get()
            k.act(qo[:], u[:], AF.Silu)
            sq = L.tp.get()
            k.tt(sq[:, 0:512], qo[:, 0:512], qo[:, 0:512], ALU.mult, eng="gpsimd")
            ss = L.smp.get()
            k.reduce(ss[:], sq[:, 0:512].re("p (g d) -> p g d", d=64))
            rn = L.smp.get()
            k.act(rn[:], ss[:], AF.Sqrt, bias=C.eps_t[:])
            k.recip(rn[:], rn[:])
            k.ts(rn[:, 0:4], rn[:, 0:4], 0.125, ALU.mult)
            k.tt(qo[:, 0:512].re("p (g d) -> p g d", d=64), qo[:, 0:512].re("p (g d) -> p g d", d=64),
                 rn[:, :, None].bc([128, 8, 64]), ALU.mult)
            k.dma("scalar", S["DNQKV"][r0:r0 + 128, :], qo[:])
            ba = sc_[:, 1024:1040].re("p (d a h) -> p d a h", d=2, a=2)
            gb = L.gbp.get()
            k.act(gb[:, 8:16].re("p (d h) -> p d h", d=2), ba[:, :, 0, :], AF.Sigmoid)
            x = L.smp.get()
            k.tt(x[:].re("p (d h) -> p d h", d=2), ba[:, :, 1, :], dtb[:].re("p (d h) -> p d h", d=2), ALU.add)
            k.act(x[:], x[:], AF.Exp)
            k.act(x[:], x[:], AF.Ln, bias=1.0)
            k.tt(gb[:, 0:8], x[:], nA[:], ALU.mult)
            k.dma("scalar", S["DNGB"][r0:r0 + 128, :], gb[:])

        run_lanes(k, 3, mkpools, body, range(NT // 128))


def dn_scan_stage(k, C, l, need_ctx):
    S = C.S
    with k.stage():
        tri = k.sb("tri", [64, 2, 64], F32)
        mneg = k.sb("mneg", [64, 2, 8, 64], F32)
        id8 = k.sb("id8", [64, 8, 64], F32)
        k.dma("sync", tri[:], C.I("c_tri")[:])
        k.dma("sync", mneg[:], C.I("c_mneg")[:])
        k.dma("sync", id8[:], C.I("c_ident8")[:])
        idn = C.ident[0:64, 0:64]
        idbt = k.sb("idbt", [64, 64], BF16)
        k.copy(idbt[:], C.ident[0:64, 0:64])
        idb = idbt[:]
        St = k.sb("St", [64, 8, 64], F32)
        k.memset(St[:], 0.0)
        pps = Pool(k, "pp", [64, 8, 64], F32, 7, space="ps")
        psm = k.ps("psm", [64, 8], F32)
        sbp = Pool(k, "w", [64, 8, 64], F32, 40)
        inv_f = Pool(k, "ivf", [64, 8, 64], F32, 8)
        inv_b = Pool(k, "ivb", [64, 8, 64], BF16, 6)
        sbb = Pool(k, "wb", [64, 8, 64], BF16, 24)
        Stb = k.sb("Stb", [64, 8, 64], BF16)
        k.memset(Stb[:], 0.0)
        qkp = Pool(k, "qk8", [64, 2, 8, 64], F32, 2)
        v8p = Pool(k, "v8", [64, 8, 64], F32, 2)
        gbp = Pool(k, "gb8", [64, 2, 8], F32, 2)
        smp = Pool(k, "sm", [64, 8], F32, 8)
        import os
        nsteps = int(os.environ.get("DN_STEPS", str(NCH)))
        part = int(os.environ.get("DN_PART", "99"))
        for s in range(nsteps):
            ch = (FORD[s], BORD[s])
            qk8, v8, gb8 = qkp.get(), v8p.get(), gbp.get()
            for d in range(2):
                r0 = ch[d] * 64
                es = slice(d * 4, d * 4 + 4)
                k.dma("sync", qk8[:, :, es, :], V(S["DNQKV"].t[r0:r0 + 64, 0:512].rearrange("p (a h d) -> p a h d", a=2, h=4), [S["DNQKV"].tok]))
                k.dma("sync", v8[:, es, :], V(S["DNQKV"].t[r0:r0 + 64, 512:768].rearrange("p (h d) -> p h d", h=4), [S["DNQKV"].tok]))
                k.dma("sync", gb8[:, 0, es], S["DNGB"][r0:r0 + 64, d * 4:d * 4 + 4])
                k.dma("sync", gb8[:, 1, es], S["DNGB"][r0:r0 + 64, 8 + d * 4:8 + d * 4 + 4])
            beta_bc = gb8[:, 1, :, None].bc([64, 8, 64])
            if part < 1:
                continue
            for d in range(2):
                k.mm(psm[:, d * 4:d * 4 + 4], tri[:, d, :], gb8[:, 0, d * 4:d * 4 + 4])
            gcc = smp.get()
            k.copy(gcc[:], psm[:], eng="scalar")
            sub = int(os.environ.get("DN_SUB", "99"))
            if sub < 2:
                continue
            grep = sbp.get()
            k.copy(grep[:], gb8[:, 0, :, None].bc([64, 8, 64]))
            pgcb = pps.get()
            for e in range(8):
                k.mm(pgcb[:, e, :], grep[:, e, :], tri[:, e // 4, :])
            if sub < 3:
                continue
            gcb = sbp.get()
            k.copy(gcb[:], pgcb[:], eng="scalar")
            D1 = sbp.get()
            k.tt(D1[:], gcb[:], gcc[:, :, None].bc([64, 8, 64]), ALU.subtract)
            egcb = sbp.get()
            k.act(egcb[:], gcb[:], AF.Exp)
            if sub < 4:
                continue
            t = sbp.get()
            k.tt(t[:], D1[:], mneg[:, 0], ALU.add, eng="gpsimd")
            E1 = sbp.get()
            k.act(E1[:], t[:], AF.Exp)
            if sub < 5:
                continue
            t = sbp.get()
            k.stt(t[:], D1[:], -1.0, mneg[:, 1], ALU.mult, ALU.add)
            E2 = sbp.get()
            k.act(E2[:], t[:], AF.Exp)
            if sub < 6:
                continue
            egc = smp.get()
            k.act(egc[:], gcc[:], AF.Exp)
            ekd = smp.get()
            k.act(ekd[:, 0:4], D1[:, 0:4, 63], AF.Exp)
            k.act(ekd[:, 4:8], D1[:, 4:8, 0], AF.Exp)
            if part < 2:
                continue
            pKT, pQT = pps.get(), pps.get()
            for e in range(8):
                k.tr(pKT[:, e, :], qk8[:, 1, e, :], idn)
            for e in range(8):
                k.tr(pQT[:, e, :], qk8[:, 0, e, :], idn)
            KT, QT = sbb.get(), sbb.get()
            k.copy(KT[:], pKT[:], eng="scalar")
            k.copy(QT[:], pQT[:], eng="vector")
            if part < 3:
                continue
            pG, pQK = pps.get(), pps.get()
            for e in range(8):
                k.mm(pG[:, e, :], KT[:, e, :], KT[:, e, :])
            for e in range(8):
                k.mm(pQK[:, e, :], KT[:, e, :], QT[:, e, :])
            t = sbp.get()
            k.tt(t[:], pG[:], E2[:], ALU.mult)
            M = sbp.get()
            k.tt(M[:], t[:], beta_bc, ALU.mult, eng="gpsimd")
            attnT = sbb.get()
            k.tt(attnT[:], pQK[:], E1[:], ALU.mult)
            pA = pps.get()
            for e in range(8):
                k.tr(pA[:, e, :], M[:, e, :], idn)
            A = sbp.get()
            k.copy(A[:], pA[:], eng="scalar")
            if part < 4:
                continue
            X = tri_inverse(k, M, A, id8, pps, inv_f, inv_b)
            if part < 5:
                continue
            Vb = sbb.get()
            k.tt(Vb[:], v8[:], beta_bc, ALU.mult, eng="gpsimd")
            bg = smp.get()
            k.tt(bg[:], gb8[:, 1, :], egc[:], ALU.mult)
            KBG = sbb.get()
            k.tt(KBG[:], qk8[:, 1], bg[:, :, None].bc([64, 8, 64]), ALU.mult, eng="gpsimd")
            pU, pW = pps.get(), pps.get()
            for e in range(8):
                k.mm(pU[:, e, :], X[:, e, :], Vb[:, e, :])
            for e in range(8):
                k.mm(pW[:, e, :], KBG[:, e, :], X[:, e, :])
            U, WT = sbp.get(), sbb.get()
            k.copy(U[:], pU[:], eng="scalar")
            k.copy(WT[:], pW[:], eng="vector")
            QdT = sbb.get()
            k.tt(QdT[:], QT[:], egcb[:], ALU.mult, eng="gpsimd")
            Kd = sbb.get()
            k.tt(Kd[:], qk8[:, 1], ekd[:, :, None].bc([64, 8, 64]), ALU.mult, eng="gpsimd")
            if part < 6:
                continue
            pv = pps.get()
            for e in range(8):
                k.mm(pv[:, e, :], WT[:, e, :], Stb[:, e, :])
            vnew = sbb.get()
            k.tt(vnew[:], U[:], pv[:], ALU.subtract)
            po = pps.get()
            for e in range(8):
                k.mm(po[:, e, :], QdT[:, e, :], Stb[:, e, :], start=True, stop=False)
                k.mm(po[:, e, :], attnT[:, e, :], vnew[:, e, :], start=False, stop=True)
            o8 = sbp.get()
            k.copy(o8[:], po[:], eng="scalar")
            for d in range(2):
                if need_ctx or ch[d] >= 4:
                    k.dma("scalar", S["DNO"][d, ch[d] * 64:(ch[d] + 1) * 64, :], o8[:, d * 4:d * 4 + 4, :].re("p h d -> p (h d)"))
            pS = pps.get()
            for e in range(8):
                k.mm(pS[:, e, :], Kd[:, e, :], vnew[:, e, :])
            glast = smp.get()
            k.copy(glast[:, 0:4], egcb[:, 0:4, 63])
            k.copy(glast[:, 4:8], egcb[:, 4:8, 0])
            t = sbp.get()
            k.tt(t[:], St[:], glast[:, :, None].bc([64, 8, 64]), ALU.mult)
            k.tt(St[:], t[:], pS[:], ALU.add)
            k.copy(Stb[:], St[:], eng="gpsimd")


def dn_post_stage(k, C, l, need_ctx):
    S = C.S
    with k.stage():
        nwt = k.sb("nwt", [128, 256], F32)
        bc_load(k, nwt[:], C.I("dn_normT").t[l], C.I("dn_normT").tok)

        def mkpools(i):
            L = Lane()
            L.op_ = Pool(k, f"o{i}", [128, 2, 256], F32, 2)
            L.zp = Pool(k, f"z{i}", [128, 256], F32, 2)
            L.tp = Pool(k, f"t{i}", [128, 256], F32, 6)
            L.smp = Pool(k, f"sm{i}", [128, 4], F32, 4)
            return L

        def body(tc, L):
            r0 = tc * 128
            o = L.op_.get()
            k.dma("sync", o[:], V(S["DNO"].t[:, r0:r0 + 128, :].rearrange("d p c -> p d c"), [S["DNO"].tok]))
            zt = L.zp.get()
            k.dma("sync", zt[:], S["DN"][r0:r0 + 128, 768:1024])
            os_ = L.tp.get()
            k.tt(os_[:], o[:, 0, :], o[:, 1, :], ALU.add)
            sq = L.tp.get()
            k.tt(sq[:], os_[:], os_[:], ALU.mult, eng="gpsimd")
            ss = L.smp.get()
            k.reduce(ss[:], sq[:].re("p (h d) -> p h d", d=64))
            rn = L.smp.get()
            k.act(rn[:], ss[:], AF.Sqrt, scale=1.0 / 64, bias=C.eps_t[:])
            k.recip(rn[:], rn[:])
            sz = L.tp.get()
            k.act(sz[:], zt[:], AF.Silu)
            a = L.tp.get()
            k.tt(a[:].re("p (h d) -> p h d", d=64), os_[:].re("p (h d) -> p h d", d=64), rn[:, :, None].bc([128, 4, 64]), ALU.mult)
            k.tt(a[:], a[:], nwt[:], ALU.mult, eng="gpsimd")
            y = L.tp.get()
            k.tt(y[:], a[:], sz[:], ALU.mult)
            k.dma("scalar", S["G"][r0:r0 + 128, 512:768], y[:])

        run_lanes(k, 3, mkpools, body, [tc for tc in range(NT // 128) if not (tc < 2 and not need_ctx)])


def dn_stage(k, C, l, need_ctx):
    dn_pre_stage(k, C, l)
    scan_stage(k, C, l, need_ctx, do_dn=True, do_rw=False)
    dn_post_stage(k, C, l, need_ctx)


for _nm, _sh in (("rw_mu", [NL, 2, 896]), ("rw_w_up", [NL, 2, 32, 256]), ("rw_a_up", [NL, 2, 32, 256]),
                 ("rw_g_up", [NL, 64, 256]), ("rw_w0", [NL, 2, 256]), ("rw_a0", [NL, 2, 256]),
                 ("rw_k_k", [NL, 256]), ("rw_k_a", [NL, 256]), ("rw_ln_w", [NL, 256]), ("rw_ln_b", [NL, 256]),
                 ("rw_r_kT", [NL, 256]), ("rw_mulrT", [NL, 64, 3, 2])):
    IN_SHAPES[_nm] = _sh
HOST_LAYOUT["rw_r_kT"] = lambda inp, b: inp["rw_r_k"].reshape(NL, 256)


def _rw_mulrT(inp, b):
    mu = inp["rw_mu"]
    o = np.zeros((NL, 64, 3, 2), np.float32)
    o[:, 0:32, 0, :] = mu[:, :, 768:800].transpose(0, 2, 1)
    o[:, 0:32, 1, :] = mu[:, :, 800:832].transpose(0, 2, 1)
    o[:, :, 2, :] = mu[:, :, 832:896].transpose(0, 2, 1)
    return o


HOST_LAYOUT["rw_mulrT"] = _rw_mulrT
SEQS = ((0, CTX), (CTX, NT))
LW_SCALE = -math.exp(-0.5)


def rw_pre_stage(k, C, l):
    S = C.S
    with k.stage():
        mulr = k.sb("mulr", [64, 3, 2], F32)
        k.dma("sync", mulr[:], C.I("rw_mulrT")[l])
        c0lr = k.sb("c0lr", [64, 3], F32)
        k.tt(c0lr[:], mulr[:, :, 0], mulr[:, :, 1], ALU.add)
        k.ts(c0lr[:], c0lr[:], -1.0, ALU.mult, 1.0, ALU.add)
        lrT = []
        for g_, ncl, fn in ((0, 32, AF.Tanh), (1, 32, None), (2, 64, AF.Sigmoid)):
            u = k.sb(f"lru{g_}", [64, NT], F32)
            o = k.sb(f"lro{g_}", [64, NT], F32)
            k.dma("sync", u[0:ncl, :], S["RWLR"][g_, 0:ncl, :])
            k.ts(o[0:ncl, :], u[0:ncl, :], c0lr[0:ncl, g_:g_ + 1], ALU.mult)
            for (a, b_) in SEQS:
                k.stt(o[0:ncl, a + 1:b_], u[0:ncl, a:b_ - 1], mulr[0:ncl, g_, 0:1], o[0:ncl, a + 1:b_], ALU.mult, ALU.add)
                k.stt(o[0:ncl, a:b_ - 1], u[0:ncl, a + 1:b_], mulr[0:ncl, g_, 1:2], o[0:ncl, a:b_ - 1], ALU.mult, ALU.add)
            if fn is not None:
                k.act(o[0:ncl, :], o[0:ncl, :], fn)
            lrT.append(o)
        wup = k.sb("wup", [32, 2, 256], F32)
        aup = k.sb("aup", [32, 2, 256], F32)
        gup = k.sb("gup", [64, 256], F32)
        w0r = k.sb("w0r", [1, 2, 256], F32)
        a0r = k.sb("a0r", [1, 2, 256], F32)
        ones1 = k.sb("ones1", [1, 128], F32)
        k.memset(ones1[:], 1.0)
        for d in range(2):
            k.dma("sync", wup[:, d, :], C.I("rw_w_up")[l, d])
            k.dma("sync", aup[:, d, :], C.I("rw_a_up")[l, d])
            k.dma("sync", w0r[:, d, :], C.I("rw_w0")[l, d:d + 1])
            k.dma("sync", a0r[:, d, :], C.I("rw_a0")[l, d:d + 1])
        k.dma("sync", gup[:], C.I("rw_g_up")[l])
        mu = k.sb("mu", [128, 2, 768], F32)
        for i in range(2):
            bc_load(k, mu[:, i, :], C.I("rw_mu").t[l, i, 0:768], C.I("rw_mu").tok)
        c0 = k.sb("c0", [128, 768], F32)
        k.tt(c0[:], mu[:, 0, :], mu[:, 1, :], ALU.add)
        k.ts(c0[:], c0[:], -1.0, ALU.mult, 1.0, ALU.add)
        kkw = k.sb("kkw", [128, 256], F32)
        kaw = k.sb("kaw", [128, 256], F32)
        omka = k.sb("omka", [128, 256], F32)
        bc_load(k, kkw[:], C.I("rw_k_k").t[l], C.I("rw_k_k").tok)
        bc_load(k, kaw[:], C.I("rw_k_a").t[l], C.I("rw_k_a").tok)
        k.ts(omka[:], kaw[:], -1.0, ALU.mult, 1.0, ALU.add)
        def mkpools(i):
            L = Lane()
            L.pc = Pool(k, f"scur{i}", [128, 768], F32, 2)
            L.pp = Pool(k, f"sprev{i}", [128, 768], F32, 2)
            L.pn = Pool(k, f"snext{i}", [128, 768], F32, 2)
            for p_ in (L.pc, L.pp, L.pn):
                for b_ in p_.bufs:
                    b_.ap_shape = b_.t.shape
            L.tp = Pool(k, f"tt{i}", [128, 768], F32, 3)
            L.op_ = Pool(k, f"out{i}", [128, 10, 256], F32, 2)
            L.t2 = Pool(k, f"t2{i}", [128, 256], F32, 6)
            L.smp = Pool(k, f"sm{i}", [128, 4], F32, 4)
            L.plp = Pool(k, f"pl{i}", [128, 2, 256], F32, 3, space="ps")
            return L

        def body(tc, L):
            pc, pp, pn, tp, op_, t2, smp, plp = L.pc, L.pp, L.pn, L.tp, L.op_, L.t2, L.smp, L.plp
            r0 = tc * 128
            sc_, sp, sn = shifted_loads(k, S["RW"], r0, 768, tc in (0, 2), tc in (1, NT // 128 - 1), (pc, pp, pn))
            o = op_.get()
            s_ = tp.get()
            t1 = tp.get()
            k.tt(s_[:], sc_[:], c0[:], ALU.mult)
            k.tt(t1[:], sp[:], mu[:, 0, :], ALU.mult, eng="gpsimd")
            k.tt(s_[:], s_[:], t1[:], ALU.add)
            t1 = tp.get()
            k.tt(t1[:], sn[:], mu[:, 1, :], ALU.mult, eng="gpsimd")
            k.tt(s_[:, 0:256], s_[:, 0:256], t1[:, 0:256], ALU.add)
            k.tt(s_[:, 256:512], s_[:, 256:512], t1[:, 256:512], ALU.add)
            k.tt(o[:, 1, :], s_[:, 512:768], t1[:, 512:768], ALU.add)
            k.copy(o[:, 0, :], s_[:, 0:256], eng="scalar")
            kcur = s_[:, 256:512]
            tok = slice(r0, r0 + 128)
            pwt, pat, pgt = plp.get(), plp.get(), plp.get()
            pw = [pwt[:, 0, :], pwt[:, 1, :]]
            pa = [pat[:, 0, :], pat[:, 1, :]]
            pg = pgt[:, 0, :]
            for d in range(2):
                k.mm(pw[d], lrT[0][0:32, tok], wup[:, d, :], start=True, stop=False)
                k.mm(pw[d], ones1[:], w0r[:, d, :], start=False, stop=True)
            for d in range(2):
                k.mm(pa[d], lrT[1][0:32, tok], aup[:, d, :], start=True, stop=False)
                k.mm(pa[d], ones1[:], a0r[:, d, :], start=False, stop=True)
            k.mm(pg, lrT[2][0:64, tok], gup[:])
            k.copy(o[:, 9, :], pg, eng="scalar")
            kx = t2.get()
            k.tt(kx[:], kcur, kkw[:], ALU.mult, eng="gpsimd")
            sq = t2.get()
            k.tt(sq[:], kx[:], kx[:], ALU.mult, eng="gpsimd")
            ss = smp.get()
            k.reduce(ss[:], sq[:].re("p (h d) -> p h d", d=64))
            rn = smp.get()
            k.act(rn[:], ss[:], AF.Sqrt, bias=C.eps_t[:])
            k.recip(rn[:], rn[:])
            kk = t2.get()
            k.tt(kk[:].re("p (h d) -> p h d", d=64), kx[:].re("p (h d) -> p h d", d=64), rn[:, :, None].bc([128, 4, 64]), ALU.mult)
            k.ts(o[:, 2, :], kk[:], -1.0, ALU.mult)
            for d in range(2):
                sg = t2.get()
                k.act(sg[:], pw[d], AF.Sigmoid)
                k.ts(o[:, 3 + d, :], sg[:], LW_SCALE, ALU.mult, eng="gpsimd")
                ar = t2.get()
                k.act(ar[:], pa[d], AF.Sigmoid)
                k.tt(o[:, 7 + d, :], kk[:], ar[:], ALU.mult, eng="gpsimd")
                t = t2.get()
                k.tt(t[:], ar[:], kaw[:], ALU.mult)
                k.tt(t[:], t[:], omka[:], ALU.add)
                k.tt(o[:, 5 + d, :], kcur, t[:], ALU.mult)
            k.dma("scalar", S["RWP"][r0:r0 + 128], o[:])


        run_lanes(k, 2, mkpools, body, range(NT // 128))


def rw_scan_stage(k, C, l, need_ctx):
    S = C.S
    with k.stage():
        tri = k.sb("tri", [64, 2, 64], F32)
        m01 = k.sb("m01", [64, 3, 8, 64], F32)
        nm = k.sb("nm", [64, 2, 8, 64], F32)
        id8 = k.sb("id8", [64, 8, 64], F32)
        ones64 = k.sb("ones64", [64, 64], F32)
        k.memset(ones64[:], 1.0)
        k.dma("sync", tri[:], C.I("c_tri")[:])
        k.dma("sync", m01[:], C.I("c_m01")[:])
        k.dma("sync", id8[:], C.I("c_ident8")[:])
        k.ts(nm[:], m01[:, 1:3], -1.0, ALU.mult)
        idn = C.ident[0:64, 0:64]
        St = k.sb("St", [64, 8, 64], F32)
        k.memset(St[:], 0.0)
        pps = Pool(k, "pp", [64, 8, 64], F32, 7, space="ps")
        psm = k.ps("psm", [64, 8, 2], F32)
        sbp = Pool(k, "w", [64, 8, 64], F32, 44)
        inv_sb = Pool(k, "iv", [64, 8, 64], F32, 8)
        inp_ = [Pool(k, f"in{i}", [64, 8, 64], F32, 2) for i in range(6)]
        smp = Pool(k, "sm", [64, 8], F32, 4)
        RWP = S["RWP"]
        slots = ((0, 0), (1, 1), (2, 2), (3, 4), (5, 6), (7, 8))
        for s in range(NCH):
            ch = (FORD[s], BORD[s])
            tl = [p.get() for p in inp_]
            for i, sl in enumerate(slots):
                for d in range(2):
                    r0 = ch[d] * 64
                    k.dma("sync", tl[i][:, d * 4:d * 4 + 4, :],
                          V(RWP.t[r0:r0 + 64, sl[d], :].rearrange("p (h d) -> p h d", h=4), [RWP.tok]))
            R8, V8, A8, LW8, K8, B8 = tl
            plc, ptot = pps.get(), pps.get()
            for d in range(2):
                k.mm(plc[:, d * 4:d * 4 + 4, :], tri[:, d, :], LW8[:, d * 4:d * 4 + 4, :])
            k.mm(ptot[:], ones64[:], LW8[:])
            for e in range(8):
                k.mm(psm[:, e, :], LW8[:, e, :], ones64[:, 0:2])
            lc, tot = sbp.get(), sbp.get()
            k.copy(lc[:], plc[:], eng="scalar")
            k.copy(tot[:], ptot[:], eng="vector")
            gCT = smp.get()
            k.act(gCT[:], psm[:, :, 0], AF.Exp)
            eg, egi, egp, ehat = sbp.get(), sbp.get(), sbp.get(), sbp.get()
            k.act(eg[:], lc[:], AF.Exp)
            k.act(egi[:], lc[:], AF.Exp, scale=-1.0)
            t = sbp.get()
            k.tt(t[:], lc[:], LW8[:], ALU.subtract, eng="gpsimd")
            k.act(egp[:], t[:], AF.Exp)
            t = sbp.get()
            k.tt(t[:], tot[:], lc[:], ALU.subtract)
            k.act(ehat[:], t[:], AF.Exp)
            At, Bt, Kt, Rt, Bh, Kh = [sbp.get() for _ in range(6)]
            k.tt(At[:], A8[:], egp[:], ALU.mult)
            k.tt(Bt[:], B8[:], egi[:], ALU.mult, eng="gpsimd")
            k.tt(Kt[:], K8[:], egi[:], ALU.mult)
            k.tt(Rt[:], R8[:], eg[:], ALU.mult, eng="gpsimd")
            k.tt(Bh[:], B8[:], ehat[:], ALU.mult)
            k.tt(Kh[:], K8[:], ehat[:], ALU.mult, eng="gpsimd")
            fmT = []
            for i, src in enumerate((At, Bt, Kt, Rt)):
                pT = pps.get()
                for e in range(8):
                    k.tr(pT[:, e, :], src[:, e, :], idn)
                dst = sbp.get()
                k.copy(dst[:], pT[:], eng=("scalar" if i % 2 == 0 else "vector"))
                fmT.append(dst)
            AtT, BtT, KtT, RtT = fmT
            def score(lhs, rhs, mask, eng):
                p_ = pps.get()
                for e in range(8):
                    k.mm(p_[:, e, :], lhs[:, e, :], rhs[:, e, :])
                o_ = sbp.get()
                k.tt(o_[:], p_[:], mask, ALU.mult, eng=eng)
                return o_
            M = score(AtT, BtT, nm[:, 1], "vector")
            A = score(BtT, AtT, nm[:, 0], "vector")
            AakT = score(KtT, AtT, m01[:, 1], "vector")
            ArbT = score(BtT, RtT, m01[:, 0], "vector")
            ArkT = score(KtT, RtT, m01[:, 0], "vector")
            X = tri_inverse(k, None, M, A, id8, pps, inv_sb)
            pW, pAkV = pps.get(), pps.get()
            for e in range(8):
                k.mm(pW[:, e, :], At[:, e, :], X[:, e, :])
            for e in range(8):
                k.mm(pAkV[:, e, :], AakT[:, e, :], V8[:, e, :])
            WT, AkV = sbp.get(), sbp.get()
            k.copy(WT[:], pW[:], eng="scalar")
            k.copy(AkV[:], pAkV[:], eng="vector")
            pUv = pps.get()
            for e in range(8):
                k.mm(pUv[:, e, :], X[:, e, :], AkV[:, e, :])
            Uv = sbp.get()
            k.copy(Uv[:], pUv[:], eng="scalar")
            pe = pps.get()
            for e in range(8):
                k.mm(pe[:, e, :], WT[:, e, :], St[:, e, :])
            E = sbp.get()
            k.tt(E[:], Uv[:], pe[:], ALU.add)
            py = pps.get()
            for e in range(8):
                k.mm(py[:, e, :], RtT[:, e, :], St[:, e, :], start=True, stop=False)
                k.mm(py[:, e, :], ArbT[:, e, :], E[:, e, :], start=False, stop=False)
                k.mm(py[:, e, :], ArkT[:, e, :], V8[:, e, :], start=False, stop=True)
            y8 = sbp.get()
            k.copy(y8[:], py[:], eng="scalar")
            for d in range(2):
                if need_ctx or ch[d] >= 4:
                    k.dma("scalar", S["RWY"][d, ch[d] * 64:(ch[d] + 1) * 64, :], y8[:, d * 4:d * 4 + 4, :].re("p h d -> p (h d)"))
            pS = pps.get()
            for e in range(8):
                k.mm(pS[:, e, :], Bh[:, e, :], E[:, e, :], start=True, stop=False)
                k.mm(pS[:, e, :], Kh[:, e, :], V8[:, e, :], start=False, stop=True)
            t = sbp.get()
            k.tt(t[:], St[:], gCT[:, :, None].bc([64, 8, 64]), ALU.mult)
            k.tt(St[:], t[:], pS[:], ALU.add)


def rw_post_stage(k, C, l, need_ctx):
    S = C.S
    with k.stage():
        lnw = k.sb("lnw", [128, 256], F32)
        lnb = k.sb("lnb", [128, 256], F32)
        rkw = k.sb("rkw", [128, 256], F32)
        bc_load(k, lnw[:], C.I("rw_ln_w").t[l], C.I("rw_ln_w").tok)
        bc_load(k, lnb[:], C.I("rw_ln_b").t[l], C.I("rw_ln_b").tok)
        bc_load(k, rkw[:], C.I("rw_r_kT").t[l], C.I("rw_r_kT").tok)
        lneps = k.sb("lneps", [128, 1], F32)
        k.memset(lneps[:], 64e-5)
        def hview(x):
            return x.re("p (h d) -> p h d", d=64)

        def mkpools(i):
            L = Lane()
            L.yp = Pool(k, f"y{i}", [128, 2, 256], F32, 2)
            L.pp = Pool(k, f"p{i}", [128, 10, 256], F32, 2)
            L.tp = Pool(k, f"t{i}", [128, 256], F32, 8)
            L.smp = Pool(k, f"sm{i}", [128, 4], F32, 6)
            return L

        def body(tc, L):
            yp, pp, tp, smp = L.yp, L.pp, L.tp, L.smp
            r0 = tc * 128
            yy = yp.get()
            k.dma("sync", yy[:], V(S["RWY"].t[:, r0:r0 + 128, :].rearrange("d p c -> p d c"), [S["RWY"].tok]))
            pr = pp.get()
            k.dma("sync", pr[:], S["RWP"][r0:r0 + 128])
            y = tp.get()
            k.tt(y[:], yy[:, 0, :], yy[:, 1, :], ALU.add)
            s1 = smp.get()
            k.reduce(s1[:], hview(y[:]))
            k.ts(s1[:], s1[:], -1.0 / 64, ALU.mult)
            yc = tp.get()
            k.tt(hview(yc[:]), hview(y[:]), s1[:, :, None].bc([128, 4, 64]), ALU.add)
            sq = tp.get()
            k.tt(sq[:], yc[:], yc[:], ALU.mult, eng="gpsimd")
            s2 = smp.get()
            k.reduce(s2[:], hview(sq[:]))
            rstd = smp.get()
            k.act(rstd[:], s2[:], AF.Sqrt, scale=1.0 / 64, bias=lneps[:])
            k.recip(rstd[:], rstd[:])
            yn = tp.get()
            k.tt(hview(yn[:]), hview(yc[:]), rstd[:, :, None].bc([128, 4, 64]), ALU.mult)
            k.tt(yn[:], yn[:], lnw[:], ALU.mult, eng="gpsimd")
            k.tt(yn[:], yn[:], lnb[:], ALU.add, eng="gpsimd")
            ks = tp.get()
            k.tt(ks[:], pr[:, 5, :], pr[:, 6, :], ALU.add, eng="gpsimd")
            k.tt(ks[:], ks[:], pr[:, 0, :], ALU.mult, eng="gpsimd")
            k.tt(ks[:], ks[:], rkw[:], ALU.mult, eng="gpsimd")
            bs = smp.get()
            k.reduce(bs[:], hview(ks[:]))
            bon = tp.get()
            k.tt(hview(bon[:]), hview(pr[:, 1, :]), bs[:, :, None].bc([128, 4, 64]), ALU.mult)
            k.tt(yn[:], yn[:], bon[:], ALU.add)
            out = tp.get()
            k.tt(out[:], yn[:], pr[:, 9, :], ALU.mult)
            k.dma("scalar", S["G"][r0:r0 + 128, 768:1024], out[:])


        run_lanes(k, 3, mkpools, body, [tc for tc in range(NT // 128) if not (tc < 2 and not need_ctx)])


def rw_stage(k, C, l, need_ctx):
    rw_pre_stage(k, C, l)
    scan_stage(k, C, l, need_ctx, do_dn=False, do_rw=True)
    rw_post_stage(k, C, l, need_ctx)


def dnrw_stage(k, C, l, need_ctx):
    dn_pre_stage(k, C, l)
    rw_pre_stage(k, C, l)
    scan_stage(k, C, l, need_ctx)
    dn_post_stage(k, C, l, need_ctx)
    rw_post_stage(k, C, l, need_ctx)


def dn_scan_setup(k, C, need_ctx, sh):
    X = Lane()
    X.S = C.S
    X.need_ctx = need_ctx
    X.tri, X.id8, X.idn = sh.tri, sh.id8, sh.idn
    X.mneg = k.sb("mneg", [64, 2, 8, 64], F32)
    k.dma("sync", X.mneg[:], C.I("c_mneg")[:])
    X.St = k.sb("dSt", [64, 8, 64], F32)
    X.Stb = k.sb("dStb", [64, 8, 64], BF16)
    k.memset(X.St[:], 0.0)
    k.memset(X.Stb[:], 0.0)
    X.pps = Pool(k, "dpp", [64, 8, 64], F32, sh.npp_dn, space="ps")
    X.psm = sh.psm_dn
    X.sbp = Pool(k, "dw", [64, 8, 64], F32, 14)
    X.sbb = Pool(k, "dwb", [64, 8, 64], BF16, 9)
    X.inv_f = Pool(k, "divf", [64, 8, 64], F32, 8)
    X.inv_b = Pool(k, "divb", [64, 8, 64], BF16, 2)
    X.qkp = Pool(k, "qk8", [64, 2, 8, 64], F32, 2)
    X.v8p = Pool(k, "v8", [64, 8, 64], F32, 2)
    X.gbp = Pool(k, "gb8", [64, 2, 8], F32, 2)
    X.smp = Pool(k, "dsm", [64, 8], F32, 8)
    return X


def dn_step(k, X, s):
    S = X.S
    tri, mneg, id8, idn, St, Stb = X.tri, X.mneg, X.id8, X.idn, X.St, X.Stb
    pps, psm, sbp, sbb, smp = X.pps, X.psm, X.sbp, X.sbb, X.smp
    ch = (FORD[s], BORD[s])
    qk8, v8, gb8 = X.qkp.get(), X.v8p.get(), X.gbp.get()
    for d in range(2):
        r0 = ch[d] * 64
        es = slice(d * 4, d * 4 + 4)
        k.dma("sync", qk8[:, :, es, :], V(S["DNQKV"].t[r0:r0 + 64, 0:512].rearrange("p (a h d) -> p a h d", a=2, h=4), [S["DNQKV"].tok]))
        k.dma("sync", v8[:, es, :], V(S["DNQKV"].t[r0:r0 + 64, 512:768].rearrange("p (h d) -> p h d", h=4), [S["DNQKV"].tok]))
        k.dma("sync", gb8[:, 0, es], S["DNGB"][r0:r0 + 64, d * 4:d * 4 + 4])
        k.dma("sync", gb8[:, 1, es], S["DNGB"][r0:r0 + 64, 8 + d * 4:8 + d * 4 + 4])
    beta_bc = gb8[:, 1, :, None].bc([64, 8, 64])
    for d in range(2):
        k.mm(psm[:, d * 4:d * 4 + 4], tri[:, d, :], gb8[:, 0, d * 4:d * 4 + 4])
    gcc = smp.get()
    k.copy(gcc[:], psm[:], eng="scalar")
    grep = sbp.get()
    k.copy(grep[:], gb8[:, 0, :, None].bc([64, 8, 64]))
    pgcb = pps.get()
    for e in range(8):
        k.mm(pgcb[:, e, :], grep[:, e, :], tri[:, e // 4, :])
    gcb = sbp.get()
    k.copy(gcb[:], pgcb[:], eng="scalar")
    D1 = sbp.get()
    k.tt(D1[:], gcb[:], gcc[:, :, None].bc([64, 8, 64]), ALU.subtract)
    egcb = sbp.get()
    k.act(egcb[:], gcb[:], AF.Exp)
    t = sbp.get()
    k.tt(t[:], D1[:], mneg[:, 0], ALU.add, eng="gpsimd")
    E1 = sbp.get()
    k.act(E1[:], t[:], AF.Exp)
    t = sbp.get()
    k.stt(t[:], D1[:], -1.0, mneg[:, 1], ALU.mult, ALU.add)
    E2 = sbp.get()
    k.act(E2[:], t[:], AF.Exp)
    egc = smp.get()
    k.act(egc[:], gcc[:], AF.Exp)
    ekd = smp.get()
    k.act(ekd[:, 0:4], D1[:, 0:4, 63], AF.Exp)
    k.act(ekd[:, 4:8], D1[:, 4:8, 0], AF.Exp)
    pKT = pps.get()
    for e in range(8):
        k.tr(pKT[:, e, :], qk8[:, 1, e, :], idn)
    KT = sbb.get()
    k.copy(KT[:], pKT[:], eng="scalar")
    pQT = pps.get()
    for e in range(8):
        k.tr(pQT[:, e, :], qk8[:, 0, e, :], idn)
    QT = sbb.get()
    k.copy(QT[:], pQT[:], eng="vector")
    pG = pps.get()
    for e in range(8):
        k.mm(pG[:, e, :], KT[:, e, :], KT[:, e, :])
    t = sbp.get()
    k.tt(t[:], pG[:], E2[:], ALU.mult)
    M = sbp.get()
    k.tt(M[:], t[:], beta_bc, ALU.mult, eng="gpsimd")
    pQK = pps.get()
    for e in range(8):
        k.mm(pQK[:, e, :], KT[:, e, :], QT[:, e, :])
    attnT = sbb.get()
    k.tt(attnT[:], pQK[:], E1[:], ALU.mult)
    pA = pps.get()
    for e in range(8):
        k.tr(pA[:, e, :], M[:, e, :], idn)
    A = sbp.get()
    k.copy(A[:], pA[:], eng="scalar")
    _, Xi = tri_inverse(k, M, A, id8, pps, X.inv_f, X.inv_b)
    Vb = sbb.get()
    k.tt(Vb[:], v8[:], beta_bc, ALU.mult, eng="gpsimd")
    bg = smp.get()
    k.tt(bg[:], gb8[:, 1, :], egc[:], ALU.mult)
    KBG = sbb.get()
    k.tt(KBG[:], qk8[:, 1], bg[:, :, None].bc([64, 8, 64]), ALU.mult, eng="gpsimd")
    pU = pps.get()
    for e in range(8):
        k.mm(pU[:, e, :], Xi[:, e, :], Vb[:, e, :])
    U = sbp.get()
    k.copy(U[:], pU[:], eng="scalar")
    pW = pps.get()
    for e in range(8):
        k.mm(pW[:, e, :], KBG[:, e, :], Xi[:, e, :])
    WT = sbb.get()
    k.copy(WT[:], pW[:], eng="vector")
    QdT = sbb.get()
    k.tt(QdT[:], QT[:], egcb[:], ALU.mult, eng="gpsimd")
    Kd = sbb.get()
    k.tt(Kd[:], qk8[:, 1], ekd[:, :, None].bc([64, 8, 64]), ALU.mult, eng="gpsimd")
    glast = smp.get()
    k.copy(glast[:, 0:4], egcb[:, 0:4, 63], eng="gpsimd")
    k.copy(glast[:, 4:8], egcb[:, 4:8, 0], eng="gpsimd")
    pv = pps.get()
    for e in range(8):
        k.mm(pv[:, e, :], WT[:, e, :], Stb[:, e, :])
    vnew = sbb.get()
    k.tt(vnew[:], U[:], pv[:], ALU.subtract)
    po = pps.get()
    for e in range(8):
        k.mm(po[:, e, :], QdT[:, e, :], Stb[:, e, :], start=True, stop=False)
        k.mm(po[:, e, :], attnT[:, e, :], vnew[:, e, :], start=False, stop=True)
    pS = pps.get()
    for e in range(8):
        k.mm(pS[:, e, :], Kd[:, e, :], vnew[:, e, :])
    t = sbp.get()
    k.tt(t[:], St[:], glast[:, :, None].bc([64, 8, 64]), ALU.mult, eng="gpsimd")
    k.tt(St[:], t[:], pS[:], ALU.add)
    k.copy(Stb[:], St[:], eng="gpsimd")
    o8 = sbp.get()
    k.copy(o8[:], po[:], eng="scalar")
    for d in range(2):
        if X.need_ctx or ch[d] >= 4:
            k.dma("scalar", S["DNO"][d, ch[d] * 64:(ch[d] + 1) * 64, :], o8[:, d * 4:d * 4 + 4, :].re("p h d -> p (h d)"))


def rw_scan_setup(k, C, need_ctx, sh):
    X = Lane()
    psl = sh.rw_psl
    X.psl = psl
    X.S = C.S
    X.need_ctx = need_ctx

    def const(name, shape, dtype=F32):
        full = [128] + list(shape[1:]) if psl.start else list(shape)
        return TV(k.sb(name, full, dtype), psl)
    X.tri = const("rtri", [64, 2, 64])
    X.id8 = const("rid8", [64, 8, 64])
    X.m01 = const("m01", [64, 3, 8, 64])
    X.nm = const("nm", [64, 2, 8, 64])
    X.ones64 = const("ones64", [64, 64])
    idf = const("ridf", [64, 64])
    X.idb = const("ridb", [64, 64], BF16)
    k.dma("sync", X.tri[:], C.I("c_tri")[:])
    k.dma("sync", X.id8[:], C.I("c_ident8")[:])
    k.dma("sync", X.m01[:], C.I("c_m01")[:])
    k.dma("sync", idf[:], C.I("identD")[0:64, 0:64])
    k.copy(X.idb[:], idf[:])
    k.memset(X.ones64[:], 1.0)
    k.ts(X.nm[:], X.m01[:, 1:3], -1.0, ALU.mult)
    X.St = const("rSt", [64, 8, 64])
    X.Stb = const("rStb", [64, 8, 64], BF16)
    k.memset(X.St[:], 0.0)
    k.memset(X.Stb[:], 0.0)
    X.pps = PoolV(k, "rpp", [64, 8, 64], F32, sh.npp_rw, psl, space="ps")
    X.ppb = PoolV(k, "rppb", [64, 16, 64], BF16, 1, psl, space="ps")
    X.psm = PoolV(k, "rpsm", [64, 8, 2], F32, 1, psl, space="ps").get()
    X.sbp = PoolV(k, "rw", [64, 8, 64], F32, 24, psl)
    X.sbb = PoolV(k, "rwb", [64, 8, 64], BF16, 9, psl)
    X.inv_f = PoolV(k, "rivf", [64, 8, 64], F32, 8, psl)
    X.inp_ = [PoolV(k, f"rin{i}", [64, 8, 64], F32, 2, psl) for i in range(6)]
    X.smp = PoolV(k, "rsm", [64, 8], F32, 4, psl)
    return X


RW_SLOTS = ((0, 0), (1, 1), (2, 2), (3, 4), (5, 6), (7, 8))


def rw_step(k, X, s):
    S = X.S
    tri, m01, nm, id8, idb, ones64, St, Stb = X.tri, X.m01, X.nm, X.id8, X.idb, X.ones64, X.St, X.Stb
    pps, ppb, psm, sbp, sbb, smp = X.pps, X.ppb, X.psm, X.sbp, X.sbb, X.smp
    RWP = S["RWP"]
    ch = (FORD[s], BORD[s])
    tl = [p.get() for p in X.inp_]
    for i, sl in enumerate(RW_SLOTS):
        for d in range(2):
            r0 = ch[d] * 64
            k.dma("sync", tl[i][:, d * 4:d * 4 + 4, :],
                  V(RWP.t[r0:r0 + 64, sl[d], :].rearrange("p (h d) -> p h d", h=4), [RWP.tok]))
    R8, V8, A8, LW8, K8, B8 = tl
    plc = pps.get()
    for d in range(2):
        k.mm(plc[:, d * 4:d * 4 + 4, :], tri[:, d, :], LW8[:, d * 4:d * 4 + 4, :])
    lc = sbp.get()
    k.copy(lc[:], plc[:], eng="scalar")
    ptot = pps.get()
    k.mm(ptot[:], ones64[:], LW8[:])
    tot = sbp.get()
    k.copy(tot[:], ptot[:], eng="vector")
    for e in range(8):
        k.mm(psm[:, e, :], LW8[:, e, :], ones64[:, 0:2])
    gCT = smp.get()
    k.act(gCT[:], psm[:, :, 0], AF.Exp)
    eg, egi, egp, ehat = sbp.get(), sbp.get(), sbp.get(), sbp.get()
    k.act(eg[:], lc[:], AF.Exp)
    k.act(egi[:], lc[:], AF.Exp, scale=-1.0)
    t = sbp.get()
    k.tt(t[:], lc[:], LW8[:], ALU.subtract, eng="gpsimd")
    k.act(egp[:], t[:], AF.Exp)
    t = sbp.get()
    k.tt(t[:], tot[:], lc[:], ALU.subtract)
    k.act(ehat[:], t[:], AF.Exp)
    At, Bh, Kh = sbp.get(), sbp.get(), sbp.get()
    Atb, Btb, Ktb, Rtb = sbb.get(), sbb.get(), sbb.get(), sbb.get()
    k.tt(At[:], A8[:], egp[:], ALU.mult)
    k.copy(Atb[:], At[:], eng="gpsimd")
    k.tt(Btb[:], B8[:], egi[:], ALU.mult, eng="gpsimd")
    k.tt(Ktb[:], K8[:], egi[:], ALU.mult)
    k.tt(Rtb[:], R8[:], eg[:], ALU.mult, eng="gpsimd")
    k.tt(Bh[:], B8[:], ehat[:], ALU.mult)
    k.tt(Kh[:], K8[:], ehat[:], ALU.mult, eng="gpsimd")
    fmT = []
    for i, src in enumerate((Atb, Btb, Ktb, Rtb)):
        pT = ppb.get()
        for e in range(8):
            k.tr(pT[:, e, :], src[:, e, :], idb[:])
        dst = sbb.get()
        k.copy(dst[:], pT[:, 0:8, :], eng=("scalar" if i % 2 == 0 else "vector"))
        fmT.append(dst)
    AtT, BtT, KtT, RtT = fmT

    def score(lhs, rhs, mask):
        p_ = pps.get()
        for e in range(8):
            k.mm(p_[:, e, :], lhs[:, e, :], rhs[:, e, :])
        o_ = sbp.get()
        k.tt(o_[:], p_[:], mask, ALU.mult)
        return o_
    M = score(AtT, BtT, nm[:, 1])
    A = score(BtT, AtT, nm[:, 0])
    AakT = score(KtT, AtT, m01[:, 1])
    ArbT = score(BtT, RtT, m01[:, 0])
    ArkT = score(KtT, RtT, m01[:, 0])
    Xi, _ = tri_inverse(k, M, A, id8, pps, X.inv_f, None)
    pW = pps.get()
    for e in range(8):
        k.mm(pW[:, e, :], At[:, e, :], Xi[:, e, :])
    WT = sbp.get()
    k.copy(WT[:], pW[:], eng="scalar")
    pAkV = pps.get()
    for e in range(8):
        k.mm(pAkV[:, e, :], AakT[:, e, :], V8[:, e, :])
    AkV = sbp.get()
    k.copy(AkV[:], pAkV[:], eng="vector")
    pUv = pps.get()
    for e in range(8):
        k.mm(pUv[:, e, :], Xi[:, e, :], AkV[:, e, :])
    Uv = sbp.get()
    k.copy(Uv[:], pUv[:], eng="scalar")
    pe = pps.get()
    for e in range(8):
        k.mm(pe[:, e, :], WT[:, e, :], St[:, e, :])
    E = sbp.get()
    k.tt(E[:], Uv[:], pe[:], ALU.add)
    py = pps.get()
    for e in range(8):
        k.mm(py[:, e, :], RtT[:, e, :], Stb[:, e, :], start=True, stop=False)
        k.mm(py[:, e, :], ArbT[:, e, :], E[:, e, :], start=False, stop=False)
        k.mm(py[:, e, :], ArkT[:, e, :], V8[:, e, :], start=False, stop=True)
    y8 = sbp.get()
    k.copy(y8[:], py[:], eng="scalar")
    for d in range(2):
        if X.need_ctx or ch[d] >= 4:
            k.dma("scalar", S["RWY"][d, ch[d] * 64:(ch[d] + 1) * 64, :], y8[:, d * 4:d * 4 + 4, :].re("p h d -> p (h d)"))
    pS = pps.get()
    for e in range(8):
        k.mm(pS[:, e, :], Bh[:, e, :], E[:, e, :], start=True, stop=False)
        k.mm(pS[:, e, :], Kh[:, e, :], V8[:, e, :], start=False, stop=True)
    t = sbp.get()
    k.tt(t[:], St[:], gCT[:, :, None].bc([64, 8, 64]), ALU.mult, eng="gpsimd")
    k.tt(St[:], t[:], pS[:], ALU.add)
    k.copy(Stb[:], St[:], eng="gpsimd")


def scan_stage(k, C, l, need_ctx, do_dn=True, do_rw=True):
    with k.stage():
        sh = Lane()
        sh.tri = k.sb("tri", [64, 2, 64], F32)
        sh.id8 = k.sb("id8", [64, 8, 64], F32)
        k.dma("sync", sh.tri[:], C.I("c_tri")[:])
        k.dma("sync", sh.id8[:], C.I("c_ident8")[:])
        sh.idn = C.ident[0:64, 0:64]
        import os
        both = do_dn and do_rw
        sh.npp_dn = 3 if both else 7
        sh.npp_rw = 2 if both else 5
        sh.rw_psl = slice(64, 128) if os.environ.get("RW_HI", "1") == "1" else slice(0, 64)
        psm = k.ps("psm", [64, 8], F32)
        sh.psm_dn = psm[:]
        Xd = dn_scan_setup(k, C, need_ctx, sh) if do_dn else None
        Xr = rw_scan_setup(k, C, need_ctx, sh) if do_rw else None
        for s in range(NCH):
            lists = []
            if do_dn:
                with k.defer() as La:
                    dn_step(k, Xd, s)
                lists.append(La)
            if do_rw:
                with k.defer() as Lb:
                    rw_step(k, Xr, s)
                lists.append(Lb)
            k.replay(lists)
```

```python
import contextlib
import numpy as np
import concourse.bass as bass
import concourse.mybir as mybir
from concourse.bass_utils import run_bass_kernel_spmd

F32 = mybir.dt.float32
BF16 = mybir.dt.bfloat16
I32 = mybir.dt.int32
AF = mybir.ActivationFunctionType
ALU = mybir.AluOpType
AX = mybir.AxisListType

ENGS = ("tensor", "vector", "scalar", "gpsimd", "sync")
N_DSEM = 12
SEM_EPOCH = 1 << 40


class Tok:
    __slots__ = ("name", "w", "r")

    def __init__(self, name):
        self.name = name
        self.w = None
        self.r = []


class T:
    def __init__(self, k, name, t, space):
        self.k = k
        self.name = name
        self.t = t
        self.space = space
        self.tok = Tok(name)
        self.subs = {}

    def __getitem__(self, key):
        return V(self.t[key] if not isinstance(self.t, bass.AP) else self.t[key], [self.tok])

    def v(self):
        return self[:]

    def sub(self, key, idx):
        if key not in self.subs:
            self.subs[key] = Tok(f"{self.name}.{key}")
        return V(self.t[idx], [self.subs[key]])


class V:
    __slots__ = ("ap", "toks")

    def __init__(self, ap, toks):
        self.ap = ap
        self.toks = toks

    def __getitem__(self, key):
        return V(self.ap[key], self.toks)

    def re(self, s, **kw):
        return V(self.ap.rearrange(s, **kw), self.toks)

    def bc(self, shape):
        return V(self.ap.to_broadcast(shape), self.toks)

    def bitcast(self, dt):
        return V(self.ap.bitcast(dt), self.toks)

    def with_toks(self, toks):
        return V(self.ap, toks)


def _ap(x):
    return x.ap if isinstance(x, V) else x


class K:
    def __init__(self):
        self.nc = bass.Bass("TRN2", target_bir_lowering=False)
        self.stack = contextlib.ExitStack()
        self.q = {e: [] for e in ENGS}
        self.cnt = {e: 0 for e in ENGS}
        self.epoch = {e: 0 for e in ENGS}
        self.sems = {}
        self.seen = {e: {} for e in ENGS}
        self.dsem = {}
        self.dcnt = {}
        self.dq_i = {e: 0 for e in ENGS}
        self.uid = 0
        self.out_deps = []
        self.stage_stack = None

    def _name(self, n):
        self.uid += 1
        return f"{n}_{self.uid}"

    def sb(self, name, shape, dtype, stack=None):
        t = (stack or self.stage_stack or self.stack).enter_context(self.nc.sbuf_tensor(self._name(name), list(shape), dtype))
        return T(self, name, t, "sb")

    def ps(self, name, shape, dtype=F32, stack=None):
        t = (stack or self.stage_stack or self.stack).enter_context(self.nc.psum_tensor(self._name(name), list(shape), dtype))
        return T(self, name, t, "ps")

    def dram(self, name, shape, dtype, kind="Internal"):
        t = self.nc.dram_tensor(name, list(shape), dtype, kind=kind).ap()
        return T(self, name, t, "dram")

    def _sem(self, key):
        if key not in self.sems:
            self.sems[key] = self.stack.enter_context(self.nc.semaphore(self._name("s")))
        return self.sems[key]

    def _deps(self, reads, writes, pe_accum=False):
        deps = []
        for v in reads:
            for t in v.toks:
                if t.w is not None:
                    deps.append(t.w)
        for v in writes:
            for t in v.toks:
                if t.w is not None and not pe_accum:
                    deps.append(t.w)
                if not pe_accum:
                    deps.extend(t.r)
        return deps

    def _commit(self, me, reads, writes, pe_accum=False):
        for v in reads:
            for t in v.toks:
                t.r.append(me)
        for v in writes:
            for t in v.toks:
                t.w = me
                if not pe_accum:
                    t.r = []

    def _waits(self, eng, deps):
        need = {}
        for (sk, val) in deps:
            if self.seen[eng].get(sk, 0) >= val:
                continue
            if need.get(sk, 0) < val:
                need[sk] = val
        for sk, val in need.items():
            self.seen[eng][sk] = val
        return list(need.items())

    @contextlib.contextmanager
    def defer(self):
        prev = getattr(self, "_defer", None)
        lst = []
        self._defer = lst
        try:
            yield lst
        finally:
            self._defer = prev

    def _est(self, kind, a, kw):
        if kind == "dma":
            eng, out, in_ = a
            return eng, 60.0, [in_], [out], 2600.0
        eng = a[0]
        reads = [r for r in kw["reads"] if isinstance(r, V)]
        writes = [w for w in kw["writes"] if isinstance(w, V)]
        n = 1
        if writes:
            for d_ in writes[0].ap.shape[1:]:
                n *= d_
        if eng == "tensor":
            f32 = bool(reads) and reads[0].ap.dtype == F32
            dur = (45.0 + 1.0 * n) if f32 else (40.0 + 0.25 * n)
        elif eng == "vector":
            dur = 90.0 + 1.05 * n
        elif eng == "scalar":
            dur = 220.0 + 1.05 * n
        else:
            dur = 250.0 + 2.1 * n
        return eng, dur, reads, writes, 0.0

    def replay(self, lists):
        lists = [l for l in lists if l]
        if not lists:
            return
        if not hasattr(self, "_sim_eng"):
            self._sim_eng, self._sim_tok = {}, {}
        HOP = 350.0
        pos = [0] * len(lists)
        total = sum(len(l) for l in lists)
        ests = [[self._est(*e) for e in l] for l in lists]
        for _ in range(total):
            best, best_t = None, None
            for i in range(len(lists)):
                if pos[i] >= len(lists[i]):
                    continue
                eng, dur, reads, writes, lat = ests[i][pos[i]]
                t = self._sim_eng.get(eng, 0.0)
                for v in reads:
                    for tk in v.toks:
                        w = self._sim_tok.get(id(tk))
                        if w is not None:
                            t = max(t, w[0] + (HOP if w[1] != eng else 0.0))
                for v in writes:
                    for tk in v.toks:
                        w = self._sim_tok.get(id(tk))
                        if w is not None:
                            t = max(t, max(w[0], w[2]) + (HOP if w[1] != eng else 0.0))
                key = (t, pos[i] / len(lists[i]))
                if best is None or key < best_t:
                    best, best_t = i, key
            i = best
            eng, dur, reads, writes, lat = ests[i][pos[i]]
            t = best_t[0]
            fin = t + dur + lat
            self._sim_eng[eng] = t + dur
            for v in reads:
                for tk in v.toks:
                    w = self._sim_tok.get(id(tk))
                    if w is None:
                        self._sim_tok[id(tk)] = [0.0, eng, fin]
                    else:
                        w[2] = max(w[2], fin)
            for v in writes:
                for tk in v.toks:
                    self._sim_tok[id(tk)] = [fin, eng, fin]
            kind, a, kw = lists[i][pos[i]]
            pos[i] += 1
            getattr(self, kind)(*a, **kw)

    def op(self, eng, fn, reads=(), writes=(), pe_accum=False, same_eng_sync=True):
        if getattr(self, "_defer", None) is not None:
            self._defer.append(("op", (eng, fn), dict(reads=reads, writes=writes, pe_accum=pe_accum, same_eng_sync=same_eng_sync)))
            return None
        reads = [r for r in reads if isinstance(r, V)]
        writes = [w for w in writes if isinstance(w, V)]
        deps = self._deps(reads, writes, pe_accum)
        if eng == "tensor" or not same_eng_sync:
            deps = [d for d in deps if d[0][0] != eng or d[0][0] == "dma"]
        if self.cnt[eng] >= SEM_EPOCH:
            self.epoch[eng] += 1
            self.cnt[eng] = 0
        sk = (eng, self.epoch[eng])
        self._sem(sk)
        self.cnt[eng] += 1
        me = (sk, self.cnt[eng])
        waits = self._waits(eng, deps)
        self.q[eng].append((fn, waits, (sk, 1), self.cnt[eng]))
        self._commit(me, reads, writes, pe_accum)
        return me

    def dma(self, eng, out, in_, **kw):
        if getattr(self, "_defer", None) is not None:
            self._defer.append(("dma", (eng, out, in_), dict(kw)))
            return None
        deps = self._deps([in_], [out])
        i = self.dq_i[eng]
        self.dq_i[eng] += 1
        sk = ("dma", eng, i % N_DSEM)
        self._sem(sk)
        prev = self.dcnt.get(sk, 0)
        if prev:
            deps.append((sk, prev))
        self.dcnt[sk] = prev + 16
        me = (sk, prev + 16)
        waits = self._waits(eng, deps)
        o, a = _ap(out), _ap(in_)
        self.q[eng].append((lambda e: e.dma_start(out=o, in_=a, **kw), waits, (sk, 16), None))
        self._commit(me, [in_], [out])
        return me

    def barrier(self):
        alld = []
        for e in ENGS:
            for ep in range(self.epoch[e] + 1):
                sk = (e, ep)
                if sk in self.sems:
                    alld.append((sk, self.cnt[e] if ep == self.epoch[e] else SEM_EPOCH))
        for sk, v in self.dcnt.items():
            alld.append((sk, v))
        for e in ENGS:
            w = self._waits(e, alld)
            if w:
                self.q[e].append((None, w, None, None))

    def flush(self):
        import bisect
        nc = self.nc
        q = self.q
        sems = self.sems
        if not hasattr(self, "base_idx"):
            self.base_idx, self.base_val = {}, {}
        targets = {}
        for ename in ENGS:
            for (fn, waits, inc, idx) in q[ename]:
                for (sk, val) in waits:
                    if sk[0] != "dma":
                        targets.setdefault(sk, set()).add(val)
        for ename in ENGS:
            sk = (ename, self.epoch[ename])
            if sk in sems:
                targets.setdefault(sk, set()).add(self.cnt[ename])
        tl = {}
        for sk, st in targets.items():
            b = self.base_idx.get(sk, 0)
            tl[sk] = sorted(v for v in st if v > b)

        def val_of(sk, v):
            b = self.base_idx.get(sk, 0)
            bv = self.base_val.get(sk, 0)
            if v <= b:
                return bv
            return bv + bisect.bisect_left(tl[sk], v) + 1

        with nc.Block() as block:
            for ename in ENGS:
                ops = q[ename]
                if not ops:
                    continue

                def body(e, ops=ops):
                    for (fn, waits, inc, idx) in ops:
                        for (sk, val) in waits:
                            if sk[0] == "dma":
                                e.wait_ge(sems[sk], val)
                            else:
                                e.wait_ge(sems[sk], val_of(sk, val))
                        if fn is not None:
                            ins = fn(e)
                            if inc is not None:
                                if inc[0][0] == "dma":
                                    ins.then_inc(sems[inc[0]], inc[1])
                                else:
                                    lst = tl.get(inc[0], ())
                                    j = bisect.bisect_left(lst, idx)
                                    if j < len(lst) and lst[j] == idx:
                                        ins.then_inc(sems[inc[0]], 1)
                getattr(block, ename)(body)
        for sk, lst in tl.items():
            self.base_val[sk] = self.base_val.get(sk, 0) + len(lst)
            if sk[0] != "dma":
                self.base_idx[sk] = self.cnt[sk[0]]
        self.q = {e: [] for e in ENGS}

    @contextlib.contextmanager
    def stage(self, name=None):
        st = contextlib.ExitStack()
        prev = self.stage_stack
        self.stage_stack = st
        self.stage_i = getattr(self, "stage_i", 0) + 1
        with st:
            yield st
            self.barrier()
            if getattr(self, "scopes", False):
                import inspect
                nm = name or inspect.stack()[2].function
                with self.nc.named_scope(f"s{self.stage_i:03d}_{nm}"):
                    self.flush()
            else:
                self.flush()
        self.stage_stack = prev

    def mm(self, out, lhsT, rhs, start=True, stop=True, **kw):
        o, l, r = _ap(out), _ap(lhsT), _ap(rhs)
        return self.op("tensor", lambda e: e.matmul(o, l, r, start=start, stop=stop, **kw),
                       reads=[lhsT, rhs], writes=[out], pe_accum=not start)

    def tr(self, out, in_, ident):
        o, i, d = _ap(out), _ap(in_), _ap(ident)
        return self.op("tensor", lambda e: e.transpose(o, i, d), reads=[in_, ident], writes=[out])

    def act(self, out, in_, func, bias=None, scale=None, accum_out=None, eng="scalar"):
        o, i = _ap(out), _ap(in_)
        kw = {}
        rd = [in_]
        if bias is not None:
            kw["bias"] = _ap(bias)
            rd.append(bias)
        if scale is not None:
            kw["scale"] = _ap(scale)
            rd.append(scale)
        wr = [out]
        if accum_out is not None:
            kw["accum_out"] = _ap(accum_out)
            wr.append(accum_out)
        return self.op("scalar", lambda e: e.activation(o, i, func, **kw), reads=rd, writes=wr)

    def tt(self, out, in0, in1, op, eng="vector"):
        o, a, b = _ap(out), _ap(in0), _ap(in1)
        return self.op(eng, lambda e: e.tensor_tensor(o, a, b, op), reads=[in0, in1], writes=[out])

    def ts(self, out, in0, s1, op0, s2=None, op1=None, eng="vector", accum_out=None):
        o, a = _ap(out), _ap(in0)
        x1, x2 = _ap(s1), _ap(s2)
        kw = {}
        if op1 is not None:
            kw["op1"] = op1
        wr = [out]
        if accum_out is not None:
            kw["accum_out"] = _ap(accum_out)
            wr.append(accum_out)
        return self.op(eng, lambda e: e.tensor_scalar(o, a, x1, x2, op0, **kw), reads=[in0, s1, s2], writes=wr)

    def stt(self, out, in0, scalar, in1, op0, op1):
        o, a, s, b = _ap(out), _ap(in0), _ap(scalar), _ap(in1)
        return self.op("vector", lambda e: e.scalar_tensor_tensor(o, a, s, b, op0, op1), reads=[in0, scalar, in1], writes=[out])

    def copy(self, out, in_, eng="vector"):
        o, i = _ap(out), _ap(in_)
        if eng == "scalar":
            return self.op("scalar", lambda e: e.copy(o, i), reads=[in_], writes=[out])
        return self.op(eng, lambda e: e.tensor_copy(o, i), reads=[in_], writes=[out])

    def memset(self, out, val, eng="vector"):
        o = _ap(out)
        return self.op(eng, lambda e: e.memset(o, val), reads=[], writes=[out])

    def recip(self, out, in_):
        o, i = _ap(out), _ap(in_)
        return self.op("vector", lambda e: e.reciprocal(o, i), reads=[in_], writes=[out])

    def reduce(self, out, in_, op=None, axis=None, **kw):
        o, i = _ap(out), _ap(in_)
        op = op or ALU.add
        axis = axis or AX.X
        return self.op("vector", lambda e: e.tensor_reduce(o, i, axis, op, **kw), reads=[in_], writes=[out])

D = 1024
SEQ = 2048
CTX = 256
NT = SEQ + CTX
DFF = 2816
NL = 2
EPS = 1e-6
TILES = [(0, 256, 1)] + [(256 + 512 * i, 512, 0) for i in range(4)]
QS = [(0, 6), (6, 6), (12, 5), (17, 5)]


class Pool:
    def __init__(self, k, name, shape, dtype, n, space="sb", stack=None):
        mk = k.sb if space == "sb" else k.ps
        self.bufs = [mk(f"{name}{i}", shape, dtype, stack=stack) for i in range(n)]
        self.i = 0

    def get(self):
        b = self.bufs[self.i % len(self.bufs)]
        self.i += 1
        return b


class Ctx:
    pass


class TV:
    def __init__(self, t, psl):
        self.t, self.psl = t, psl
        self.tok = t.tok

    def __getitem__(self, key):
        if not isinstance(key, tuple):
            key = (key,)
        assert key[0] == slice(None), key
        return V(self.t.t[(self.psl,) + tuple(key[1:])], [self.t.tok])


class PoolV:
    def __init__(self, k, name, shape, dtype, n, psl, space="sb"):
        mk = k.sb if space == "sb" else k.ps
        full = [128] + list(shape[1:]) if psl.start else list(shape)
        self.bufs = [TV(mk(f"{name}{i}", full, dtype), psl) for i in range(n)]
        self.i = 0

    def get(self):
        b = self.bufs[self.i % len(self.bufs)]
        self.i += 1
        return b


def xt_view(C, t0, n):
    toks = [C.XT.subtok(i) for i, (s, sz, w) in enumerate(TILES) if s < t0 + n and t0 < s + sz]
    ap = C.XT.t.rearrange("(c p) t -> p c t", p=128)[:, :, t0:t0 + n]
    return V(ap, toks)


IN_SHAPES = {
    "xT": [D, NT], "cT": [128, 8, 2], "b_modT": [NL, 128, 72], "norm_wT": [NL, 128, 24],
    "w_mod": [NL, D, 9 * D], "ffn_w_gu": [NL, 2, D, 2 * DFF], "ffn_w_down": [NL, 2, DFF, D],
    "w_in": [NL, D, 3472], "w_out": [NL, D, D], "identD": [128, 128],
}


class LazyIn:
    def __init__(self, k, C):
        self.k, self.C, self.d = k, C, {}

    def __call__(self, name):
        if name not in self.d:
            self.d[name] = self.k.dram(name, IN_SHAPES[name], IN_DTYPES.get(name, F32), kind="ExternalInput")
        return self.d[name]


def declare_io(k, C, debug_out=()):
    C.I = LazyIn(k, C)
    C.XT = C.I("xT")
    C.XT.subtok = lambda i: C.XT.subs.setdefault(i, Tok(f"XT.{i}"))
    declare_scratch(k, C)
    if debug_out == "gin":
        IN_SHAPES["g_in"] = [NT, 1024]
        C.S["G"] = C.I("g_in")
    C.OUT = k.dram("outT", [D, SEQ], F32, kind="ExternalOutput")
    C.DBG = k.dram("dbgT", [D, NT], F32, kind="ExternalOutput") if debug_out else None


def setup_consts(k, C):
    C.ones_bf = k.sb("ones_bf", [128, 128], BF16, stack=k.stack)
    k.memset(C.ones_bf[:], 1.0)
    C.eps_t = k.sb("eps_t", [128, 1], F32, stack=k.stack)
    k.memset(C.eps_t[:], EPS)
    C.ident = k.sb("ident", [128, 128], F32, stack=k.stack)
    k.dma("sync", C.ident[:], C.I("identD")[:])
    C.P = [k.sb(f"P{l}", [128, 9, 8, 2], F32, stack=k.stack) for l in range(NL)]


import os
MODQ = int(os.environ.get('MODQ', '1'))


def mod_stage(k, C):
    with k.stage():
        ct = k.sb("ct", [128, 8, 2], F32)
        sc = k.sb("sc", [128, 8, 2], F32)
        k.dma("sync", ct[:], C.I("cT")[:])
        k.act(sc[:], ct[:], AF.Silu)
        wpool = Pool(k, "wm", [128, 8, 512], F32, 3)
        for l in range(NL):
            bm = k.sb(f"bm{l}", [128, 72], F32)
            nw = k.sb(f"nw{l}", [128, 24], F32)
            k.dma("sync", bm[:], C.I("b_modT")[l])
            k.dma("sync", nw[:], C.I("norm_wT")[l])
            pm = k.ps(f"pm{l}", [128, 72, 2], F32)
            wsrc = C.I("w_mod").t[l].rearrange("(kc p) n -> p kc n", p=128)
            for g in range(18):
                wm = wpool.get()
                k.dma(("sync", "scalar", "gpsimd")[g % MODQ] if MODQ > 1 else "sync", wm[:], V(wsrc[:, :, g * 512:(g + 1) * 512], [C.I("w_mod").tok]))
                for c4 in range(4):
                    ci = g * 4 + c4
                    for kc in range(8):
                        k.mm(pm[:, ci, :], wm[:, kc, c4 * 128:(c4 + 1) * 128], sc[:, kc, :],
                             start=(kc == 0), stop=(kc == 7))
            P = C.P[l]
            Pv = P[:].re("p a c w -> p (a c) w")
            k.tt(Pv, pm[:], bm[:, :, None].bc([128, 72, 2]), ALU.add)
            for s in range(3):
                k.stt(P[:, 3 * s + 1], P[:, 3 * s + 1], 1.0,
                      nw[:, s * 8:(s + 1) * 8, None].bc([128, 8, 2]), ALU.add, ALU.mult)
                if s != 1:
                    k.ts(P[:, 3 * s + 2], P[:, 3 * s + 2], 0.5, ALU.mult)


def hv_(hall, ti, c, t0, n):
    return hall.sub(ti, (slice(None), c, slice(t0, t0 + n)))


def norm_pass(k, C, P, s, hall, xpool, sq, pss, rs, tmpp, skip_ctx=False, only=None):
    for ti, (t0, n, w) in enumerate(TILES):
        if w == 1 and skip_ctx:
            continue
        if only is not None and ti != only:
            continue
        xt = xpool.get()
        k.dma("sync", xt[:, :, :n], xt_view(C, t0, n))
        k.act(sq[:, :, :n], xt[:, :, :n], AF.Square)
        for c in range(8):
            k.mm(pss[:, :n], C.ones_bf[:], sq[:, c, :n], start=(c == 0), stop=(c == 7))
        k.act(rs[:, :n], pss[:, :n], AF.Sqrt, scale=1.0 / D, bias=C.eps_t[:])
        k.recip(rs[:, :n], rs[:, :n])
        for c in range(8):
            tmp = tmpp.get()
            k.stt(tmp[:, :n], xt[:, c, :n], P[:, 3 * s + 1, c, w:w + 1], rs[:, :n], ALU.mult, ALU.mult)
            k.act(hv_(hall, ti, c, t0, n), tmp[:, :n], AF.Identity, bias=P[:, 3 * s, c, w:w + 1])


def ffn_tile(k, C, P, s, hall, ti, t0, n, w, nj, wg, wd, actp, pgp, pup, pdp, sgp, xpool):
    act = actp.get()
    for jj in range(nj):
        pg = pgp.get()
        pu = pup.get()
        for kc in range(8):
            k.mm(pg[:, :n], wg[:, kc, 0, jj * 128:(jj + 1) * 128], hv_(hall, ti, kc, t0, n),
                 start=(kc == 0), stop=(kc == 7))
        for kc in range(8):
            k.mm(pu[:, :n], wg[:, kc, 1, jj * 128:(jj + 1) * 128], hv_(hall, ti, kc, t0, n),
                 start=(kc == 0), stop=(kc == 7))
        sg = sgp.get()
        k.act(sg[:, :n], pg[:, :n], AF.Silu)
        k.tt(act[:, jj, :n], sg[:, :n], pu[:, :n], ALU.mult)
    xr = xpool.get()
    k.dma("sync", xr[:, :, :n], xt_view(C, t0, n))
    for m in range(8):
        pd = pdp.get()
        for jj in range(nj):
            k.mm(pd[:, :n], wd[:, jj, m * 128:(m + 1) * 128], act[:, jj, :n],
                 start=(jj == 0), stop=(jj == nj - 1))
        k.stt(xr[:, m, :n], pd[:, :n], P[:, 3 * s + 2, m, w:w + 1], xr[:, m, :n], ALU.mult, ALU.add)
    k.dma("sync", xt_view(C, t0, n), xr[:, :, :n])


def ffn_stage(k, C, l, f, s, skip_ctx=False):
    P = C.P[l]
    Wgu = C.I("ffn_w_gu").t[l, f].rearrange("(kc p) n -> p kc n", p=128)
    Wdn = C.I("ffn_w_down").t[l, f].rearrange("(j p) n -> p j n", p=128)
    with k.stage():
        hall = k.sb("hall", [128, 8, NT], BF16)
        xpool = Pool(k, "xt", [128, 8, 512], F32, 2)
        sq = k.sb("sq", [128, 8, 512], BF16)
        tmpp = Pool(k, "tmp", [128, 512], F32, 2)
        rs = k.sb("rs", [128, 512], F32)
        pss = k.ps("pss", [128, 512], F32)
        wgp = Pool(k, "wg", [128, 8, 2, 6 * 128], BF16, 2)
        wdp = Pool(k, "wd", [128, 6, D], BF16, 2)
        actp = Pool(k, "act", [128, 6, 512], BF16, 2)
        sgp = Pool(k, "sg", [128, 512], F32, 2)
        pgp = Pool(k, "pg", [128, 512], F32, 2, space="ps")
        pup = Pool(k, "pu", [128, 512], F32, 2, space="ps")
        pdp = Pool(k, "pd", [128, 512], F32, 2, space="ps")

        def hv(ti, c, t0, n):
            return hall.sub(ti, (slice(None), c, slice(t0, t0 + n)))

        norm_pass(k, C, P, s, hall, xpool, sq, pss, rs, tmpp, skip_ctx)
        for qi, (j0, nj) in enumerate(QS):
            wg = wgp.get()
            wd = wdp.get()
            k.dma("gpsimd", wg[:, :, 0, :nj * 128], V(Wgu[:, :, j0 * 128:(j0 + nj) * 128], [C.I("ffn_w_gu").tok]))
            k.dma("gpsimd", wg[:, :, 1, :nj * 128], V(Wgu[:, :, DFF + j0 * 128:DFF + (j0 + nj) * 128], [C.I("ffn_w_gu").tok]))
            k.dma("gpsimd", wd[:, :nj, :], V(Wdn[:, j0:j0 + nj, :], [C.I("ffn_w_down").tok]))
            for ti, (t0, n, w) in enumerate(TILES):
                if w == 1 and skip_ctx:
                    continue
                ffn_tile(k, C, P, s, hall, ti, t0, n, w, nj, wg, wd, actp, pgp, pup, pdp, sgp, xpool)
                continue
                act = actp.get()
                for jj in range(nj):
                    pg = pgp.get()
                    pu = pup.get()
                    for kc in range(8):
                        k.mm(pg[:, :n], wg[:, kc, 0, jj * 128:(jj + 1) * 128], hv(ti, kc, t0, n),
                             start=(kc == 0), stop=(kc == 7))
                    for kc in range(8):
                        k.mm(pu[:, :n], wg[:, kc, 1, jj * 128:(jj + 1) * 128], hv(ti, kc, t0, n),
                             start=(kc == 0), stop=(kc == 7))
                    sg = sgp.get()
                    k.act(sg[:, :n], pg[:, :n], AF.Silu)
                    k.tt(act[:, jj, :n], sg[:, :n], pu[:, :n], ALU.mult)
                xr = xpool.get()
                k.dma("sync", xr[:, :, :n], xt_view(C, t0, n))
                for m in range(8):
                    pd = pdp.get()
                    for jj in range(nj):
                        k.mm(pd[:, :n], wd[:, jj, m * 128:(m + 1) * 128], act[:, jj, :n],
                             start=(jj == 0), stop=(jj == nj - 1))
                    k.stt(xr[:, m, :n], pd[:, :n], P[:, 3 * s + 2, m, w:w + 1], xr[:, m, :n], ALU.mult, ALU.add)
                k.dma("sync", xt_view(C, t0, n), xr[:, :, :n])


PT = 3472
TM_GROUPS = [
    (0, 512, "HY", 0), (512, 256, "HY", 512),
    (1280, 256, "NAV", 0),
    (1536, 512, "DN", 0), (2048, 512, "DN", 512), (2560, 16, "DN", 1024),
    (2576, 512, "RW", 0), (3088, 256, "RW", 512),
]
FM_CHUNKS = [(768, 128, "NAQK", 0), (896, 128, "NAQK", 1), (1024, 128, "NAQK", 2), (1152, 128, "NAQK", 3),
             (3344, 32, "RWLR", 0), (3376, 32, "RWLR", 1), (3408, 64, "RWLR", 2)]


def declare_scratch(k, C):
    def mk(name, shape):
        kind = "ExternalOutput" if name in C.dbg_names else "Internal"
        return k.dram("s_" + name.lower(), shape, F32, kind=kind)
    C.S = {
        "HY": mk("HY", [NT, 768]),
        "NAV": mk("NAV", [NT, 256]),
        "DN": mk("DN", [NT, 1040]),
        "RW": mk("RW", [NT, 768]),
        "NAQK": mk("NAQK", [512, NT]),
        "RWLR": mk("RWLR", [3, 64, NT]),
        "RWP": mk("RWP", [NT, 10, 256]),
        "RWY": mk("RWY", [2, NT, 256]),
        "G": mk("G", [NT, 1024]),
        "DNQKV": mk("DNQKV", [NT, 768]),
        "DNGB": mk("DNGB", [NT, 16]),
        "DNO": mk("DNO", [2, NT, 256]),
    }


def proj_stage(k, C, l):
    P = C.P[l]
    Win = C.I("w_in").t[l].rearrange("(kc p) n -> p kc n", p=128)
    with k.stage():
        hall = k.sb("hall", [128, 8, NT], BF16)
        xpool = Pool(k, "xt", [128, 8, 512], F32, 2)
        sq = k.sb("sq", [128, 8, 512], BF16)
        tmpp = Pool(k, "tmp", [128, 512], F32, 2)
        rs = k.sb("rs", [128, 512], F32)
        pss = k.ps("pss", [128, 512], F32)
        win = k.sb("win", [128, 8, PT], BF16)
        for kc in range(8):
            k.dma("gpsimd", win[:, kc, :], V(Win[:, kc, :], [C.I("w_in").tok]))
        norm_pass(k, C, P, 1, hall, xpool, sq, pss, rs, tmpp)
        pp = Pool(k, "pp", [128, 512], F32, 4, space="ps")
        rowp = Pool(k, "row", [128, 2832], F32, 2)
        fmp = Pool(k, "fm", [128, 7, 512], F32, 2)
        ev = 0
        for tc in range(NT // 128):
            t0 = tc * 128
            ti = 0 if t0 < 256 else 1 + (t0 - 256) // 512
            row = rowp.get()
            off = 0
            offs = []
            for (c0, nc_, name, dcol) in TM_GROUPS:
                ps = pp.get()
                for kc in range(8):
                    k.mm(ps[:, :nc_], hv_(hall, ti, kc, t0, 128), win[:, kc, c0:c0 + nc_],
                         start=(kc == 0), stop=(kc == 7))
                k.copy(row[:, off:off + nc_], ps[:, :nc_], eng=("scalar" if ev % 2 else "vector"))
                ev += 1
                offs.append((off, nc_, name, dcol))
                off += nc_
            for name in ("HY", "NAV", "DN", "RW"):
                gs = [g for g in offs if g[2] == name]
                o0 = gs[0][0]
                tot = sum(g[1] for g in gs)
                k.dma("sync", C.S[name][t0:t0 + 128, 0:tot], row[:, o0:o0 + tot])
        for ti, (t0, n, w) in enumerate(TILES):
            fm = fmp.get()
            for i, (c0, ncl, name, di) in enumerate(FM_CHUNKS):
                ps = pp.get()
                for kc in range(8):
                    k.mm(ps[0:ncl, :n], win[:, kc, c0:c0 + ncl], hv_(hall, ti, kc, t0, n),
                         start=(kc == 0), stop=(kc == 7))
                k.copy(fm[0:ncl, i, :n], ps[0:ncl, :n], eng=("scalar" if ev % 2 else "vector"))
                ev += 1
            k.dma("sync", V(C.S["NAQK"].t.rearrange("(c p) t -> p c t", p=128)[:, :, t0:t0 + n], [C.S["NAQK"].tok]),
                  fm[:, 0:4, :n])
            for g_, ncl in ((0, 32), (1, 32), (2, 64)):
                k.dma("sync", C.S["RWLR"][g_, 0:ncl, t0:t0 + n], fm[0:ncl, 4 + g_, :n])


def outproj_stage(k, C, l, need_ctx):
    P = C.P[l]
    Wout = C.I("w_out").t[l].rearrange("(kc p) n -> p kc n", p=128)
    with k.stage():
        wo = k.sb("wo", [128, 8, D], BF16)
        k.dma("gpsimd", wo[:], V(Wout, [C.I("w_out").tok]))
        gpool = Pool(k, "gt", [128, D], F32, 3)
        gT = Pool(k, "gT", [128, 8, 512], BF16, 2)
        xpool = Pool(k, "xt", [128, 8, 512], F32, 2)
        ptp = Pool(k, "ptp", [128, 4, 128], F32, 2, space="ps")
        pyp = Pool(k, "py", [128, 512], F32, 2, space="ps")
        ev = 0
        for ti, (t0, n, w) in enumerate(TILES):
            if w == 1 and not need_ctx:
                continue
            g = gT.get()
            for sc_ in range(n // 128):
                gt = gpool.get()
                k.dma("sync", gt[:], C.S["G"][t0 + sc_ * 128:t0 + (sc_ + 1) * 128, :])
                for half in range(2):
                    pt = ptp.get()
                    for q in range(4):
                        c = half * 4 + q
                        k.tr(pt[:, q, :], gt[:, c * 128:(c + 1) * 128], C.ident[:])
                    k.copy(g[:, half * 4:(half + 1) * 4, sc_ * 128:(sc_ + 1) * 128], pt[:],
                           eng=("scalar" if ev % 2 else "vector"))
                    ev += 1
            xr = xpool.get()
            k.dma("sync", xr[:, :, :n], xt_view(C, t0, n))
            for m in range(8):
                py = pyp.get()
                for kc in range(8):
                    k.mm(py[:, :n], wo[:, kc, m * 128:(m + 1) * 128], g[:, kc, :n], start=(kc == 0), stop=(kc == 7))
                k.stt(xr[:, m, :n], py[:, :n], P[:, 5, m, w:w + 1], xr[:, m, :n], ALU.mult, ALU.add)
            k.dma("gpsimd", xt_view(C, t0, n), xr[:, :, :n])


def out_stage(k, C):
    with k.stage():
        xpool = Pool(k, "xo", [128, 8, 512], F32, 2)
        last = []
        for i in range(4):
            xo = xpool.get()
            k.dma("sync", xo[:], xt_view(C, 256 + 512 * i, 512))
            dst = V(C.OUT.t.rearrange("(c p) t -> p c t", p=128)[:, :, 512 * i:512 * (i + 1)], [C.OUT.tok])
            k.dma("sync", dst, xo[:])
        if C.DBG is not None:
            for ti, (t0, n, w) in enumerate(TILES):
                xo = xpool.get()
                k.dma("sync", xo[:, :, :n], xt_view(C, t0, n))
                dst = V(C.DBG.t.rearrange("(c p) t -> p c t", p=128)[:, :, t0:t0 + n], [C.DBG.tok])
                k.dma("sync", dst, xo[:, :, :n])


def build(plan=None, dbg_names=(), scopes=False):
    k = K()
    k.scopes = scopes
    C = Ctx()
    C.dbg_names = set(dbg_names)
    declare_io(k, C, debug_out=(False if plan in (None, 'full') else ("gin" if plan == "m3" else True)))
    with k.stage():
        setup_consts(k, C)
    mod_stage(k, C)
    plan = plan or "full"
    if plan == "full":
        for l in range(NL):
            need_ctx = l < NL - 1
            ffn_stage(k, C, l, 0, 0)
            proj_stage(k, C, l)
            for nm in ("hy", "na", "dnrw"):
                globals()[nm + "_stage"](k, C, l, need_ctx)
            outproj_stage(k, C, l, need_ctx)
            ffn_stage(k, C, l, 1, 2, skip_ctx=not need_ctx)
    if plan == "ffn1":
        ffn_stage(k, C, 0, 0, 0)
    if plan == "m1":
        proj_stage(k, C, 0)
    if plan == "m3":
        outproj_stage(k, C, 0, True)
    if plan.startswith("mix:"):
        proj_stage(k, C, 0)
        for nm in plan[4:].split(","):
            globals()[nm + "_stage"](k, C, 0, True)
    out_stage(k, C)
    return k, C


def host_prep(inp, b, used):
    f = np.float32
    m = {}
    for name in used:
        if name == "xT":
            v = np.concatenate([inp["ctx"][b], inp["x"][b]], axis=0).T
        elif name == "cT":
            v = np.stack([inp["c"][b].reshape(8, 128).T, inp["c_ctx"].reshape(8, 128).T], axis=-1)
        elif name == "b_modT":
            v = inp["b_mod"].reshape(NL, 72, 128).transpose(0, 2, 1)
        elif name == "norm_wT":
            v = inp["norm_w"].reshape(NL, 24, 128).transpose(0, 2, 1)
        elif name == "identD":
            v = np.eye(128, dtype=f)
        elif name in HOST_LAYOUT:
            v = HOST_LAYOUT[name](inp, b)
        else:
            v = inp[name]
        m[name] = np.ascontiguousarray(np.asarray(v, dtype=(ml_dtypes.bfloat16 if name in IN_DTYPES else f)))
    return m


HOST_LAYOUT = {}


def run(inputs, plan=None, dbg_names=(), extra=None, ncores=8):
    k, C = build(plan, dbg_names)
    extra = extra or {}
    used = [n for n in C.I.d.keys() if n not in extra]
    in_maps = [host_prep(inputs, b, used) for b in range(ncores)]
    for m in in_maps:
        m.update(extra)
    res = run_bass_kernel_spmd(k.nc, in_maps, core_ids=list(range(ncores)))
    return res


def kernel(**inputs):
    inputs = {kk: np.asarray(v) for kk, v in inputs.items()}
    res = run(inputs)
    out = np.stack([np.ascontiguousarray(r["outT"].T) for r in res.results], axis=0)
    return out.astype(np.float32)


IN_SHAPES["na_nT"] = [NL, 128, 2]
IN_SHAPES["na_biasT"] = [NL, 128, 8, 4, 4, 64]


def _na_nT(inp, b):
    return np.stack([np.tile(inp["na_q_norm"], (1, 2)), np.tile(inp["na_k_norm"], (1, 2))], axis=-1)


def _na_biasT(inp, b):
    a = np.arange(2)[:, None, None, None, None]
    kcol = np.arange(64)[None, :, None, None, None]
    rho = np.arange(8)[None, None, :, None, None]
    i = np.arange(4)[None, None, None, :, None]
    qcol = np.arange(64)[None, None, None, None, :]
    drow = 2 * i + a - rho + 0 * kcol + 0 * qcol
    dcol = np.clip(kcol - qcol, -15, 15) + 0 * drow
    cs = np.clip(qcol - 8, 0, 48)
    inwin = ((kcol >= cs) & (kcol < cs + 16)) & (drow > -100)
    rpb = inp["na_rpb"]
    g = rpb[:, :, drow + 7, dcol + 15]
    g = np.where(inwin[None, None], g, np.float32(-30000.0))
    g = g.transpose(0, 2, 3, 4, 1, 5, 6).reshape(NL, 128, 8, 4, 4, 64)
    return g


HOST_LAYOUT["na_nT"] = _na_nT
HOST_LAYOUT["na_biasT"] = _na_biasT


def na_stage(k, C, l, need_ctx):
    with contextlib.ExitStack() as outer:
        na_stage_(k, C, l, need_ctx, outer)


def na_stage_(k, C, l, need_ctx, outer):
    E = k.sb("E", [128, 8, 4, 4, 64], BF16, stack=outer)
    QK = k.sb("QK", [128, 4, NT], BF16, stack=outer)
    QM = k.sb("QM", [128, 2, 2, NT], BF16, stack=outer)
    Ve = k.sb("Ve", [128, 18, 4, 65], BF16, stack=outer)
    Vo = k.sb("Vo", [128, 17, 4, 65], BF16, stack=outer)
    with k.stage():
        bd = k.sb("bd", [128, 128], BF16)
        k.memset(bd[:], 0.0)
        k.memset(bd[0:64, 0:64], 1.0)
        k.memset(bd[64:128, 64:128], 1.0)
        nw = k.sb("nw", [128, 2], F32)
        k.dma("sync", nw[:], C.I("na_nT")[l])
        k.memset(QM[:], 0.0, eng="gpsimd")
        for half in range(2):
            bt = k.sb(f"bt{half}", [128, 4, 4, 4, 64], F32)
            k.dma("sync", bt[:], C.I("na_biasT")[l, :, half * 4:(half + 1) * 4])
            k.act(E[:, half * 4:(half + 1) * 4], bt[:], AF.Exp)
        qkp = Pool(k, "qk", [128, 4, 512], F32, 2)
        sqp = Pool(k, "sq", [128, 4, 512], BF16, 2)
        rnp = Pool(k, "rn", [128, 512], F32, 2)
        pss = Pool(k, "pss", [128, 512], F32, 2, space="ps")
        src = C.S["NAQK"].t.rearrange("(c p) t -> p c t", p=128)
        for ti, (t0, n, w) in enumerate(TILES):
            qk = qkp.get()
            k.dma("sync", qk[:, :, :n], V(src[:, :, t0:t0 + n], [C.S["NAQK"].tok]))
            sq = sqp.get()
            k.act(sq[:, :, :n], qk[:, :, :n], AF.Square)
            for c in range(4):
                ps = pss.get()
                k.mm(ps[:, :n], bd[:], sq[:, c, :n])
                rn = rnp.get()
                k.act(rn[:, :n], ps[:, :n], AF.Sqrt, scale=1.0 / 64, bias=C.eps_t[:])
                k.recip(rn[:, :n], rn[:, :n])
                if c >= 2:
                    k.stt(QK[:, c, t0:t0 + n], qk[:, c, :n], nw[:, 1:2], rn[:, :n], ALU.mult, ALU.mult)
                else:
                    for par in range(2):
                        pp_ = slice(par * 64, par * 64 + 64)
                        k.stt(QM[pp_, c, par, t0:t0 + n], qk[pp_, c, :n], nw[pp_, 0:1], rn[pp_, :n], ALU.mult, ALU.mult)
        k.memset(Ve[:], 1.0)
        k.memset(Vo[:], 1.0, eng="gpsimd")
        nav = C.S["NAV"]
        vst = k.sb("vst", [128, 18, 256], F32)
        vso = k.sb("vso", [128, 17, 256], F32)
        for c0 in range(0, 18, 6):
            c1 = min(18, c0 + 6)
            k.dma("sync", vst[:, c0:c1, :], V(nav.t[c0 * 128:c1 * 128, :].rearrange("(c p) d -> p c d", p=128), [nav.tok]))
            c1o = min(17, c0 + 6)
            k.dma("sync", vso[:, c0:c1o, :], V(nav.t[64 + c0 * 128:64 + c1o * 128, :].rearrange("(c p) d -> p c d", p=128), [nav.tok]))
        k.copy(Ve[:, :, :, 0:64], vst[:].re("p c (h d) -> p c h d", d=64), eng="vector")
        k.copy(Vo[:, :, :, 0:64], vso[:].re("p c (h d) -> p c h d", d=64), eng="gpsimd")
    import os
    if os.environ.get("NA_STOP") == "1":
        return
    with k.stage():
        psp = Pool(k, "ps", [128, 4, 6, 64], F32, 2, space="ps")
        pop = Pool(k, "po", [64, 4, 65], F32, 2, space="ps")
        Pp = Pool(k, "P", [128, 4, 6, 64], BF16, 2)
        rdp = Pool(k, "rd", [64, 4], F32, 2)
        orp = Pool(k, "orow", [64, 256], F32, 3)
        G = C.S["G"]
        for r in range(32):
            start = min(max(r - 4, 0), 24)
            rho = r - start
            qt0 = 256 + 64 * r
            ps = psp.get()
            for h in range(4):
                hp = slice((h % 2) * 64, (h % 2) * 64 + 64)
                for i in range(6):
                    kt0 = (256 + 64 * (start + 2 * i)) if i < 4 else (i - 4) * 128
                    k.mm(ps[:, h, i, :], QK[:, 2 + h // 2, kt0:kt0 + 128], QM[:, h // 2, h % 2, qt0:qt0 + 64])
            Pt = Pp.get()
            k.act(Pt[:], ps[:], AF.Exp, scale=0.125)
            k.tt(Pt[:, :, 0:4, :], Pt[:, :, 0:4, :], E[:, rho], ALU.mult)
            po = pop.get()
            for h in range(4):
                for i in range(6):
                    if i < 4:
                        r0 = start + 2 * i
                        vv = Ve[:, 2 + r0 // 2, h, :] if r0 % 2 == 0 else Vo[:, (r0 + 3) // 2, h, :]
                    else:
                        vv = Ve[:, i - 4, h, :]
                    k.mm(po[:, h, :], Pt[:, h, i, :], vv, start=(i == 0), stop=(i == 5))
            rd = rdp.get()
            k.recip(rd[:], po[:, :, 64])
            orow = orp.get()
            k.tt(orow[:].re("p (h d) -> p h d", d=64), po[:, :, 0:64], rd[:, :, None].bc([64, 4, 64]), ALU.mult)
            k.dma("sync", G[qt0:qt0 + 64, 256:512], orow[:])
    if os.environ.get("NA_STOP") == "2":
        return
    with k.stage():
        G = C.S["G"]
        if need_ctx:
            pcp = Pool(k, "pc", [128, 2, 256], F32, 2, space="ps")
            pocp = Pool(k, "poc", [128, 65], F32, 2, space="ps")
            Pcp = Pool(k, "Pc", [128, 2, 256], BF16, 2)
            oc = k.sb("oc", [128, 2, 256], F32)
            rdc = Pool(k, "rdc", [128, 1], F32, 2)
            for h in range(4):
                hp = slice((h % 2) * 64, (h % 2) * 64 + 64)
                pc = pcp.get()
                for kc in range(2):
                    k.mm(pc[:, kc, :], QK[:, 2 + h // 2, kc * 128:(kc + 1) * 128], QM[:, h // 2, h % 2, 0:256])
                Pc = Pcp.get()
                k.act(Pc[:], pc[:], AF.Exp, scale=0.125)
                for qc in range(2):
                    poc = pocp.get()
                    for kc in range(2):
                        k.mm(poc[:], Pc[:, kc, qc * 128:(qc + 1) * 128], Ve[:, kc, h, :], start=(kc == 0), stop=(kc == 1))
                    rd = rdc.get()
                    k.recip(rd[:], poc[:, 64:65])
                    k.ts(oc[:, qc, h * 64:(h + 1) * 64], poc[:, 0:64], rd[:, 0:1], ALU.mult)
            k.dma("sync", V(G.t[0:256, 256:512].rearrange("(c p) n -> p c n", p=128), [G.tok]), oc[:])


import math
import ml_dtypes

HY_BANDS = 16


def _hy_consts(n):
    f64 = np.float64
    N = 2 * n
    nch = n // 128
    t = np.linspace(0.0, 1.0, n, dtype=np.float32).astype(f64)
    ang = 2.0 * math.pi * np.arange(n, dtype=f64) / n
    bands = np.linspace(1e-4, HY_BANDS - 1, HY_BANDS, dtype=np.float32).astype(f64)[None]
    z = np.concatenate([t[:, None], np.cos(bands * ang[:, None]), -np.sin(bands * ang[:, None])], axis=-1)
    max_decay = math.log(1e-2) / 0.3
    min_decay = math.log(1e-2) / 1.5
    deltas = np.abs(np.linspace(min_decay, max_decay, 512, dtype=np.float32).astype(f64))
    idx = np.arange(n, dtype=f64) + 0.5
    ph = 2.0 * math.pi * np.outer(idx, idx) / N
    C4 = np.cos(ph)
    S4 = np.sin(ph)

    def tile_(M):
        return np.ascontiguousarray(M.reshape(nch, 128, nch, 128).transpose(2, 1, 0, 3).reshape(nch, 128, nch * 128))
    w = 2.0 * math.pi * idx / N
    cs = np.stack([(2.0 / N) * np.cos(w / 2), (2.0 / N) * np.sin(w / 2), -(2.0 / N) * np.cos(w / 2)], axis=-1)
    return {
        "zT": np.ascontiguousarray(z.T.astype(np.float32)),
        "ntcol": np.ascontiguousarray((-t).reshape(nch, 128).T.astype(np.float32)),
        "absdelta": np.ascontiguousarray(np.tile(deltas[None, :], (128, 1)).astype(np.float32)),
        "C4t": tile_(C4).astype(ml_dtypes.bfloat16),
        "S4t": tile_(S4).astype(ml_dtypes.bfloat16),
        "cs": np.ascontiguousarray(cs.reshape(nch, 128, 3).transpose(1, 0, 2).astype(np.float32)),
    }


_HYC = {}


def hy_const(n, name):
    if n not in _HYC:
        _HYC[n] = _hy_consts(n)
    return _HYC[n][name]


IN_DTYPES = {}
for _n in (SEQ, CTX):
    _nch = _n // 128
    IN_SHAPES[f"hy_zT_{_n}"] = [33, _n]
    IN_SHAPES[f"hy_ntcol_{_n}"] = [128, _nch]
    IN_SHAPES[f"hy_cs_{_n}"] = [128, _nch, 3]
    IN_SHAPES[f"hy_C4t_{_n}"] = [_nch, 128, _nch * 128]
    IN_SHAPES[f"hy_S4t_{_n}"] = [_nch, 128, _nch * 128]
    IN_DTYPES[f"hy_C4t_{_n}"] = BF16
    IN_DTYPES[f"hy_S4t_{_n}"] = BF16
    for _nm in ("zT", "ntcol", "cs", "C4t", "S4t"):
        HOST_LAYOUT[f"hy_{_nm}_{_n}"] = (lambda inp, b, _n=_n, _nm=_nm: hy_const(_n, _nm))
IN_SHAPES["hy_absdelta"] = [128, 512]
HOST_LAYOUT["hy_absdelta"] = lambda inp, b: hy_const(CTX, "absdelta")
IN_SHAPES["hy_fcol"] = [NL, 64, 3]
HOST_LAYOUT["hy_fcol"] = lambda inp, b: np.stack([inp["hy_f_b1"], inp["hy_f_b2"], inp["hy_f_freq"]], axis=-1)
for _nm, _sh in (("hy_f_w1", [NL, 33, 64]), ("hy_f_w2", [NL, 64, 64]), ("hy_f_w3", [NL, 64, 1024]),
                 ("hy_conv", [NL, 3, 768]), ("hy_bias", [NL, 2, 256])):
    IN_SHAPES[_nm] = _sh


def bc_load(k, dst, src_ap, tok, eng="sync"):
    P = dst.ap.shape[0]
    k.dma(eng, dst, V(src_ap.partition_broadcast(P), [tok]))


def sin_rr(k, out, in_ps, scale_col, bias_col, pools, npi, nfree):
    a1, a2, ai = pools
    t1 = a1.get()
    t2 = a2.get()
    ti = ai.get()
    P = out.ap.shape[0]
    k.act(t1[:P, :nfree], in_ps, AF.Identity, scale=scale_col, bias=bias_col)
    k.ts(t2[:P, :nfree], t1[:P, :nfree], 1.0 / (2.0 * math.pi), ALU.mult, 64.5, ALU.add)
    k.copy(ti[:P, :nfree], t2[:P, :nfree])
    k.tt(t1[:P, :nfree], t2[:P, :nfree], ti[:P, :nfree], ALU.subtract)
    k.stt(t2[:P, :nfree], t1[:P, :nfree], 0.0, t1[:P, :nfree], ALU.is_lt, ALU.add)
    k.act(out, t2[:P, :nfree], AF.Sin, scale=2.0 * math.pi, bias=npi[:P, 0:1])


def hyf_stage(k, C, l, n, KRI):
    nch = n // 128
    TW = min(512, n)
    with contextlib.ExitStack() as outer:
        hyf_stage_(k, C, l, n, KRI, nch, TW, outer)


def hyf_stage_(k, C, l, n, KRI, nch, TW, outer):
    Pb = k.sb("Pb", [128, nch, 512], BF16, stack=outer)
    Qb = k.sb("Qb", [128, nch, 512], BF16, stack=outer)
    cs = k.sb("cs", [128, nch, 3], F32, stack=outer)
    with k.stage():
        w1 = k.sb("w1", [33, 64], F32)
        w2 = k.sb("w2", [64, 64], F32)
        w3 = k.sb("w3", [64, 1024], F32)
        fcol = k.sb("fcol", [64, 3], F32)
        fb = k.sb("fb", [64, 2], F32)
        zT = k.sb("zT", [33, n], F32)
        npi = k.sb("npi", [128, 1], F32)
        ones_f = k.sb("ones_f", [128, 128], F32)
        k.memset(npi[:], -math.pi)
        k.memset(ones_f[:], 1.0)
        k.dma("sync", w1[:], C.I("hy_f_w1")[l])
        k.dma("sync", w2[:], C.I("hy_f_w2")[l])
        k.dma("sync", w3[:], C.I("hy_f_w3")[l])
        k.dma("sync", fcol[:], C.I("hy_fcol")[l])
        k.dma("sync", zT[:], C.I(f"hy_zT_{n}")[:])
        k.tt(fb[:], fcol[:, 0:2], fcol[:, 2:3].bc([64, 2]), ALU.mult)
        hid1 = k.sb("hid1", [64, n], F32)
        hid2 = k.sb("hid2", [64, n], F32)
        pools = (Pool(k, "a1", [64, 512], F32, 2), Pool(k, "a2", [64, 512], F32, 2), Pool(k, "ai", [64, 512], I32, 2))
        pmp = Pool(k, "pm", [64, 512], F32, 2, space="ps")
        for t0 in range(0, n, TW):
            ps = pmp.get()
            k.mm(ps[:, :TW], w1[:], zT[:, t0:t0 + TW])
            sin_rr(k, hid1[:, t0:t0 + TW], ps[:, :TW], fcol[:, 2:3], fb[:, 0:1], pools, npi, TW)
        for t0 in range(0, n, TW):
            ps = pmp.get()
            k.mm(ps[:, :TW], w2[:], hid1[:, t0:t0 + TW])
            sin_rr(k, hid2[:, t0:t0 + TW], ps[:, :TW], fcol[:, 2:3], fb[:, 1:2], pools, npi, TW)
        absd = k.sb("absd", [128, 512], F32)
        ntc = k.sb("ntc", [128, nch], F32)
        k.dma("sync", absd[:], C.I("hy_absdelta")[:])
        k.dma("sync", ntc[:], C.I(f"hy_ntcol_{n}")[:])
        k.dma("sync", cs[:], C.I(f"hy_cs_{n}")[:])
        HF = k.sb("HF", [128, nch, 512], F32)
        HB = k.sb("HB", [128, nch, 512], F32)
        php = Pool(k, "ph", [128, 512], F32, 3, space="ps")
        pl1 = k.ps("pl1", [128, 512], F32)
        winp = Pool(k, "win", [128, 512], F32, 2)
        absp = Pool(k, "abs", [128, 512], F32, 3)
        for c in range(nch):
            ph0 = php.get()
            ph1 = php.get()
            k.mm(ph0[:], hid2[:, c * 128:(c + 1) * 128], w3[:, 0:512])
            k.mm(ph1[:], hid2[:, c * 128:(c + 1) * 128], w3[:, 512:1024])
            win = winp.get()
            k.act(win[:], absd[:], AF.Exp, scale=ntc[:, c:c + 1])
            k.tt(HF[:, c, :], ph0[:], win[:], ALU.mult)
            k.tt(HB[:, c, :], ph1[:], win[:], ALU.mult)
            if c == 0:
                k.memset(HB[0:1, 0, :], 0.0)
            for j, src in enumerate((HF, HB)):
                ab = absp.get()
                k.act(ab[:], src[:, c, :], AF.Abs)
                k.mm(pl1[:], ones_f[:], ab[:], start=(c == 0 and j == 0), stop=(c == nch - 1 and j == 1))
        rn = k.sb("rn", [128, 512], F32)
        k.recip(rn[:], pl1[:])
        for c in range(nch):
            t = absp.get()
            k.tt(t[:], HF[:, c, :], HB[:, c, :], ALU.add, eng="gpsimd")
            k.tt(Pb[:, c, :], t[:], rn[:], ALU.mult, eng="gpsimd")
            t = absp.get()
            k.tt(t[:], HF[:, c, :], HB[:, c, :], ALU.subtract)
            k.tt(Qb[:, c, :], t[:], rn[:], ALU.mult)
    with k.stage():
        absp = Pool(k, "abs2", [128, 512], F32, 3)
        cbp = Pool(k, "cb", [128, nch, 128], BF16, 2)
        sbp = Pool(k, "sbk", [128, nch, 128], BF16, 2)
        psp = Pool(k, "psq", [128, 512], F32, 8, space="ps")
        kop = Pool(k, "ko", [128, 2, 512], F32, 2)
        C4 = C.I(f"hy_C4t_{n}")
        S4 = C.I(f"hy_S4t_{n}")
        for fc in range(nch):
            cb = cbp.get()
            sb_ = sbp.get()
            k.dma("sync", cb[:].re("p c m -> p (c m)"), C4[fc])
            k.dma("sync", sb_[:].re("p c m -> p (c m)"), S4[fc])
            pPc, pPs, pQc, pQs = psp.get(), psp.get(), psp.get(), psp.get()
            for c in range(nch):
                st, sp = (c == 0), (c == nch - 1)
                k.mm(pPc[:], cb[:, c, :], Pb[:, c, :], start=st, stop=sp)
                k.mm(pPs[:], sb_[:, c, :], Pb[:, c, :], start=st, stop=sp)
                k.mm(pQc[:], cb[:, c, :], Qb[:, c, :], start=st, stop=sp)
                k.mm(pQs[:], sb_[:, c, :], Qb[:, c, :], start=st, stop=sp)
            ko = kop.get()
            t = absp.get()
            k.ts(t[:], pPc[:], cs[:, fc, 0:1], ALU.mult)
            k.stt(ko[:, 0, :], pPs[:], cs[:, fc, 1:2], t[:], ALU.mult, ALU.add)
            t = absp.get()
            k.ts(t[:], pQc[:], cs[:, fc, 1:2], ALU.mult)
            k.stt(ko[:, 1, :], pQs[:], cs[:, fc, 2:3], t[:], ALU.mult, ALU.add)
            k.dma("gpsimd", KRI[fc * 128:(fc + 1) * 128], ko[:])


def hyc_stage(k, C, l, n, base, KRI):
    nch = n // 128
    HYs = C.S["HY"]
    G = C.S["G"]
    with k.stage():
        cw = k.sb("cw", [128, 3, 768], F32)
        hb = k.sb("hb", [128, 2, 256], F32)
        for i in range(3):
            bc_load(k, cw[:, i, :], C.I("hy_conv").t[l, i], C.I("hy_conv").tok)
        for i in range(2):
            bc_load(k, hb[:, i, :], C.I("hy_bias").t[l, i], C.I("hy_bias").tok)
        v = k.sb("v", [128, nch, 256], F32)
        x1 = k.sb("x1", [128, nch, 256], F32)
        x2 = k.sb("x2", [128, nch, 256], F32)
        vb = k.sb("vb", [128, nch, 256], BF16)
        z = k.sb("z", [128, nch, 256], F32)
        zb = k.sb("zb", [128, nch, 256], BF16)
        Yc = k.sb("Yc", [128, nch, 256], BF16)
        Ys = k.sb("Ys", [128, nch, 256], BF16)
        sp_ = Pool(k, "scur", [128, 768], F32, 2)
        pp_ = Pool(k, "sprev", [128, 768], F32, 2)
        np_ = Pool(k, "snext", [128, 768], F32, 2)
        up = Pool(k, "u", [128, 768], F32, 2)
        tp = Pool(k, "tt", [128, 768], F32, 2)
        for c in range(nch):
            r0 = base + c * 128
            sc_, sp, sn = sp_.get(), pp_.get(), np_.get()
            k.dma("sync", sc_[:], HYs[r0:r0 + 128, :])
            if c == 0:
                k.memset(sp[0:1, :], 0.0)
                k.dma("sync", sp[1:128, :], HYs[r0:r0 + 127, :])
            else:
                k.dma("sync", sp[:], HYs[r0 - 1:r0 + 127, :])
            if c == nch - 1:
                k.memset(sn[:], 0.0)
                k.dma("sync", sn[0:127, :], HYs[r0 + 1:r0 + 128, :])
            else:
                k.dma("sync", sn[:], HYs[r0 + 1:r0 + 129, :])
            u = up.get()
            t1 = tp.get()
            t2 = tp.get()
            k.tt(u[:], sc_[:], cw[:, 1, :], ALU.mult)
            k.tt(t1[:], sp[:], cw[:, 0, :], ALU.mult, eng="gpsimd")
            k.tt(t2[:], sn[:], cw[:, 2, :], ALU.mult, eng="gpsimd")
            k.tt(u[:], u[:], t1[:], ALU.add)
            k.tt(v[:, c, :], u[:, 0:256], t2[:, 0:256], ALU.add)
            k.tt(x1[:, c, :], u[:, 256:512], t2[:, 256:512], ALU.add)
            k.tt(x2[:, c, :], u[:, 512:768], t2[:, 512:768], ALU.add)
            k.copy(vb[:, c, :], v[:, c, :], eng="scalar")
        cbp = Pool(k, "cb", [128, nch, 128], BF16, 3)
        sbp = Pool(k, "sbk", [128, nch, 128], BF16, 3)
        pup = Pool(k, "pU", [128, 256], F32, 4, space="ps")
        pyp = Pool(k, "pY", [128, 256], F32, 2, space="ps")
        krp = Pool(k, "kr", [128, 2, 512], F32, 2)
        usp = Pool(k, "us", [128, 2, 256], F32, 2)
        tp2 = Pool(k, "t2", [128, 256], F32, 6)
        outp = Pool(k, "yo", [128, 256], F32, 2)
        C4 = C.I(f"hy_C4t_{n}")
        S4 = C.I(f"hy_S4t_{n}")
        for o in range(2):
            src = vb if o == 0 else zb
            for fc in range(nch):
                cb = cbp.get()
                sb_ = sbp.get()
                k.dma("sync", cb[:].re("p c m -> p (c m)"), C4[fc])
                k.dma("sync", sb_[:].re("p c m -> p (c m)"), S4[fc])
                pUc, pUs = pup.get(), pup.get()
                for c in range(nch):
                    st, sp = (c == 0), (c == nch - 1)
                    k.mm(pUc[:], cb[:, c, :], src[:, c, :], start=st, stop=sp)
                    k.mm(pUs[:], sb_[:, c, :], src[:, c, :], start=st, stop=sp)
                kr = krp.get()
                k.dma("sync", kr[:], KRI[fc * 128:(fc + 1) * 128])
                us = usp.get()
                k.copy(us[:, 0, :], pUc[:], eng="scalar")
                k.copy(us[:, 1, :], pUs[:], eng="scalar")
                KR = kr[:, 0, o * 256:(o + 1) * 256]
                KI = kr[:, 1, o * 256:(o + 1) * 256]
                a, b_, c_, d_ = tp2.get(), tp2.get(), tp2.get(), tp2.get()
                k.tt(a[:], us[:, 0, :], KR, ALU.mult)
                k.tt(b_[:], us[:, 1, :], KI, ALU.mult, eng="gpsimd")
                k.tt(Yc[:, fc, :], a[:], b_[:], ALU.add)
                k.tt(c_[:], us[:, 1, :], KR, ALU.mult, eng="gpsimd")
                k.tt(d_[:], us[:, 0, :], KI, ALU.mult)
                k.tt(Ys[:, fc, :], c_[:], d_[:], ALU.subtract, eng="gpsimd")
            for tc in range(nch):
                cb = cbp.get()
                sb_ = sbp.get()
                k.dma("sync", cb[:].re("p c m -> p (c m)"), C4[tc])
                k.dma("sync", sb_[:].re("p c m -> p (c m)"), S4[tc])
                py = pyp.get()
                for f in range(nch):
                    k.mm(py[:], cb[:, f, :], Yc[:, f, :], start=(f == 0), stop=False)
                    k.mm(py[:], sb_[:, f, :], Ys[:, f, :], start=False, stop=(f == nch - 1))
                a, b_ = tp2.get(), tp2.get()
                if o == 0:
                    k.tt(a[:], v[:, tc, :], hb[:, 0, :], ALU.mult, eng="gpsimd")
                    k.tt(b_[:], py[:], a[:], ALU.add)
                    k.tt(z[:, tc, :], b_[:], x1[:, tc, :], ALU.mult)
                    k.copy(zb[:, tc, :], z[:, tc, :], eng="scalar")
                else:
                    k.tt(a[:], z[:, tc, :], hb[:, 1, :], ALU.mult, eng="gpsimd")
                    k.tt(b_[:], py[:], a[:], ALU.add)
                    yo = outp.get()
                    k.tt(yo[:], b_[:], x2[:, tc, :], ALU.mult)
                    k.dma("scalar", G[base + tc * 128:base + (tc + 1) * 128, 0:256], yo[:])


def hy_stage(k, C, l, need_ctx):
    if not hasattr(C, "KRI"):
        C.KRI = {n: k.dram(f"s_kri{n}", [n, 2, 512], F32) for n in (SEQ, CTX)}
    hyf_stage(k, C, l, SEQ, C.KRI[SEQ])
    hyc_stage(k, C, l, SEQ, CTX, C.KRI[SEQ])
    if need_ctx:
        hyf_stage(k, C, l, CTX, C.KRI[CTX])
        hyc_stage(k, C, l, CTX, 0, C.KRI[CTX])


def _chunk_consts():
    i = np.arange(64)
    tri = np.zeros((64, 2, 64), np.float32)
    tri[:, 0, :] = (i[:, None] <= i[None, :])
    tri[:, 1, :] = (i[:, None] >= i[None, :])
    after_eq = np.zeros((64, 8, 64), bool)
    after_st = np.zeros((64, 8, 64), bool)
    before_st = np.zeros((64, 8, 64), bool)
    for e in range(8):
        if e < 4:
            after_eq[:, e, :] = i[None, :] >= i[:, None]
            after_st[:, e, :] = i[None, :] > i[:, None]
            before_st[:, e, :] = i[None, :] < i[:, None]
        else:
            after_eq[:, e, :] = i[None, :] <= i[:, None]
            after_st[:, e, :] = i[None, :] < i[:, None]
            before_st[:, e, :] = i[None, :] > i[:, None]
    NEG = np.float32(-30000.0)
    mneg = np.stack([np.where(after_eq, 0, NEG), np.where(before_st, 0, NEG)], axis=1).astype(np.float32)
    m01 = np.stack([after_eq, after_st, before_st], axis=1).astype(np.float32)
    ident8 = np.tile(np.eye(64, dtype=np.float32)[:, None, :], (1, 8, 1))
    return {"c_tri": tri, "c_mneg": np.ascontiguousarray(mneg), "c_m01": np.ascontiguousarray(m01), "c_ident8": ident8}


_CC = {}


def _cc(name):
    if not _CC:
        _CC.update(_chunk_consts())
    return _CC[name]


for _nm, _sh in (("c_tri", [64, 2, 64]), ("c_mneg", [64, 2, 8, 64]), ("c_m01", [64, 3, 8, 64]), ("c_ident8", [64, 8, 64])):
    IN_SHAPES[_nm] = _sh
    HOST_LAYOUT[_nm] = (lambda inp, b, _nm=_nm: _cc(_nm))

NCH = NT // 64
FORD = list(range(NCH))
BORD = [3, 2, 1, 0] + list(range(NCH - 1, 3, -1))


def tri_inverse(k, M, A, ident8, pool_ps, pool_f, pool_b, levels=5):
    X = pool_f.get()
    k.tt(X[:], ident8[:], A[:], ALU.subtract)
    Mk, Ak = M, A
    for lev in range(1, levels + 1):
        pM = pool_ps.get()
        for e in range(8):
            k.mm(pM[:, e, :], Ak[:, e, :], Mk[:, e, :])
        if lev < levels:
            pA = pool_ps.get()
            for e in range(8):
                k.mm(pA[:, e, :], Mk[:, e, :], Ak[:, e, :])
            An = pool_f.get()
            k.copy(An[:], pA[:], eng="vector")
        Mn = pool_f.get()
        k.copy(Mn[:], pM[:], eng="scalar")
        pX = pool_ps.get()
        for e in range(8):
            k.mm(pX[:, e, :], Mn[:, e, :], X[:, e, :])
        Xn = pool_f.get()
        k.tt(Xn[:], X[:], pX[:], ALU.add)
        X = Xn
        Mk = Mn
        if lev < levels:
            Ak = An
    Xb = None
    if pool_b is not None:
        Xb = pool_b.get()
        k.copy(Xb[:], X[:], eng="gpsimd")
    return X, Xb


IN_SHAPES["dn_conv"] = [NL, 3, 768]
IN_SHAPES["dn_dtb8"] = [NL, 8]
IN_SHAPES["dn_alog8"] = [NL, 8]
IN_SHAPES["dn_normT"] = [NL, 256]
HOST_LAYOUT["dn_dtb8"] = lambda inp, b: inp["dn_dt_bias"].reshape(NL, 8)
HOST_LAYOUT["dn_alog8"] = lambda inp, b: inp["dn_a_log"].reshape(NL, 8)
HOST_LAYOUT["dn_normT"] = lambda inp, b: np.tile(inp["dn_norm"], (1, 4))


class Lane:
    pass


def run_lanes(k, nlanes, mkpools, body, items):
    lanes = [mkpools(i) for i in range(nlanes)]
    items = list(items)
    for i0 in range(0, len(items), nlanes):
        lists = []
        for j, it in enumerate(items[i0:i0 + nlanes]):
            with k.defer() as L:
                body(it, lanes[j])
            lists.append(L)
        k.replay(lists)


def shifted_loads(k, src, r0, ncols, seq_first, seq_last, pools):
    pc, pp, pn = pools
    sc_, sp, sn = pc.get(), pp.get(), pn.get()
    W = sc_.ap_shape[1]
    k.dma("sync", sc_[:], src[r0:r0 + 128, 0:W])
    if seq_first:
        k.memset(sp[0:1, :], 0.0)
        k.dma("sync", sp[1:128, :], src[r0:r0 + 127, 0:ncols])
    else:
        k.dma("sync", sp[:], src[r0 - 1:r0 + 127, 0:ncols])
    if seq_last:
        k.memset(sn[:], 0.0)
        k.dma("sync", sn[0:127, :], src[r0 + 1:r0 + 128, 0:ncols])
    else:
        k.dma("sync", sn[:], src[r0 + 1:r0 + 129, 0:ncols])
    return sc_, sp, sn


def dn_pre_stage(k, C, l):
    S = C.S
    with k.stage():
        cw = k.sb("cw", [128, 3, 768], F32)
        for i in range(3):
            bc_load(k, cw[:, i, :], C.I("dn_conv").t[l, i], C.I("dn_conv").tok)
        dtb = k.sb("dtb", [128, 8], F32)
        nA = k.sb("nA", [128, 8], F32)
        bc_load(k, dtb[:], C.I("dn_dtb8").t[l], C.I("dn_dtb8").tok)
        bc_load(k, nA[:], C.I("dn_alog8").t[l], C.I("dn_alog8").tok)
        k.act(nA[:], nA[:], AF.Exp)
        k.ts(nA[:], nA[:], -1.0, ALU.mult)
        def mkpools(i):
            L = Lane()
            L.pc = Pool(k, f"scur{i}", [128, 1040], F32, 2)
            L.pp = Pool(k, f"sprev{i}", [128, 768], F32, 2)
            L.pn = Pool(k, f"snext{i}", [128, 768], F32, 2)
            for p_ in (L.pc, L.pp, L.pn):
                for b_ in p_.bufs:
                    b_.ap_shape = b_.t.shape
            L.up = Pool(k, f"u{i}", [128, 768], F32, 1)
            L.tp = Pool(k, f"tt{i}", [128, 768], F32, 3)
            L.qp = Pool(k, f"qo{i}", [128, 768], F32, 2)
            L.gbp = Pool(k, f"gb{i}", [128, 16], F32, 2)
            L.smp = Pool(k, f"sm{i}", [128, 8], F32, 4)
            return L

        def body(tc, L):
            r0 = tc * 128
            sc_, sp, sn = shifted_loads(k, S["DN"], r0, 768, tc in (0, 2), tc in (1, NT // 128 - 1), (L.pc, L.pp, L.pn))
            u, t1, t2 = L.up.get(), L.tp.get(), L.tp.get()
            k.tt(u[:], sc_[:, 0:768], cw[:, 1, :], ALU.mult)
            k.tt(t1[:], sp[:], cw[:, 0, :], ALU.mult, eng="gpsimd")
            k.tt(t2[:], sn[:], cw[:, 2, :], ALU.mult, eng="gpsimd")
            k.tt(u[:], u[:], t1[:], ALU.add)
            k.tt(u[:], u[:], t2[:], ALU.add)
            qo = L.qp.get()
            k.act(qo[:], u[:], AF.Silu)
            sq = L.tp.get()
            k.tt(sq[:, 0:512], qo[:, 0:512], qo[:, 0:512], ALU.mult, eng="gpsimd")
            ss = L.smp.get()
            k.reduce(ss[:], sq[:, 0:512].re("p (g d) -> p g d", d=64))
            rn = L.smp.get()
            k.act(rn[:], ss[:], AF.Sqrt, bias=C.eps_t[:])
            k.recip(rn[:], rn[:])
            k.ts(rn[:, 0:4], rn[:, 0:4], 0.125, ALU.mult)
            k.tt(qo[:, 0:512].re("p (g d) -> p g d", d=64), qo[:, 0:512].re("p (g d) -> p g d", d=64),
                 rn[:, :, None].bc([128, 8, 64]), ALU.mult)
            k.dma("scalar", S["DNQKV"][r0:r0 + 128, :], qo[:])
            ba = sc_[:, 1024:1040].re("p (d a h) -> p d a h", d=2, a=2)
            gb = L.gbp.get()
            k.act(gb[:, 8:16].re("p (d h) -> p d h", d=2), ba[:, :, 0, :], AF.Sigmoid)
            x = L.smp.get()
            k.tt(x[:].re("p (d h) -> p d h", d=2), ba[:, :, 1, :], dtb[:].re("p (d h) -> p d h", d=2), ALU.add)
            k.act(x[:], x[:], AF.Exp)
            k.act(x[:], x[:], AF.Ln, bias=1.0)
            k.tt(gb[:, 0:8], x[:], nA[:], ALU.mult)
            k.dma("scalar", S["DNGB"][r0:r0 + 128, :], gb[:])

        run_lanes(k, 3, mkpools, body, range(NT // 128))


def dn_scan_stage(k, C, l, need_ctx):
    S = C.S
    with k.stage():
        tri = k.sb("tri", [64, 2, 64], F32)
        mneg = k.sb("mneg", [64, 2, 8, 64], F32)
        id8 = k.sb("id8", [64, 8, 64], F32)
        k.dma("sync", tri[:], C.I("c_tri")[:])
        k.dma("sync", mneg[:], C.I("c_mneg")[:])
        k.dma("sync", id8[:], C.I("c_ident8")[:])
        idn = C.ident[0:64, 0:64]
        idbt = k.sb("idbt", [64, 64], BF16)
        k.copy(idbt[:], C.ident[0:64, 0:64])
        idb = idbt[:]
        St = k.sb("St", [64, 8, 64], F32)
        k.memset(St[:], 0.0)
        pps = Pool(k, "pp", [64, 8, 64], F32, 7, space="ps")
        psm = k.ps("psm", [64, 8], F32)
        sbp = Pool(k, "w", [64, 8, 64], F32, 40)
        inv_f = Pool(k, "ivf", [64, 8, 64], F32, 8)
        inv_b = Pool(k, "ivb", [64, 8, 64], BF16, 6)
        sbb = Pool(k, "wb", [64, 8, 64], BF16, 24)
        Stb = k.sb("Stb", [64, 8, 64], BF16)
        k.memset(Stb[:], 0.0)
        qkp = Pool(k, "qk8", [64, 2, 8, 64], F32, 2)
        v8p = Pool(k, "v8", [64, 8, 64], F32, 2)
        gbp = Pool(k, "gb8", [64, 2, 8], F32, 2)
        smp = Pool(k, "sm", [64, 8], F32, 8)
        import os
        nsteps = int(os.environ.get("DN_STEPS", str(NCH)))
        part = int(os.environ.get("DN_PART", "99"))
        for s in range(nsteps):
            ch = (FORD[s], BORD[s])
            qk8, v8, gb8 = qkp.get(), v8p.get(), gbp.get()
            for d in range(2):
                r0 = ch[d] * 64
                es = slice(d * 4, d * 4 + 4)
                k.dma("sync", qk8[:, :, es, :], V(S["DNQKV"].t[r0:r0 + 64, 0:512].rearrange("p (a h d) -> p a h d", a=2, h=4), [S["DNQKV"].tok]))
                k.dma("sync", v8[:, es, :], V(S["DNQKV"].t[r0:r0 + 64, 512:768].rearrange("p (h d) -> p h d", h=4), [S["DNQKV"].tok]))
                k.dma("sync", gb8[:, 0, es], S["DNGB"][r0:r0 + 64, d * 4:d * 4 + 4])
                k.dma("sync", gb8[:, 1, es], S["DNGB"][r0:r0 + 64, 8 + d * 4:8 + d * 4 + 4])
            beta_bc = gb8[:, 1, :, None].bc([64, 8, 64])
            if part < 1:
                continue
            for d in range(2):
                k.mm(psm[:, d * 4:d * 4 + 4], tri[:, d, :], gb8[:, 0, d * 4:d * 4 + 4])
            gcc = smp.get()
            k.copy(gcc[:], psm[:], eng="scalar")
            sub = int(os.environ.get("DN_SUB", "99"))
            if sub < 2:
                continue
            grep = sbp.get()
            k.copy(grep[:], gb8[:, 0, :, None].bc([64, 8, 64]))
            pgcb = pps.get()
            for e in range(8):
                k.mm(pgcb[:, e, :], grep[:, e, :], tri[:, e // 4, :])
            if sub < 3:
                continue
            gcb = sbp.get()
            k.copy(gcb[:], pgcb[:], eng="scalar")
            D1 = sbp.get()
            k.tt(D1[:], gcb[:], gcc[:, :, None].bc([64, 8, 64]), ALU.subtract)
            egcb = sbp.get()
            k.act(egcb[:], gcb[:], AF.Exp)
            if sub < 4:
                continue
            t = sbp.get()
            k.tt(t[:], D1[:], mneg[:, 0], ALU.add, eng="gpsimd")
            E1 = sbp.get()
            k.act(E1[:], t[:], AF.Exp)
            if sub < 5:
                continue
            t = sbp.get()
            k.stt(t[:], D1[:], -1.0, mneg[:, 1], ALU.mult, ALU.add)
            E2 = sbp.get()
            k.act(E2[:], t[:], AF.Exp)
            if sub < 6:
                continue
            egc = smp.get()
            k.act(egc[:], gcc[:], AF.Exp)
            ekd = smp.get()
            k.act(ekd[:, 0:4], D1[:, 0:4, 63], AF.Exp)
            k.act(ekd[:, 4:8], D1[:, 4:8, 0], AF.Exp)
            if part < 2:
                continue
            pKT, pQT = pps.get(), pps.get()
            for e in range(8):
                k.tr(pKT[:, e, :], qk8[:, 1, e, :], idn)
            for e in range(8):
                k.tr(pQT[:, e, :], qk8[:, 0, e, :], idn)
            KT, QT = sbb.get(), sbb.get()
            k.copy(KT[:], pKT[:], eng="scalar")
            k.copy(QT[:], pQT[:], eng="vector")
            if part < 3:
                continue
            pG, pQK = pps.get(), pps.get()
            for e in range(8):
                k.mm(pG[:, e, :], KT[:, e, :], KT[:, e, :])
            for e in range(8):
                k.mm(pQK[:, e, :], KT[:, e, :], QT[:, e, :])
            t = sbp.get()
            k.tt(t[:], pG[:], E2[:], ALU.mult)
            M = sbp.get()
            k.tt(M[:], t[:], beta_bc, ALU.mult, eng="gpsimd")
            attnT = sbb.get()
            k.tt(attnT[:], pQK[:], E1[:], ALU.mult)
            pA = pps.get()
            for e in range(8):
                k.tr(pA[:, e, :], M[:, e, :], idn)
            A = sbp.get()
            k.copy(A[:], pA[:], eng="scalar")
            if part < 4:
                continue
            X = tri_inverse(k, M, A, id8, pps, inv_f, inv_b)
            if part < 5:
                continue
            Vb = sbb.get()
            k.tt(Vb[:], v8[:], beta_bc, ALU.mult, eng="gpsimd")
            bg = smp.get()
            k.tt(bg[:], gb8[:, 1, :], egc[:], ALU.mult)
            KBG = sbb.get()
            k.tt(KBG[:], qk8[:, 1], bg[:, :, None].bc([64, 8, 64]), ALU.mult, eng="gpsimd")
            pU, pW = pps.get(), pps.get()
            for e in range(8):
                k.mm(pU[:, e, :], X[:, e, :], Vb[:, e, :])
            for e in range(8):
                k.mm(pW[:, e, :], KBG[:, e, :], X[:, e, :])
            U, WT = sbp.get(), sbb.get()
            k.copy(U[:], pU[:], eng="scalar")
            k.copy(WT[:], pW[:], eng="vector")
            QdT = sbb.get()
            k.tt(QdT[:], QT[:], egcb[:], ALU.mult, eng="gpsimd")
            Kd = sbb.get()
            k.tt(Kd[:], qk8[:, 1], ekd[:, :, None].bc([64, 8, 64]), ALU.mult, eng="gpsimd")
            if part < 6:
                continue
            pv = pps.get()
            for e in range(8):
                k.mm(pv[:, e, :], WT[:, e, :], Stb[:, e, :])
            vnew = sbb.get()
            k.tt(vnew[:], U[:], pv[:], ALU.subtract)
            po = pps.get()
            for e in range(8):
                k.mm(po[:, e, :], QdT[:, e, :], Stb[:, e, :], start=True, stop=False)
                k.mm(po[:, e, :], attnT[:, e, :], vnew[:, e, :], start=False, stop=True)
            o8 = sbp.get()
            k.copy(o8[:], po[:], eng="scalar")
            for d in range(2):
                if need_ctx or ch[d] >= 4:
                    k.dma("scalar", S["DNO"][d, ch[d] * 64:(ch[d] + 1) * 64, :], o8[:, d * 4:d * 4 + 4, :].re("p h d -> p (h d)"))
            pS = pps.get()
            for e in range(8):
                k.mm(pS[:, e, :], Kd[:, e, :], vnew[:, e, :])
            glast = smp.get()
            k.copy(glast[:, 0:4], egcb[:, 0:4, 63])
            k.copy(glast[:, 4:8], egcb[:, 4:8, 0])
            t = sbp.get()
            k.tt(t[:], St[:], glast[:, :, None].bc([64, 8, 64]), ALU.mult)
            k.tt(St[:], t[:], pS[:], ALU.add)
            k.copy(Stb[:], St[:], eng="gpsimd")


def dn_post_stage(k, C, l, need_ctx):
    S = C.S
    with k.stage():
        nwt = k.sb("nwt", [128, 256], F32)
        bc_load(k, nwt[:], C.I("dn_normT").t[l], C.I("dn_normT").tok)

        def mkpools(i):
            L = Lane()
            L.op_ = Pool(k, f"o{i}", [128, 2, 256], F32, 2)
            L.zp = Pool(k, f"z{i}", [128, 256], F32, 2)
            L.tp = Pool(k, f"t{i}", [128, 256], F32, 6)
            L.smp = Pool(k, f"sm{i}", [128, 4], F32, 4)
            return L

        def body(tc, L):
            r0 = tc * 128
            o = L.op_.get()
            k.dma("sync", o[:], V(S["DNO"].t[:, r0:r0 + 128, :].rearrange("d p c -> p d c"), [S["DNO"].tok]))
            zt = L.zp.get()
            k.dma("sync", zt[:], S["DN"][r0:r0 + 128, 768:1024])
            os_ = L.tp.get()
            k.tt(os_[:], o[:, 0, :], o[:, 1, :], ALU.add)
            sq = L.tp.get()
            k.tt(sq[:], os_[:], os_[:], ALU.mult, eng="gpsimd")
            ss = L.smp.get()
            k.reduce(ss[:], sq[:].re("p (h d) -> p h d", d=64))
            rn = L.smp.get()
            k.act(rn[:], ss[:], AF.Sqrt, scale=1.0 / 64, bias=C.eps_t[:])
            k.recip(rn[:], rn[:])
            sz = L.tp.get()
            k.act(sz[:], zt[:], AF.Silu)
            a = L.tp.get()
            k.tt(a[:].re("p (h d) -> p h d", d=64), os_[:].re("p (h d) -> p h d", d=64), rn[:, :, None].bc([128, 4, 64]), ALU.mult)
            k.tt(a[:], a[:], nwt[:], ALU.mult, eng="gpsimd")
            y = L.tp.get()
            k.tt(y[:], a[:], sz[:], ALU.mult)
            k.dma("scalar", S["G"][r0:r0 + 128, 512:768], y[:])

        run_lanes(k, 3, mkpools, body, [tc for tc in range(NT // 128) if not (tc < 2 and not need_ctx)])


def dn_stage(k, C, l, need_ctx):
    dn_pre_stage(k, C, l)
    scan_stage(k, C, l, need_ctx, do_dn=True, do_rw=False)
    dn_post_stage(k, C, l, need_ctx)


for _nm, _sh in (("rw_mu", [NL, 2, 896]), ("rw_w_up", [NL, 2, 32, 256]), ("rw_a_up", [NL, 2, 32, 256]),
                 ("rw_g_up", [NL, 64, 256]), ("rw_w0", [NL, 2, 256]), ("rw_a0", [NL, 2, 256]),
                 ("rw_k_k", [NL, 256]), ("rw_k_a", [NL, 256]), ("rw_ln_w", [NL, 256]), ("rw_ln_b", [NL, 256]),
                 ("rw_r_kT", [NL, 256]), ("rw_mulrT", [NL, 64, 3, 2])):
    IN_SHAPES[_nm] = _sh
HOST_LAYOUT["rw_r_kT"] = lambda inp, b: inp["rw_r_k"].reshape(NL, 256)


def _rw_mulrT(inp, b):
    mu = inp["rw_mu"]
    o = np.zeros((NL, 64, 3, 2), np.float32)
    o[:, 0:32, 0, :] = mu[:, :, 768:800].transpose(0, 2, 1)
    o[:, 0:32, 1, :] = mu[:, :, 800:832].transpose(0, 2, 1)
    o[:, :, 2, :] = mu[:, :, 832:896].transpose(0, 2, 1)
    return o


HOST_LAYOUT["rw_mulrT"] = _rw_mulrT
SEQS = ((0, CTX), (CTX, NT))
LW_SCALE = -math.exp(-0.5)


def rw_pre_stage(k, C, l):
    S = C.S
    with k.stage():
        mulr = k.sb("mulr", [64, 3, 2], F32)
        k.dma("sync", mulr[:], C.I("rw_mulrT")[l])
        c0lr = k.sb("c0lr", [64, 3], F32)
        k.tt(c0lr[:], mulr[:, :, 0], mulr[:, :, 1], ALU.add)
        k.ts(c0lr[:], c0lr[:], -1.0, ALU.mult, 1.0, ALU.add)
        lrT = []
        for g_, ncl, fn in ((0, 32, AF.Tanh), (1, 32, None), (2, 64, AF.Sigmoid)):
            u = k.sb(f"lru{g_}", [64, NT], F32)
            o = k.sb(f"lro{g_}", [64, NT], F32)
            k.dma("sync", u[0:ncl, :], S["RWLR"][g_, 0:ncl, :])
            k.ts(o[0:ncl, :], u[0:ncl, :], c0lr[0:ncl, g_:g_ + 1], ALU.mult)
            for (a, b_) in SEQS:
                k.stt(o[0:ncl, a + 1:b_], u[0:ncl, a:b_ - 1], mulr[0:ncl, g_, 0:1], o[0:ncl, a + 1:b_], ALU.mult, ALU.add)
                k.stt(o[0:ncl, a:b_ - 1], u[0:ncl, a + 1:b_], mulr[0:ncl, g_, 1:2], o[0:ncl, a:b_ - 1], ALU.mult, ALU.add)
            if fn is not None:
                k.act(o[0:ncl, :], o[0:ncl, :], fn)
            lrT.append(o)
        wup = k.sb("wup", [32, 2, 256], F32)
        aup = k.sb("aup", [32, 2, 256], F32)
        gup = k.sb("gup", [64, 256], F32)
        w0r = k.sb("w0r", [1, 2, 256], F32)
        a0r = k.sb("a0r", [1, 2, 256], F32)
        ones1 = k.sb("ones1", [1, 128], F32)
        k.memset(ones1[:], 1.0)
        for d in range(2):
            k.dma("sync", wup[:, d, :], C.I("rw_w_up")[l, d])
            k.dma("sync", aup[:, d, :], C.I("rw_a_up")[l, d])
            k.dma("sync", w0r[:, d, :], C.I("rw_w0")[l, d:d + 1])
            k.dma("sync", a0r[:, d, :], C.I("rw_a0")[l, d:d + 1])
        k.dma("sync", gup[:], C.I("rw_g_up")[l])
        mu = k.sb("mu", [128, 2, 768], F32)
        for i in range(2):
            bc_load(k, mu[:, i, :], C.I("rw_mu").t[l, i, 0:768], C.I("rw_mu").tok)
        c0 = k.sb("c0", [128, 768], F32)
        k.tt(c0[:], mu[:, 0, :], mu[:, 1, :], ALU.add)
        k.ts(c0[:], c0[:], -1.0, ALU.mult, 1.0, ALU.add)
        kkw = k.sb("kkw", [128, 256], F32)
        kaw = k.sb("kaw", [128, 256], F32)
        omka = k.sb("omka", [128, 256], F32)
        bc_load(k, kkw[:], C.I("rw_k_k").t[l], C.I("rw_k_k").tok)
        bc_load(k, kaw[:], C.I("rw_k_a").t[l], C.I("rw_k_a").tok)
        k.ts(omka[:], kaw[:], -1.0, ALU.mult, 1.0, ALU.add)
        def mkpools(i):
            L = Lane()
            L.pc = Pool(k, f"scur{i}", [128, 768], F32, 2)
            L.pp = Pool(k, f"sprev{i}", [128, 768], F32, 2)
            L.pn = Pool(k, f"snext{i}", [128, 768], F32, 2)
            for p_ in (L.pc, L.pp, L.pn):
                for b_ in p_.bufs:
                    b_.ap_shape = b_.t.shape
            L.tp = Pool(k, f"tt{i}", [128, 768], F32, 3)
            L.op_ = Pool(k, f"out{i}", [128, 10, 256], F32, 2)
            L.t2 = Pool(k, f"t2{i}", [128, 256], F32, 6)
            L.smp = Pool(k, f"sm{i}", [128, 4], F32, 4)
            L.plp = Pool(k, f"pl{i}", [128, 2, 256], F32, 3, space="ps")
            return L

        def body(tc, L):
            pc, pp, pn, tp, op_, t2, smp, plp = L.pc, L.pp, L.pn, L.tp, L.op_, L.t2, L.smp, L.plp
            r0 = tc * 128
            sc_, sp, sn = shifted_loads(k, S["RW"], r0, 768, tc in (0, 2), tc in (1, NT // 128 - 1), (pc, pp, pn))
            o = op_.get()
            s_ = tp.get()
            t1 = tp.get()
            k.tt(s_[:], sc_[:], c0[:], ALU.mult)
            k.tt(t1[:], sp[:], mu[:, 0, :], ALU.mult, eng="gpsimd")
            k.tt(s_[:], s_[:], t1[:], ALU.add)
            t1 = tp.get()
            k.tt(t1[:], sn[:], mu[:, 1, :], ALU.mult, eng="gpsimd")
            k.tt(s_[:, 0:256], s_[:, 0:256], t1[:, 0:256], ALU.add)
            k.tt(s_[:, 256:512], s_[:, 256:512], t1[:, 256:512], ALU.add)
            k.tt(o[:, 1, :], s_[:, 512:768], t1[:, 512:768], ALU.add)
            k.copy(o[:, 0, :], s_[:, 0:256], eng="scalar")
            kcur = s_[:, 256:512]
            tok = slice(r0, r0 + 128)
            pwt, pat, pgt = plp.get(), plp.get(), plp.get()
            pw = [pwt[:, 0, :], pwt[:, 1, :]]
            pa = [pat[:, 0, :], pat[:, 1, :]]
            pg = pgt[:, 0, :]
            for d in range(2):
                k.mm(pw[d], lrT[0][0:32, tok], wup[:, d, :], start=True, stop=False)
                k.mm(pw[d], ones1[:], w0r[:, d, :], start=False, stop=True)
            for d in range(2):
                k.mm(pa[d], lrT[1][0:32, tok], aup[:, d, :], start=True, stop=False)
                k.mm(pa[d], ones1[:], a0r[:, d, :], start=False, stop=True)
            k.mm(pg, lrT[2][0:64, tok], gup[:])
            k.copy(o[:, 9, :], pg, eng="scalar")
            kx = t2.get()
            k.tt(kx[:], kcur, kkw[:], ALU.mult, eng="gpsimd")
            sq = t2.get()
            k.tt(sq[:], kx[:], kx[:], ALU.mult, eng="gpsimd")
            ss = smp.get()
            k.reduce(ss[:], sq[:].re("p (h d) -> p h d", d=64))
            rn = smp.get()
            k.act(rn[:], ss[:], AF.Sqrt, bias=C.eps_t[:])
            k.recip(rn[:], rn[:])
            kk = t2.get()
            k.tt(kk[:].re("p (h d) -> p h d", d=64), kx[:].re("p (h d) -> p h d", d=64), rn[:, :, None].bc([128, 4, 64]), ALU.mult)
            k.ts(o[:, 2, :], kk[:], -1.0, ALU.mult)
            for d in range(2):
                sg = t2.get()
                k.act(sg[:], pw[d], AF.Sigmoid)
                k.ts(o[:, 3 + d, :], sg[:], LW_SCALE, ALU.mult, eng="gpsimd")
                ar = t2.get()
                k.act(ar[:], pa[d], AF.Sigmoid)
                k.tt(o[:, 7 + d, :], kk[:], ar[:], ALU.mult, eng="gpsimd")
                t = t2.get()
                k.tt(t[:], ar[:], kaw[:], ALU.mult)
                k.tt(t[:], t[:], omka[:], ALU.add)
                k.tt(o[:, 5 + d, :], kcur, t[:], ALU.mult)
            k.dma("scalar", S["RWP"][r0:r0 + 128], o[:])


        run_lanes(k, 2, mkpools, body, range(NT // 128))


def rw_scan_stage(k, C, l, need_ctx):
    S = C.S
    with k.stage():
        tri = k.sb("tri", [64, 2, 64], F32)
        m01 = k.sb("m01", [64, 3, 8, 64], F32)
        nm = k.sb("nm", [64, 2, 8, 64], F32)
        id8 = k.sb("id8", [64, 8, 64], F32)
        ones64 = k.sb("ones64", [64, 64], F32)
        k.memset(ones64[:], 1.0)
        k.dma("sync", tri[:], C.I("c_tri")[:])
        k.dma("sync", m01[:], C.I("c_m01")[:])
        k.dma("sync", id8[:], C.I("c_ident8")[:])
        k.ts(nm[:], m01[:, 1:3], -1.0, ALU.mult)
        idn = C.ident[0:64, 0:64]
        St = k.sb("St", [64, 8, 64], F32)
        k.memset(St[:], 0.0)
        pps = Pool(k, "pp", [64, 8, 64], F32, 7, space="ps")
        psm = k.ps("psm", [64, 8, 2], F32)
        sbp = Pool(k, "w", [64, 8, 64], F32, 44)
        inv_sb = Pool(k, "iv", [64, 8, 64], F32, 8)
        inp_ = [Pool(k, f"in{i}", [64, 8, 64], F32, 2) for i in range(6)]
        smp = Pool(k, "sm", [64, 8], F32, 4)
        RWP = S["RWP"]
        slots = ((0, 0), (1, 1), (2, 2), (3, 4), (5, 6), (7, 8))
        for s in range(NCH):
            ch = (FORD[s], BORD[s])
            tl = [p.get() for p in inp_]
            for i, sl in enumerate(slots):
                for d in range(2):
                    r0 = ch[d] * 64
                    k.dma("sync", tl[i][:, d * 4:d * 4 + 4, :],
                          V(RWP.t[r0:r0 + 64, sl[d], :].rearrange("p (h d) -> p h d", h=4), [RWP.tok]))
            R8, V8, A8, LW8, K8, B8 = tl
            plc, ptot = pps.get(), pps.get()
            for d in range(2):
                k.mm(plc[:, d * 4:d * 4 + 4, :], tri[:, d, :], LW8[:, d * 4:d * 4 + 4, :])
            k.mm(ptot[:], ones64[:], LW8[:])
            for e in range(8):
                k.mm(psm[:, e, :], LW8[:, e, :], ones64[:, 0:2])
            lc, tot = sbp.get(), sbp.get()
            k.copy(lc[:], plc[:], eng="scalar")
            k.copy(tot[:], ptot[:], eng="vector")
            gCT = smp.get()
            k.act(gCT[:], psm[:, :, 0], AF.Exp)
            eg, egi, egp, ehat = sbp.get(), sbp.get(), sbp.get(), sbp.get()
            k.act(eg[:], lc[:], AF.Exp)
            k.act(egi[:], lc[:], AF.Exp, scale=-1.0)
            t = sbp.get()
            k.tt(t[:], lc[:], LW8[:], ALU.subtract, eng="gpsimd")
            k.act(egp[:], t[:], AF.Exp)
            t = sbp.get()
            k.tt(t[:], tot[:], lc[:], ALU.subtract)
            k.act(ehat[:], t[:], AF.Exp)
            At, Bt, Kt, Rt, Bh, Kh = [sbp.get() for _ in range(6)]
            k.tt(At[:], A8[:], egp[:], ALU.mult)
            k.tt(Bt[:], B8[:], egi[:], ALU.mult, eng="gpsimd")
            k.tt(Kt[:], K8[:], egi[:], ALU.mult)
            k.tt(Rt[:], R8[:], eg[:], ALU.mult, eng="gpsimd")
            k.tt(Bh[:], B8[:], ehat[:], ALU.mult)
            k.tt(Kh[:], K8[:], ehat[:], ALU.mult, eng="gpsimd")
            fmT = []
            for i, src in enumerate((At, Bt, Kt, Rt)):
                pT = pps.get()
                for e in range(8):
                    k.tr(pT[:, e, :], src[:, e, :], idn)
                dst = sbp.get()
                k.copy(dst[:], pT[:], eng=("scalar" if i % 2 == 0 else "vector"))
                fmT.append(dst)
            AtT, BtT, KtT, RtT = fmT
            def score(lhs, rhs, mask, eng):
                p_ = pps.get()
                for e in range(8):
                    k.mm(p_[:, e, :], lhs[:, e, :], rhs[:, e, :])
                o_ = sbp.get()
                k.tt(o_[:], p_[:], mask, ALU.mult, eng=eng)
                return o_
            M = score(AtT, BtT, nm[:, 1], "vector")
            A = score(BtT, AtT, nm[:, 0], "vector")
            AakT = score(KtT, AtT, m01[:, 1], "vector")
            ArbT = score(BtT, RtT, m01[:, 0], "vector")
            ArkT = score(KtT, RtT, m01[:, 0], "vector")
            X = tri_inverse(k, None, M, A, id8, pps, inv_sb)
            pW, pAkV = pps.get(), pps.get()
            for e in range(8):
                k.mm(pW[:, e, :], At[:, e, :], X[:, e, :])
            for e in range(8):
                k.mm(pAkV[:, e, :], AakT[:, e, :], V8[:, e, :])
            WT, AkV = sbp.get(), sbp.get()
            k.copy(WT[:], pW[:], eng="scalar")
            k.copy(AkV[:], pAkV[:], eng="vector")
            pUv = pps.get()
            for e in range(8):
                k.mm(pUv[:, e, :], X[:, e, :], AkV[:, e, :])
            Uv = sbp.get()
            k.copy(Uv[:], pUv[:], eng="scalar")
            pe = pps.get()
            for e in range(8):
                k.mm(pe[:, e, :], WT[:, e, :], St[:, e, :])
            E = sbp.get()
            k.tt(E[:], Uv[:], pe[:], ALU.add)
            py = pps.get()
            for e in range(8):
                k.mm(py[:, e, :], RtT[:, e, :], St[:, e, :], start=True, stop=False)
                k.mm(py[:, e, :], ArbT[:, e, :], E[:, e, :], start=False, stop=False)
                k.mm(py[:, e, :], ArkT[:, e, :], V8[:, e, :], start=False, stop=True)
            y8 = sbp.get()
            k.copy(y8[:], py[:], eng="scalar")
            for d in range(2):
                if need_ctx or ch[d] >= 4:
                    k.dma("scalar", S["RWY"][d, ch[d] * 64:(ch[d] + 1) * 64, :], y8[:, d * 4:d * 4 + 4, :].re("p h d -> p (h d)"))
            pS = pps.get()
            for e in range(8):
                k.mm(pS[:, e, :], Bh[:, e, :], E[:, e, :], start=True, stop=False)
                k.mm(pS[:, e, :], Kh[:, e, :], V8[:, e, :], start=False, stop=True)
            t = sbp.get()
            k.tt(t[:], St[:], gCT[:, :, None].bc([64, 8, 64]), ALU.mult)
            k.tt(St[:], t[:], pS[:], ALU.add)


def rw_post_stage(k, C, l, need_ctx):
    S = C.S
    with k.stage():
        lnw = k.sb("lnw", [128, 256], F32)
        lnb = k.sb("lnb", [128, 256], F32)
        rkw = k.sb("rkw", [128, 256], F32)
        bc_load(k, lnw[:], C.I("rw_ln_w").t[l], C.I("rw_ln_w").tok)
        bc_load(k, lnb[:], C.I("rw_ln_b").t[l], C.I("rw_ln_b").tok)
        bc_load(k, rkw[:], C.I("rw_r_kT").t[l], C.I("rw_r_kT").tok)
        lneps = k.sb("lneps", [128, 1], F32)
        k.memset(lneps[:], 64e-5)
        def hview(x):
            return x.re("p (h d) -> p h d", d=64)

        def mkpools(i):
            L = Lane()
            L.yp = Pool(k, f"y{i}", [128, 2, 256], F32, 2)
            L.pp = Pool(k, f"p{i}", [128, 10, 256], F32, 2)
            L.tp = Pool(k, f"t{i}", [128, 256], F32, 8)
            L.smp = Pool(k, f"sm{i}", [128, 4], F32, 6)
            return L

        def body(tc, L):
            yp, pp, tp, smp = L.yp, L.pp, L.tp, L.smp
            r0 = tc * 128
            yy = yp.get()
            k.dma("sync", yy[:], V(S["RWY"].t[:, r0:r0 + 128, :].rearrange("d p c -> p d c"), [S["RWY"].tok]))
            pr = pp.get()
            k.dma("sync", pr[:], S["RWP"][r0:r0 + 128])
            y = tp.get()
            k.tt(y[:], yy[:, 0, :], yy[:, 1, :], ALU.add)
            s1 = smp.get()
            k.reduce(s1[:], hview(y[:]))
            k.ts(s1[:], s1[:], -1.0 / 64, ALU.mult)
            yc = tp.get()
            k.tt(hview(yc[:]), hview(y[:]), s1[:, :, None].bc([128, 4, 64]), ALU.add)
            sq = tp.get()
            k.tt(sq[:], yc[:], yc[:], ALU.mult, eng="gpsimd")
            s2 = smp.get()
            k.reduce(s2[:], hview(sq[:]))
            rstd = smp.get()
            k.act(rstd[:], s2[:], AF.Sqrt, scale=1.0 / 64, bias=lneps[:])
            k.recip(rstd[:], rstd[:])
            yn = tp.get()
            k.tt(hview(yn[:]), hview(yc[:]), rstd[:, :, None].bc([128, 4, 64]), ALU.mult)
            k.tt(yn[:], yn[:], lnw[:], ALU.mult, eng="gpsimd")
            k.tt(yn[:], yn[:], lnb[:], ALU.add, eng="gpsimd")
            ks = tp.get()
            k.tt(ks[:], pr[:, 5, :], pr[:, 6, :], ALU.add, eng="gpsimd")
            k.tt(ks[:], ks[:], pr[:, 0, :], ALU.mult, eng="gpsimd")
            k.tt(ks[:], ks[:], rkw[:], ALU.mult, eng="gpsimd")
            bs = smp.get()
            k.reduce(bs[:], hview(ks[:]))
            bon = tp.get()
            k.tt(hview(bon[:]), hview(pr[:, 1, :]), bs[:, :, None].bc([128, 4, 64]), ALU.mult)
            k.tt(yn[:], yn[:], bon[:], ALU.add)
            out = tp.get()
            k.tt(out[:], yn[:], pr[:, 9, :], ALU.mult)
            k.dma("scalar", S["G"][r0:r0 + 128, 768:1024], out[:])


        run_lanes(k, 3, mkpools, body, [tc for tc in range(NT // 128) if not (tc < 2 and not need_ctx)])


def rw_stage(k, C, l, need_ctx):
    rw_pre_stage(k, C, l)
    scan_stage(k, C, l, need_ctx, do_dn=False, do_rw=True)
    rw_post_stage(k, C, l, need_ctx)


def dnrw_stage(k, C, l, need_ctx):
    dn_pre_stage(k, C, l)
    rw_pre_stage(k, C, l)
    scan_stage(k, C, l, need_ctx)
    dn_post_stage(k, C, l, need_ctx)
    rw_post_stage(k, C, l, need_ctx)


def dn_scan_setup(k, C, need_ctx, sh):
    X = Lane()
    X.S = C.S
    X.need_ctx = need_ctx
    X.tri, X.id8, X.idn = sh.tri, sh.id8, sh.idn
    X.mneg = k.sb("mneg", [64, 2, 8, 64], F32)
    k.dma("sync", X.mneg[:], C.I("c_mneg")[:])
    X.St = k.sb("dSt", [64, 8, 64], F32)
    X.Stb = k.sb("dStb", [64, 8, 64], BF16)
    k.memset(X.St[:], 0.0)
    k.memset(X.Stb[:], 0.0)
    X.pps = Pool(k, "dpp", [64, 8, 64], F32, sh.npp_dn, space="ps")
    X.psm = sh.psm_dn
    X.sbp = Pool(k, "dw", [64, 8, 64], F32, 14)
    X.sbb = Pool(k, "dwb", [64, 8, 64], BF16, 9)
    X.inv_f = Pool(k, "divf", [64, 8, 64], F32, 8)
    X.inv_b = Pool(k, "divb", [64, 8, 64], BF16, 2)
    X.qkp = Pool(k, "qk8", [64, 2, 8, 64], F32, 2)
    X.v8p = Pool(k, "v8", [64, 8, 64], F32, 2)
    X.gbp = Pool(k, "gb8", [64, 2, 8], F32, 2)
    X.smp = Pool(k, "dsm", [64, 8], F32, 8)
    return X


def dn_step(k, X, s):
    S = X.S
    tri, mneg, id8, idn, St, Stb = X.tri, X.mneg, X.id8, X.idn, X.St, X.Stb
    pps, psm, sbp, sbb, smp = X.pps, X.psm, X.sbp, X.sbb, X.smp
    ch = (FORD[s], BORD[s])
    qk8, v8, gb8 = X.qkp.get(), X.v8p.get(), X.gbp.get()
    for d in range(2):
        r0 = ch[d] * 64
        es = slice(d * 4, d * 4 + 4)
        k.dma("sync", qk8[:, :, es, :], V(S["DNQKV"].t[r0:r0 + 64, 0:512].rearrange("p (a h d) -> p a h d", a=2, h=4), [S["DNQKV"].tok]))
        k.dma("sync", v8[:, es, :], V(S["DNQKV"].t[r0:r0 + 64, 512:768].rearrange("p (h d) -> p h d", h=4), [S["DNQKV"].tok]))
        k.dma("sync", gb8[:, 0, es], S["DNGB"][r0:r0 + 64, d * 4:d * 4 + 4])
        k.dma("sync", gb8[:, 1, es], S["DNGB"][r0:r0 + 64, 8 + d * 4:8 + d * 4 + 4])
    beta_bc = gb8[:, 1, :, None].bc([64, 8, 64])
    for d in range(2):
        k.mm(psm[:, d * 4:d * 4 + 4], tri[:, d, :], gb8[:, 0, d * 4:d * 4 + 4])
    gcc = smp.get()
    k.copy(gcc[:], psm[:], eng="scalar")
    grep = sbp.get()
    k.copy(grep[:], gb8[:, 0, :, None].bc([64, 8, 64]))
    pgcb = pps.get()
    for e in range(8):
        k.mm(pgcb[:, e, :], grep[:, e, :], tri[:, e // 4, :])
    gcb = sbp.get()
    k.copy(gcb[:], pgcb[:], eng="scalar")
    D1 = sbp.get()
    k.tt(D1[:], gcb[:], gcc[:, :, None].bc([64, 8, 64]), ALU.subtract)
    egcb = sbp.get()
    k.act(egcb[:], gcb[:], AF.Exp)
    t = sbp.get()
    k.tt(t[:], D1[:], mneg[:, 0], ALU.add, eng="gpsimd")
    E1 = sbp.get()
    k.act(E1[:], t[:], AF.Exp)
    t = sbp.get()
    k.stt(t[:], D1[:], -1.0, mneg[:, 1], ALU.mult, ALU.add)
    E2 = sbp.get()
    k.act(E2[:], t[:], AF.Exp)
    egc = smp.get()
    k.act(egc[:], gcc[:], AF.Exp)
    ekd = smp.get()
    k.act(ekd[:, 0:4], D1[:, 0:4, 63], AF.Exp)
    k.act(ekd[:, 4:8], D1[:, 4:8, 0], AF.Exp)
    pKT = pps.get()
    for e in range(8):
        k.tr(pKT[:, e, :], qk8[:, 1, e, :], idn)
    KT = sbb.get()
    k.copy(KT[:], pKT[:], eng="scalar")
    pQT = pps.get()
    for e in range(8):
        k.tr(pQT[:, e, :], qk8[:, 0, e, :], idn)
    QT = sbb.get()
    k.copy(QT[:], pQT[:], eng="vector")
    pG = pps.get()
    for e in range(8):
        k.mm(pG[:, e, :], KT[:, e, :], KT[:, e, :])
    t = sbp.get()
    k.tt(t[:], pG[:], E2[:], ALU.mult)
    M = sbp.get()
    k.tt(M[:], t[:], beta_bc, ALU.mult, eng="gpsimd")
    pQK = pps.get()
    for e in range(8):
        k.mm(pQK[:, e, :], KT[:, e, :], QT[:, e, :])
    attnT = sbb.get()
    k.tt(attnT[:], pQK[:], E1[:], ALU.mult)
    pA = pps.get()
    for e in range(8):
        k.tr(pA[:, e, :], M[:, e, :], idn)
    A = sbp.get()
    k.copy(A[:], pA[:], eng="scalar")
    _, Xi = tri_inverse(k, M, A, id8, pps, X.inv_f, X.inv_b)
    Vb = sbb.get()
    k.tt(Vb[:], v8[:], beta_bc, ALU.mult, eng="gpsimd")
    bg = smp.get()
    k.tt(bg[:], gb8[:, 1, :], egc[:], ALU.mult)
    KBG = sbb.get()
    k.tt(KBG[:], qk8[:, 1], bg[:, :, None].bc([64, 8, 64]), ALU.mult, eng="gpsimd")
    pU = pps.get()
    for e in range(8):
        k.mm(pU[:, e, :], Xi[:, e, :], Vb[:, e, :])
    U = sbp.get()
    k.copy(U[:], pU[:], eng="scalar")
    pW = pps.get()
    for e in range(8):
        k.mm(pW[:, e, :], KBG[:, e, :], Xi[:, e, :])
    WT = sbb.get()
    k.copy(WT[:], pW[:], eng="vector")
    QdT = sbb.get()
    k.tt(QdT[:], QT[:], egcb[:], ALU.mult, eng="gpsimd")
    Kd = sbb.get()
    k.tt(Kd[:], qk8[:, 1], ekd[:, :, None].bc([64, 8, 64]), ALU.mult, eng="gpsimd")
    glast = smp.get()
    k.copy(glast[:, 0:4], egcb[:, 0:4, 63], eng="gpsimd")
    k.copy(glast[:, 4:8], egcb[:, 4:8, 0], eng="gpsimd")
    pv = pps.get()
    for e in range(8):
        k.mm(pv[:, e, :], WT[:, e, :], Stb[:, e, :])
    vnew = sbb.get()
    k.tt(vnew[:], U[:], pv[:], ALU.subtract)
    po = pps.get()
    for e in range(8):
        k.mm(po[:, e, :], QdT[:, e, :], Stb[:, e, :], start=True, stop=False)
        k.mm(po[:, e, :], attnT[:, e, :], vnew[:, e, :], start=False, stop=True)
    pS = pps.get()
    for e in range(8):
        k.mm(pS[:, e, :], Kd[:, e, :], vnew[:, e, :])
    t = sbp.get()
    k.tt(t[:], St[:], glast[:, :, None].bc([64, 8, 64]), ALU.mult, eng="gpsimd")
    k.tt(St[:], t[:], pS[:], ALU.add)
    k.copy(Stb[:], St[:], eng="gpsimd")
    o8 = sbp.get()
    k.copy(o8[:], po[:], eng="scalar")
    for d in range(2):
        if X.need_ctx or ch[d] >= 4:
            k.dma("scalar", S["DNO"][d, ch[d] * 64:(ch[d] + 1) * 64, :], o8[:, d * 4:d * 4 + 4, :].re("p h d -> p (h d)"))


def rw_scan_setup(k, C, need_ctx, sh):
    X = Lane()
    psl = sh.rw_psl
    X.psl = psl
    X.S = C.S
    X.need_ctx = need_ctx

    def const(name, shape, dtype=F32):
        full = [128] + list(shape[1:]) if psl.start else list(shape)
        return TV(k.sb(name, full, dtype), psl)
    X.tri = const("rtri", [64, 2, 64])
    X.id8 = const("rid8", [64, 8, 64])
    X.m01 = const("m01", [64, 3, 8, 64])
    X.nm = const("nm", [64, 2, 8, 64])
    X.ones64 = const("ones64", [64, 64])
    idf = const("ridf", [64, 64])
    X.idb = const("ridb", [64, 64], BF16)
    k.dma("sync", X.tri[:], C.I("c_tri")[:])
    k.dma("sync", X.id8[:], C.I("c_ident8")[:])
    k.dma("sync", X.m01[:], C.I("c_m01")[:])
    k.dma("sync", idf[:], C.I("identD")[0:64, 0:64])
    k.copy(X.idb[:], idf[:])
    k.memset(X.ones64[:], 1.0)
    k.ts(X.nm[:], X.m01[:, 1:3], -1.0, ALU.mult)
    X.St = const("rSt", [64, 8, 64])
    X.Stb = const("rStb", [64, 8, 64], BF16)
    k.memset(X.St[:], 0.0)
    k.memset(X.Stb[:], 0.0)
    X.pps = PoolV(k, "rpp", [64, 8, 64], F32, sh.npp_rw, psl, space="ps")
    X.ppb = PoolV(k, "rppb", [64, 16, 64], BF16, 1, psl, space="ps")
    X.psm = PoolV(k, "rpsm", [64, 8, 2], F32, 1, psl, space="ps").get()
    X.sbp = PoolV(k, "rw", [64, 8, 64], F32, 24, psl)
    X.sbb = PoolV(k, "rwb", [64, 8, 64], BF16, 9, psl)
    X.inv_f = PoolV(k, "rivf", [64, 8, 64], F32, 8, psl)
    X.inp_ = [PoolV(k, f"rin{i}", [64, 8, 64], F32, 2, psl) for i in range(6)]
    X.smp = PoolV(k, "rsm", [64, 8], F32, 4, psl)
    return X


RW_SLOTS = ((0, 0), (1, 1), (2, 2), (3, 4), (5, 6), (7, 8))


def rw_step(k, X, s):
    S = X.S
    tri, m01, nm, id8, idb, ones64, St, Stb = X.tri, X.m01, X.nm, X.id8, X.idb, X.ones64, X.St, X.Stb
    pps, ppb, psm, sbp, sbb, smp = X.pps, X.ppb, X.psm, X.sbp, X.sbb, X.smp
    RWP = S["RWP"]
    ch = (FORD[s], BORD[s])
    tl = [p.get() for p in X.inp_]
    for i, sl in enumerate(RW_SLOTS):
        for d in range(2):
            r0 = ch[d] * 64
            k.dma("sync", tl[i][:, d * 4:d * 4 + 4, :],
                  V(RWP.t[r0:r0 + 64, sl[d], :].rearrange("p (h d) -> p h d", h=4), [RWP.tok]))
    R8, V8, A8, LW8, K8, B8 = tl
    plc = pps.get()
    for d in range(2):
        k.mm(plc[:, d * 4:d * 4 + 4, :], tri[:, d, :], LW8[:, d * 4:d * 4 + 4, :])
    lc = sbp.get()
    k.copy(lc[:], plc[:], eng="scalar")
    ptot = pps.get()
    k.mm(ptot[:], ones64[:], LW8[:])
    tot = sbp.get()
    k.copy(tot[:], ptot[:], eng="vector")
    for e in range(8):
        k.mm(psm[:, e, :], LW8[:, e, :], ones64[:, 0:2])
    gCT = smp.get()
    k.act(gCT[:], psm[:, :, 0], AF.Exp)
    eg, egi, egp, ehat = sbp.get(), sbp.get(), sbp.get(), sbp.get()
    k.act(eg[:], lc[:], AF.Exp)
    k.act(egi[:], lc[:], AF.Exp, scale=-1.0)
    t = sbp.get()
    k.tt(t[:], lc[:], LW8[:], ALU.subtract, eng="gpsimd")
    k.act(egp[:], t[:], AF.Exp)
    t = sbp.get()
    k.tt(t[:], tot[:], lc[:], ALU.subtract)
    k.act(ehat[:], t[:], AF.Exp)
    At, Bh, Kh = sbp.get(), sbp.get(), sbp.get()
    Atb, Btb, Ktb, Rtb = sbb.get(), sbb.get(), sbb.get(), sbb.get()
    k.tt(At[:], A8[:], egp[:], ALU.mult)
    k.copy(Atb[:], At[:], eng="gpsimd")
    k.tt(Btb[:], B8[:], egi[:], ALU.mult, eng="gpsimd")
    k.tt(Ktb[:], K8[:], egi[:], ALU.mult)
    k.tt(Rtb[:], R8[:], eg[:], ALU.mult, eng="gpsimd")
    k.tt(Bh[:], B8[:], ehat[:], ALU.mult)
    k.tt(Kh[:], K8[:], ehat[:], ALU.mult, eng="gpsimd")
    fmT = []
    for i, src in enumerate((Atb, Btb, Ktb, Rtb)):
        pT = ppb.get()
        for e in range(8):
            k.tr(pT[:, e, :], src[:, e, :], idb[:])
        dst = sbb.get()
        k.copy(dst[:], pT[:, 0:8, :], eng=("scalar" if i % 2 == 0 else "vector"))
        fmT.append(dst)
    AtT, BtT, KtT, RtT = fmT

    def score(lhs, rhs, mask):
        p_ = pps.get()
        for e in range(8):
            k.mm(p_[:, e, :], lhs[:, e, :], rhs[:, e, :])
        o_ = sbp.get()
        k.tt(o_[:], p_[:], mask, ALU.mult)
        return o_
    M = score(AtT, BtT, nm[:, 1])
    A = score(BtT, AtT, nm[:, 0])
    AakT = score(KtT, AtT, m01[:, 1])
    ArbT = score(BtT, RtT, m01[:, 0])
    ArkT = score(KtT, RtT, m01[:, 0])
    Xi, _ = tri_inverse(k, M, A, id8, pps, X.inv_f, None)
    pW = pps.get()
    for e in range(8):
        k.mm(pW[:, e, :], At[:, e, :], Xi[:, e, :])
    WT = sbp.get()
    k.copy(WT[:], pW[:], eng="scalar")
    pAkV = pps.get()
    for e in range(8):
        k.mm(pAkV[:, e, :], AakT[:, e, :], V8[:, e, :])
    AkV = sbp.get()
    k.copy(AkV[:], pAkV[:], eng="vector")
    pUv = pps.get()
    for e in range(8):
        k.mm(pUv[:, e, :], Xi[:, e, :], AkV[:, e, :])
    Uv = sbp.get()
    k.copy(Uv[:], pUv[:], eng="scalar")
    pe = pps.get()
    for e in range(8):
        k.mm(pe[:, e, :], WT[:, e, :], St[:, e, :])
    E = sbp.get()
    k.tt(E[:], Uv[:], pe[:], ALU.add)
    py = pps.get()
    for e in range(8):
        k.mm(py[:, e, :], RtT[:, e, :], Stb[:, e, :], start=True, stop=False)
        k.mm(py[:, e, :], ArbT[:, e, :], E[:, e, :], start=False, stop=False)
        k.mm(py[:, e, :], ArkT[:, e, :], V8[:, e, :], start=False, stop=True)
    y8 = sbp.get()
    k.copy(y8[:], py[:], eng="scalar")
    for d in range(2):
        if X.need_ctx or ch[d] >= 4:
            k.dma("scalar", S["RWY"][d, ch[d] * 64:(ch[d] + 1) * 64, :], y8[:, d * 4:d * 4 + 4, :].re("p h d -> p (h d)"))
    pS = pps.get()
    for e in range(8):
        k.mm(pS[:, e, :], Bh[:, e, :], E[:, e, :], start=True, stop=False)
        k.mm(pS[:, e, :], Kh[:, e, :], V8[:, e, :], start=False, stop=True)
    t = sbp.get()
    k.tt(t[:], St[:], gCT[:, :, None].bc([64, 8, 64]), ALU.mult, eng="gpsimd")
    k.tt(St[:], t[:], pS[:], ALU.add)
    k.copy(Stb[:], St[:], eng="gpsimd")


def scan_stage(k, C, l, need_ctx, do_dn=True, do_rw=True):
    with k.stage():
        sh = Lane()
        sh.tri = k.sb("tri", [64, 2, 64], F32)
        sh.id8 = k.sb("id8", [64, 8, 64], F32)
        k.dma("sync", sh.tri[:], C.I("c_tri")[:])
        k.dma("sync", sh.id8[:], C.I("c_ident8")[:])
        sh.idn = C.ident[0:64, 0:64]
        import os
        both = do_dn and do_rw
        sh.npp_dn = 3 if both else 7
        sh.npp_rw = 2 if both else 5
        sh.rw_psl = slice(64, 128) if os.environ.get("RW_HI", "1") == "1" else slice(0, 64)
        psm = k.ps("psm", [64, 8], F32)
        sh.psm_dn = psm[:]
        Xd = dn_scan_setup(k, C, need_ctx, sh) if do_dn else None
        Xr = rw_scan_setup(k, C, need_ctx, sh) if do_rw else None
        for s in range(NCH):
            lists = []
            if do_dn:
                with k.defer() as La:
                    dn_step(k, Xd, s)
                lists.append(La)
            if do_rw:
                with k.defer() as Lb:
                    rw_step(k, Xr, s)
                lists.append(Lb)
            k.replay(lists)
```
